# Optimizing a Trainium2 kernel written in Bass

```python
import math
import jax, jax.numpy as jnp
from jax import lax
import numpy as np

D_MODEL = 1024
BATCH = 8
SEQ = 4096
DEPTH = 2

HEAD_DIM = 64
ATT_HEADS = 6
CONV_GROUPS = 4
RWKV_HEADS = 6
ATT_W = ATT_HEADS * HEAD_DIM
CONV_W = CONV_GROUPS * HEAD_DIM
RWKV_W = RWKV_HEADS * HEAD_DIM
D_MIX = ATT_W + CONV_W + RWKV_W
DIFF_HALF = HEAD_DIM // 2
CONV_K = 3
DECAY_LORA = 64
ICLR_LORA = 64
GATE_LORA = 128
RWKV_SPLITS = (RWKV_W, RWKV_W, RWKV_W, DECAY_LORA, ICLR_LORA, GATE_LORA)
RWKV_COLS = 3 * RWKV_W + DECAY_LORA + ICLR_LORA + GATE_LORA
N_IN = 3 * ATT_W + 3 * CONV_W + RWKV_COLS
D_FF = 4 * D_MODEL
Q_BLOCK = 128
NORM_EPS = 1e-6
GN_EPS = 64e-5

kernel_name = 'hymba_diffattn_shortconv_rwkv7'


def _split_points(sizes):
    return [int(s) for s in np.cumsum(sizes)[:-1]]


def rms_norm(x, g):
    xf = x.astype(jnp.float32)
    y = xf * lax.rsqrt(jnp.mean(xf * xf, axis=-1, keepdims=True) + NORM_EPS)
    return (y * g.astype(jnp.float32)).astype(x.dtype)


def diff_attention(q, k, v, lam, subln_g, lambda_init):
    bsz, seq, _ = q.shape
    nb = seq // Q_BLOCK
    q = q.astype(jnp.float32).reshape(bsz, seq, ATT_HEADS, 2, DIFF_HALF).transpose(3, 0, 2, 1, 4)
    k = k.astype(jnp.float32).reshape(bsz, seq, ATT_HEADS, 2, DIFF_HALF).transpose(3, 0, 2, 1, 4)
    v = v.astype(jnp.float32).reshape(bsz, seq, ATT_HEADS, HEAD_DIM).transpose(0, 2, 1, 3)
    q_blocks = q.reshape(2, bsz, ATT_HEADS, nb, Q_BLOCK, DIFF_HALF).transpose(3, 0, 1, 2, 4, 5)
    key_pos = jnp.arange(seq)
    lam = lam.astype(jnp.float32)
    scale = DIFF_HALF ** -0.5

    def one_block(args):
        q_blk, blk = args
        q_pos = blk * Q_BLOCK + jnp.arange(Q_BLOCK)
        causal = key_pos[None, :] <= q_pos[:, None]
        s = jnp.einsum('nbhqd,nbhkd->nbhqk', q_blk, k) * scale
        p = jax.nn.softmax(jnp.where(causal, s, -jnp.inf), axis=-1)
        return jnp.einsum('bhqk,bhkd->bhqd', p[0] - lam * p[1], v)

    o = lax.map(one_block, (q_blocks, jnp.arange(nb)))
    o = o.transpose(1, 0, 3, 2, 4).reshape(bsz, seq, ATT_HEADS, HEAD_DIM)
    o = o * lax.rsqrt(jnp.mean(o * o, axis=-1, keepdims=True) + NORM_EPS)
    o = o * subln_g.astype(jnp.float32) * (1.0 - lambda_init)
    return o.reshape(bsz, seq, ATT_W)


def short_conv(b, c, u, conv_w):
    seq = u.shape[1]
    z = c * u
    zp = jnp.pad(z, ((0, 0), (CONV_K - 1, 0), (0, 0)))
    y = sum(conv_w[j] * zp[:, j:j + seq] for j in range(CONV_K))
    return b * y


def rwkv7_time_mix(p, shift_mu, w0, w_up, a0, a_up, g_up, k_k, k_a, r_k, lnx_g, lnx_b):
    f32 = jnp.float32
    bsz, seq, _ = p.shape
    p = p.astype(f32)
    p_prev = jnp.pad(p, ((0, 0), (1, 0), (0, 0)))[:, :seq]
    z = p + shift_mu.astype(f32) * (p_prev - p)
    r, k, v, w_dn, a_dn, g_dn = jnp.split(z, _split_points(RWKV_SPLITS), axis=-1)
    w = w0.astype(f32) + jnp.tanh(w_dn) @ w_up.astype(f32)
    decay = jnp.exp(-jnp.exp(-jax.nn.softplus(-w) - 0.5))
    a = jax.nn.sigmoid(a0.astype(f32) + a_dn @ a_up.astype(f32))
    g = jax.nn.sigmoid(g_dn) @ g_up.astype(f32)
    kk = k * k_k.astype(f32)
    k = k * (1.0 + (a - 1.0) * k_a.astype(f32))
    heads = lambda t: t.reshape(bsz, seq, RWKV_HEADS, HEAD_DIM)
    r, decay, k, v, kk, a = [heads(t) for t in (r, decay, k, v, kk, a)]
    kk = kk / jnp.maximum(jnp.sqrt(jnp.sum(kk * kk, axis=-1, keepdims=True)), 1e-12)

    def step(state, inp):
        r_t, w_t, k_t, v_t, kk_t, a_t = inp
        sa = jnp.einsum('bhvk,bhk->bhv', state, -kk_t)
        state = (state * w_t[:, :, None, :]
                 + sa[..., None] * (kk_t * a_t)[:, :, None, :]
                 + v_t[..., None] * k_t[:, :, None, :])
        y_t = jnp.einsum('bhvk,bhk->bhv', state, r_t)
        return state, y_t

    xs = tuple(t.transpose(1, 0, 2, 3) for t in (r, decay, k, v, kk, a))
    state0 = jnp.zeros((bsz, RWKV_HEADS, HEAD_DIM, HEAD_DIM), f32)
    _, y = lax.scan(step, state0, xs)
    y = y.transpose(1, 0, 2, 3)
    mu = jnp.mean(y, axis=-1, keepdims=True)
    var = jnp.mean(jnp.square(y - mu), axis=-1, keepdims=True)
    yn = (y - mu) * lax.rsqrt(var + GN_EPS)
    yn = yn * lnx_g.astype(f32).reshape(RWKV_HEADS, HEAD_DIM) + lnx_b.astype(f32).reshape(RWKV_HEADS, HEAD_DIM)
    bonus = jnp.sum(r * k * r_k.astype(f32), axis=-1, keepdims=True) * v
    return (yn + bonus).reshape(bsz, seq, RWKV_W) * g


def setup_inputs(seed: int = 0) -> dict:
    key = jax.random.key(seed)
    ks = iter(jax.random.split(key, 32))
    f32 = jnp.float32
    nrm = lambda shape, scale: jax.random.normal(next(ks), shape, f32) * scale
    uni = lambda shape: jax.random.uniform(next(ks), shape, f32)
    L = DEPTH
    return {
        'x': nrm((BATCH, SEQ, D_MODEL), 1.0),
        'norm_mix_g': 1.0 + nrm((L, D_MODEL), 0.02),
        'w_in': nrm((L, D_MODEL, N_IN), D_MODEL ** -0.5),
        'lam_q1': nrm((L, DIFF_HALF), 0.1),
        'lam_k1': nrm((L, DIFF_HALF), 0.1),
        'lam_q2': nrm((L, DIFF_HALF), 0.1),
        'lam_k2': nrm((L, DIFF_HALF), 0.1),
        'subln_g': 1.0 + nrm((L, HEAD_DIM), 0.02),
        'conv_w': nrm((L, CONV_K, CONV_W), CONV_K ** -0.5),
        'shift_mu': uni((L, RWKV_COLS)),
        'rwkv_w0': -6.0 + 5.0 * uni((L, RWKV_W)),
        'rwkv_w_up': nrm((L, DECAY_LORA, RWKV_W), 0.1 * DECAY_LORA ** -0.5),
        'rwkv_a0': nrm((L, RWKV_W), 0.1),
        'rwkv_a_up': nrm((L, ICLR_LORA, RWKV_W), 0.5 * ICLR_LORA ** -0.5),
        'rwkv_g_up': nrm((L, GATE_LORA, RWKV_W), GATE_LORA ** -0.5),
        'rwkv_k_k': 0.85 + nrm((L, RWKV_W), 0.02),
        'rwkv_k_a': 1.0 + nrm((L, RWKV_W), 0.02),
        'rwkv_r_k': nrm((L, RWKV_HEADS, HEAD_DIM), 0.1),
        'lnx_g': 1.0 + nrm((L, RWKV_W), 0.02),
        'lnx_b': nrm((L, RWKV_W), 0.02),
        'w_out': nrm((L, D_MIX, D_MODEL), D_MIX ** -0.5),
        'norm_mlp_g': 1.0 + nrm((L, D_MODEL), 0.02),
        'w_mlp_up': nrm((L, D_MODEL, D_FF), D_MODEL ** -0.5),
        'w_mlp_down': nrm((L, D_FF, D_MODEL), D_FF ** -0.5),
        'final_norm_g': 1.0 + nrm((D_MODEL,), 0.02),
    }


def reference(x, norm_mix_g, w_in, lam_q1, lam_k1, lam_q2, lam_k2, subln_g, conv_w,
              shift_mu, rwkv_w0, rwkv_w_up, rwkv_a0, rwkv_a_up, rwkv_g_up, rwkv_k_k,
              rwkv_k_a, rwkv_r_k, lnx_g, lnx_b, w_out, norm_mlp_g, w_mlp_up, w_mlp_down,
              final_norm_g):
    f32 = jnp.float32
    for l in range(DEPTH):
        h = rms_norm(x, norm_mix_g[l])
        proj = h @ w_in[l]
        p_att, p_conv, p_rwkv = jnp.split(proj, [3 * ATT_W, 3 * ATT_W + 3 * CONV_W], axis=-1)

        q, k, v = jnp.split(p_att, 3, axis=-1)
        lambda_init = 0.8 - 0.6 * math.exp(-0.3 * l)
        lam = (jnp.exp(jnp.sum(lam_q1[l].astype(f32) * lam_k1[l].astype(f32)))
               - jnp.exp(jnp.sum(lam_q2[l].astype(f32) * lam_k2[l].astype(f32)))
               + lambda_init)
        o_att = diff_attention(q, k, v, lam, subln_g[l], lambda_init)

        b_gate, c_gate, u = jnp.split(p_conv, 3, axis=-1)
        o_conv = short_conv(b_gate, c_gate, u, conv_w[l])

        o_rwkv = rwkv7_time_mix(p_rwkv, shift_mu[l], rwkv_w0[l], rwkv_w_up[l], rwkv_a0[l],
                                rwkv_a_up[l], rwkv_g_up[l], rwkv_k_k[l], rwkv_k_a[l],
                                rwkv_r_k[l], lnx_g[l], lnx_b[l])

        mixed = jnp.concatenate([o_att.astype(x.dtype), o_conv.astype(x.dtype),
                                 o_rwkv.astype(x.dtype)], axis=-1)
        x = x + mixed @ w_out[l]

        h = rms_norm(x, norm_mlp_g[l])
        x = x + jnp.square(jax.nn.relu(h @ w_mlp_up[l])) @ w_mlp_down[l]
    return rms_norm(x, final_norm_g)
```

```python
import math
from contextlib import ExitStack
import numpy as np
import concourse.bass as bass
import concourse.mybir as mybir
from concourse.bass_utils import run_bass_kernel_spmd

F32 = mybir.dt.float32
BF16 = mybir.dt.bfloat16
ALU = mybir.AluOpType
AF = mybir.ActivationFunctionType
AX = mybir.AxisListType

S = 4096
D = 1024
DFF = 4096
NIN = 3328
G = 512
NG = S // G
NB = G // 128
DEPTH = 2
HD = 64
NORM_EPS = 1e-6
GN_EPS = 64e-5

INPUT_SHAPES = {
    'x': [S, D], 'norm_mix_g': [2, D], 'w_in': [2, D, NIN], 'lam_q1': [2, 32], 'lam_k1': [2, 32],
    'lam_q2': [2, 32], 'lam_k2': [2, 32], 'subln_g': [2, 64], 'conv_w': [2, 3, 256],
    'shift_mu': [2, 1408], 'rwkv_w0': [2, 384], 'rwkv_w_up': [2, 64, 384], 'rwkv_a0': [2, 384],
    'rwkv_a_up': [2, 64, 384], 'rwkv_g_up': [2, 128, 384], 'rwkv_k_k': [2, 384], 'rwkv_k_a': [2, 384],
    'rwkv_r_k': [2, 6, 64], 'lnx_g': [2, 384], 'lnx_b': [2, 384], 'w_out': [2, D, D],
    'norm_mlp_g': [2, D], 'w_mlp_up': [2, D, DFF], 'w_mlp_down': [2, DFF, D], 'final_norm_g': [D],
}


class Tk:
    __slots__ = ("w", "r", "name")

    def __init__(self, name=""):
        self.w = None
        self.r = {}
        self.name = name


class KB:
    EPOCH = 6000
    NDS = 24

    def __init__(self, nc):
        self.nc = nc
        self.eng = {"pe": nc.tensor, "act": nc.scalar, "dve": nc.vector, "pool": nc.gpsimd, "sp": nc.sync}
        self.cnt = {e: 0 for e in self.eng}
        self.semh = {}
        self.seen = {e: {} for e in self.eng}
        self.maxep = {e: {} for e in self.eng}
        self.dsem = [("dma", i) for i in range(self.NDS)]
        for kx in self.dsem:
            self.semh[kx] = nc.alloc_semaphore(f"dma{kx[1]}")
        self.dval = [0] * self.NDS
        self.dnext = 0
        self.ndma = 0
        self.nwait = 0
        self._n = 0
        self.root = ExitStack()
        self.scope = None

    def sb(self, name, shape, dt):
        self._n += 1
        cm = self.nc.sbuf_tensor("%s_%d" % (name, self._n), list(shape), dt)
        return (self.scope or self.root).enter_context(cm)

    def ps(self, name, shape, dt):
        self._n += 1
        cm = self.nc.psum_tensor("%s_%d" % (name, self._n), list(shape), dt)
        return (self.scope or self.root).enter_context(cm)

    def push_scope(self):
        self.scope = ExitStack()

    def pop_scope(self):
        self.barrier()
        self.scope.close()
        self.scope = None

    def _cursem(self, e):
        key = (e, self.cnt[e] // self.EPOCH)
        if key not in self.semh:
            self.semh[key] = self.nc.alloc_semaphore(f"s_{e}_{key[1]}")
        return key

    def _wait(self, e, tok):
        key, val = tok[0], tok[1]
        seen = self.seen[e]
        if seen.get(key, 0) >= val:
            return
        if key[0] != "dma":
            if self.maxep[e].get(key[0], -1) > key[1]:
                return
            self.maxep[e][key[0]] = max(self.maxep[e].get(key[0], -1), key[1])
        self.eng[e].wait_ge(self.semh[key], val)
        self.nwait += 1
        seen[key] = val

    def _deps(self, e, reads, writes, is_dma):
        for t in reads:
            if t.w is not None:
                tok = t.w
                if (not is_dma) and tok[2] == e and e == "pe":
                    continue
                self._wait(e, tok)
        for t in writes:
            if t.w is not None:
                tok = t.w
                if is_dma or tok[2] != e or tok[3]:
                    self._wait(e, tok)
            for rk, tok in t.r.items():
                if isinstance(tok, list):
                    for tk in tok:
                        self._wait(e, tk)
                elif is_dma or tok[2] != e:
                    self._wait(e, tok)

    def _record(self, tok, reads, writes):
        for t in reads:
            if tok[3]:
                t.r.setdefault("dma", []).append(tok)
            else:
                t.r[tok[2]] = tok
        for t in writes:
            t.w = tok
            t.r = {}

    def op(self, e, fn, r=(), w=()):
        self._deps(e, r, w, False)
        key = self._cursem(e)
        ins = fn(self.eng[e])
        self.cnt[e] += 1
        val = self.cnt[e] - key[1] * self.EPOCH
        ins.then_inc(self.semh[key], 1)
        tok = (key, val, e, False)
        self._record(tok, r, w)
        return tok

    def dma(self, q, out, in_, r=(), w=(), **kw):
        self._deps(q, r, w, True)
        slot = self.dnext
        self.dnext = (self.dnext + 1) % self.NDS
        key = self.dsem[slot]
        if self.dval[slot] > 0:
            self._wait(q, (key, self.dval[slot], "dma", True))
        ins = self.eng[q].dma_start(out=out, in_=in_, **kw)
        self.dval[slot] += 16
        ins.then_inc(self.semh[key], 16)
        tok = (key, self.dval[slot], "dma", True)
        self._record(tok, r, w)
        self.ndma += 1
        return tok

    def barrier(self):
        lasts = {}
        for e in ("pe", "act", "dve", "pool"):
            if self.cnt[e] > 0:
                key = (e, (self.cnt[e] - 1) // self.EPOCH)
                lasts[e] = (key, self.cnt[e] - key[1] * self.EPOCH, e, False)
        for e in self.eng:
            for e2, tok in lasts.items():
                if e2 != e:
                    self._wait(e, tok)
            for slot in range(self.NDS):
                if self.dval[slot] > 0:
                    self._wait(e, (self.dsem[slot], self.dval[slot], "dma", True))

    def finish(self):
        for slot in range(self.NDS):
            if self.dval[slot] > 0:
                self._wait("sp", (self.dsem[slot], self.dval[slot], "dma", True))


def bcast_mid(ap2d, n):
    p, j = ap2d.shape
    return ap2d.unsqueeze(2).to_broadcast([p, j, n])


class Prog:
    def __init__(self, nc, phases, dbg=None):
        self.nc = nc
        self.k = KB(nc)
        self.phases = phases
        self.dbg = dbg if dbg is not None else {}
        k = self.k
        self.inp = {}
        for name, shp in INPUT_SHAPES.items():
            self.inp[name] = nc.dram_tensor(name, shp, F32, kind="ExternalInput").ap()
        self.y = nc.dram_tensor("y", [S, D], F32, kind="ExternalOutput").ap()
        self.xa = nc.dram_tensor("xa", [S, D], F32).ap()
        self.xb = nc.dram_tensor("xb", [S, D], F32).ap()
        def dram(name, shape, dt):
            kind = "ExternalOutput" if name in self.dbg else ("ExternalInput" if ("in:" + name) in self.dbg else "Internal")
            return nc.dram_tensor(name, shape, dt, kind=kind).ap()
        self.prw = dram("prw", [1408, S], F32)
        self.t_prw = [Tk() for _ in range(NG)]
        self.mxa = dram("mxa", [5 * 128, S], BF16)
        self.mxb = dram("mxb", [3 * 128, S], BF16) if "mxb" in self.dbg else None
        self.t_mxb = Tk()
        self.t_mxa = [Tk() for _ in range(NG)]
        self.t_xa = [Tk("xa%d" % i) for i in range(NG)]
        self.t_xb = [Tk("xb%d" % i) for i in range(NG)]
        self.t_x = [Tk("x%d" % i) for i in range(NG)]
        self.t_y = [Tk("y%d" % i) for i in range(NG)]
        self.ident_b = k.sb("ident_b", [128, 128], BF16)
        self.ident_f = k.sb("ident_f", [128, 128], F32)
        self.t_const = Tk("const")
        self.eps_t = k.sb("eps_t", [128, 1], F32)
        k.op("pool", lambda h: h.memset(self.ident_b[:], 1.0), w=[self.t_const])
        k.op("pool", lambda h: h.affine_select(out=self.ident_b[:], in_=self.ident_b[:], pattern=[[1, 128]],
                                               compare_op=ALU.is_equal, fill=0.0, base=0, channel_multiplier=-1),
             r=[self.t_const], w=[self.t_const])
        k.op("pool", lambda h: h.memset(self.ident_f[:], 1.0), w=[self.t_const])
        k.op("pool", lambda h: h.affine_select(out=self.ident_f[:], in_=self.ident_f[:], pattern=[[1, 128]],
                                               compare_op=ALU.is_equal, fill=0.0, base=0, channel_multiplier=-1),
             r=[self.t_const], w=[self.t_const])
        k.op("pool", lambda h: h.memset(self.eps_t[:], NORM_EPS), w=[self.t_const])
        self.gcol = k.sb("gcol", [128, 4, 8], F32)
        self.t_gcol = Tk("gcol")
        for n, (nm, l) in enumerate([("norm_mix_g", 0), ("norm_mix_g", 1), ("norm_mlp_g", 0), ("norm_mlp_g", 1)]):
            k.dma("sp", self.gcol[:, n, :], self.inp[nm][l].rearrange("(j p) -> p j", p=128),
                  w=[self.t_gcol], allow_slow_non_contiguous=True)
        self.gfin = k.sb("gfin", [128, D], F32)
        self.t_gfin = Tk("gfin")
        k.dma("sp", self.gfin[:], self.inp["final_norm_g"].unsqueeze(0).partition_broadcast(128), w=[self.t_gfin])
        self.xt = [k.sb("xt%d" % i, [128, D], F32) for i in range(2)]
        self.t_xt = [Tk("xt%d" % i) for i in range(2)]
        self.xn = [k.sb("xn%d" % i, [128, D], BF16) for i in range(2)]
        self.t_xn = [Tk("xn%d" % i) for i in range(2)]
        self.junk = k.sb("junk", [128, D], BF16)
        self.t_junk = Tk("junk")
        self.stat = [k.sb("stat%d" % i, [128, 4], F32) for i in range(2)]
        self.t_stat = [Tk("stat%d" % i) for i in range(2)]
        self.hT = k.sb("hT", [128, 8, G], BF16)
        self.t_hT = [Tk("hT%d" % i) for i in range(NB)]
        self.psT = [k.ps("psT%d" % i, [128, 8, 128], BF16) for i in range(1)]
        self.t_psT = [Tk("psT%d" % i) for i in range(1)]
        self.nblk = 0
        self.dumped = set()

    def dump(self, name, ap, toks):
        if "dump" not in self.dbg or name in self.dumped:
            return
        self.dumped.add(name)
        d = self.nc.dram_tensor("d_" + name, list(ap.shape), ap.dtype, kind="ExternalOutput").ap()
        self.k.dma("sp", d, ap, r=list(toks), w=[Tk()])

    def norm_block(self, src_ap, src_tk, gidx, blk):
        k = self.k
        i = self.nblk % 2
        self.nblk += 1
        xt, t_xt, xn, t_xn, st, t_st = self.xt[i], self.t_xt[i], self.xn[i], self.t_xn[i], self.stat[i], self.t_stat[i]
        k.dma("sp", xt[:], src_ap, r=[src_tk], w=[t_xt])
        k.op("act", lambda h: h.activation(out=self.junk[:], in_=xt[:], func=AF.Square, accum_out=st[:, 0:1]),
             r=[t_xt], w=[self.t_junk, t_st])
        k.op("act", lambda h: h.activation(out=st[:, 1:2], in_=st[:, 0:1], func=AF.Sqrt, scale=1.0 / D,
                                           bias=self.eps_t[:, 0:1]),
             r=[t_st, self.t_const], w=[t_st])
        k.op("dve", lambda h: h.reciprocal(out=st[:, 2:3], in_=st[:, 1:2]), r=[t_st], w=[t_st])
        k.op("dve", lambda h: h.tensor_scalar(out=xn[:], in0=xt[:], scalar1=st[:, 2:3], scalar2=None, op0=ALU.mult),
             r=[t_xt, t_st], w=[t_xn])
        pT, t_pT = self.psT[0], self.t_psT[0]
        for j in range(8):
            k.op("pe", lambda h, j=j: h.transpose(out=pT[:, j, :], in_=xn[:, j * 128:(j + 1) * 128],
                                                   identity=self.ident_b[:]),
                 r=[t_xn, self.t_const], w=[t_pT])
        k.op("dve", lambda h: h.tensor_tensor(out=self.hT[:, :, blk * 128:(blk + 1) * 128], in0=pT[:],
                                              in1=bcast_mid(self.gcol[:, gidx, :], 128), op=ALU.mult),
             r=[t_pT, self.t_gcol], w=[self.t_hT[blk]])

    def pass_ffn(self, l, src, t_src, dst, t_dst, final):
        k = self.k
        k.push_scope()
        wup = k.sb("wup", [128, 8, DFF], BF16)
        wdn = k.sb("wdn", [128, 32, D], BF16)
        t_wup = [Tk() for _ in range(8)]
        t_wdn = [Tk() for _ in range(8)]
        for kc in range(8):
            k.dma("pool", wup[:, kc, :], self.inp["w_mlp_up"][l, kc * 128:(kc + 1) * 128, :], w=[t_wup[kc]])
        for c4 in range(8):
            k.dma("pool", wdn[:, c4 * 4:(c4 + 1) * 4, :],
                  self.inp["w_mlp_down"][l, c4 * 512:(c4 + 1) * 512, :].rearrange("(c p) n -> p c n", p=128),
                  w=[t_wdn[c4]])
        aT = k.sb("aT", [128, 32, G], BF16)
        t_aT = [Tk() for _ in range(32)]
        rt = [k.sb("rt%d" % i, [128, G], F32) for i in range(2)]
        t_rt = [Tk() for _ in range(2)]
        psU = [k.ps("psU%d" % i, [128, G], F32) for i in range(2)]
        t_psU = [Tk() for _ in range(2)]
        psD = [k.ps("psD%d" % i, [128, 512], F32) for i in range(2)]
        t_psD = [Tk() for _ in range(2)]
        xr = [k.sb("xr%d" % i, [128, D], F32) for i in range(2)]
        t_xr = [Tk() for _ in range(2)]
        xo, t_xo = xr, t_xr
        st2 = [k.sb("st2_%d" % i, [128, 4], F32) for i in range(2)]
        t_st2 = [Tk() for _ in range(2)]
        n_o = 0
        for g in range(NG):
            for blk in range(NB):
                r0 = g * G + blk * 128
                self.norm_block(src[r0:r0 + 128, :], t_src[g], 2 + l, blk)
            for c in range(32):
                ps, t_ps = psU[c % 2], t_psU[c % 2]
                for kc in range(8):
                    k.op("pe", lambda h, kc=kc, c=c, ps=ps: h.matmul(ps[:], lhsT=wup[:, kc, c * 128:(c + 1) * 128],
                                                                    rhs=self.hT[:, kc, :], start=(kc == 0),
                                                                    stop=(kc == 7)),
                         r=[t_wup[kc]] + self.t_hT, w=[t_ps])
                r_, t_r = rt[c % 2], t_rt[c % 2]
                k.op("act", lambda h, ps=ps, r_=r_: h.activation(out=r_[:], in_=ps[:], func=AF.Relu),
                     r=[t_ps], w=[t_r])
                k.op("pool", lambda h, r_=r_, c=c: h.tensor_tensor(out=aT[:, c, :], in0=r_[:], in1=r_[:],
                                                                   op=ALU.mult),
                     r=[t_r], w=[t_aT[c]])
            for blk in range(NB):
                r0 = g * G + blk * 128
                i = n_o % 2
                n_o += 1
                k.dma("sp", xr[i][:], src[r0:r0 + 128, :], r=[t_src[g]], w=[t_xr[i]])
                for half in range(2):
                    ps, t_ps = psD[half], t_psD[half]
                    for c in range(32):
                        k.op("pe", lambda h, c=c, ps=ps, half=half, blk=blk: h.matmul(
                            ps[:], lhsT=aT[:, c, blk * 128:(blk + 1) * 128],
                            rhs=wdn[:, c, half * 512:(half + 1) * 512], start=(c == 0), stop=(c == 31)),
                             r=[t_aT[c], t_wdn[c // 4]], w=[t_ps])
                    k.op("dve", lambda h, ps=ps, half=half, i=i: h.tensor_tensor(
                        out=xo[i][:, half * 512:(half + 1) * 512], in0=ps[:],
                        in1=xr[i][:, half * 512:(half + 1) * 512], op=ALU.add),
                         r=[t_ps, t_xr[i]], w=[t_xr[i]])
                if final:
                    st, t_st = st2[i], t_st2[i]
                    k.op("act", lambda h, i=i, st=st: h.activation(out=self.junk[:], in_=xo[i][:], func=AF.Square,
                                                                   accum_out=st[:, 0:1]),
                         r=[t_xo[i]], w=[self.t_junk, t_st])
                    k.op("act", lambda h, st=st: h.activation(out=st[:, 1:2], in_=st[:, 0:1], func=AF.Sqrt,
                                                              scale=1.0 / D, bias=self.eps_t[:, 0:1]),
                         r=[t_st, self.t_const], w=[t_st])
                    k.op("dve", lambda h, st=st: h.reciprocal(out=st[:, 2:3], in_=st[:, 1:2]), r=[t_st], w=[t_st])
                    k.op("dve", lambda h, i=i, st=st: h.scalar_tensor_tensor(
                        out=xo[i][:], in0=xo[i][:], scalar=st[:, 2:3], in1=self.gfin[:], op0=ALU.mult, op1=ALU.mult),
                         r=[t_xo[i], t_st, self.t_gfin], w=[t_xo[i]])
                k.dma("sp", dst[r0:r0 + 128, :], xo[i][:], r=[t_xo[i]], w=[t_dst[g]])
        k.pop_scope()


    def pass_mixa(self, l, src, t_src):
        k = self.k
        k.push_scope()
        inp = self.inp
        lambda_init = 0.8 - 0.6 * math.exp(-0.3 * l)
        win = k.sb("win", [128, 8, NIN], BF16)
        t_win = [Tk() for _ in range(8)]
        for kc in range(8):
            k.dma("pool", win[:, kc, :], inp["w_in"][l, kc * 128:(kc + 1) * 128, :], w=[t_win[kc]])
        cst = k.sb("cst", [128, 8], F32)
        t_cst = Tk()
        lq = k.sb("lq", [128, 4, 32], F32)
        t_lq = Tk()
        for i, nm in enumerate(["lam_q1", "lam_k1", "lam_q2", "lam_k2"]):
            k.dma("sp", lq[:, i, :], inp[nm][l:l + 1, :].partition_broadcast(128), w=[t_lq])
        k.op("dve", lambda h: h.tensor_tensor(out=lq[:, 0, :], in0=lq[:, 0, :], in1=lq[:, 1, :], op=ALU.mult),
             r=[t_lq], w=[t_lq])
        k.op("dve", lambda h: h.tensor_tensor(out=lq[:, 2, :], in0=lq[:, 2, :], in1=lq[:, 3, :], op=ALU.mult),
             r=[t_lq], w=[t_lq])
        k.op("dve", lambda h: h.tensor_reduce(out=cst[:, 0:1], in_=lq[:, 0, :], axis=AX.X, op=ALU.add),
             r=[t_lq], w=[t_cst])
        k.op("dve", lambda h: h.tensor_reduce(out=cst[:, 1:2], in_=lq[:, 2, :], axis=AX.X, op=ALU.add),
             r=[t_lq], w=[t_cst])
        k.op("act", lambda h: h.activation(out=cst[:, 2:4], in_=cst[:, 0:2], func=AF.Exp), r=[t_cst], w=[t_cst])
        k.op("dve", lambda h: h.tensor_tensor(out=cst[:, 4:5], in0=cst[:, 2:3], in1=cst[:, 3:4], op=ALU.subtract),
             r=[t_cst], w=[t_cst])
        k.op("dve", lambda h: h.tensor_scalar(out=cst[:, 5:6], in0=cst[:, 4:5], scalar1=float(lambda_init),
                                              scalar2=None, op0=ALU.add), r=[t_cst], w=[t_cst])
        k.op("pool", lambda h: h.memset(cst[:, 6:7], NORM_EPS), w=[t_cst])
        subg = k.sb("subg", [128, 64], F32)
        t_subg = Tk()
        k.dma("sp", subg[:], inp["subln_g"][l:l + 1, :].partition_broadcast(128), w=[t_subg])
        k.op("dve", lambda h: h.tensor_scalar(out=subg[:], in0=subg[:], scalar1=float(1.0 - lambda_init),
                                              scalar2=None, op0=ALU.mult), r=[t_subg], w=[t_subg])
        cw = k.sb("cw", [128, 2, 3], F32)
        t_cw = Tk()
        for j in range(2):
            k.dma("sp", cw[:, j, :], inp["conv_w"][l, :, j * 128:(j + 1) * 128].rearrange("t p -> p t"),
                  w=[t_cw], allow_slow_non_contiguous=True)
        mask = k.sb("mask", [128, 128], BF16)
        t_mask = Tk()
        k.op("pool", lambda h: h.memset(mask[:], 1.0), w=[t_mask])
        k.op("pool", lambda h: h.affine_select(out=mask[:], in_=mask[:], pattern=[[1, 128]], compare_op=ALU.is_ge,
                                               fill=0.0, base=0, channel_multiplier=-1), r=[t_mask], w=[t_mask])
        kT = k.sb("kT", [128, 3, S], BF16)
        t_kT = [[Tk() for _ in range(NG)] for _ in range(3)]
        Vt = k.sb("Vt", [128, S // 128, 6, 65], BF16)
        t_V = [Tk() for _ in range(S // 128)]
        for b8 in range(0, S // 128, 8):
            k.op("pool", lambda h, b8=b8: h.memset(Vt[:, b8:b8 + 8], 1.0), w=t_V[b8:b8 + 8])
        qT = k.sb("qT", [128, 3, G], BF16)
        t_qT = [Tk() for _ in range(3)]
        cv = k.sb("cv", [128, 6, G], F32)
        t_cv = [Tk() for _ in range(6)]
        zb = k.sb("zb", [128, 2, G + 2], F32)
        t_zb = [Tk() for _ in range(2)]
        k.op("pool", lambda h: h.memset(zb[:], 0.0), w=t_zb)
        ycv = k.sb("ycv", [128, G], F32)
        t_ycv = Tk()
        stg = [k.sb("stg%d" % i, [128, G], F32) for i in range(3)]
        t_stg = [Tk() for _ in range(3)]
        eT = [k.sb("eT%d" % i, [128, G], BF16) for i in range(4)]
        t_eT = [Tk() for _ in range(4)]
        uT = k.sb("uT", [128, 2, G], F32)
        t_uT = [Tk() for _ in range(2)]
        oat = k.sb("oat", [128, NB, 6, 64], F32)
        t_oat = Tk()
        osq = k.sb("osq", [128, NB, 6, 64], F32)
        t_osq = Tk()
        t1 = k.sb("t1", [128, NB, 64], F32)
        t_t1 = Tk()
        rl = k.sb("rl", [128, 2, NB], F32)
        t_rl = Tk()
        ss = k.sb("ss", [128, 3, NB * 6], F32)
        t_ss = Tk()
        oab = k.sb("oab", [128, NB, 384], BF16)
        t_oab = Tk()
        mixA = k.sb("mixA", [128, 5, G], BF16)
        t_mixA = [Tk() for _ in range(5)]
        psA = [k.ps("psA%d" % i, [128, G], F32) for i in range(3)]
        t_psA = [Tk() for _ in range(3)]
        psO = [k.ps("psO%d" % i, [128, G], F32) for i in range(2)]
        t_psO = [Tk() for _ in range(2)]
        psTr = [k.ps("psTr%d" % i, [128, NB, 65], F32) for i in range(2)]
        t_psTr = [Tk() for _ in range(2)]
        nA = 0
        nS = 0
        nE = 0
        sc = 1.0 / math.sqrt(32.0)
        for g in range(NG):
            for blk in range(NB):
                r0 = g * G + blk * 128
                self.norm_block(src[r0:r0 + 128, :], t_src[g], l, blk)
            for c in list(range(0, 6)) + list(range(9, 26)):
                ps, t_ps = psA[nA % 3], t_psA[nA % 3]
                nA += 1
                for kc in range(8):
                    k.op("pe", lambda h, kc=kc, c=c, ps=ps: h.matmul(ps[:], lhsT=win[:, kc, c * 128:(c + 1) * 128],
                                                                    rhs=self.hT[:, kc, :], start=(kc == 0),
                                                                    stop=(kc == 7)),
                         r=[t_win[kc]] + self.t_hT, w=[t_ps])
                if c < 3:
                    k.op("dve", lambda h, ps=ps, c=c: h.tensor_copy(out=qT[:, c, :], in_=ps[:]), r=[t_ps], w=[t_qT[c]])
                elif c < 6:
                    k.op("dve", lambda h, ps=ps, c=c: h.tensor_copy(out=kT[:, c - 3, g * G:(g + 1) * G], in_=ps[:]),
                         r=[t_ps], w=[t_kT[c - 3][g]])
                elif c < 15:
                    k.op("dve", lambda h, ps=ps, c=c: h.tensor_copy(out=cv[:, c - 9, :], in_=ps[:]),
                         r=[t_ps], w=[t_cv[c - 9]])
                else:
                    i = nS % 3
                    nS += 1
                    k.op("dve", lambda h, ps=ps, i=i: h.tensor_copy(out=stg[i][:], in_=ps[:]), r=[t_ps], w=[t_stg[i]])
                    k.dma("sp", self.prw[(c - 15) * 128:(c - 14) * 128, g * G:(g + 1) * G], stg[i][:],
                          r=[t_stg[i]], w=[self.t_prw[g]])
            for blk in range(NB):
                ps, t_ps = psA[nA % 3], t_psA[nA % 3]
                nA += 1
                for kc in range(8):
                    k.op("pe", lambda h, kc=kc, ps=ps, blk=blk: h.matmul(
                        ps[:, 0:384], lhsT=self.hT[:, kc, blk * 128:(blk + 1) * 128], rhs=win[:, kc, 768:1152],
                        start=(kc == 0), stop=(kc == 7)), r=[t_win[kc], self.t_hT[blk]], w=[t_ps])
                k.op("dve", lambda h, ps=ps, blk=blk: h.tensor_copy(
                    out=Vt[:, g * NB + blk, :, 0:64], in_=ps[:, 0:384].rearrange("p (a b) -> p a b", a=6)),
                     r=[t_ps], w=[t_V[g * NB + blk]])
            for j in range(2):
                k.op("pool", lambda h, j=j: h.tensor_tensor(out=zb[:, j, 2:G + 2], in0=cv[:, 2 + j, :],
                                                            in1=cv[:, 4 + j, :], op=ALU.mult),
                     r=[t_cv[2 + j], t_cv[4 + j]], w=[t_zb[j]])
                k.op("pool", lambda h, j=j: h.tensor_scalar(out=ycv[:], in0=zb[:, j, 0:G], scalar1=cw[:, j, 0:1],
                                                            scalar2=None, op0=ALU.mult),
                     r=[t_zb[j], t_cw], w=[t_ycv])
                k.op("dve", lambda h, j=j: h.scalar_tensor_tensor(out=ycv[:], in0=zb[:, j, 1:G + 1],
                                                                   scalar=cw[:, j, 1:2], in1=ycv[:],
                                                                   op0=ALU.mult, op1=ALU.add),
                     r=[t_zb[j], t_cw, t_ycv], w=[t_ycv])
                k.op("dve", lambda h, j=j: h.scalar_tensor_tensor(out=ycv[:], in0=zb[:, j, 2:G + 2],
                                                                   scalar=cw[:, j, 2:3], in1=ycv[:],
                                                                   op0=ALU.mult, op1=ALU.add),
                     r=[t_zb[j], t_cw, t_ycv], w=[t_ycv])
                k.op("pool", lambda h, j=j: h.tensor_tensor(out=mixA[:, 3 + j, :], in0=ycv[:], in1=cv[:, j, :],
                                                            op=ALU.mult),
                     r=[t_ycv, t_cv[j]], w=[t_mixA[3 + j]])
                k.op("pool", lambda h, j=j: h.tensor_copy(out=zb[:, j, 0:2], in_=zb[:, j, G:G + 2]),
                     r=[t_zb[j]], w=[t_zb[j]])
            nkb = 4 * g + 4
            for hd in range(6):
                qc = hd // 2
                for half in range(2):
                    pb = (hd % 2) * 64 + half * 32
                    for j in range(nkb):
                        off = max(0, j - 4 * g) * 128
                        ps, t_ps = psA[nA % 3], t_psA[nA % 3]
                        nA += 1
                        k.op("pe", lambda h, ps=ps, pb=pb, qc=qc, j=j, off=off: h.matmul(
                            ps[:, off:G], lhsT=kT[pb:pb + 32, qc, j * 128:(j + 1) * 128],
                            rhs=qT[pb:pb + 32, qc, off:G], start=True, stop=True, tile_position=(pb, 0)),
                             r=[t_kT[qc][j // NB], t_qT[qc]], w=[t_ps])
                        e, t_e = eT[nE % 4], t_eT[nE % 4]
                        nE += 1
                        k.op("act", lambda h, ps=ps, e=e, off=off: h.activation(out=e[:, off:G], in_=ps[:, off:G],
                                                                              func=AF.Exp, scale=sc),
                             r=[t_ps], w=[t_e])
                        if j >= 4 * g:
                            k.op("pool", lambda h, e=e, off=off: h.tensor_tensor(
                                out=e[:, off:off + 128], in0=e[:, off:off + 128], in1=mask[:], op=ALU.mult),
                                 r=[t_e, t_mask], w=[t_e])
                        k.op("pe", lambda h, e=e, j=j, hd=hd, half=half, off=off: h.matmul(
                            psO[half][0:65, off:G], lhsT=Vt[:, j, hd, :], rhs=e[:, off:G], start=(j == 0),
                            stop=(j == nkb - 1)), r=[t_e, t_V[j]], w=[t_psO[half]])
                    k.op("dve", lambda h, half=half: h.tensor_copy(out=uT[0:65, half, :], in_=psO[half][0:65, :]),
                         r=[t_psO[half]], w=[t_uT[half]])
                    for blk in range(NB):
                        k.op("pe", lambda h, half=half, blk=blk: h.transpose(
                            out=psTr[half][:, blk, :], in_=uT[0:65, half, blk * 128:(blk + 1) * 128],
                            identity=self.ident_f[0:65, 0:65]), r=[t_uT[half], self.t_const], w=[t_psTr[half]])
                    k.op("dve", lambda h, half=half: h.reciprocal(out=rl[:, half, :], in_=psTr[half][:, :, 64]),
                         r=[t_psTr[half]], w=[t_rl])
                k.op("dve", lambda h: h.tensor_scalar(out=rl[:, 1, :], in0=rl[:, 1, :], scalar1=cst[:, 5:6],
                                                      scalar2=None, op0=ALU.mult), r=[t_rl, t_cst], w=[t_rl])
                k.op("dve", lambda h: h.tensor_tensor(out=t1[:], in0=psTr[0][:, :, 0:64],
                                                      in1=bcast_mid(rl[:, 0, :], 64), op=ALU.mult),
                     r=[t_psTr[0], t_rl], w=[t_t1])
                k.op("dve", lambda h, hd=hd: h.tensor_tensor(out=oat[:, :, hd, :], in0=psTr[1][:, :, 0:64],
                                                             in1=bcast_mid(rl[:, 1, :], 64), op=ALU.mult),
                     r=[t_psTr[1], t_rl], w=[t_oat])
                k.op("dve", lambda h, hd=hd: h.tensor_tensor(out=oat[:, :, hd, :], in0=t1[:], in1=oat[:, :, hd, :],
                                                             op=ALU.subtract), r=[t_t1, t_oat], w=[t_oat])
            k.op("pool", lambda h: h.tensor_tensor(out=osq[:], in0=oat[:], in1=oat[:], op=ALU.mult),
                 r=[t_oat], w=[t_osq])
            k.op("dve", lambda h: h.tensor_reduce(out=ss[:, 0, :], in_=osq[:].rearrange("p a b c -> p (a b) c"),
                                                  axis=AX.X, op=ALU.add), r=[t_osq], w=[t_ss])
            k.op("act", lambda h: h.activation(out=ss[:, 1, :], in_=ss[:, 0, :], func=AF.Sqrt, scale=1.0 / 64.0,
                                               bias=cst[:, 6:7]), r=[t_ss, t_cst], w=[t_ss])
            k.op("dve", lambda h: h.reciprocal(out=ss[:, 2, :], in_=ss[:, 1, :]), r=[t_ss], w=[t_ss])
            k.op("dve", lambda h: h.tensor_tensor(out=osq[:].rearrange("p a b c -> p (a b) c"),
                                                  in0=oat[:].rearrange("p a b c -> p (a b) c"),
                                                  in1=bcast_mid(ss[:, 2, :], 64), op=ALU.mult),
                 r=[t_oat, t_ss], w=[t_osq])
            k.op("dve", lambda h: h.tensor_tensor(
                out=oab[:].rearrange("p a (b c) -> p (a b) c", c=64), in0=osq[:].rearrange("p a b c -> p (a b) c"),
                in1=subg[:].unsqueeze(1).to_broadcast([128, NB * 6, 64]), op=ALU.mult),
                 r=[t_osq, t_subg], w=[t_oab])
            pT, t_pT = self.psT[0], self.t_psT[0]
            for blk in range(NB):
                for c in range(3):
                    k.op("pe", lambda h, blk=blk, c=c: h.transpose(out=pT[:, blk * 2 + (c % 2), :] if False else pT[:, c, :],
                                                                   in_=oab[:, blk, c * 128:(c + 1) * 128],
                                                                   identity=self.ident_b[:]),
                         r=[t_oab, self.t_const], w=[t_pT])
                k.op("dve", lambda h, blk=blk: h.tensor_copy(out=mixA[:, 0:3, blk * 128:(blk + 1) * 128],
                                                             in_=pT[:, 0:3, :]),
                     r=[t_pT], w=t_mixA[0:3])
            k.dma("sp", self.mxa[:, g * G:(g + 1) * G].rearrange("(c p) t -> p c t", p=128), mixA[:],
                  r=t_mixA, w=[self.t_mxa[g]])
        k.pop_scope()


    def pass_mixb(self, l, src, t_src, dst, t_dst):
        k = self.k
        k.push_scope()
        inp = self.inp
        C = 64
        NCH = G // C
        c0 = math.exp(-0.5)
        wout = k.sb("wout", [128, 8, D], BF16)
        t_wout = [Tk() for _ in range(8)]
        for kc in range(8):
            k.dma("pool", wout[:, kc, :], inp["w_out"][l, kc * 128:(kc + 1) * 128, :], w=[t_wout[kc]])
        waup = k.sb("waup", [128, 384], BF16)
        gup = k.sb("gup", [128, 384], BF16)
        t_lw = Tk()
        k.dma("pool", waup[0:64, :], inp["rwkv_w_up"][l], w=[t_lw])
        k.dma("pool", waup[64:128, :], inp["rwkv_a_up"][l], w=[t_lw])
        k.dma("pool", gup[:], inp["rwkv_g_up"][l], w=[t_lw])
        pc = k.sb("pc", [128, 8, 3], F32)
        t_pc = Tk()
        for n, nm in enumerate(["rwkv_w0", "rwkv_a0", "rwkv_k_k", "rwkv_k_a", "rwkv_r_k"]):
            srcap = inp[nm][l]
            if nm == "rwkv_r_k":
                srcap = srcap.rearrange("a b -> (a b)")
            k.dma("sp", pc[:, n, :], srcap.rearrange("(j p) -> p j", p=128), w=[t_pc],
                  allow_slow_non_contiguous=True)
        k.op("dve", lambda h: h.tensor_scalar(out=pc[:, 5, :], in0=pc[:, 3, :], scalar1=-1.0, scalar2=1.0,
                                              op0=ALU.mult, op1=ALU.add), r=[t_pc], w=[t_pc])
        k.op("pool", lambda h: h.memset(pc[:, 6, :], GN_EPS), w=[t_pc])
        mu = k.sb("mu", [128, 11], F32)
        t_mu = Tk()
        k.dma("sp", mu[:], inp["shift_mu"][l].rearrange("(j p) -> p j", p=128), w=[t_mu],
              allow_slow_non_contiguous=True)
        lng = k.sb("lng", [128, 2, 384], F32)
        t_lng = Tk()
        k.dma("sp", lng[:, 0, :], inp["lnx_g"][l:l + 1, :].partition_broadcast(128), w=[t_lng])
        k.dma("sp", lng[:, 1, :], inp["lnx_b"][l:l + 1, :].partition_broadcast(128), w=[t_lng])
        bones = k.sb("bones", [128, 128], BF16)
        t_bones = Tk()
        k.op("pool", lambda h: h.memset(bones[:], 0.0), w=[t_bones])
        k.op("pool", lambda h: h.memset(bones[0:64, 0:64], 1.0), w=[t_bones])
        k.op("pool", lambda h: h.memset(bones[64:128, 64:128], 1.0), w=[t_bones])
        msk = k.sb("msk", [128, 3, 64], F32)
        t_msk = Tk()
        k.op("pool", lambda h: h.memset(msk[:], 1.0), w=[t_msk])
        k.op("pool", lambda h: h.affine_select(out=msk[0:64, 0, :], in_=msk[0:64, 0, :], pattern=[[1, 64]],
                                               compare_op=ALU.is_ge, fill=0.0, base=-1, channel_multiplier=-1),
             r=[t_msk], w=[t_msk])
        k.op("pool", lambda h: h.affine_select(out=msk[0:64, 1, :], in_=msk[0:64, 1, :], pattern=[[1, 64]],
                                               compare_op=ALU.is_ge, fill=0.0, base=0, channel_multiplier=-1),
             r=[t_msk], w=[t_msk])
        k.op("pool", lambda h: h.affine_select(out=msk[0:64, 2, :], in_=msk[0:64, 2, :], pattern=[[-1, 64]],
                                               compare_op=ALU.is_ge, fill=0.0, base=-1, channel_multiplier=1),
             r=[t_msk], w=[t_msk])
        k.dma("sp", msk[64:128], msk[0:64], r=[t_msk], w=[t_msk])

        rmask = k.sb("rmask", [128, G], F32)
        t_rmask = Tk()
        k.op("pool", lambda h: h.memset(rmask[:], 1.0), w=[t_rmask])
        k.op("pool", lambda h: h.memset(rmask[:].rearrange("p (c t) -> p c t", t=C)[:, :, 0:1], 0.0), w=[t_rmask])
        pt = k.sb("pt", [128, 11, G], F32)
        t_pt = Tk()
        halo = k.sb("halo", [128, 11, 1], F32)
        t_halo = Tk()
        k.op("pool", lambda h: h.memset(halo[:], 0.0), w=[t_halo])
        z = k.sb("z", [128, 11, G], F32)
        t_z = Tk()
        F3 = [128, 3, G]
        sig = k.sb("sig", F3, F32); t_sig = Tk()
        aa = k.sb("aa", F3, F32); t_aa = Tk()
        gT = k.sb("gT", F3, F32); t_gT = Tk()
        kkn = k.sb("kkn", F3, F32); t_kkn = Tk()
        kmod = k.sb("kmod", F3, F32); t_kmod = Tk()
        beta = k.sb("beta", F3, F32); t_beta = Tk()
        bonus = pt[:, 6:9, :]; t_bonus = Tk()
        cs = k.sb("cs", F3, F32); t_cs = Tk()
        tA = pt[:, 0:3, :]; t_tA = Tk()
        tB = pt[:, 3:6, :]; t_tB = Tk()
        t_alias = [t_tA, t_tB, t_bonus]
        b16 = k.sb("b16", F3, BF16); t_b16 = Tk()
        rt = k.sb("rt", F3, BF16); t_rt = Tk()
        kt = k.sb("kt", F3, BF16); t_kt = Tk()
        bt = k.sb("bt", F3, BF16); t_bt = Tk()
        at = k.sb("at", F3, BF16); t_at = Tk()
        k2 = k.sb("k2", F3, BF16); t_k2 = Tk()
        b2 = k.sb("b2", F3, BF16); t_b2 = Tk()
        twa = k.sb("twa", [128, G], BF16); t_twa = Tk()
        sg = k.sb("sg", [128, G], BF16); t_sg = Tk()
        WC = k.sb("WC", [128, 3, NCH], F32); t_WC = Tk()
        tk2 = lambda: [Tk(), Tk()]
        Vg = k.sb("Vg", [128, NCH, 3, 64], BF16); t_Vg = [tk2() for _ in range(NCH)]
        Tst = k.sb("Tst", [128, 3, 64], F32); t_T = tk2()
        Tb = k.sb("Tb", [128, 3, 64], BF16); t_Tb = tk2()
        k.op("pool", lambda h: h.memset(Tst[:], 0.0), w=t_T)
        k.op("pool", lambda h: h.memset(Tb[:], 0.0), w=t_Tb)
        H6 = [128, 3, 64]
        X = [k.sb("X%d" % i, H6, BF16) for i in range(2)]; t_X = [tk2() for _ in range(2)]
        Xt = [k.sb("Xt%d" % i, H6, BF16) for i in range(2)]; t_Xt = [tk2() for _ in range(2)]
        P = [k.sb("P%d" % i, H6, BF16) for i in range(2)]; t_P = [tk2() for _ in range(2)]
        Pt = [k.sb("Pt%d" % i, H6, BF16) for i in range(2)]; t_Pt = [tk2() for _ in range(2)]
        Lak = k.sb("Lak", H6, BF16); t_Lak = tk2()
        Arb = k.sb("Arb", H6, BF16); t_Arb = tk2()
        Ark = k.sb("Ark", H6, BF16); t_Ark = tk2()
        r0b = k.sb("r0b", H6, BF16); t_r0b = tk2()
        Ub = k.sb("Ub", H6, BF16); t_Ub = tk2()
        kbtok = k.sb("kbtok", [128, 2, 3, 64], BF16); t_kbtok = tk2()
        idpl = k.sb("idpl", [128, 64], BF16); t_idpl = Tk()
        k.op("pool", lambda h: h.tensor_copy(out=idpl[0:64, :], in_=self.ident_b[0:64, 0:64]),
             r=[self.t_const], w=[t_idpl])
        k.op("pool", lambda h: h.tensor_copy(out=idpl[64:128, :], in_=self.ident_b[64:128, 64:128]),
             r=[self.t_const], w=[t_idpl])
        ysb = k.sb("ysb", [128, NCH // 2, 384], F32); t_ysb = [Tk() for _ in range(NCH)]
        ysq = k.sb("ysq", [128, NCH // 2, 384], F32); t_ysq = Tk()
        gst = k.sb("gst", [128, 6, NCH * 3], F32); t_gst = Tk()
        mixT = k.sb("mixT", [128, 8, G], BF16); t_mixT = [Tk() for _ in range(8)]
        xr = [k.sb("xrb%d" % i, [128, D], F32) for i in range(2)]; t_xr = [Tk() for _ in range(2)]
        psG = [k.ps("psG%d" % i, [128, 512], F32) for i in range(6)]
        t_psG = [Tk() for _ in range(6)]
        st = {"n": 0, "x": 0}

        def nps():
            i = st["n"] % 6
            st["n"] += 1
            return psG[i], t_psG[i]

        def npair():
            i = ((st["n"] + 1) // 2) % 3
            st["n"] = 2 * ((st["n"] + 1) // 2) + 2
            return ((psG[2 * i], t_psG[2 * i]), (psG[2 * i + 1], t_psG[2 * i + 1]))

        def v3(ps, hb):
            return ps[hb:hb + 64, 0:192].rearrange("p (a b) -> p a b", a=3)

        def b3(col):
            return bcast_mid(col, G)

        for g in range(self.dbg.get('mb_ng', NG) if isinstance(self.dbg, dict) else NG):
            gs = slice(g * G, (g + 1) * G)
            k.dma("sp", pt[:], self.prw[:, gs].rearrange("(c p) t -> p c t", p=128),
                  r=[self.t_prw[g]], w=[t_pt] + t_alias)
            k.dma("sp", mixT[:, 0:5, :], self.mxa[:, gs].rearrange("(c p) t -> p c t", p=128),
                  r=[self.t_mxa[g]], w=t_mixT[0:5])
            k.op("pool", lambda h: h.tensor_tensor(out=z[:, :, 1:G], in0=pt[:, :, 0:G - 1], in1=pt[:, :, 1:G],
                                                   op=ALU.subtract), r=[t_pt] + t_alias, w=[t_z])
            k.op("pool", lambda h: h.tensor_tensor(out=z[:, :, 0:1], in0=halo[:], in1=pt[:, :, 0:1],
                                                   op=ALU.subtract), r=[t_pt, t_halo] + t_alias, w=[t_z])
            k.op("dve", lambda h: h.tensor_tensor(out=z[:], in0=z[:], in1=bcast_mid(mu[:], G), op=ALU.mult),
                 r=[t_z, t_mu], w=[t_z])
            k.op("pool", lambda h: h.tensor_tensor(out=z[:], in0=z[:], in1=pt[:], op=ALU.add),
                 r=[t_z, t_pt] + t_alias, w=[t_z])
            k.op("pool", lambda h: h.tensor_copy(out=halo[:], in_=pt[:, :, G - 1:G]), r=[t_pt] + t_alias,
                 w=[t_halo])
            zr, zk, zv = z[:, 0:3, :], z[:, 3:6, :], z[:, 6:9, :]
            k.op("act", lambda h: h.activation(out=twa[0:64, :], in_=z[0:64, 9, :], func=AF.Tanh), r=[t_z], w=[t_twa])
            k.op("act", lambda h: h.activation(out=twa[64:128, :], in_=z[64:128, 9, :], func=AF.Copy),
                 r=[t_z], w=[t_twa])
            k.op("act", lambda h: h.activation(out=sg[:], in_=z[:, 10, :], func=AF.Sigmoid), r=[t_z], w=[t_sg])
            for fc in range(3):
                ps, t_ps = nps()
                k.op("pe", lambda h, ps=ps, fc=fc: h.matmul(ps[:], lhsT=waup[0:64, fc * 128:(fc + 1) * 128],
                                                            rhs=twa[0:64, :], start=True, stop=True,
                                                            tile_position=(0, 0)), r=[t_lw, t_twa], w=[t_ps])
                k.op("act", lambda h, ps=ps, fc=fc: h.activation(out=sig[:, fc, :], in_=ps[:], func=AF.Sigmoid,
                                                                 bias=pc[:, 0, fc:fc + 1]), r=[t_ps, t_pc], w=[t_sig])
            for fc in range(3):
                ps, t_ps = nps()
                k.op("pe", lambda h, ps=ps, fc=fc: h.matmul(ps[:], lhsT=waup[64:128, fc * 128:(fc + 1) * 128],
                                                            rhs=twa[64:128, :], start=True, stop=True,
                                                            tile_position=(64, 0)), r=[t_lw, t_twa], w=[t_ps])
                k.op("act", lambda h, ps=ps, fc=fc: h.activation(out=aa[:, fc, :], in_=ps[:], func=AF.Sigmoid,
                                                                 bias=pc[:, 1, fc:fc + 1]), r=[t_ps, t_pc], w=[t_aa])
            for fc in range(3):
                ps, t_ps = nps()
                k.op("pe", lambda h, ps=ps, fc=fc: h.matmul(ps[:], lhsT=gup[:, fc * 128:(fc + 1) * 128], rhs=sg[:],
                                                            start=True, stop=True), r=[t_lw, t_sg], w=[t_ps])
                k.op("dve", lambda h, ps=ps, fc=fc: h.tensor_copy(out=gT[:, fc, :], in_=ps[:]), r=[t_ps], w=[t_gT])
            k.op("pool", lambda h: h.tensor_tensor(out=tA, in0=zk, in1=b3(pc[:, 2, :]), op=ALU.mult),
                 r=[t_z, t_pc], w=[t_tA])
            k.op("pool", lambda h: h.tensor_tensor(out=b16[:], in0=tA, in1=tA, op=ALU.mult),
                 r=[t_tA], w=[t_b16])
            for fc in range(3):
                ps, t_ps = nps()
                k.op("pe", lambda h, ps=ps, fc=fc: h.matmul(ps[:], lhsT=bones[:], rhs=b16[:, fc, :], start=True,
                                                            stop=True), r=[t_bones, t_b16], w=[t_ps])
                k.op("act", lambda h, ps=ps, fc=fc: h.activation(out=tB[:, fc, :], in_=ps[:], func=AF.Sqrt),
                     r=[t_ps], w=[t_tB])
            k.op("dve", lambda h: h.tensor_scalar(out=tB, in0=tB, scalar1=1e-12, scalar2=None, op0=ALU.max),
                 r=[t_tB], w=[t_tB])
            k.op("dve", lambda h: h.reciprocal(out=tB, in_=tB), r=[t_tB], w=[t_tB])
            k.op("pool", lambda h: h.tensor_tensor(out=kkn[:], in0=tA, in1=tB, op=ALU.mult),
                 r=[t_tA, t_tB], w=[t_kkn])
            k.op("pool", lambda h: h.tensor_tensor(out=tA, in0=aa[:], in1=b3(pc[:, 3, :]), op=ALU.mult),
                 r=[t_aa, t_pc], w=[t_tA])
            k.op("pool", lambda h: h.tensor_tensor(out=tA, in0=tA, in1=b3(pc[:, 5, :]), op=ALU.add),
                 r=[t_tA, t_pc], w=[t_tA])
            k.op("dve", lambda h: h.tensor_tensor(out=kmod[:], in0=zk, in1=tA, op=ALU.mult),
                 r=[t_z, t_tA], w=[t_kmod])
            k.op("pool", lambda h: h.tensor_tensor(out=beta[:], in0=kkn[:], in1=aa[:], op=ALU.mult),
                 r=[t_kkn, t_aa], w=[t_beta])
            k.op("dve", lambda h: h.tensor_tensor(out=tA, in0=zr, in1=kmod[:], op=ALU.mult),
                 r=[t_z, t_kmod], w=[t_tA])
            k.op("pool", lambda h: h.tensor_tensor(out=b16[:], in0=tA, in1=b3(pc[:, 4, :]), op=ALU.mult),
                 r=[t_tA, t_pc], w=[t_b16])
            for fc in range(3):
                ps, t_ps = nps()
                k.op("pe", lambda h, ps=ps, fc=fc: h.matmul(ps[:], lhsT=bones[:], rhs=b16[:, fc, :], start=True,
                                                            stop=True), r=[t_bones, t_b16], w=[t_ps])
                k.op("dve", lambda h, ps=ps, fc=fc: h.tensor_tensor(out=bonus[:, fc, :], in0=ps[:],
                                                                    in1=z[:, 6 + fc, :], op=ALU.mult),
                     r=[t_ps, t_z], w=[t_bonus])
            for fc in range(3):
                k.op("dve", lambda h, fc=fc: h.tensor_tensor_scan(out=cs[:, fc, :], data0=rmask[:],
                                                                  data1=sig[:, fc, :], initial=0.0, op0=ALU.mult,
                                                                  op1=ALU.add), r=[t_rmask, t_sig], w=[t_cs])
            csC = cs[:].rearrange("p f (c t) -> p f c t", t=C)[:, :, :, C - 1]
            k.op("act", lambda h: h.activation(out=tA, in_=cs[:], func=AF.Exp, scale=-c0), r=[t_cs], w=[t_tA])
            k.op("dve", lambda h: h.tensor_tensor(out=rt[:], in0=zr, in1=tA, op=ALU.mult),
                 r=[t_z, t_tA], w=[t_rt])
            k.op("act", lambda h: h.activation(out=tB, in_=cs[:], func=AF.Exp, scale=c0), r=[t_cs], w=[t_tB])
            k.op("pool", lambda h: h.tensor_tensor(out=kt[:], in0=kmod[:], in1=tB, op=ALU.mult),
                 r=[t_kmod, t_tB], w=[t_kt])
            k.op("dve", lambda h: h.tensor_tensor(out=bt[:], in0=beta[:], in1=tB, op=ALU.mult),
                 r=[t_beta, t_tB], w=[t_bt])
            k.op("pool", lambda h: h.tensor_tensor(out=tA, in0=cs[:], in1=sig[:], op=ALU.subtract),
                 r=[t_cs, t_sig, t_rt], w=[t_tA])
            k.op("act", lambda h: h.activation(out=tA, in_=tA, func=AF.Exp, scale=-c0), r=[t_tA], w=[t_tA])
            k.op("dve", lambda h: h.scalar_tensor_tensor(out=at[:], in0=kkn[:], scalar=-1.0, in1=tA,
                                                         op0=ALU.mult, op1=ALU.mult), r=[t_kkn, t_tA], w=[t_at])
            k.op("pool", lambda h: h.tensor_tensor(
                out=tB.rearrange("p f (c t) -> p f c t", t=C),
                in0=csC.unsqueeze(3).to_broadcast([128, 3, NCH, C]),
                in1=cs[:].rearrange("p f (c t) -> p f c t", t=C), op=ALU.subtract),
                 r=[t_cs, t_kt, t_bt], w=[t_tB])
            k.op("act", lambda h: h.activation(out=tB, in_=tB, func=AF.Exp, scale=-c0), r=[t_tB], w=[t_tB])
            k.op("pool", lambda h: h.tensor_tensor(out=k2[:], in0=kmod[:], in1=tB, op=ALU.mult),
                 r=[t_kmod, t_tB], w=[t_k2])
            k.op("dve", lambda h: h.tensor_tensor(out=b2[:], in0=beta[:], in1=tB, op=ALU.mult),
                 r=[t_beta, t_tB], w=[t_b2])
            k.op("act", lambda h: h.activation(out=WC[:], in_=csC, func=AF.Exp, scale=-c0), r=[t_cs], w=[t_WC])
            for nm_, ap_, tk_ in [("z", z[:], [t_z]), ("sig", sig[:], [t_sig]), ("aa", aa[:], [t_aa]),
                                  ("kkn", kkn[:], [t_kkn]), ("kmod", kmod[:], [t_kmod]), ("beta", beta[:], [t_beta]),
                                  ("cs", cs[:], [t_cs]), ("rt", rt[:], [t_rt]), ("kt", kt[:], [t_kt]),
                                  ("bt", bt[:], [t_bt]), ("at", at[:], [t_at]), ("k2", k2[:], [t_k2]),
                                  ("b2", b2[:], [t_b2]), ("gT", gT[:], [t_gT]), ("bonus", bonus, [t_bonus]),
                                  ("WC", WC[:], [t_WC])]:
                self.dump(nm_, ap_, tk_)
            for c in range(NCH):
                cl = slice(c * C, (c + 1) * C)
                pair = npair()
                for fc in range(3):
                    for par in range(2):
                        hb = par * 64
                        ps, t_ps = pair[par]
                        k.op("pe", lambda h, ps=ps, fc=fc, hb=hb, cl=cl: h.matmul(
                            ps[hb:hb + 64, fc * 64:(fc + 1) * 64], lhsT=z[hb:hb + 64, 6 + fc, cl],
                            rhs=self.ident_f[hb:hb + 64, hb:hb + 64], start=True, stop=True,
                            tile_position=(hb, hb)), r=[t_z, self.t_const], w=[t_ps])
                for par in range(2):
                    hb = par * 64
                    ps, t_ps = pair[par]
                    k.op("act", lambda h, ps=ps, c=c, hb=hb: h.activation(
                        out=Vg[hb:hb + 64, c, :, :], in_=v3(ps, hb), func=AF.Copy), r=[t_ps], w=[t_Vg[c][par]])
            for c in range(NCH if "mb_nochunk" not in self.dbg else 0):
                cl = slice(c * C, (c + 1) * C)

                def fm(tile, hd):
                    hb = (hd % 2) * 64
                    return tile[hb:hb + 64, hd // 2, cl]

                def pl(tile, hd):
                    hb = (hd % 2) * 64
                    return tile[hb:hb + 64, hd // 2, :]

                def mm6(pair, lf, rf, rtoks, lf2=None, rf2=None, rtoks2=None):
                    for fc in range(3):
                        for par in range(2):
                            hd = 2 * fc + par
                            hb = par * 64
                            ps, t_ps = pair[par]
                            k.op("pe", lambda h, ps=ps, hd=hd, hb=hb, fc=fc: h.matmul(
                                ps[hb:hb + 64, fc * 64:(fc + 1) * 64], lhsT=lf(hd), rhs=rf(hd), start=True,
                                stop=(lf2 is None), tile_position=(hb, hb)), r=rtoks(par), w=[t_ps])
                            if lf2 is not None:
                                k.op("pe", lambda h, ps=ps, hd=hd, hb=hb, fc=fc: h.matmul(
                                    ps[hb:hb + 64, fc * 64:(fc + 1) * 64], lhsT=lf2(hd), rhs=rf2(hd), start=False,
                                    stop=True, tile_position=(hb, hb)), r=rtoks2(par), w=[t_ps])

                def ev_mask(pair, dst, t_dst, mi):
                    for par in range(2):
                        hb = par * 64
                        ps, t_ps = pair[par]
                        k.op("dve", lambda h, ps=ps, hb=hb: h.tensor_tensor(
                            out=dst[hb:hb + 64], in0=v3(ps, hb),
                            in1=msk[hb:hb + 64, mi, :].unsqueeze(1).to_broadcast([64, 3, 64]), op=ALU.mult),
                             r=[t_ps, t_msk], w=[t_dst[par]])

                def ev_copy(pair, dst, t_dst):
                    for par in range(2):
                        hb = par * 64
                        ps, t_ps = pair[par]
                        k.op("act", lambda h, ps=ps, hb=hb: h.activation(out=dst[hb:hb + 64], in_=v3(ps, hb),
                                                                         func=AF.Copy), r=[t_ps], w=[t_dst[par]])

                def ev_add(pair, dst, t_dst, old, t_old):
                    for par in range(2):
                        hb = par * 64
                        ps, t_ps = pair[par]
                        k.op("dve", lambda h, ps=ps, hb=hb: h.tensor_tensor(
                            out=dst[hb:hb + 64], in0=v3(ps, hb), in1=old[hb:hb + 64], op=ALU.add),
                             r=[t_ps, t_old[par]], w=[t_dst[par]])
                fmt = lambda t1, t2: (lambda par: [t1, t2])
                pair = npair()
                mm6(pair, lambda hd: fm(bt, hd), lambda hd: fm(at, hd), fmt(t_bt, t_at))
                ev_mask(pair, X[0], t_X[0], 0)
                pair = npair()
                mm6(pair, lambda hd: fm(at, hd), lambda hd: fm(bt, hd), fmt(t_bt, t_at))
                ev_mask(pair, Xt[0], t_Xt[0], 2)
                pair = npair()
                mm6(pair, lambda hd: fm(kt, hd), lambda hd: fm(at, hd), fmt(t_kt, t_at))
                ev_mask(pair, Lak, t_Lak, 0)
                pair = npair()
                mm6(pair, lambda hd: fm(bt, hd), lambda hd: fm(rt, hd), fmt(t_bt, t_rt))
                ev_mask(pair, Arb, t_Arb, 1)
                pair = npair()
                mm6(pair, lambda hd: fm(kt, hd), lambda hd: fm(rt, hd), fmt(t_kt, t_rt))
                ev_mask(pair, Ark, t_Ark, 1)
                self.dump("X0", X[0][:], t_X[0]); self.dump("Xt0", Xt[0][:], t_Xt[0])
                self.dump("Lak", Lak[:], t_Lak); self.dump("Arb", Arb[:], t_Arb); self.dump("Ark", Ark[:], t_Ark)
                self.dump("Vg", Vg[:], [t_Vg[i_][j_] for i_ in range(NCH) for j_ in range(2)])
                idb = idpl[:].unsqueeze(1).to_broadcast([128, 3, 64])
                k.op("pool", lambda h: h.tensor_tensor(out=P[0][:], in0=X[0][:], in1=idb, op=ALU.add),
                     r=t_X[0] + [t_idpl], w=t_P[0])
                k.op("pool", lambda h: h.tensor_tensor(out=Pt[0][:], in0=Xt[0][:], in1=idb, op=ALU.add),
                     r=t_Xt[0] + [t_idpl], w=t_Pt[0])
                cur = 0
                nsteps = 5
                for stp in range(nsteps):
                    nxt = 1 - cur
                    last = (stp == nsteps - 1)
                    pair = npair()
                    mm6(pair, lambda hd: pl(Xt[cur], hd), lambda hd: pl(X[cur], hd),
                        lambda par: [t_Xt[cur][par], t_X[cur][par]])
                    ev_copy(pair, X[nxt], t_X[nxt])
                    if not last:
                        pair = npair()
                        mm6(pair, lambda hd: pl(X[cur], hd), lambda hd: pl(Xt[cur], hd),
                            lambda par: [t_Xt[cur][par], t_X[cur][par]])
                        ev_copy(pair, Xt[nxt], t_Xt[nxt])
                    pair = npair()
                    mm6(pair, lambda hd: pl(Pt[cur], hd), lambda hd: pl(X[nxt], hd),
                        lambda par: [t_Pt[cur][par], t_X[nxt][par]])
                    ev_add(pair, P[nxt], t_P[nxt], P[cur], t_P[cur])
                    if not last:
                        pair = npair()
                        mm6(pair, lambda hd: pl(X[nxt], hd), lambda hd: pl(Pt[cur], hd),
                            lambda par: [t_Pt[cur][par], t_X[nxt][par]])
                        ev_add(pair, Pt[nxt], t_Pt[nxt], Pt[cur], t_Pt[cur])
                    cur = nxt
                Pf, t_Pf = P[cur], t_P[cur]
                self.dump("Pf", Pf[:], t_Pf)
                pair = npair()
                for n_, (tile_, tk_) in enumerate([(k2, t_k2), (b2, t_b2)]):
                    for fc in range(3):
                        for par in range(2):
                            hb = par * 64
                            ps, t_ps = pair[par]
                            k.op("pe", lambda h, ps=ps, n_=n_, fc=fc, hb=hb, tile_=tile_: h.matmul(
                                ps[hb:hb + 64, (n_ * 3 + fc) * 64:(n_ * 3 + fc + 1) * 64],
                                lhsT=tile_[hb:hb + 64, fc, cl], rhs=self.ident_b[hb:hb + 64, hb:hb + 64],
                                start=True, stop=True, tile_position=(hb, hb)), r=[tk_, self.t_const], w=[t_ps])
                for par in range(2):
                    hb = par * 64
                    ps, t_ps = pair[par]
                    k.op("act", lambda h, ps=ps, hb=hb: h.activation(
                        out=kbtok[hb:hb + 64].rearrange("p a b c -> p (a b c)"),
                        in_=ps[hb:hb + 64, 0:384], func=AF.Copy), r=[t_ps], w=[t_kbtok[par]])
                pair = npair()
                mm6(pair, lambda hd: fm(at, hd), lambda hd: Tb[(hd % 2) * 64:(hd % 2) * 64 + 64, hd // 2, :],
                    lambda par: [t_at, t_Tb[par]],
                    lambda hd: pl(Lak, hd), lambda hd: Vg[(hd % 2) * 64:(hd % 2) * 64 + 64, c, hd // 2, :],
                    lambda par: [t_Lak[par], t_Vg[c][par]])
                ev_copy(pair, r0b, t_r0b)
                pair = npair()
                mm6(pair, lambda hd: pl(Pf, hd), lambda hd: pl(r0b, hd), lambda par: [t_Pf[par], t_r0b[par]])
                ev_copy(pair, Ub, t_Ub)
                self.dump("r0b", r0b[:], t_r0b); self.dump("Ub", Ub[:], t_Ub); self.dump("kbtok", kbtok[:], t_kbtok)
                pair = npair()
                po = (c % 2) * 64
                for fc in range(3):
                    for par in range(2):
                        hd = 2 * fc + par
                        hb = par * 64
                        ps, t_ps = pair[par]
                        fs = slice(fc * 64, (fc + 1) * 64)
                        k.op("pe", lambda h, ps=ps, hd=hd, hb=hb, fs=fs, po=po, fc=fc: h.matmul(
                            ps[po:po + 64, fs], lhsT=fm(rt, hd), rhs=Tb[hb:hb + 64, fc, :], start=True, stop=False,
                            tile_position=(hb, po)), r=[t_rt, t_Tb[par]], w=[t_ps])
                        k.op("pe", lambda h, ps=ps, hd=hd, hb=hb, fs=fs, po=po: h.matmul(
                            ps[po:po + 64, fs], lhsT=pl(Arb, hd), rhs=pl(Ub, hd), start=False, stop=False,
                            tile_position=(hb, po)), r=[t_Arb[par], t_Ub[par]], w=[t_ps])
                        k.op("pe", lambda h, ps=ps, hd=hd, hb=hb, fs=fs, po=po, fc=fc: h.matmul(
                            ps[po:po + 64, fs], lhsT=pl(Ark, hd), rhs=Vg[hb:hb + 64, c, fc, :], start=False,
                            stop=True, tile_position=(hb, po)), r=[t_Ark[par], t_Vg[c][par]], w=[t_ps])
                for par in range(2):
                    ps, t_ps = pair[par]
                    k.op("act", lambda h, ps=ps, c=c, po=po, par=par: h.activation(
                        out=ysb[po:po + 64, c // 2, :].rearrange("p (a b d) -> p a b d", a=3, b=2)[:, :, par, :],
                        in_=ps[po:po + 64, 0:192].rearrange("p (a d) -> p a d", a=3), func=AF.Copy),
                         r=[t_ps], w=[t_ysb[c]])
                pair = npair()
                mm6(pair, lambda hd: kbtok[(hd % 2) * 64:(hd % 2) * 64 + 64, 1, hd // 2, :], lambda hd: pl(Ub, hd),
                    lambda par: [t_kbtok[par], t_Ub[par]],
                    lambda hd: kbtok[(hd % 2) * 64:(hd % 2) * 64 + 64, 0, hd // 2, :],
                    lambda hd: Vg[(hd % 2) * 64:(hd % 2) * 64 + 64, c, hd // 2, :],
                    lambda par: [t_kbtok[par], t_Vg[c][par]])
                for par in range(2):
                    hb = par * 64
                    ps, t_ps = pair[par]
                    k.op("dve", lambda h, c=c, hb=hb: h.tensor_tensor(
                        out=Tst[hb:hb + 64], in0=Tst[hb:hb + 64], in1=bcast_mid(WC[hb:hb + 64, :, c], 64),
                        op=ALU.mult), r=[t_T[par], t_WC], w=[t_T[par]])
                    k.op("dve", lambda h, ps=ps, hb=hb: h.tensor_tensor(
                        out=Tst[hb:hb + 64], in0=v3(ps, hb), in1=Tst[hb:hb + 64], op=ALU.add),
                         r=[t_ps, t_T[par]], w=[t_T[par]])
                    k.op("pool", lambda h, hb=hb: h.tensor_copy(out=Tb[hb:hb + 64], in_=Tst[hb:hb + 64]),
                         r=[t_T[par]], w=[t_Tb[par]])
                self.dump("T1", Tst[:], t_T)
            self.dump("ysb", ysb[:], t_ysb)
            yv = ysb[:].rearrange("p c (a b) -> p (c a) b", b=64)
            qv = ysq[:].rearrange("p c (a b) -> p (c a) b", b=64)
            k.op("dve", lambda h: h.tensor_reduce(out=gst[:, 0, :], in_=yv, axis=AX.X, op=ALU.add),
                 r=t_ysb, w=[t_gst])
            k.op("pool", lambda h: h.tensor_tensor(out=ysq[:], in0=ysb[:], in1=ysb[:], op=ALU.mult),
                 r=t_ysb, w=[t_ysq])
            k.op("dve", lambda h: h.tensor_reduce(out=gst[:, 1, :], in_=qv, axis=AX.X, op=ALU.add),
                 r=[t_ysq], w=[t_gst])
            k.op("dve", lambda h: h.tensor_scalar(out=gst[:, 0:2, :], in0=gst[:, 0:2, :], scalar1=1.0 / 64.0,
                                                  scalar2=None, op0=ALU.mult), r=[t_gst], w=[t_gst])
            k.op("dve", lambda h: h.tensor_tensor(out=gst[:, 2, :], in0=gst[:, 0, :], in1=gst[:, 0, :], op=ALU.mult),
                 r=[t_gst], w=[t_gst])
            k.op("dve", lambda h: h.tensor_tensor(out=gst[:, 3, :], in0=gst[:, 1, :], in1=gst[:, 2, :],
                                                  op=ALU.subtract), r=[t_gst], w=[t_gst])
            k.op("act", lambda h: h.activation(out=gst[:, 4, :], in_=gst[:, 3, :], func=AF.Sqrt,
                                               bias=pc[:, 6, 0:1]), r=[t_gst, t_pc], w=[t_gst])
            k.op("dve", lambda h: h.reciprocal(out=gst[:, 5, :], in_=gst[:, 4, :]), r=[t_gst], w=[t_gst])
            k.op("dve", lambda h: h.tensor_tensor(out=qv, in0=yv, in1=bcast_mid(gst[:, 0, :], 64), op=ALU.subtract),
                 r=t_ysb + [t_gst, t_ysq], w=[t_ysq])
            k.op("dve", lambda h: h.tensor_tensor(out=qv, in0=qv, in1=bcast_mid(gst[:, 5, :], 64), op=ALU.mult),
                 r=[t_gst, t_ysq], w=[t_ysq])
            k.op("pool", lambda h: h.tensor_tensor(out=ysq[:], in0=ysq[:],
                                                   in1=lng[:, 0, :].unsqueeze(1).to_broadcast([128, NCH // 2, 384]),
                                                   op=ALU.mult), r=[t_ysq, t_lng], w=[t_ysq])
            k.op("pool", lambda h: h.tensor_tensor(out=ysq[:], in0=ysq[:],
                                                   in1=lng[:, 1, :].unsqueeze(1).to_broadcast([128, NCH // 2, 384]),
                                                   op=ALU.add), r=[t_ysq, t_lng], w=[t_ysq])
            self.dump("ysq", ysq[:], [t_ysq])
            for fc in range(3):
                ps, t_ps = nps()
                for c2 in range(NCH // 2):
                    k.op("pe", lambda h, ps=ps, fc=fc, c2=c2: h.transpose(
                        out=ps[:, c2 * 128:(c2 + 1) * 128], in_=ysq[:, c2, fc * 128:(fc + 1) * 128],
                        identity=self.ident_f[:]), r=[t_ysq, self.t_const], w=[t_ps])
                k.op("dve", lambda h, ps=ps, fc=fc: h.tensor_tensor(out=tA[:, fc, :], in0=ps[:], in1=bonus[:, fc, :],
                                                                    op=ALU.add), r=[t_ps, t_bonus], w=[t_tA])
                k.op("pool", lambda h, fc=fc: h.tensor_tensor(out=mixT[:, 5 + fc, :], in0=tA[:, fc, :],
                                                              in1=gT[:, fc, :], op=ALU.mult),
                     r=[t_tA, t_gT], w=[t_mixT[5 + fc]])
            if self.mxb is not None:
                k.dma("sp", self.mxb[:, gs].rearrange("(c p) t -> p c t", p=128), mixT[:, 5:8, :],
                      r=t_mixT[5:8], w=[self.t_mxb])
            for blk in range(NB):
                r0 = g * G + blk * 128
                i = st["x"] % 2
                st["x"] += 1
                k.dma("sp", xr[i][:], src[r0:r0 + 128, :], r=[t_src[g]], w=[t_xr[i]])
                pairw = npair()
                for half in range(2):
                    ps, t_ps = pairw[half]
                    for kc in range(8):
                        k.op("pe", lambda h, ps=ps, kc=kc, blk=blk, half=half: h.matmul(
                            ps[:], lhsT=mixT[:, kc, blk * 128:(blk + 1) * 128],
                            rhs=wout[:, kc, half * 512:(half + 1) * 512], start=(kc == 0), stop=(kc == 7)),
                             r=[t_mixT[kc], t_wout[kc]], w=[t_ps])
                    k.op("dve", lambda h, ps=ps, half=half, i=i: h.tensor_tensor(
                        out=xr[i][:, half * 512:(half + 1) * 512], in0=ps[:],
                        in1=xr[i][:, half * 512:(half + 1) * 512], op=ALU.add), r=[t_ps, t_xr[i]], w=[t_xr[i]])
                k.dma("sp", dst[r0:r0 + 128, :], xr[i][:], r=[t_xr[i]], w=[t_dst[g]])
        k.pop_scope()

    def build(self):
        for ph in self.phases:
            kind = ph[0]
            srcs = {"x": (self.inp["x"], self.t_x), "xa": (self.xa, self.t_xa), "xb": (self.xb, self.t_xb)}
            if kind == "MA":
                _, l, src = ph
                self.pass_mixa(l, srcs[src][0], srcs[src][1])
            dsts = {"xa": (self.xa, self.t_xa), "xb": (self.xb, self.t_xb), "y": (self.y, self.t_y)}
            if kind == "MB":
                _, l, src, dst = ph
                self.pass_mixb(l, srcs[src][0], srcs[src][1], dsts[dst][0], dsts[dst][1])
            if kind == "F":
                _, l, src, dst, final = ph
                srcs = {"x": (self.inp["x"], self.t_x), "xa": (self.xa, self.t_xa), "xb": (self.xb, self.t_xb)}
                dsts = {"xa": (self.xa, self.t_xa), "xb": (self.xb, self.t_xb), "y": (self.y, self.t_y)}
                self.pass_ffn(l, srcs[src][0], srcs[src][1], dsts[dst][0], dsts[dst][1], final)
        self.k.finish()


FULL_PHASES = [("MA", 0, "x"), ("MB", 0, "x", "xa"), ("F", 0, "xa", "xb", False),
               ("MA", 1, "xb"), ("MB", 1, "xb", "xa"), ("F", 1, "xa", "y", True)]


def build_nc(phases=None, dbg=None):
    dbg = dbg or {}
    nc = bass.Bass("TRN2", target_bir_lowering=False)
    p = Prog(nc, phases or FULL_PHASES, dbg)
    p.build()
    return nc, p


def kernel(**inputs):
    nc, _ = build_nc()
    in_maps = []
    for b in range(8):
        m = {}
        for name in INPUT_SHAPES:
            a = np.asarray(inputs[name], dtype=np.float32)
            m[name] = np.ascontiguousarray(a[b]) if name == 'x' else np.ascontiguousarray(a)
        in_maps.append(m)
    res = run_bass_kernel_spmd(nc, in_maps, core_ids=list(range(8)))
    return np.stack([np.asarray(r["y"]) for r in res.results], axis=0).astype(np.float32)
```

```python
import math
from contextlib import ExitStack
import numpy as np
import concourse.bass as bass
import concourse.mybir as mybir
from concourse.bass_utils import run_bass_kernel_spmd

F32 = mybir.dt.float32
BF16 = mybir.dt.bfloat16
ALU = mybir.AluOpType
AF = mybir.ActivationFunctionType
AX = mybir.AxisListType

S = 4096
D = 1024
DFF = 4096
NIN = 3328
G = 512
NG = S // G
NB = G // 128
DEPTH = 2
HD = 64
NORM_EPS = 1e-6
GN_EPS = 64e-5

INPUT_SHAPES = {
    'x': [S, D], 'norm_mix_g': [2, D], 'w_in': [2, D, NIN], 'lam_q1': [2, 32], 'lam_k1': [2, 32],
    'lam_q2': [2, 32], 'lam_k2': [2, 32], 'subln_g': [2, 64], 'conv_w': [2, 3, 256],
    'shift_mu': [2, 1408], 'rwkv_w0': [2, 384], 'rwkv_w_up': [2, 64, 384], 'rwkv_a0': [2, 384],
    'rwkv_a_up': [2, 64, 384], 'rwkv_g_up': [2, 128, 384], 'rwkv_k_k': [2, 384], 'rwkv_k_a': [2, 384],
    'rwkv_r_k': [2, 6, 64], 'lnx_g': [2, 384], 'lnx_b': [2, 384], 'w_out': [2, D, D],
    'norm_mlp_g': [2, D], 'w_mlp_up': [2, D, DFF], 'w_mlp_down': [2, DFF, D], 'final_norm_g': [D],
}


class Tk:
    __slots__ = ("w", "r", "name")

    def __init__(self, name=""):
        self.w = None
        self.r = {}
        self.name = name


class KB:
    EPOCH = 6000
    NDS = 24

    def __init__(self, nc):
        self.nc = nc
        self.eng = {"pe": nc.tensor, "act": nc.scalar, "dve": nc.vector, "pool": nc.gpsimd, "sp": nc.sync}
        self.cnt = {e: 0 for e in self.eng}
        self.semh = {}
        self.seen = {e: {} for e in self.eng}
        self.maxep = {e: {} for e in self.eng}
        self.dsem = [("dma", i) for i in range(self.NDS)]
        for kx in self.dsem:
            self.semh[kx] = nc.alloc_semaphore(f"dma{kx[1]}")
        self.dval = [0] * self.NDS
        self.dnext = 0
        self.ndma = 0
        self.nwait = 0
        self._n = 0
        self.root = ExitStack()
        self.scope = None

    def sb(self, name, shape, dt):
        self._n += 1
        cm = self.nc.sbuf_tensor("%s_%d" % (name, self._n), list(shape), dt)
        return (self.scope or self.root).enter_context(cm)

    def ps(self, name, shape, dt):
        self._n += 1
        cm = self.nc.psum_tensor("%s_%d" % (name, self._n), list(shape), dt)
        return (self.scope or self.root).enter_context(cm)

    def push_scope(self):
        self.scope = ExitStack()

    def pop_scope(self):
        self.barrier()
        self.scope.close()
        self.scope = None

    def _cursem(self, e):
        key = (e, self.cnt[e] // self.EPOCH)
        if key not in self.semh:
            self.semh[key] = self.nc.alloc_semaphore(f"s_{e}_{key[1]}")
        return key

    def _wait(self, e, tok):
        key, val = tok[0], tok[1]
        seen = self.seen[e]
        if seen.get(key, 0) >= val:
            return
        if key[0] != "dma":
            if self.maxep[e].get(key[0], -1) > key[1]:
                return
            self.maxep[e][key[0]] = max(self.maxep[e].get(key[0], -1), key[1])
        self.eng[e].wait_ge(self.semh[key], val)
        self.nwait += 1
        seen[key] = val

    def _deps(self, e, reads, writes, is_dma):
        for t in reads:
            if t.w is not None:
                tok = t.w
                if (not is_dma) and tok[2] == e and e == "pe":
                    continue
                self._wait(e, tok)
        for t in writes:
            if t.w is not None:
                tok = t.w
                if is_dma or tok[2] != e or tok[3] or e != "pe":
                    self._wait(e, tok)
            for rk, tok in t.r.items():
                if isinstance(tok, list):
                    for tk in tok:
                        self._wait(e, tk)
                elif is_dma or tok[2] != e or e != "pe":
                    self._wait(e, tok)

    def _record(self, tok, reads, writes):
        for t in reads:
            if tok[3]:
                t.r.setdefault("dma", []).append(tok)
            else:
                t.r[tok[2]] = tok
        for t in writes:
            t.w = tok
            t.r = {}

    def op(self, e, fn, r=(), w=()):
        self._deps(e, r, w, False)
        key = self._cursem(e)
        ins = fn(self.eng[e])
        self.cnt[e] += 1
        val = self.cnt[e] - key[1] * self.EPOCH
        ins.then_inc(self.semh[key], 1)
        tok = (key, val, e, False)
        self._record(tok, r, w)
        return tok

    def dma(self, q, out, in_, r=(), w=(), **kw):
        self._deps(q, r, w, True)
        slot = self.dnext
        self.dnext = (self.dnext + 1) % self.NDS
        key = self.dsem[slot]
        if self.dval[slot] > 0:
            self._wait(q, (key, self.dval[slot], "dma", True))
        ins = self.eng[q].dma_start(out=out, in_=in_, **kw)
        self.dval[slot] += 16
        ins.then_inc(self.semh[key], 16)
        tok = (key, self.dval[slot], "dma", True)
        self._record(tok, r, w)
        self.ndma += 1
        return tok

    def barrier(self):
        lasts = {}
        for e in ("pe", "act", "dve", "pool"):
            if self.cnt[e] > 0:
                key = (e, (self.cnt[e] - 1) // self.EPOCH)
                lasts[e] = (key, self.cnt[e] - key[1] * self.EPOCH, e, False)
        for e in self.eng:
            for e2, tok in lasts.items():
                if e2 != e:
                    self._wait(e, tok)
            for slot in range(self.NDS):
                if self.dval[slot] > 0:
                    self._wait(e, (self.dsem[slot], self.dval[slot], "dma", True))

    def finish(self):
        for slot in range(self.NDS):
            if self.dval[slot] > 0:
                self._wait("sp", (self.dsem[slot], self.dval[slot], "dma", True))


def bcast_mid(ap2d, n):
    p, j = ap2d.shape
    return ap2d.unsqueeze(2).to_broadcast([p, j, n])


class Prog:
    def __init__(self, nc, phases, dbg=None):
        self.nc = nc
        self.k = KB(nc)
        self.phases = phases
        self.dbg = dbg if dbg is not None else {}
        k = self.k
        self.inp = {}
        for name, shp in INPUT_SHAPES.items():
            self.inp[name] = nc.dram_tensor(name, shp, F32, kind="ExternalInput").ap()
        self.y = nc.dram_tensor("y", [S, D], F32, kind="ExternalOutput").ap()
        self.xa = nc.dram_tensor("xa", [S, D], F32).ap()
        self.xb = nc.dram_tensor("xb", [S, D], F32).ap()
        def dram(name, shape, dt):
            kind = "ExternalOutput" if name in self.dbg else ("ExternalInput" if ("in:" + name) in self.dbg else "Internal")
            return nc.dram_tensor(name, shape, dt, kind=kind).ap()
        self.prw = dram("prw", [1408, S], F32)
        self.t_prw = [Tk() for _ in range(NG)]
        self.mxa = dram("mxa", [5 * 128, S], BF16)
        self.mxb = dram("mxb", [3 * 128, S], BF16) if "mxb" in self.dbg else None
        self.t_mxb = Tk()
        self.t_mxa = [Tk() for _ in range(NG)]
        self.t_xa = [Tk("xa%d" % i) for i in range(NG)]
        self.t_xb = [Tk("xb%d" % i) for i in range(NG)]
        self.t_x = [Tk("x%d" % i) for i in range(NG)]
        self.t_y = [Tk("y%d" % i) for i in range(NG)]
        self.ident_b = k.sb("ident_b", [128, 128], BF16)
        self.ident_f = k.sb("ident_f", [128, 128], F32)
        self.t_const = Tk("const")
        self.eps_t = k.sb("eps_t", [128, 1], F32)
        k.op("pool", lambda h: h.memset(self.ident_b[:], 1.0), w=[self.t_const])
        k.op("pool", lambda h: h.affine_select(out=self.ident_b[:], in_=self.ident_b[:], pattern=[[1, 128]],
                                               compare_op=ALU.is_equal, fill=0.0, base=0, channel_multiplier=-1),
             r=[self.t_const], w=[self.t_const])
        k.op("pool", lambda h: h.memset(self.ident_f[:], 1.0), w=[self.t_const])
        k.op("pool", lambda h: h.affine_select(out=self.ident_f[:], in_=self.ident_f[:], pattern=[[1, 128]],
                                               compare_op=ALU.is_equal, fill=0.0, base=0, channel_multiplier=-1),
             r=[self.t_const], w=[self.t_const])
        k.op("pool", lambda h: h.memset(self.eps_t[:], NORM_EPS), w=[self.t_const])
        self.gcol = k.sb("gcol", [128, 4, 8], F32)
        self.t_gcol = Tk("gcol")
        for n, (nm, l) in enumerate([("norm_mix_g", 0), ("norm_mix_g", 1), ("norm_mlp_g", 0), ("norm_mlp_g", 1)]):
            k.dma("sp", self.gcol[:, n, :], self.inp[nm][l].rearrange("(j p) -> p j", p=128),
                  w=[self.t_gcol], allow_slow_non_contiguous=True)
        self.xt = [k.sb("xt%d" % i, [128, D], F32) for i in range(2)]
        self.t_xt = [Tk("xt%d" % i) for i in range(2)]
        self.xn = [k.sb("xn%d" % i, [128, D], BF16) for i in range(2)]
        self.t_xn = [Tk("xn%d" % i) for i in range(2)]
        self.junk = k.sb("junk", [128, D], BF16)
        self.t_junk = Tk("junk")
        self.stat = [k.sb("stat%d" % i, [128, 4], F32) for i in range(2)]
        self.t_stat = [Tk("stat%d" % i) for i in range(2)]
        self.hT = k.sb("hT", [128, 8, G], BF16)
        self.t_hT = [Tk("hT%d" % i) for i in range(NB)]
        self.psT = [k.ps("psT%d" % i, [128, 8, 128], BF16) for i in range(1)]
        self.t_psT = [Tk("psT%d" % i) for i in range(1)]
        self.nblk = 0
        self.dumped = set()

    def dump(self, name, ap, toks):
        if "dump" not in self.dbg or name in self.dumped:
            return
        self.dumped.add(name)
        d = self.nc.dram_tensor("d_" + name, list(ap.shape), ap.dtype, kind="ExternalOutput").ap()
        self.k.dma("sp", d, ap, r=list(toks), w=[Tk()])

    def norm_block(self, src_ap, src_tk, gidx, blk, hT=None, t_hT=None):
        hT = self.hT if hT is None else hT
        t_hT = self.t_hT if t_hT is None else t_hT
        k = self.k
        i = self.nblk % 2
        self.nblk += 1
        xt, t_xt, xn, t_xn, st, t_st = self.xt[i], self.t_xt[i], self.xn[i], self.t_xn[i], self.stat[i], self.t_stat[i]
        k.dma("sp", xt[:], src_ap, r=[src_tk], w=[t_xt])
        k.op("act", lambda h: h.activation(out=self.junk[:], in_=xt[:], func=AF.Square, accum_out=st[:, 0:1]),
             r=[t_xt], w=[self.t_junk, t_st])
        k.op("act", lambda h: h.activation(out=st[:, 1:2], in_=st[:, 0:1], func=AF.Sqrt, scale=1.0 / D,
                                           bias=self.eps_t[:, 0:1]),
             r=[t_st, self.t_const], w=[t_st])
        k.op("dve", lambda h: h.reciprocal(out=st[:, 2:3], in_=st[:, 1:2]), r=[t_st], w=[t_st])
        k.op("dve", lambda h: h.tensor_scalar(out=xn[:], in0=xt[:], scalar1=st[:, 2:3], scalar2=None, op0=ALU.mult),
             r=[t_xt, t_st], w=[t_xn])
        pT, t_pT = self.psT[0], self.t_psT[0]
        for j in range(8):
            k.op("pe", lambda h, j=j: h.transpose(out=pT[:, j, :], in_=xn[:, j * 128:(j + 1) * 128],
                                                   identity=self.ident_b[:]),
                 r=[t_xn, self.t_const], w=[t_pT])
        k.op("dve", lambda h: h.tensor_tensor(out=hT[:, :, blk * 128:(blk + 1) * 128], in0=pT[:],
                                              in1=bcast_mid(self.gcol[:, gidx, :], 128), op=ALU.mult),
             r=[t_pT, self.t_gcol], w=[t_hT[blk]])

    def pass_ffn(self, l, src, t_src, dst, t_dst, final):
        k = self.k
        k.push_scope()
        if final:
            self.gfin = k.sb("gfin", [128, D], F32)
            self.t_gfin = Tk("gfin")
            k.dma("sp", self.gfin[:], self.inp["final_norm_g"].unsqueeze(0).partition_broadcast(128),
                  w=[self.t_gfin])
        wup = k.sb("wup", [128, 8, DFF], BF16)
        wdn = k.sb("wdn", [128, 32, D], BF16)
        t_wup = [Tk() for _ in range(8)]
        t_wdn = [Tk() for _ in range(8)]
        for kc in range(8):
            k.dma("pool", wup[:, kc, :], self.inp["w_mlp_up"][l, kc * 128:(kc + 1) * 128, :], w=[t_wup[kc]])
        for c4 in range(8):
            k.dma("pool", wdn[:, c4 * 4:(c4 + 1) * 4, :],
                  self.inp["w_mlp_down"][l, c4 * 512:(c4 + 1) * 512, :].rearrange("(c p) n -> p c n", p=128),
                  w=[t_wdn[c4]])
        aT = k.sb("aT", [128, 32, G], BF16)
        t_aT = [Tk() for _ in range(32)]
        rt = [k.sb("rt%d" % i, [128, G], F32) for i in range(2)]
        t_rt = [Tk() for _ in range(2)]
        psU = [k.ps("psU%d" % i, [128, G], F32) for i in range(2)]
        t_psU = [Tk() for _ in range(2)]
        psD = [k.ps("psD%d" % i, [128, 512], F32) for i in range(2)]
        t_psD = [Tk() for _ in range(2)]
        xr = [k.sb("xr%d" % i, [128, D], F32) for i in range(2)]
        t_xr = [Tk() for _ in range(2)]
        xo, t_xo = xr, t_xr
        st2 = [k.sb("st2_%d" % i, [128, 4], F32) for i in range(2)]
        t_st2 = [Tk() for _ in range(2)]
        n_o = 0
        for g in range(NG):
            for blk in range(NB):
                r0 = g * G + blk * 128
                self.norm_block(src[r0:r0 + 128, :], t_src[g], 2 + l, blk)
            for c in range(32):
                ps, t_ps = psU[c % 2], t_psU[c % 2]
                for kc in range(8):
                    k.op("pe", lambda h, kc=kc, c=c, ps=ps: h.matmul(ps[:], lhsT=wup[:, kc, c * 128:(c + 1) * 128],
                                                                    rhs=self.hT[:, kc, :], start=(kc == 0),
                                                                    stop=(kc == 7)),
                         r=[t_wup[kc]] + self.t_hT, w=[t_ps])
                r_, t_r = rt[c % 2], t_rt[c % 2]
                k.op("act", lambda h, ps=ps, r_=r_: h.activation(out=r_[:], in_=ps[:], func=AF.Relu),
                     r=[t_ps], w=[t_r])
                k.op("pool", lambda h, r_=r_, c=c: h.tensor_tensor(out=aT[:, c, :], in0=r_[:], in1=r_[:],
                                                                   op=ALU.mult),
                     r=[t_r], w=[t_aT[c]])
            for blk in range(NB):
                r0 = g * G + blk * 128
                i = n_o % 2
                n_o += 1
                k.dma("sp", xr[i][:], src[r0:r0 + 128, :], r=[t_src[g]], w=[t_xr[i]])
                for half in range(2):
                    ps, t_ps = psD[half], t_psD[half]
                    for c in range(32):
                        k.op("pe", lambda h, c=c, ps=ps, half=half, blk=blk: h.matmul(
                            ps[:], lhsT=aT[:, c, blk * 128:(blk + 1) * 128],
                            rhs=wdn[:, c, half * 512:(half + 1) * 512], start=(c == 0), stop=(c == 31)),
                             r=[t_aT[c], t_wdn[c // 4]], w=[t_ps])
                    k.op("dve", lambda h, ps=ps, half=half, i=i: h.tensor_tensor(
                        out=xo[i][:, half * 512:(half + 1) * 512], in0=ps[:],
                        in1=xr[i][:, half * 512:(half + 1) * 512], op=ALU.add),
                         r=[t_ps, t_xr[i]], w=[t_xr[i]])
                if final:
                    st, t_st = st2[i], t_st2[i]
                    k.op("act", lambda h, i=i, st=st: h.activation(out=self.junk[:], in_=xo[i][:], func=AF.Square,
                                                                   accum_out=st[:, 0:1]),
                         r=[t_xo[i]], w=[self.t_junk, t_st])
                    k.op("act", lambda h, st=st: h.activation(out=st[:, 1:2], in_=st[:, 0:1], func=AF.Sqrt,
                                                              scale=1.0 / D, bias=self.eps_t[:, 0:1]),
                         r=[t_st, self.t_const], w=[t_st])
                    k.op("dve", lambda h, st=st: h.reciprocal(out=st[:, 2:3], in_=st[:, 1:2]), r=[t_st], w=[t_st])
                    k.op("dve", lambda h, i=i, st=st: h.scalar_tensor_tensor(
                        out=xo[i][:], in0=xo[i][:], scalar=st[:, 2:3], in1=self.gfin[:], op0=ALU.mult, op1=ALU.mult),
                         r=[t_xo[i], t_st, self.t_gfin], w=[t_xo[i]])
                k.dma("sp", dst[r0:r0 + 128, :], xo[i][:], r=[t_xo[i]], w=[t_dst[g]])
        k.pop_scope()


    def pass_mixa(self, l, src, t_src):
        k = self.k
        k.push_scope()
        inp = self.inp
        lambda_init = 0.8 - 0.6 * math.exp(-0.3 * l)
        win = k.sb("win", [128, 8, NIN], BF16)
        t_win = [Tk() for _ in range(8)]
        for kc in range(8):
            k.dma("pool", win[:, kc, :], inp["w_in"][l, kc * 128:(kc + 1) * 128, :], w=[t_win[kc]])
        cst = k.sb("cst", [128, 8], F32)
        t_cst = Tk()
        lq = k.sb("lq", [128, 4, 32], F32)
        t_lq = Tk()
        for i, nm in enumerate(["lam_q1", "lam_k1", "lam_q2", "lam_k2"]):
            k.dma("sp", lq[:, i, :], inp[nm][l:l + 1, :].partition_broadcast(128), w=[t_lq])
        k.op("dve", lambda h: h.tensor_tensor(out=lq[:, 0, :], in0=lq[:, 0, :], in1=lq[:, 1, :], op=ALU.mult),
             r=[t_lq], w=[t_lq])
        k.op("dve", lambda h: h.tensor_tensor(out=lq[:, 2, :], in0=lq[:, 2, :], in1=lq[:, 3, :], op=ALU.mult),
             r=[t_lq], w=[t_lq])
        k.op("dve", lambda h: h.tensor_reduce(out=cst[:, 0:1], in_=lq[:, 0, :], axis=AX.X, op=ALU.add),
             r=[t_lq], w=[t_cst])
        k.op("dve", lambda h: h.tensor_reduce(out=cst[:, 1:2], in_=lq[:, 2, :], axis=AX.X, op=ALU.add),
             r=[t_lq], w=[t_cst])
        k.op("act", lambda h: h.activation(out=cst[:, 2:4], in_=cst[:, 0:2], func=AF.Exp), r=[t_cst], w=[t_cst])
        k.op("dve", lambda h: h.tensor_tensor(out=cst[:, 4:5], in0=cst[:, 2:3], in1=cst[:, 3:4], op=ALU.subtract),
             r=[t_cst], w=[t_cst])
        k.op("dve", lambda h: h.tensor_scalar(out=cst[:, 5:6], in0=cst[:, 4:5], scalar1=float(lambda_init),
                                              scalar2=None, op0=ALU.add), r=[t_cst], w=[t_cst])
        k.op("pool", lambda h: h.memset(cst[:, 6:7], NORM_EPS), w=[t_cst])
        subg = k.sb("subg", [128, 64], F32)
        t_subg = Tk()
        k.dma("sp", subg[:], inp["subln_g"][l:l + 1, :].partition_broadcast(128), w=[t_subg])
        k.op("dve", lambda h: h.tensor_scalar(out=subg[:], in0=subg[:], scalar1=float(1.0 - lambda_init),
                                              scalar2=None, op0=ALU.mult), r=[t_subg], w=[t_subg])
        cw = k.sb("cw", [128, 2, 3], F32)
        t_cw = Tk()
        for j in range(2):
            k.dma("sp", cw[:, j, :], inp["conv_w"][l, :, j * 128:(j + 1) * 128].rearrange("t p -> p t"),
                  w=[t_cw], allow_slow_non_contiguous=True)
        mask = k.sb("mask", [128, 128], BF16)
        t_mask = Tk()
        k.op("pool", lambda h: h.memset(mask[:], 1.0), w=[t_mask])
        k.op("pool", lambda h: h.affine_select(out=mask[:], in_=mask[:], pattern=[[1, 128]], compare_op=ALU.is_ge,
                                               fill=0.0, base=0, channel_multiplier=-1), r=[t_mask], w=[t_mask])
        kT = k.sb("kT", [128, 3, S], BF16)
        t_kT = [[Tk() for _ in range(NG)] for _ in range(3)]
        Vt = k.sb("Vt", [128, S // 128, 6, 65], BF16)
        t_V = [Tk() for _ in range(S // 128)]
        for b8 in range(0, S // 128, 8):
            k.op("pool", lambda h, b8=b8: h.memset(Vt[:, b8:b8 + 8], 1.0), w=t_V[b8:b8 + 8])
        qT = k.sb("qT", [128, 3, G], BF16)
        t_qT = [Tk() for _ in range(3)]
        cv = k.sb("cv", [128, 6, G], F32)
        t_cv = [Tk() for _ in range(6)]
        zb = k.sb("zb", [128, 2, G + 2], F32)
        t_zb = [Tk() for _ in range(2)]
        k.op("pool", lambda h: h.memset(zb[:], 0.0), w=t_zb)
        ycv = k.sb("ycv", [128, G], F32)
        t_ycv = Tk()
        stg = [k.sb("stg%d" % i, [128, G], F32) for i in range(2)]
        t_stg = [Tk() for _ in range(2)]
        eT = [k.sb("eT%d" % i, [128, G], BF16) for i in range(4)]
        t_eT = [Tk() for _ in range(4)]
        uT = k.sb("uT", [128, 2, G], F32)
        t_uT = [Tk() for _ in range(2)]
        oat = k.sb("oat", [128, NB, 6, 64], F32)
        t_oat = Tk()
        osq = k.sb("osq", [128, NB, 6, 64], F32)
        t_osq = Tk()
        t1 = k.sb("t1", [128, NB, 64], F32)
        t_t1 = Tk()
        rl = k.sb("rl", [128, 2, NB], F32)
        t_rl = Tk()
        ss = k.sb("ss", [128, 3, NB * 6], F32)
        t_ss = Tk()
        oab = k.sb("oab", [128, NB, 384], BF16)
        t_oab = Tk()
        mixA = k.sb("mixA", [128, 5, G], BF16)
        t_mixA = [Tk() for _ in range(5)]
        NPA = 4
        psA = [k.ps("psA%d" % i, [128, G], F32) for i in range(NPA)]
        t_psA = [Tk() for _ in range(NPA)]
        psO = [k.ps("psO%d" % i, [128, G], F32) for i in range(2)]
        t_psO = [Tk() for _ in range(2)]
        psTr1 = k.ps("psTr", [128, NB, 65], F32)
        t_psTr1 = Tk()
        hTs = [self.hT, k.sb("hT2", [128, 8, G], BF16)]
        t_hTs = [self.t_hT, [Tk() for _ in range(NB)]]
        qTs = [qT, k.sb("qT2", [128, 3, G], BF16)]
        t_qTs = [t_qT, [Tk() for _ in range(3)]]
        cvs = [cv, k.sb("cv2", [128, 6, G], F32)]
        t_cvs = [t_cv, [Tk() for _ in range(6)]]
        cnt = {"A": 0, "S": 0, "E": 0}
        sc = 1.0 / math.sqrt(32.0)

        def nbank():
            i = cnt["A"] % NPA
            cnt["A"] += 1
            return psA[i], t_psA[i]

        def front(g):
            hT, t_hT = hTs[g % 2], t_hTs[g % 2]
            qT_, t_qT_ = qTs[g % 2], t_qTs[g % 2]
            cv_, t_cv_ = cvs[g % 2], t_cvs[g % 2]
            for blk in range(NB):
                r0 = g * G + blk * 128
                self.norm_block(src[r0:r0 + 128, :], t_src[g], l, blk, hT, t_hT)
                yield
            for c in list(range(0, 6)) + list(range(9, 26)):
                ps, t_ps = nbank()
                for kc in range(8):
                    k.op("pe", lambda h, kc=kc, c=c, ps=ps: h.matmul(ps[:], lhsT=win[:, kc, c * 128:(c + 1) * 128],
                                                                    rhs=hT[:, kc, :], start=(kc == 0),
                                                                    stop=(kc == 7)),
                         r=[t_win[kc]] + t_hT, w=[t_ps])
                if c < 3:
                    k.op("dve", lambda h, ps=ps, c=c: h.tensor_copy(out=qT_[:, c, :], in_=ps[:]),
                         r=[t_ps], w=[t_qT_[c]])
                elif c < 6:
                    k.op("dve", lambda h, ps=ps, c=c: h.tensor_copy(out=kT[:, c - 3, g * G:(g + 1) * G], in_=ps[:]),
                         r=[t_ps], w=[t_kT[c - 3][g]])
                elif c < 15:
                    k.op("dve", lambda h, ps=ps, c=c: h.tensor_copy(out=cv_[:, c - 9, :], in_=ps[:]),
                         r=[t_ps], w=[t_cv_[c - 9]])
                else:
                    i = cnt["S"] % 2
                    cnt["S"] += 1
                    k.op("dve", lambda h, ps=ps, i=i: h.tensor_copy(out=stg[i][:], in_=ps[:]), r=[t_ps], w=[t_stg[i]])
                    k.dma("sp", self.prw[(c - 15) * 128:(c - 14) * 128, g * G:(g + 1) * G], stg[i][:],
                          r=[t_stg[i]], w=[self.t_prw[g]])
                yield
            for blk in range(NB):
                ps, t_ps = nbank()
                for kc in range(8):
                    k.op("pe", lambda h, kc=kc, ps=ps, blk=blk: h.matmul(
                        ps[:, 0:384], lhsT=hT[:, kc, blk * 128:(blk + 1) * 128], rhs=win[:, kc, 768:1152],
                        start=(kc == 0), stop=(kc == 7)), r=[t_win[kc], t_hT[blk]], w=[t_ps])
                k.op("dve", lambda h, ps=ps, blk=blk: h.tensor_copy(
                    out=Vt[:, g * NB + blk, :, 0:64], in_=ps[:, 0:384].rearrange("p (a b) -> p a b", a=6)),
                     r=[t_ps], w=[t_V[g * NB + blk]])
                yield
            for j in range(2):
                k.op("pool", lambda h, j=j: h.tensor_tensor(out=zb[:, j, 2:G + 2], in0=cv_[:, 2 + j, :],
                                                            in1=cv_[:, 4 + j, :], op=ALU.mult),
                     r=[t_cv_[2 + j], t_cv_[4 + j]], w=[t_zb[j]])
                k.op("pool", lambda h, j=j: h.tensor_scalar(out=ycv[:], in0=zb[:, j, 0:G], scalar1=cw[:, j, 0:1],
                                                            scalar2=None, op0=ALU.mult),
                     r=[t_zb[j], t_cw], w=[t_ycv])
                k.op("dve", lambda h, j=j: h.scalar_tensor_tensor(out=ycv[:], in0=zb[:, j, 1:G + 1],
                                                                  scalar=cw[:, j, 1:2], in1=ycv[:],
                                                                  op0=ALU.mult, op1=ALU.add),
                     r=[t_zb[j], t_cw, t_ycv], w=[t_ycv])
                k.op("dve", lambda h, j=j: h.scalar_tensor_tensor(out=ycv[:], in0=zb[:, j, 2:G + 2],
                                                                  scalar=cw[:, j, 2:3], in1=ycv[:],
                                                                  op0=ALU.mult, op1=ALU.add),
                     r=[t_zb[j], t_cw, t_ycv], w=[t_ycv])
                k.op("pool", lambda h, j=j: h.tensor_tensor(out=mixAs[g % 2][:, 3 + j, :], in0=ycv[:],
                                                            in1=cv_[:, j, :], op=ALU.mult),
                     r=[t_ycv, t_cv_[j]], w=[t_mixAs[g % 2][3 + j]])
                k.op("pool", lambda h, j=j: h.tensor_copy(out=zb[:, j, 0:2], in_=zb[:, j, G:G + 2]),
                     r=[t_zb[j]], w=[t_zb[j]])
                yield

        def back(g):
            qT_, t_qT_ = qTs[g % 2], t_qTs[g % 2]
            mixA_, t_mixA_ = mixAs[g % 2], t_mixAs[g % 2]
            nkb = 4 * g + 4
            LOOK = 2

            def emit_S(hd, half, j):
                qc = hd // 2
                pb = (hd % 2) * 64 + half * 32
                off = max(0, j - 4 * g) * 128
                ps, t_ps = nbank()
                k.op("pe", lambda h: h.matmul(
                    ps[:, off:G], lhsT=kT[pb:pb + 32, qc, j * 128:(j + 1) * 128],
                    rhs=qT_[pb:pb + 32, qc, off:G], start=True, stop=True, tile_position=(pb, 0)),
                     r=[t_kT[qc][j // NB], t_qT_[qc]], w=[t_ps])
                e, t_e = eT[cnt["E"] % 4], t_eT[cnt["E"] % 4]
                cnt["E"] += 1
                k.op("act", lambda h: h.activation(out=e[:, off:G], in_=ps[:, off:G], func=AF.Exp, scale=sc),
                     r=[t_ps], w=[t_e])
                if j >= 4 * g:
                    k.op("pool", lambda h: h.tensor_tensor(out=e[:, off:off + 128], in0=e[:, off:off + 128],
                                                           in1=mask[:], op=ALU.mult), r=[t_e, t_mask], w=[t_e])
                return (hd, half, j, off, e, t_e)

            def emit_PV(item):
                hd, half, j, off, e, t_e = item
                k.op("pe", lambda h: h.matmul(psO[half][0:65, off:G], lhsT=Vt[:, j, hd, :], rhs=e[:, off:G],
                                              start=(j == 0), stop=(j == nkb - 1)),
                     r=[t_e, t_V[j]], w=[t_psO[half]])
                if j != nkb - 1:
                    return
                k.op("dve", lambda h: h.tensor_copy(out=uT[0:65, half, :], in_=psO[half][0:65, :]),
                     r=[t_psO[half]], w=[t_uT[half]])
                for blk in range(NB):
                    k.op("pe", lambda h, blk=blk: h.transpose(
                        out=psTr1[:, blk, :], in_=uT[0:65, half, blk * 128:(blk + 1) * 128],
                        identity=self.ident_f[0:65, 0:65]), r=[t_uT[half], self.t_const], w=[t_psTr1])
                k.op("dve", lambda h: h.reciprocal(out=rl[:, half, :], in_=psTr1[:, :, 64]),
                     r=[t_psTr1], w=[t_rl])
                if half == 0:
                    k.op("dve", lambda h: h.tensor_tensor(out=t1[:], in0=psTr1[:, :, 0:64],
                                                          in1=bcast_mid(rl[:, 0, :], 64), op=ALU.mult),
                         r=[t_psTr1, t_rl], w=[t_t1])
                    return
                k.op("dve", lambda h: h.tensor_scalar(out=rl[:, 1, :], in0=rl[:, 1, :], scalar1=cst[:, 5:6],
                                                      scalar2=None, op0=ALU.mult), r=[t_rl, t_cst], w=[t_rl])
                k.op("dve", lambda h: h.tensor_tensor(out=oat[:, :, hd, :], in0=psTr1[:, :, 0:64],
                                                      in1=bcast_mid(rl[:, 1, :], 64), op=ALU.mult),
                     r=[t_psTr1, t_rl], w=[t_oat])
                k.op("dve", lambda h: h.tensor_tensor(out=oat[:, :, hd, :], in0=t1[:], in1=oat[:, :, hd, :],
                                                      op=ALU.subtract), r=[t_t1, t_oat], w=[t_oat])

            pend = []
            for hd in range(6):
                for half in range(2):
                    for j in range(nkb):
                        pend.append(emit_S(hd, half, j))
                        if len(pend) > LOOK:
                            emit_PV(pend.pop(0))
                        yield
            while pend:
                emit_PV(pend.pop(0))
            yield
            k.op("pool", lambda h: h.tensor_tensor(out=osq[:], in0=oat[:], in1=oat[:], op=ALU.mult),
                 r=[t_oat], w=[t_osq])
            k.op("dve", lambda h: h.tensor_reduce(out=ss[:, 0, :], in_=osq[:].rearrange("p a b c -> p (a b) c"),
                                                  axis=AX.X, op=ALU.add), r=[t_osq], w=[t_ss])
            k.op("act", lambda h: h.activation(out=ss[:, 1, :], in_=ss[:, 0, :], func=AF.Sqrt, scale=1.0 / 64.0,
                                               bias=cst[:, 6:7]), r=[t_ss, t_cst], w=[t_ss])
            k.op("dve", lambda h: h.reciprocal(out=ss[:, 2, :], in_=ss[:, 1, :]), r=[t_ss], w=[t_ss])
            k.op("dve", lambda h: h.tensor_tensor(out=osq[:].rearrange("p a b c -> p (a b) c"),
                                                  in0=oat[:].rearrange("p a b c -> p (a b) c"),
                                                  in1=bcast_mid(ss[:, 2, :], 64), op=ALU.mult),
                 r=[t_oat, t_ss], w=[t_osq])
            k.op("dve", lambda h: h.tensor_tensor(
                out=oab[:].rearrange("p a (b c) -> p (a b) c", c=64), in0=osq[:].rearrange("p a b c -> p (a b) c"),
                in1=subg[:].unsqueeze(1).to_broadcast([128, NB * 6, 64]), op=ALU.mult),
                 r=[t_osq, t_subg], w=[t_oab])
            yield
            pT, t_pT = self.psT[0], self.t_psT[0]
            for blk in range(NB):
                for c in range(3):
                    k.op("pe", lambda h, blk=blk, c=c: h.transpose(out=pT[:, c, :],
                                                                   in_=oab[:, blk, c * 128:(c + 1) * 128],
                                                                   identity=self.ident_b[:]),
                         r=[t_oab, self.t_const], w=[t_pT])
                k.op("dve", lambda h, blk=blk: h.tensor_copy(out=mixA_[:, 0:3, blk * 128:(blk + 1) * 128],
                                                             in_=pT[:, 0:3, :]),
                     r=[t_pT], w=t_mixA_[0:3])
                yield
            k.dma("sp", self.mxa[:, g * G:(g + 1) * G].rearrange("(c p) t -> p c t", p=128), mixA_[:],
                  r=t_mixA_, w=[self.t_mxa[g]])

        mixAs = [mixA, k.sb("mixA2", [128, 5, G], BF16)]
        t_mixAs = [t_mixA, [Tk() for _ in range(5)]]
        for _ in front(0):
            pass
        for g in range(NG):
            bk = back(g)
            fr = front(g + 1) if g + 1 < NG else iter(())
            n_att = 12 * (4 * g + 4)
            n_fr = 4 + 23 + 4 + 2
            ratio = max(1, n_att // n_fr)
            fr_done = False
            i = 0
            for _ in bk:
                i += 1
                if not fr_done and i % ratio == 0:
                    try:
                        next(fr)
                    except StopIteration:
                        fr_done = True
            for _ in fr:
                pass
        k.pop_scope()

    def pass_mixb(self, l, src, t_src, dst, t_dst):
        k = self.k
        k.push_scope()
        inp = self.inp
        C = 64
        NCH = G // C
        c0 = math.exp(-0.5)
        wout = k.sb("wout", [128, 8, D], BF16)
        t_wout = [Tk() for _ in range(8)]
        for kc in range(8):
            k.dma("pool", wout[:, kc, :], inp["w_out"][l, kc * 128:(kc + 1) * 128, :], w=[t_wout[kc]])
        waup = k.sb("waup", [128, 384], BF16)
        gup = k.sb("gup", [128, 384], BF16)
        t_lw = Tk()
        k.dma("pool", waup[0:64, :], inp["rwkv_w_up"][l], w=[t_lw])
        k.dma("pool", waup[64:128, :], inp["rwkv_a_up"][l], w=[t_lw])
        k.dma("pool", gup[:], inp["rwkv_g_up"][l], w=[t_lw])
        pc = k.sb("pc", [128, 8, 3], F32)
        t_pc = Tk()
        for n, nm in enumerate(["rwkv_w0", "rwkv_a0", "rwkv_k_k", "rwkv_k_a", "rwkv_r_k"]):
            srcap = inp[nm][l]
            if nm == "rwkv_r_k":
                srcap = srcap.rearrange("a b -> (a b)")
            k.dma("sp", pc[:, n, :], srcap.rearrange("(j p) -> p j", p=128), w=[t_pc],
                  allow_slow_non_contiguous=True)
        k.op("dve", lambda h: h.tensor_scalar(out=pc[:, 5, :], in0=pc[:, 3, :], scalar1=-1.0, scalar2=1.0,
                                              op0=ALU.mult, op1=ALU.add), r=[t_pc], w=[t_pc])
        k.op("pool", lambda h: h.memset(pc[:, 6, :], GN_EPS), w=[t_pc])
        mu = k.sb("mu", [128, 11], F32)
        t_mu = Tk()
        k.dma("sp", mu[:], inp["shift_mu"][l].rearrange("(j p) -> p j", p=128), w=[t_mu],
              allow_slow_non_contiguous=True)
        lng = k.sb("lng", [128, 2, 384], F32)
        t_lng = Tk()
        k.dma("sp", lng[:, 0, :], inp["lnx_g"][l:l + 1, :].partition_broadcast(128), w=[t_lng])
        k.dma("sp", lng[:, 1, :], inp["lnx_b"][l:l + 1, :].partition_broadcast(128), w=[t_lng])
        bones = k.sb("bones", [128, 128], BF16)
        t_bones = Tk()
        k.op("pool", lambda h: h.memset(bones[:], 0.0), w=[t_bones])
        k.op("pool", lambda h: h.memset(bones[0:64, 0:64], 1.0), w=[t_bones])
        k.op("pool", lambda h: h.memset(bones[64:128, 64:128], 1.0), w=[t_bones])
        msk = k.sb("msk", [128, 3, 64], F32)
        t_msk = Tk()
        k.op("pool", lambda h: h.memset(msk[:], 1.0), w=[t_msk])
        k.op("pool", lambda h: h.affine_select(out=msk[0:64, 0, :], in_=msk[0:64, 0, :], pattern=[[1, 64]],
                                               compare_op=ALU.is_ge, fill=0.0, base=-1, channel_multiplier=-1),
             r=[t_msk], w=[t_msk])
        k.op("pool", lambda h: h.affine_select(out=msk[0:64, 1, :], in_=msk[0:64, 1, :], pattern=[[1, 64]],
                                               compare_op=ALU.is_ge, fill=0.0, base=0, channel_multiplier=-1),
             r=[t_msk], w=[t_msk])
        k.op("pool", lambda h: h.affine_select(out=msk[0:64, 2, :], in_=msk[0:64, 2, :], pattern=[[-1, 64]],
                                               compare_op=ALU.is_ge, fill=0.0, base=-1, channel_multiplier=1),
             r=[t_msk], w=[t_msk])
        k.dma("sp", msk[64:128], msk[0:64], r=[t_msk], w=[t_msk])

        rmask = k.sb("rmask", [128, G], F32)
        t_rmask = Tk()
        k.op("pool", lambda h: h.memset(rmask[:], 1.0), w=[t_rmask])
        k.op("pool", lambda h: h.memset(rmask[:].rearrange("p (c t) -> p c t", t=C)[:, :, 0:1], 0.0), w=[t_rmask])
        pt = k.sb("pt", [128, 11, G], F32)
        t_pt = Tk()
        halo = k.sb("halo", [128, 11, 1], F32)
        t_halo = Tk()
        k.op("pool", lambda h: h.memset(halo[:], 0.0), w=[t_halo])
        z = k.sb("z", [128, 11, G], F32)
        t_z = Tk()
        F3 = [128, 3, G]
        sig = k.sb("sig", F3, F32); t_sig = Tk()
        aa = k.sb("aa", F3, F32); t_aa = Tk()
        gT = k.sb("gT", F3, F32); t_gT = Tk()
        kkn = k.sb("kkn", F3, F32); t_kkn = Tk()
        kmod = k.sb("kmod", F3, F32); t_kmod = Tk()
        beta = k.sb("beta", F3, F32); t_beta = Tk()
        bonus = pt[:, 6:9, :]; t_bonus = Tk()
        cs = k.sb("cs", F3, F32); t_cs = Tk()
        tA = pt[:, 0:3, :]; t_tA = Tk()
        tB = pt[:, 3:6, :]; t_tB = Tk()
        t_alias = [t_tA, t_tB, t_bonus]
        b16 = k.sb("b16", F3, BF16); t_b16 = Tk()
        rt = k.sb("rt", F3, BF16); t_rt = Tk()
        kt = k.sb("kt", F3, BF16); t_kt = Tk()
        bt = k.sb("bt", F3, BF16); t_bt = Tk()
        at = k.sb("at", F3, BF16); t_at = Tk()
        k2 = k.sb("k2", F3, BF16); t_k2 = Tk()
        b2 = k.sb("b2", F3, BF16); t_b2 = Tk()
        twa = k.sb("twa", [128, G], BF16); t_twa = Tk()
        sg = k.sb("sg", [128, G], BF16); t_sg = Tk()
        WC = k.sb("WC", [128, 3, NCH], F32); t_WC = Tk()
        tk2 = lambda: [Tk(), Tk()]
        Vg = k.sb("Vg", [128, NCH, 3, 64], BF16); t_Vg = [tk2() for _ in range(NCH)]
        Tst = k.sb("Tst", [128, 3, 64], F32); t_T = tk2()
        Tb = k.sb("Tb", [128, 3, 64], BF16); t_Tb = tk2()
        k.op("pool", lambda h: h.memset(Tst[:], 0.0), w=t_T)
        k.op("pool", lambda h: h.memset(Tb[:], 0.0), w=t_Tb)
        H6 = [128, 3, 64]
        X = [k.sb("X%d" % i, H6, BF16) for i in range(2)]; t_X = [tk2() for _ in range(2)]
        Xt = [k.sb("Xt%d" % i, H6, BF16) for i in range(2)]; t_Xt = [tk2() for _ in range(2)]
        P = [k.sb("P%d" % i, H6, BF16) for i in range(2)]; t_P = [tk2() for _ in range(2)]
        Pt = [k.sb("Pt%d" % i, H6, BF16) for i in range(2)]; t_Pt = [tk2() for _ in range(2)]
        Lak = k.sb("Lak", H6, BF16); t_Lak = tk2()
        Arb = k.sb("Arb", H6, BF16); t_Arb = tk2()
        Ark = k.sb("Ark", H6, BF16); t_Ark = tk2()
        r0b = k.sb("r0b", H6, BF16); t_r0b = tk2()
        Ub = k.sb("Ub", H6, BF16); t_Ub = tk2()
        kbtok = k.sb("kbtok", [128, 2, 3, 64], BF16); t_kbtok = tk2()
        idpl = k.sb("idpl", [128, 64], BF16); t_idpl = Tk()
        k.op("pool", lambda h: h.tensor_copy(out=idpl[0:64, :], in_=self.ident_b[0:64, 0:64]),
             r=[self.t_const], w=[t_idpl])
        k.op("pool", lambda h: h.tensor_copy(out=idpl[64:128, :], in_=self.ident_b[64:128, 64:128]),
             r=[self.t_const], w=[t_idpl])
        ysb = k.sb("ysb", [128, NCH // 2, 384], F32); t_ysb = [Tk() for _ in range(NCH)]
        ysq = k.sb("ysq", [128, NCH // 2, 384], F32); t_ysq = Tk()
        gst = k.sb("gst", [128, 6, NCH * 3], F32); t_gst = Tk()
        mixT = k.sb("mixT", [128, 8, G], BF16); t_mixT = [Tk() for _ in range(8)]
        xr = [k.sb("xrb%d" % i, [128, D], F32) for i in range(2)]; t_xr = [Tk() for _ in range(2)]
        psG = [k.ps("psG%d" % i, [128, 512], F32) for i in range(6)]
        t_psG = [Tk() for _ in range(6)]
        st = {"n": 0, "x": 0}

        def nps():
            i = st["n"] % 6
            st["n"] += 1
            return psG[i], t_psG[i]

        def npair():
            i = ((st["n"] + 1) // 2) % 3
            st["n"] = 2 * ((st["n"] + 1) // 2) + 2
            return ((psG[2 * i], t_psG[2 * i]), (psG[2 * i + 1], t_psG[2 * i + 1]))

        def v3(ps, hb):
            return ps[hb:hb + 64, 0:192].rearrange("p (a b) -> p a b", a=3)

        def b3(col):
            return bcast_mid(col, G)

        for g in range(self.dbg.get('mb_ng', NG) if isinstance(self.dbg, dict) else NG):
            gs = slice(g * G, (g + 1) * G)
            k.dma("sp", pt[:], self.prw[:, gs].rearrange("(c p) t -> p c t", p=128),
                  r=[self.t_prw[g]], w=[t_pt] + t_alias)
            k.dma("sp", mixT[:, 0:5, :], self.mxa[:, gs].rearrange("(c p) t -> p c t", p=128),
                  r=[self.t_mxa[g]], w=t_mixT[0:5])
            k.op("pool", lambda h: h.tensor_tensor(out=z[:, :, 1:G], in0=pt[:, :, 0:G - 1], in1=pt[:, :, 1:G],
                                                   op=ALU.subtract), r=[t_pt] + t_alias, w=[t_z])
            k.op("pool", lambda h: h.tensor_tensor(out=z[:, :, 0:1], in0=halo[:], in1=pt[:, :, 0:1],
                                                   op=ALU.subtract), r=[t_pt, t_halo] + t_alias, w=[t_z])
            k.op("dve", lambda h: h.tensor_tensor(out=z[:], in0=z[:], in1=bcast_mid(mu[:], G), op=ALU.mult),
                 r=[t_z, t_mu], w=[t_z])
            k.op("pool", lambda h: h.tensor_tensor(out=z[:], in0=z[:], in1=pt[:], op=ALU.add),
                 r=[t_z, t_pt] + t_alias, w=[t_z])
            k.op("pool", lambda h: h.tensor_copy(out=halo[:], in_=pt[:, :, G - 1:G]), r=[t_pt] + t_alias,
                 w=[t_halo])
            zr, zk, zv = z[:, 0:3, :], z[:, 3:6, :], z[:, 6:9, :]
            k.op("act", lambda h: h.activation(out=twa[0:64, :], in_=z[0:64, 9, :], func=AF.Tanh), r=[t_z], w=[t_twa])
            k.op("act", lambda h: h.activation(out=twa[64:128, :], in_=z[64:128, 9, :], func=AF.Copy),
                 r=[t_z], w=[t_twa])
            k.op("act", lambda h: h.activation(out=sg[:], in_=z[:, 10, :], func=AF.Sigmoid), r=[t_z], w=[t_sg])
            for fc in range(3):
                ps, t_ps = nps()
                k.op("pe", lambda h, ps=ps, fc=fc: h.matmul(ps[:], lhsT=waup[0:64, fc * 128:(fc + 1) * 128],
                                                            rhs=twa[0:64, :], start=True, stop=True,
                                                            tile_position=(0, 0)), r=[t_lw, t_twa], w=[t_ps])
                k.op("act", lambda h, ps=ps, fc=fc: h.activation(out=sig[:, fc, :], in_=ps[:], func=AF.Sigmoid,
                                                                 bias=pc[:, 0, fc:fc + 1]), r=[t_ps, t_pc], w=[t_sig])
            for fc in range(3):
                ps, t_ps = nps()
                k.op("pe", lambda h, ps=ps, fc=fc: h.matmul(ps[:], lhsT=waup[64:128, fc * 128:(fc + 1) * 128],
                                                            rhs=twa[64:128, :], start=True, stop=True,
                                                            tile_position=(64, 0)), r=[t_lw, t_twa], w=[t_ps])
                k.op("act", lambda h, ps=ps, fc=fc: h.activation(out=aa[:, fc, :], in_=ps[:], func=AF.Sigmoid,
                                                                 bias=pc[:, 1, fc:fc + 1]), r=[t_ps, t_pc], w=[t_aa])
            for fc in range(3):
                ps, t_ps = nps()
                k.op("pe", lambda h, ps=ps, fc=fc: h.matmul(ps[:], lhsT=gup[:, fc * 128:(fc + 1) * 128], rhs=sg[:],
                                                            start=True, stop=True), r=[t_lw, t_sg], w=[t_ps])
                k.op("dve", lambda h, ps=ps, fc=fc: h.tensor_copy(out=gT[:, fc, :], in_=ps[:]), r=[t_ps], w=[t_gT])
            k.op("pool", lambda h: h.tensor_tensor(out=tA, in0=zk, in1=b3(pc[:, 2, :]), op=ALU.mult),
                 r=[t_z, t_pc], w=[t_tA])
            k.op("pool", lambda h: h.tensor_tensor(out=b16[:], in0=tA, in1=tA, op=ALU.mult),
                 r=[t_tA], w=[t_b16])
            for fc in range(3):
                ps, t_ps = nps()
                k.op("pe", lambda h, ps=ps, fc=fc: h.matmul(ps[:], lhsT=bones[:], rhs=b16[:, fc, :], start=True,
                                                            stop=True), r=[t_bones, t_b16], w=[t_ps])
                k.op("act", lambda h, ps=ps, fc=fc: h.activation(out=tB[:, fc, :], in_=ps[:], func=AF.Sqrt),
                     r=[t_ps], w=[t_tB])
            k.op("dve", lambda h: h.tensor_scalar(out=tB, in0=tB, scalar1=1e-12, scalar2=None, op0=ALU.max),
                 r=[t_tB], w=[t_tB])
            k.op("dve", lambda h: h.reciprocal(out=tB, in_=tB), r=[t_tB], w=[t_tB])
            k.op("pool", lambda h: h.tensor_tensor(out=kkn[:], in0=tA, in1=tB, op=ALU.mult),
                 r=[t_tA, t_tB], w=[t_kkn])
            k.op("pool", lambda h: h.tensor_tensor(out=tA, in0=aa[:], in1=b3(pc[:, 3, :]), op=ALU.mult),
                 r=[t_aa, t_pc], w=[t_tA])
            k.op("pool", lambda h: h.tensor_tensor(out=tA, in0=tA, in1=b3(pc[:, 5, :]), op=ALU.add),
                 r=[t_tA, t_pc], w=[t_tA])
            k.op("dve", lambda h: h.tensor_tensor(out=kmod[:], in0=zk, in1=tA, op=ALU.mult),
                 r=[t_z, t_tA], w=[t_kmod])
            k.op("pool", lambda h: h.tensor_tensor(out=beta[:], in0=kkn[:], in1=aa[:], op=ALU.mult),
                 r=[t_kkn, t_aa], w=[t_beta])
            k.op("dve", lambda h: h.tensor_tensor(out=tA, in0=zr, in1=kmod[:], op=ALU.mult),
                 r=[t_z, t_kmod], w=[t_tA])
            k.op("pool", lambda h: h.tensor_tensor(out=b16[:], in0=tA, in1=b3(pc[:, 4, :]), op=ALU.mult),
                 r=[t_tA, t_pc], w=[t_b16])
            for fc in range(3):
                ps, t_ps = nps()
                k.op("pe", lambda h, ps=ps, fc=fc: h.matmul(ps[:], lhsT=bones[:], rhs=b16[:, fc, :], start=True,
                                                            stop=True), r=[t_bones, t_b16], w=[t_ps])
                k.op("dve", lambda h, ps=ps, fc=fc: h.tensor_tensor(out=bonus[:, fc, :], in0=ps[:],
                                                                    in1=z[:, 6 + fc, :], op=ALU.mult),
                     r=[t_ps, t_z], w=[t_bonus])
            for fc in range(3):
                k.op("dve", lambda h, fc=fc: h.tensor_tensor_scan(out=cs[:, fc, :], data0=rmask[:],
                                                                  data1=sig[:, fc, :], initial=0.0, op0=ALU.mult,
                                                                  op1=ALU.add), r=[t_rmask, t_sig], w=[t_cs])
            csC = cs[:].rearrange("p f (c t) -> p f c t", t=C)[:, :, :, C - 1]
            k.op("act", lambda h: h.activation(out=tA, in_=cs[:], func=AF.Exp, scale=-c0), r=[t_cs], w=[t_tA])
            k.op("dve", lambda h: h.tensor_tensor(out=rt[:], in0=zr, in1=tA, op=ALU.mult),
                 r=[t_z, t_tA], w=[t_rt])
            k.op("act", lambda h: h.activation(out=tB, in_=cs[:], func=AF.Exp, scale=c0), r=[t_cs], w=[t_tB])
            k.op("pool", lambda h: h.tensor_tensor(out=kt[:], in0=kmod[:], in1=tB, op=ALU.mult),
                 r=[t_kmod, t_tB], w=[t_kt])
            k.op("dve", lambda h: h.tensor_tensor(out=bt[:], in0=beta[:], in1=tB, op=ALU.mult),
                 r=[t_beta, t_tB], w=[t_bt])
            k.op("pool", lambda h: h.tensor_tensor(out=tA, in0=cs[:], in1=sig[:], op=ALU.subtract),
                 r=[t_cs, t_sig, t_rt], w=[t_tA])
            k.op("act", lambda h: h.activation(out=tA, in_=tA, func=AF.Exp, scale=-c0), r=[t_tA], w=[t_tA])
            k.op("dve", lambda h: h.scalar_tensor_tensor(out=at[:], in0=kkn[:], scalar=-1.0, in1=tA,
                                                         op0=ALU.mult, op1=ALU.mult), r=[t_kkn, t_tA], w=[t_at])
            k.op("pool", lambda h: h.tensor_tensor(
                out=tB.rearrange("p f (c t) -> p f c t", t=C),
                in0=csC.unsqueeze(3).to_broadcast([128, 3, NCH, C]),
                in1=cs[:].rearrange("p f (c t) -> p f c t", t=C), op=ALU.subtract),
                 r=[t_cs, t_kt, t_bt], w=[t_tB])
            k.op("act", lambda h: h.activation(out=tB, in_=tB, func=AF.Exp, scale=-c0), r=[t_tB], w=[t_tB])
            k.op("pool", lambda h: h.tensor_tensor(out=k2[:], in0=kmod[:], in1=tB, op=ALU.mult),
                 r=[t_kmod, t_tB], w=[t_k2])
            k.op("dve", lambda h: h.tensor_tensor(out=b2[:], in0=beta[:], in1=tB, op=ALU.mult),
                 r=[t_beta, t_tB], w=[t_b2])
            k.op("act", lambda h: h.activation(out=WC[:], in_=csC, func=AF.Exp, scale=-c0), r=[t_cs], w=[t_WC])
            for nm_, ap_, tk_ in [("z", z[:], [t_z]), ("sig", sig[:], [t_sig]), ("aa", aa[:], [t_aa]),
                                  ("kkn", kkn[:], [t_kkn]), ("kmod", kmod[:], [t_kmod]), ("beta", beta[:], [t_beta]),
                                  ("cs", cs[:], [t_cs]), ("rt", rt[:], [t_rt]), ("kt", kt[:], [t_kt]),
                                  ("bt", bt[:], [t_bt]), ("at", at[:], [t_at]), ("k2", k2[:], [t_k2]),
                                  ("b2", b2[:], [t_b2]), ("gT", gT[:], [t_gT]), ("bonus", bonus, [t_bonus]),
                                  ("WC", WC[:], [t_WC])]:
                self.dump(nm_, ap_, tk_)
            for c in range(NCH):
                cl = slice(c * C, (c + 1) * C)
                pair = npair()
                for fc in range(3):
                    for par in range(2):
                        hb = par * 64
                        ps, t_ps = pair[par]
                        k.op("pe", lambda h, ps=ps, fc=fc, hb=hb, cl=cl: h.matmul(
                            ps[hb:hb + 64, fc * 64:(fc + 1) * 64], lhsT=z[hb:hb + 64, 6 + fc, cl],
                            rhs=self.ident_f[hb:hb + 64, hb:hb + 64], start=True, stop=True,
                            tile_position=(hb, hb)), r=[t_z, self.t_const], w=[t_ps])
                for par in range(2):
                    hb = par * 64
                    ps, t_ps = pair[par]
                    k.op("act", lambda h, ps=ps, c=c, hb=hb: h.activation(
                        out=Vg[hb:hb + 64, c, :, :], in_=v3(ps, hb), func=AF.Copy), r=[t_ps], w=[t_Vg[c][par]])
            for c in range(NCH if "mb_nochunk" not in self.dbg else 0):
                cl = slice(c * C, (c + 1) * C)

                def fm(tile, hd):
                    hb = (hd % 2) * 64
                    return tile[hb:hb + 64, hd // 2, cl]

                def pl(tile, hd):
                    hb = (hd % 2) * 64
                    return tile[hb:hb + 64, hd // 2, :]

                def mm6(pair, lf, rf, rtoks, lf2=None, rf2=None, rtoks2=None):
                    for fc in range(3):
                        for par in range(2):
                            hd = 2 * fc + par
                            hb = par * 64
                            ps, t_ps = pair[par]
                            k.op("pe", lambda h, ps=ps, hd=hd, hb=hb, fc=fc: h.matmul(
                                ps[hb:hb + 64, fc * 64:(fc + 1) * 64], lhsT=lf(hd), rhs=rf(hd), start=True,
                                stop=(lf2 is None), tile_position=(hb, hb)), r=rtoks(par), w=[t_ps])
                            if lf2 is not None:
                                k.op("pe", lambda h, ps=ps, hd=hd, hb=hb, fc=fc: h.matmul(
                                    ps[hb:hb + 64, fc * 64:(fc + 1) * 64], lhsT=lf2(hd), rhs=rf2(hd), start=False,
                                    stop=True, tile_position=(hb, hb)), r=rtoks2(par), w=[t_ps])

                def ev_mask(pair, dst, t_dst, mi):
                    for par in range(2):
                        hb = par * 64
                        ps, t_ps = pair[par]
                        k.op("dve", lambda h, ps=ps, hb=hb: h.tensor_tensor(
                            out=dst[hb:hb + 64], in0=v3(ps, hb),
                            in1=msk[hb:hb + 64, mi, :].unsqueeze(1).to_broadcast([64, 3, 64]), op=ALU.mult),
                             r=[t_ps, t_msk], w=[t_dst[par]])

                def ev_copy(pair, dst, t_dst):
                    for par in range(2):
                        hb = par * 64
                        ps, t_ps = pair[par]
                        k.op("act", lambda h, ps=ps, hb=hb: h.activation(out=dst[hb:hb + 64], in_=v3(ps, hb),
                                                                         func=AF.Copy), r=[t_ps], w=[t_dst[par]])

                def ev_add(pair, dst, t_dst, old, t_old):
                    for par in range(2):
                        hb = par * 64
                        ps, t_ps = pair[par]
                        k.op("dve", lambda h, ps=ps, hb=hb: h.tensor_tensor(
                            out=dst[hb:hb + 64], in0=v3(ps, hb), in1=old[hb:hb + 64], op=ALU.add),
                             r=[t_ps, t_old[par]], w=[t_dst[par]])
                fmt = lambda t1, t2: (lambda par: [t1, t2])
                pair = npair()
                mm6(pair, lambda hd: fm(bt, hd), lambda hd: fm(at, hd), fmt(t_bt, t_at))
                ev_mask(pair, X[0], t_X[0], 0)
                pair = npair()
                mm6(pair, lambda hd: fm(at, hd), lambda hd: fm(bt, hd), fmt(t_bt, t_at))
                ev_mask(pair, Xt[0], t_Xt[0], 2)
                pair = npair()
                mm6(pair, lambda hd: fm(kt, hd), lambda hd: fm(at, hd), fmt(t_kt, t_at))
                ev_mask(pair, Lak, t_Lak, 0)
                pair = npair()
                mm6(pair, lambda hd: fm(bt, hd), lambda hd: fm(rt, hd), fmt(t_bt, t_rt))
                ev_mask(pair, Arb, t_Arb, 1)
                pair = npair()
                mm6(pair, lambda hd: fm(kt, hd), lambda hd: fm(rt, hd), fmt(t_kt, t_rt))
                ev_mask(pair, Ark, t_Ark, 1)
                self.dump("X0", X[0][:], t_X[0]); self.dump("Xt0", Xt[0][:], t_Xt[0])
                self.dump("Lak", Lak[:], t_Lak); self.dump("Arb", Arb[:], t_Arb); self.dump("Ark", Ark[:], t_Ark)
                self.dump("Vg", Vg[:], [t_Vg[i_][j_] for i_ in range(NCH) for j_ in range(2)])
                idb = idpl[:].unsqueeze(1).to_broadcast([128, 3, 64])
                k.op("pool", lambda h: h.tensor_tensor(out=P[0][:], in0=X[0][:], in1=idb, op=ALU.add),
                     r=t_X[0] + [t_idpl], w=t_P[0])
                k.op("pool", lambda h: h.tensor_tensor(out=Pt[0][:], in0=Xt[0][:], in1=idb, op=ALU.add),
                     r=t_Xt[0] + [t_idpl], w=t_Pt[0])
                cur = 0
                nsteps = 5
                for stp in range(nsteps):
                    nxt = 1 - cur
                    last = (stp == nsteps - 1)
                    pair = npair()
                    mm6(pair, lambda hd: pl(Xt[cur], hd), lambda hd: pl(X[cur], hd),
                        lambda par: [t_Xt[cur][par], t_X[cur][par]])
                    ev_copy(pair, X[nxt], t_X[nxt])
                    if not last:
                        pair = npair()
                        mm6(pair, lambda hd: pl(X[cur], hd), lambda hd: pl(Xt[cur], hd),
                            lambda par: [t_Xt[cur][par], t_X[cur][par]])
                        ev_copy(pair, Xt[nxt], t_Xt[nxt])
                    pair = npair()
                    mm6(pair, lambda hd: pl(Pt[cur], hd), lambda hd: pl(X[nxt], hd),
                        lambda par: [t_Pt[cur][par], t_X[nxt][par]])
                    ev_add(pair, P[nxt], t_P[nxt], P[cur], t_P[cur])
                    if not last:
                        pair = npair()
                        mm6(pair, lambda hd: pl(X[nxt], hd), lambda hd: pl(Pt[cur], hd),
                            lambda par: [t_Pt[cur][par], t_X[nxt][par]])
                        ev_add(pair, Pt[nxt], t_Pt[nxt], Pt[cur], t_Pt[cur])
                    cur = nxt
                Pf, t_Pf = P[cur], t_P[cur]
                self.dump("Pf", Pf[:], t_Pf)
                pair = npair()
                for n_, (tile_, tk_) in enumerate([(k2, t_k2), (b2, t_b2)]):
                    for fc in range(3):
                        for par in range(2):
                            hb = par * 64
                            ps, t_ps = pair[par]
                            k.op("pe", lambda h, ps=ps, n_=n_, fc=fc, hb=hb, tile_=tile_: h.matmul(
                                ps[hb:hb + 64, (n_ * 3 + fc) * 64:(n_ * 3 + fc + 1) * 64],
                                lhsT=tile_[hb:hb + 64, fc, cl], rhs=self.ident_b[hb:hb + 64, hb:hb + 64],
                                start=True, stop=True, tile_position=(hb, hb)), r=[tk_, self.t_const], w=[t_ps])
                for par in range(2):
                    hb = par * 64
                    ps, t_ps = pair[par]
                    k.op("act", lambda h, ps=ps, hb=hb: h.activation(
                        out=kbtok[hb:hb + 64].rearrange("p a b c -> p (a b c)"),
                        in_=ps[hb:hb + 64, 0:384], func=AF.Copy), r=[t_ps], w=[t_kbtok[par]])
                pair = npair()
                mm6(pair, lambda hd: fm(at, hd), lambda hd: Tb[(hd % 2) * 64:(hd % 2) * 64 + 64, hd // 2, :],
                    lambda par: [t_at, t_Tb[par]],
                    lambda hd: pl(Lak, hd), lambda hd: Vg[(hd % 2) * 64:(hd % 2) * 64 + 64, c, hd // 2, :],
                    lambda par: [t_Lak[par], t_Vg[c][par]])
                ev_copy(pair, r0b, t_r0b)
                pair = npair()
                mm6(pair, lambda hd: pl(Pf, hd), lambda hd: pl(r0b, hd), lambda par: [t_Pf[par], t_r0b[par]])
                ev_copy(pair, Ub, t_Ub)
                self.dump("r0b", r0b[:], t_r0b); self.dump("Ub", Ub[:], t_Ub); self.dump("kbtok", kbtok[:], t_kbtok)
                pair = npair()
                po = (c % 2) * 64
                for fc in range(3):
                    for par in range(2):
                        hd = 2 * fc + par
                        hb = par * 64
                        ps, t_ps = pair[par]
                        fs = slice(fc * 64, (fc + 1) * 64)
                        k.op("pe", lambda h, ps=ps, hd=hd, hb=hb, fs=fs, po=po, fc=fc: h.matmul(
                            ps[po:po + 64, fs], lhsT=fm(rt, hd), rhs=Tb[hb:hb + 64, fc, :], start=True, stop=False,
                            tile_position=(hb, po)), r=[t_rt, t_Tb[par]], w=[t_ps])
                        k.op("pe", lambda h, ps=ps, hd=hd, hb=hb, fs=fs, po=po: h.matmul(
                            ps[po:po + 64, fs], lhsT=pl(Arb, hd), rhs=pl(Ub, hd), start=False, stop=False,
                            tile_position=(hb, po)), r=[t_Arb[par], t_Ub[par]], w=[t_ps])
                        k.op("pe", lambda h, ps=ps, hd=hd, hb=hb, fs=fs, po=po, fc=fc: h.matmul(
                            ps[po:po + 64, fs], lhsT=pl(Ark, hd), rhs=Vg[hb:hb + 64, c, fc, :], start=False,
                            stop=True, tile_position=(hb, po)), r=[t_Ark[par], t_Vg[c][par]], w=[t_ps])
                for par in range(2):
                    ps, t_ps = pair[par]
                    k.op("act", lambda h, ps=ps, c=c, po=po, par=par: h.activation(
                        out=ysb[po:po + 64, c // 2, :].rearrange("p (a b d) -> p a b d", a=3, b=2)[:, :, par, :],
                        in_=ps[po:po + 64, 0:192].rearrange("p (a d) -> p a d", a=3), func=AF.Copy),
                         r=[t_ps], w=[t_ysb[c]])
                pair = npair()
                mm6(pair, lambda hd: kbtok[(hd % 2) * 64:(hd % 2) * 64 + 64, 1, hd // 2, :], lambda hd: pl(Ub, hd),
                    lambda par: [t_kbtok[par], t_Ub[par]],
                    lambda hd: kbtok[(hd % 2) * 64:(hd % 2) * 64 + 64, 0, hd // 2, :],
                    lambda hd: Vg[(hd % 2) * 64:(hd % 2) * 64 + 64, c, hd // 2, :],
                    lambda par: [t_kbtok[par], t_Vg[c][par]])
                for par in range(2):
                    hb = par * 64
                    ps, t_ps = pair[par]
                    k.op("dve", lambda h, c=c, hb=hb: h.tensor_tensor(
                        out=Tst[hb:hb + 64], in0=Tst[hb:hb + 64], in1=bcast_mid(WC[hb:hb + 64, :, c], 64),
                        op=ALU.mult), r=[t_T[par], t_WC], w=[t_T[par]])
                    k.op("dve", lambda h, ps=ps, hb=hb: h.tensor_tensor(
                        out=Tst[hb:hb + 64], in0=v3(ps, hb), in1=Tst[hb:hb + 64], op=ALU.add),
                         r=[t_ps, t_T[par]], w=[t_T[par]])
                    k.op("pool", lambda h, hb=hb: h.tensor_copy(out=Tb[hb:hb + 64], in_=Tst[hb:hb + 64]),
                         r=[t_T[par]], w=[t_Tb[par]])
                self.dump("T1", Tst[:], t_T)
            self.dump("ysb", ysb[:], t_ysb)
            yv = ysb[:].rearrange("p c (a b) -> p (c a) b", b=64)
            qv = ysq[:].rearrange("p c (a b) -> p (c a) b", b=64)
            k.op("dve", lambda h: h.tensor_reduce(out=gst[:, 0, :], in_=yv, axis=AX.X, op=ALU.add),
                 r=t_ysb, w=[t_gst])
            k.op("pool", lambda h: h.tensor_tensor(out=ysq[:], in0=ysb[:], in1=ysb[:], op=ALU.mult),
                 r=t_ysb, w=[t_ysq])
            k.op("dve", lambda h: h.tensor_reduce(out=gst[:, 1, :], in_=qv, axis=AX.X, op=ALU.add),
                 r=[t_ysq], w=[t_gst])
            k.op("dve", lambda h: h.tensor_scalar(out=gst[:, 0:2, :], in0=gst[:, 0:2, :], scalar1=1.0 / 64.0,
                                                  scalar2=None, op0=ALU.mult), r=[t_gst], w=[t_gst])
            k.op("dve", lambda h: h.tensor_tensor(out=gst[:, 2, :], in0=gst[:, 0, :], in1=gst[:, 0, :], op=ALU.mult),
                 r=[t_gst], w=[t_gst])
            k.op("dve", lambda h: h.tensor_tensor(out=gst[:, 3, :], in0=gst[:, 1, :], in1=gst[:, 2, :],
                                                  op=ALU.subtract), r=[t_gst], w=[t_gst])
            k.op("act", lambda h: h.activation(out=gst[:, 4, :], in_=gst[:, 3, :], func=AF.Sqrt,
                                               bias=pc[:, 6, 0:1]), r=[t_gst, t_pc], w=[t_gst])
            k.op("dve", lambda h: h.reciprocal(out=gst[:, 5, :], in_=gst[:, 4, :]), r=[t_gst], w=[t_gst])
            k.op("dve", lambda h: h.tensor_tensor(out=qv, in0=yv, in1=bcast_mid(gst[:, 0, :], 64), op=ALU.subtract),
                 r=t_ysb + [t_gst, t_ysq], w=[t_ysq])
            k.op("dve", lambda h: h.tensor_tensor(out=qv, in0=qv, in1=bcast_mid(gst[:, 5, :], 64), op=ALU.mult),
                 r=[t_gst, t_ysq], w=[t_ysq])
            k.op("pool", lambda h: h.tensor_tensor(out=ysq[:], in0=ysq[:],
                                                   in1=lng[:, 0, :].unsqueeze(1).to_broadcast([128, NCH // 2, 384]),
                                                   op=ALU.mult), r=[t_ysq, t_lng], w=[t_ysq])
            k.op("pool", lambda h: h.tensor_tensor(out=ysq[:], in0=ysq[:],
                                                   in1=lng[:, 1, :].unsqueeze(1).to_broadcast([128, NCH // 2, 384]),
                                                   op=ALU.add), r=[t_ysq, t_lng], w=[t_ysq])
            self.dump("ysq", ysq[:], [t_ysq])
            for fc in range(3):
                ps, t_ps = nps()
                for c2 in range(NCH // 2):
                    k.op("pe", lambda h, ps=ps, fc=fc, c2=c2: h.transpose(
                        out=ps[:, c2 * 128:(c2 + 1) * 128], in_=ysq[:, c2, fc * 128:(fc + 1) * 128],
                        identity=self.ident_f[:]), r=[t_ysq, self.t_const], w=[t_ps])
                k.op("dve", lambda h, ps=ps, fc=fc: h.tensor_tensor(out=tA[:, fc, :], in0=ps[:], in1=bonus[:, fc, :],
                                                                    op=ALU.add), r=[t_ps, t_bonus], w=[t_tA])
                k.op("pool", lambda h, fc=fc: h.tensor_tensor(out=mixT[:, 5 + fc, :], in0=tA[:, fc, :],
                                                              in1=gT[:, fc, :], op=ALU.mult),
                     r=[t_tA, t_gT], w=[t_mixT[5 + fc]])
            if self.mxb is not None:
                k.dma("sp", self.mxb[:, gs].rearrange("(c p) t -> p c t", p=128), mixT[:, 5:8, :],
                      r=t_mixT[5:8], w=[self.t_mxb])
            for blk in range(NB):
                r0 = g * G + blk * 128
                i = st["x"] % 2
                st["x"] += 1
                k.dma("sp", xr[i][:], src[r0:r0 + 128, :], r=[t_src[g]], w=[t_xr[i]])
                pairw = npair()
                for half in range(2):
                    ps, t_ps = pairw[half]
                    for kc in range(8):
                        k.op("pe", lambda h, ps=ps, kc=kc, blk=blk, half=half: h.matmul(
                            ps[:], lhsT=mixT[:, kc, blk * 128:(blk + 1) * 128],
                            rhs=wout[:, kc, half * 512:(half + 1) * 512], start=(kc == 0), stop=(kc == 7)),
                             r=[t_mixT[kc], t_wout[kc]], w=[t_ps])
                    k.op("dve", lambda h, ps=ps, half=half, i=i: h.tensor_tensor(
                        out=xr[i][:, half * 512:(half + 1) * 512], in0=ps[:],
                        in1=xr[i][:, half * 512:(half + 1) * 512], op=ALU.add), r=[t_ps, t_xr[i]], w=[t_xr[i]])
                k.dma("sp", dst[r0:r0 + 128, :], xr[i][:], r=[t_xr[i]], w=[t_dst[g]])
        k.pop_scope()

    def build(self):
        for ph in self.phases:
            kind = ph[0]
            srcs = {"x": (self.inp["x"], self.t_x), "xa": (self.xa, self.t_xa), "xb": (self.xb, self.t_xb)}
            if kind == "MA":
                _, l, src = ph
                self.pass_mixa(l, srcs[src][0], srcs[src][1])
            dsts = {"xa": (self.xa, self.t_xa), "xb": (self.xb, self.t_xb), "y": (self.y, self.t_y)}
            if kind == "MB":
                _, l, src, dst = ph
                self.pass_mixb(l, srcs[src][0], srcs[src][1], dsts[dst][0], dsts[dst][1])
            if kind == "F":
                _, l, src, dst, final = ph
                srcs = {"x": (self.inp["x"], self.t_x), "xa": (self.xa, self.t_xa), "xb": (self.xb, self.t_xb)}
                dsts = {"xa": (self.xa, self.t_xa), "xb": (self.xb, self.t_xb), "y": (self.y, self.t_y)}
                self.pass_ffn(l, srcs[src][0], srcs[src][1], dsts[dst][0], dsts[dst][1], final)
        self.k.finish()


FULL_PHASES = [("MA", 0, "x"), ("MB", 0, "x", "xa"), ("F", 0, "xa", "xb", False),
               ("MA", 1, "xb"), ("MB", 1, "xb", "xa"), ("F", 1, "xa", "y", True)]


def build_nc(phases=None, dbg=None):
    dbg = dbg or {}
    nc = bass.Bass("TRN2", target_bir_lowering=False)
    p = Prog(nc, phases or FULL_PHASES, dbg)
    p.build()
    return nc, p


def kernel(**inputs):
    nc, _ = build_nc()
    in_maps = []
    for b in range(8):
        m = {}
        for name in INPUT_SHAPES:
            a = np.asarray(inputs[name], dtype=np.float32)
            m[name] = np.ascontiguousarray(a[b]) if name == 'x' else np.ascontiguousarray(a)
        in_maps.append(m)
    res = run_bass_kernel_spmd(nc, in_maps, core_ids=list(range(8)))
    return np.stack([np.asarray(r["y"]) for r in res.results], axis=0).astype(np.float32)
```

```python
import math
from contextlib import ExitStack
import numpy as np
import concourse.bass as bass
import concourse.mybir as mybir
from concourse.bass_utils import run_bass_kernel_spmd

F32 = mybir.dt.float32
BF16 = mybir.dt.bfloat16
ALU = mybir.AluOpType
AF = mybir.ActivationFunctionType
AX = mybir.AxisListType

S = 4096
D = 1024
DFF = 4096
NIN = 3328
G = 512
NG = S // G
NB = G // 128
DEPTH = 2
HD = 64
NORM_EPS = 1e-6
GN_EPS = 64e-5

INPUT_SHAPES = {
    'x': [S, D], 'norm_mix_g': [2, D], 'w_in': [2, D, NIN], 'lam_q1': [2, 32], 'lam_k1': [2, 32],
    'lam_q2': [2, 32], 'lam_k2': [2, 32], 'subln_g': [2, 64], 'conv_w': [2, 3, 256],
    'shift_mu': [2, 1408], 'rwkv_w0': [2, 384], 'rwkv_w_up': [2, 64, 384], 'rwkv_a0': [2, 384],
    'rwkv_a_up': [2, 64, 384], 'rwkv_g_up': [2, 128, 384], 'rwkv_k_k': [2, 384], 'rwkv_k_a': [2, 384],
    'rwkv_r_k': [2, 6, 64], 'lnx_g': [2, 384], 'lnx_b': [2, 384], 'w_out': [2, D, D],
    'norm_mlp_g': [2, D], 'w_mlp_up': [2, D, DFF], 'w_mlp_down': [2, DFF, D], 'final_norm_g': [D],
}


class Tk:
    __slots__ = ("w", "r", "name")

    def __init__(self, name=""):
        self.w = None
        self.r = {}
        self.name = name


class KB:
    EPOCH = 6000
    NDS = 32

    def __init__(self, nc):
        self.nc = nc
        self.eng = {"pe": nc.tensor, "act": nc.scalar, "dve": nc.vector, "pool": nc.gpsimd, "sp": nc.sync}
        self.cnt = {e: 0 for e in self.eng}
        self.semh = {}
        self.seen = {e: {} for e in self.eng}
        self.maxep = {e: {} for e in self.eng}
        self.dsem = [("dma", i) for i in range(self.NDS)]
        for kx in self.dsem:
            self.semh[kx] = nc.alloc_semaphore(f"dma{kx[1]}")
        self.dval = [0] * self.NDS
        self.dnext = 0
        self.dnext_sw = 0
        self.ndma = 0
        self.nwait = 0
        self._n = 0
        self.root = ExitStack()
        self.scope = None

    def sb(self, name, shape, dt):
        self._n += 1
        cm = self.nc.sbuf_tensor("%s_%d" % (name, self._n), list(shape), dt)
        return (self.scope or self.root).enter_context(cm)

    def ps(self, name, shape, dt):
        self._n += 1
        cm = self.nc.psum_tensor("%s_%d" % (name, self._n), list(shape), dt)
        return (self.scope or self.root).enter_context(cm)

    def push_scope(self):
        self.scope = ExitStack()

    def pop_scope(self):
        self.barrier()
        self.scope.close()
        self.scope = None

    def _cursem(self, e):
        key = (e, self.cnt[e] // self.EPOCH)
        if key not in self.semh:
            self.semh[key] = self.nc.alloc_semaphore(f"s_{e}_{key[1]}")
        return key

    def _wait(self, e, tok):
        key, val = tok[0], tok[1]
        seen = self.seen[e]
        if seen.get(key, 0) >= val:
            return
        if key[0] != "dma":
            if self.maxep[e].get(key[0], -1) > key[1]:
                return
            self.maxep[e][key[0]] = max(self.maxep[e].get(key[0], -1), key[1])
        self.eng[e].wait_ge(self.semh[key], val)
        self.nwait += 1
        seen[key] = val

    def _deps(self, e, reads, writes, is_dma):
        for t in reads:
            if t.w is not None:
                tok = t.w
                if (not is_dma) and tok[2] == e and e == "pe":
                    continue
                self._wait(e, tok)
        for t in writes:
            if t.w is not None:
                tok = t.w
                if is_dma or tok[2] != e or tok[3] or e != "pe":
                    self._wait(e, tok)
            for rk, tok in t.r.items():
                if isinstance(tok, list):
                    for tk in tok:
                        self._wait(e, tk)
                elif is_dma or tok[2] != e or e != "pe":
                    self._wait(e, tok)

    def _record(self, tok, reads, writes):
        for t in reads:
            if tok[3]:
                t.r.setdefault("dma", []).append(tok)
            else:
                t.r[tok[2]] = tok
        for t in writes:
            t.w = tok
            t.r = {}

    def op(self, e, fn, r=(), w=()):
        self._deps(e, r, w, False)
        key = self._cursem(e)
        ins = fn(self.eng[e])
        self.cnt[e] += 1
        val = self.cnt[e] - key[1] * self.EPOCH
        ins.then_inc(self.semh[key], 1)
        tok = (key, val, e, False)
        self._record(tok, r, w)
        return tok

    def dma(self, q, out, in_, r=(), w=(), **kw):
        self._deps(q, r, w, True)
        if q == "pool":
            slot = self.NDS - 8 + self.dnext_sw
            self.dnext_sw = (self.dnext_sw + 1) % 8
        else:
            slot = self.dnext
            self.dnext = (self.dnext + 1) % (self.NDS - 8)
        key = self.dsem[slot]
        if self.dval[slot] > 0:
            self._wait(q, (key, self.dval[slot], "dma", True))
        ins = self.eng[q].dma_start(out=out, in_=in_, **kw)
        self.dval[slot] += 16
        ins.then_inc(self.semh[key], 16)
        tok = (key, self.dval[slot], "dma", True)
        self._record(tok, r, w)
        self.ndma += 1
        return tok

    def barrier(self):
        lasts = {}
        for e in ("pe", "act", "dve", "pool"):
            if self.cnt[e] > 0:
                key = (e, (self.cnt[e] - 1) // self.EPOCH)
                lasts[e] = (key, self.cnt[e] - key[1] * self.EPOCH, e, False)
        for e in self.eng:
            for e2, tok in lasts.items():
                if e2 != e:
                    self._wait(e, tok)
            for slot in range(self.NDS):
                if self.dval[slot] > 0:
                    self._wait(e, (self.dsem[slot], self.dval[slot], "dma", True))

    def finish(self):
        for slot in range(self.NDS):
            if self.dval[slot] > 0:
                self._wait("sp", (self.dsem[slot], self.dval[slot], "dma", True))


def bcast_mid(ap2d, n):
    p, j = ap2d.shape
    return ap2d.unsqueeze(2).to_broadcast([p, j, n])


class Prog:
    def __init__(self, nc, phases, dbg=None):
        self.nc = nc
        self.k = KB(nc)
        self.phases = phases
        self.dbg = dbg if dbg is not None else {}
        k = self.k
        self.inp = {}
        for name, shp in INPUT_SHAPES.items():
            self.inp[name] = nc.dram_tensor(name, shp, F32, kind="ExternalInput").ap()
        self.y = nc.dram_tensor("y", [S, D], F32, kind="ExternalOutput").ap()
        self.xa = nc.dram_tensor("xa", [S, D], F32).ap()
        self.xb = nc.dram_tensor("xb", [S, D], F32).ap()
        def dram(name, shape, dt):
            kind = "ExternalOutput" if name in self.dbg else ("ExternalInput" if ("in:" + name) in self.dbg else "Internal")
            return nc.dram_tensor(name, shape, dt, kind=kind).ap()
        self.prw = dram("prw", [1408, S], F32)
        self.t_prw = [Tk() for _ in range(NG)]
        self.mxa = dram("mxa", [5 * 128, S], BF16)
        self.mxb = dram("mxb", [3 * 128, S], BF16) if "mxb" in self.dbg else None
        self.t_mxb = Tk()
        self.t_mxa = [Tk() for _ in range(NG)]
        self.t_xa = [Tk("xa%d" % i) for i in range(NG)]
        self.t_xb = [Tk("xb%d" % i) for i in range(NG)]
        self.t_x = [Tk("x%d" % i) for i in range(NG)]
        self.t_y = [Tk("y%d" % i) for i in range(NG)]
        self.ident_b = k.sb("ident_b", [128, 128], BF16)
        self.ident_f = k.sb("ident_f", [128, 128], F32)
        self.t_const = Tk("const")
        self.eps_t = k.sb("eps_t", [128, 1], F32)
        k.op("pool", lambda h: h.memset(self.ident_b[:], 1.0), w=[self.t_const])
        k.op("pool", lambda h: h.affine_select(out=self.ident_b[:], in_=self.ident_b[:], pattern=[[1, 128]],
                                               compare_op=ALU.is_equal, fill=0.0, base=0, channel_multiplier=-1),
             r=[self.t_const], w=[self.t_const])
        k.op("pool", lambda h: h.memset(self.ident_f[:], 1.0), w=[self.t_const])
        k.op("pool", lambda h: h.affine_select(out=self.ident_f[:], in_=self.ident_f[:], pattern=[[1, 128]],
                                               compare_op=ALU.is_equal, fill=0.0, base=0, channel_multiplier=-1),
             r=[self.t_const], w=[self.t_const])
        k.op("pool", lambda h: h.memset(self.eps_t[:], NORM_EPS), w=[self.t_const])
        self.gcol = k.sb("gcol", [128, 4, 8], F32)
        self.t_gcol = Tk("gcol")
        for n, (nm, l) in enumerate([("norm_mix_g", 0), ("norm_mix_g", 1), ("norm_mlp_g", 0), ("norm_mlp_g", 1)]):
            k.dma("sp", self.gcol[:, n, :], self.inp[nm][l].rearrange("(j p) -> p j", p=128),
                  w=[self.t_gcol], allow_slow_non_contiguous=True)
        self.nblk = 0
        self.dumped = set()


    def alloc_norm(self):
        k = self.k
        self.xt = [k.sb("xt%d" % i, [128, D], F32) for i in range(2)]
        self.t_xt = [Tk("xt%d" % i) for i in range(2)]
        self.xn = [k.sb("xn%d" % i, [128, D], BF16) for i in range(2)]
        self.t_xn = [Tk("xn%d" % i) for i in range(2)]
        self.junk = k.sb("junk", [128, D], BF16)
        self.t_junk = Tk("junk")
        self.stat = [k.sb("stat%d" % i, [128, 4], F32) for i in range(2)]
        self.t_stat = [Tk("stat%d" % i) for i in range(2)]
        self.hT = k.sb("hT", [128, 8, G], BF16)
        self.t_hT = [Tk("hT%d" % i) for i in range(NB)]
        self.psT = [k.ps("psT%d" % i, [128, 8, 128], BF16) for i in range(1)]
        self.t_psT = [Tk("psT%d" % i) for i in range(1)]

    def dump(self, name, ap, toks):
        if "dump" not in self.dbg or name in self.dumped:
            return
        self.dumped.add(name)
        d = self.nc.dram_tensor("d_" + name, list(ap.shape), ap.dtype, kind="ExternalOutput").ap()
        self.k.dma("sp", d, ap, r=list(toks), w=[Tk()])

    def norm_block(self, src_ap, src_tk, gidx, blk, hT=None, t_hT=None):
        hT = self.hT if hT is None else hT
        t_hT = self.t_hT if t_hT is None else t_hT
        k = self.k
        i = self.nblk % 2
        self.nblk += 1
        xt, t_xt, xn, t_xn, st, t_st = self.xt[i], self.t_xt[i], self.xn[i], self.t_xn[i], self.stat[i], self.t_stat[i]
        k.dma("sp", xt[:], src_ap, r=[src_tk], w=[t_xt])
        k.op("act", lambda h: h.activation(out=self.junk[:], in_=xt[:], func=AF.Square, accum_out=st[:, 0:1]),
             r=[t_xt], w=[self.t_junk, t_st])
        k.op("act", lambda h: h.activation(out=st[:, 1:2], in_=st[:, 0:1], func=AF.Sqrt, scale=1.0 / D,
                                           bias=self.eps_t[:, 0:1]),
             r=[t_st, self.t_const], w=[t_st])
        k.op("dve", lambda h: h.reciprocal(out=st[:, 2:3], in_=st[:, 1:2]), r=[t_st], w=[t_st])
        k.op("dve", lambda h: h.tensor_scalar(out=xn[:], in0=xt[:], scalar1=st[:, 2:3], scalar2=None, op0=ALU.mult),
             r=[t_xt, t_st], w=[t_xn])
        pT, t_pT = self.psT[0], self.t_psT[0]
        for j in range(8):
            k.op("pe", lambda h, j=j: h.transpose(out=pT[:, j, :], in_=xn[:, j * 128:(j + 1) * 128],
                                                   identity=self.ident_b[:]),
                 r=[t_xn, self.t_const], w=[t_pT])
        k.op("dve", lambda h: h.tensor_tensor(out=hT[:, :, blk * 128:(blk + 1) * 128], in0=pT[:],
                                              in1=bcast_mid(self.gcol[:, gidx, :], 128), op=ALU.mult),
             r=[t_pT, self.t_gcol], w=[t_hT[blk]])

    def pass_ffn(self, l, src, t_src, dst, t_dst, final):
        k = self.k
        k.push_scope()
        self.alloc_norm()
        if final:
            self.gfin = k.sb("gfin", [128, D], F32)
            self.t_gfin = Tk("gfin")
            k.dma("sp", self.gfin[:], self.inp["final_norm_g"].unsqueeze(0).partition_broadcast(128),
                  w=[self.t_gfin])
        wup = k.sb("wup", [128, 8, DFF], BF16)
        wdn = k.sb("wdn", [128, 32, D], BF16)
        t_wup = [Tk() for _ in range(8)]
        t_wdn = [Tk() for _ in range(8)]
        for kc in range(8):
            k.dma("pool", wup[:, kc, :], self.inp["w_mlp_up"][l, kc * 128:(kc + 1) * 128, :], w=[t_wup[kc]])
        for c4 in range(8):
            k.dma("pool", wdn[:, c4 * 4:(c4 + 1) * 4, :],
                  self.inp["w_mlp_down"][l, c4 * 512:(c4 + 1) * 512, :].rearrange("(c p) n -> p c n", p=128),
                  w=[t_wdn[c4]])
        aT = k.sb("aT", [128, 32, G], BF16)
        t_aT = [Tk() for _ in range(32)]
        rt = [k.sb("rt%d" % i, [128, G], F32) for i in range(2)]
        t_rt = [Tk() for _ in range(2)]
        psU = [k.ps("psU%d" % i, [128, G], F32) for i in range(2)]
        t_psU = [Tk() for _ in range(2)]
        psD = [k.ps("psD%d" % i, [128, 512], F32) for i in range(2)]
        t_psD = [Tk() for _ in range(2)]
        xr = [k.sb("xr%d" % i, [128, D], F32) for i in range(2)]
        t_xr = [Tk() for _ in range(2)]
        xo, t_xo = xr, t_xr
        st2 = [k.sb("st2_%d" % i, [128, 4], F32) for i in range(2)]
        t_st2 = [Tk() for _ in range(2)]
        n_o = 0
        for g in range(NG):
            for blk in range(NB):
                r0 = g * G + blk * 128
                self.norm_block(src[r0:r0 + 128, :], t_src[g], 2 + l, blk)
            for c in range(32):
                ps, t_ps = psU[c % 2], t_psU[c % 2]
                for kc in range(8):
                    k.op("pe", lambda h, kc=kc, c=c, ps=ps: h.matmul(ps[:], lhsT=wup[:, kc, c * 128:(c + 1) * 128],
                                                                    rhs=self.hT[:, kc, :], start=(kc == 0),
                                                                    stop=(kc == 7)),
                         r=[t_wup[kc]] + self.t_hT, w=[t_ps])
                r_, t_r = rt[c % 2], t_rt[c % 2]
                k.op("act", lambda h, ps=ps, r_=r_: h.activation(out=r_[:], in_=ps[:], func=AF.Relu),
                     r=[t_ps], w=[t_r])
                k.op("pool", lambda h, r_=r_, c=c: h.tensor_tensor(out=aT[:, c, :], in0=r_[:], in1=r_[:],
                                                                   op=ALU.mult),
                     r=[t_r], w=[t_aT[c]])
            for blk in range(NB):
                r0 = g * G + blk * 128
                i = n_o % 2
                n_o += 1
                k.dma("sp", xr[i][:], src[r0:r0 + 128, :], r=[t_src[g]], w=[t_xr[i]])
                for half in range(2):
                    ps, t_ps = psD[half], t_psD[half]
                    for c in range(32):
                        k.op("pe", lambda h, c=c, ps=ps, half=half, blk=blk: h.matmul(
                            ps[:], lhsT=aT[:, c, blk * 128:(blk + 1) * 128],
                            rhs=wdn[:, c, half * 512:(half + 1) * 512], start=(c == 0), stop=(c == 31)),
                             r=[t_aT[c], t_wdn[c // 4]], w=[t_ps])
                    k.op("dve", lambda h, ps=ps, half=half, i=i: h.tensor_tensor(
                        out=xo[i][:, half * 512:(half + 1) * 512], in0=ps[:],
                        in1=xr[i][:, half * 512:(half + 1) * 512], op=ALU.add),
                         r=[t_ps, t_xr[i]], w=[t_xr[i]])
                if final:
                    st, t_st = st2[i], t_st2[i]
                    k.op("act", lambda h, i=i, st=st: h.activation(out=self.junk[:], in_=xo[i][:], func=AF.Square,
                                                                   accum_out=st[:, 0:1]),
                         r=[t_xo[i]], w=[self.t_junk, t_st])
                    k.op("act", lambda h, st=st: h.activation(out=st[:, 1:2], in_=st[:, 0:1], func=AF.Sqrt,
                                                              scale=1.0 / D, bias=self.eps_t[:, 0:1]),
                         r=[t_st, self.t_const], w=[t_st])
                    k.op("dve", lambda h, st=st: h.reciprocal(out=st[:, 2:3], in_=st[:, 1:2]), r=[t_st], w=[t_st])
                    k.op("dve", lambda h, i=i, st=st: h.scalar_tensor_tensor(
                        out=xo[i][:], in0=xo[i][:], scalar=st[:, 2:3], in1=self.gfin[:], op0=ALU.mult, op1=ALU.mult),
                         r=[t_xo[i], t_st, self.t_gfin], w=[t_xo[i]])
                k.dma("sp", dst[r0:r0 + 128, :], xo[i][:], r=[t_xo[i]], w=[t_dst[g]])
        k.pop_scope()


    def pass_mixa(self, l, src, t_src):
        k = self.k
        k.push_scope()
        self.alloc_norm()
        inp = self.inp
        lambda_init = 0.8 - 0.6 * math.exp(-0.3 * l)
        win = k.sb("win", [128, 8, NIN], BF16)
        t_win = [Tk() for _ in range(8)]
        for kc in range(8):
            k.dma("pool", win[:, kc, :], inp["w_in"][l, kc * 128:(kc + 1) * 128, :], w=[t_win[kc]])
        cst = k.sb("cst", [128, 8], F32)
        t_cst = Tk()
        lq = k.sb("lq", [128, 4, 32], F32)
        t_lq = Tk()
        for i, nm in enumerate(["lam_q1", "lam_k1", "lam_q2", "lam_k2"]):
            k.dma("sp", lq[:, i, :], inp[nm][l:l + 1, :].partition_broadcast(128), w=[t_lq])
        k.op("dve", lambda h: h.tensor_tensor(out=lq[:, 0, :], in0=lq[:, 0, :], in1=lq[:, 1, :], op=ALU.mult),
             r=[t_lq], w=[t_lq])
        k.op("dve", lambda h: h.tensor_tensor(out=lq[:, 2, :], in0=lq[:, 2, :], in1=lq[:, 3, :], op=ALU.mult),
             r=[t_lq], w=[t_lq])
        k.op("dve", lambda h: h.tensor_reduce(out=cst[:, 0:1], in_=lq[:, 0, :], axis=AX.X, op=ALU.add),
             r=[t_lq], w=[t_cst])
        k.op("dve", lambda h: h.tensor_reduce(out=cst[:, 1:2], in_=lq[:, 2, :], axis=AX.X, op=ALU.add),
             r=[t_lq], w=[t_cst])
        k.op("act", lambda h: h.activation(out=cst[:, 2:4], in_=cst[:, 0:2], func=AF.Exp), r=[t_cst], w=[t_cst])
        k.op("dve", lambda h: h.tensor_tensor(out=cst[:, 4:5], in0=cst[:, 2:3], in1=cst[:, 3:4], op=ALU.subtract),
             r=[t_cst], w=[t_cst])
        k.op("dve", lambda h: h.tensor_scalar(out=cst[:, 5:6], in0=cst[:, 4:5], scalar1=float(lambda_init),
                                              scalar2=None, op0=ALU.add), r=[t_cst], w=[t_cst])
        k.op("pool", lambda h: h.memset(cst[:, 6:7], NORM_EPS), w=[t_cst])
        subg = k.sb("subg", [128, 64], F32)
        t_subg = Tk()
        k.dma("sp", subg[:], inp["subln_g"][l:l + 1, :].partition_broadcast(128), w=[t_subg])
        k.op("dve", lambda h: h.tensor_scalar(out=subg[:], in0=subg[:], scalar1=float(1.0 - lambda_init),
                                              scalar2=None, op0=ALU.mult), r=[t_subg], w=[t_subg])
        cw = k.sb("cw", [128, 2, 3], F32)
        t_cw = Tk()
        for j in range(2):
            k.dma("sp", cw[:, j, :], inp["conv_w"][l, :, j * 128:(j + 1) * 128].rearrange("t p -> p t"),
                  w=[t_cw], allow_slow_non_contiguous=True)
        mask = k.sb("mask", [128, 128], BF16)
        t_mask = Tk()
        k.op("pool", lambda h: h.memset(mask[:], 1.0), w=[t_mask])
        k.op("pool", lambda h: h.affine_select(out=mask[:], in_=mask[:], pattern=[[1, 128]], compare_op=ALU.is_ge,
                                               fill=0.0, base=0, channel_multiplier=-1), r=[t_mask], w=[t_mask])
        kT = k.sb("kT", [128, 3, S], BF16)
        t_kT = [[Tk() for _ in range(NG)] for _ in range(3)]
        Vt = k.sb("Vt", [128, S // 128, 6, 65], BF16)
        t_V = [Tk() for _ in range(S // 128)]
        for b8 in range(0, S // 128, 8):
            k.op("pool", lambda h, b8=b8: h.memset(Vt[:, b8:b8 + 8], 1.0), w=t_V[b8:b8 + 8])
        qT = k.sb("qT", [128, 3, G], BF16)
        t_qT = [Tk() for _ in range(3)]
        cv = k.sb("cv", [128, 6, G], F32)
        t_cv = [Tk() for _ in range(6)]
        zb = k.sb("zb", [128, 2, G + 2], F32)
        t_zb = [Tk() for _ in range(2)]
        k.op("pool", lambda h: h.memset(zb[:], 0.0), w=t_zb)
        ycv = k.sb("ycv", [128, G], F32)
        t_ycv = Tk()
        stg = [k.sb("stg%d" % i, [128, G], F32) for i in range(2)]
        t_stg = [Tk() for _ in range(2)]
        eT = [k.sb("eT%d" % i, [128, G], BF16) for i in range(4)]
        t_eT = [Tk() for _ in range(4)]
        uT = k.sb("uT", [128, 2, G], F32)
        t_uT = [Tk() for _ in range(2)]
        oat = k.sb("oat", [128, NB, 6, 64], F32)
        t_oat = Tk()
        osq = k.sb("osq", [128, NB, 6, 64], F32)
        t_osq = Tk()
        t1 = k.sb("t1", [128, NB, 64], F32)
        t_t1 = Tk()
        rl = k.sb("rl", [128, 2, NB], F32)
        t_rl = Tk()
        ss = k.sb("ss", [128, 3, NB * 6], F32)
        t_ss = Tk()
        oab = k.sb("oab", [128, NB, 384], BF16)
        t_oab = Tk()
        mixA = k.sb("mixA", [128, 5, G], BF16)
        t_mixA = [Tk() for _ in range(5)]
        NPA = 4
        psA = [k.ps("psA%d" % i, [128, G], F32) for i in range(NPA)]
        t_psA = [Tk() for _ in range(NPA)]
        psO = [k.ps("psO%d" % i, [128, G], F32) for i in range(2)]
        t_psO = [Tk() for _ in range(2)]
        psTr1 = k.ps("psTr", [128, NB, 65], F32)
        t_psTr1 = Tk()
        hTs = [self.hT, k.sb("hT2", [128, 8, G], BF16)]
        t_hTs = [self.t_hT, [Tk() for _ in range(NB)]]
        qTs = [qT, k.sb("qT2", [128, 3, G], BF16)]
        t_qTs = [t_qT, [Tk() for _ in range(3)]]
        cvs = [cv, k.sb("cv2", [128, 6, G], F32)]
        t_cvs = [t_cv, [Tk() for _ in range(6)]]
        cnt = {"A": 0, "S": 0, "E": 0}
        sc = 1.0 / math.sqrt(32.0)

        def nbank():
            i = cnt["A"] % NPA
            cnt["A"] += 1
            return psA[i], t_psA[i]

        def front(g):
            hT, t_hT = hTs[g % 2], t_hTs[g % 2]
            qT_, t_qT_ = qTs[g % 2], t_qTs[g % 2]
            cv_, t_cv_ = cvs[g % 2], t_cvs[g % 2]
            for blk in range(NB):
                r0 = g * G + blk * 128
                self.norm_block(src[r0:r0 + 128, :], t_src[g], l, blk, hT, t_hT)
                yield
            for c in list(range(0, 6)) + list(range(9, 26)):
                ps, t_ps = nbank()
                for kc in range(8):
                    k.op("pe", lambda h, kc=kc, c=c, ps=ps: h.matmul(ps[:], lhsT=win[:, kc, c * 128:(c + 1) * 128],
                                                                    rhs=hT[:, kc, :], start=(kc == 0),
                                                                    stop=(kc == 7)),
                         r=[t_win[kc]] + t_hT, w=[t_ps])
                if c < 3:
                    k.op("dve", lambda h, ps=ps, c=c: h.tensor_copy(out=qT_[:, c, :], in_=ps[:]),
                         r=[t_ps], w=[t_qT_[c]])
                elif c < 6:
                    k.op("dve", lambda h, ps=ps, c=c: h.tensor_copy(out=kT[:, c - 3, g * G:(g + 1) * G], in_=ps[:]),
                         r=[t_ps], w=[t_kT[c - 3][g]])
                elif c < 15:
                    k.op("dve", lambda h, ps=ps, c=c: h.tensor_copy(out=cv_[:, c - 9, :], in_=ps[:]),
                         r=[t_ps], w=[t_cv_[c - 9]])
                else:
                    i = cnt["S"] % 2
                    cnt["S"] += 1
                    k.op("dve", lambda h, ps=ps, i=i: h.tensor_copy(out=stg[i][:], in_=ps[:]), r=[t_ps], w=[t_stg[i]])
                    k.dma("sp", self.prw[(c - 15) * 128:(c - 14) * 128, g * G:(g + 1) * G], stg[i][:],
                          r=[t_stg[i]], w=[self.t_prw[g]])
                yield
            for blk in range(NB):
                ps, t_ps = nbank()
                for kc in range(8):
                    k.op("pe", lambda h, kc=kc, ps=ps, blk=blk: h.matmul(
                        ps[:, 0:384], lhsT=hT[:, kc, blk * 128:(blk + 1) * 128], rhs=win[:, kc, 768:1152],
                        start=(kc == 0), stop=(kc == 7)), r=[t_win[kc], t_hT[blk]], w=[t_ps])
                k.op("dve", lambda h, ps=ps, blk=blk: h.tensor_copy(
                    out=Vt[:, g * NB + blk, :, 0:64], in_=ps[:, 0:384].rearrange("p (a b) -> p a b", a=6)),
                     r=[t_ps], w=[t_V[g * NB + blk]])
                yield
            for j in range(2):
                k.op("pool", lambda h, j=j: h.tensor_tensor(out=zb[:, j, 2:G + 2], in0=cv_[:, 2 + j, :],
                                                            in1=cv_[:, 4 + j, :], op=ALU.mult),
                     r=[t_cv_[2 + j], t_cv_[4 + j]], w=[t_zb[j]])
                k.op("pool", lambda h, j=j: h.tensor_scalar(out=ycv[:], in0=zb[:, j, 0:G], scalar1=cw[:, j, 0:1],
                                                            scalar2=None, op0=ALU.mult),
                     r=[t_zb[j], t_cw], w=[t_ycv])
                k.op("dve", lambda h, j=j: h.scalar_tensor_tensor(out=ycv[:], in0=zb[:, j, 1:G + 1],
                                                                  scalar=cw[:, j, 1:2], in1=ycv[:],
                                                                  op0=ALU.mult, op1=ALU.add),
                     r=[t_zb[j], t_cw, t_ycv], w=[t_ycv])
                k.op("dve", lambda h, j=j: h.scalar_tensor_tensor(out=ycv[:], in0=zb[:, j, 2:G + 2],
                                                                  scalar=cw[:, j, 2:3], in1=ycv[:],
                                                                  op0=ALU.mult, op1=ALU.add),
                     r=[t_zb[j], t_cw, t_ycv], w=[t_ycv])
                k.op("pool", lambda h, j=j: h.tensor_tensor(out=mixAs[g % 2][:, 3 + j, :], in0=ycv[:],
                                                            in1=cv_[:, j, :], op=ALU.mult),
                     r=[t_ycv, t_cv_[j]], w=[t_mixAs[g % 2][3 + j]])
                k.op("pool", lambda h, j=j: h.tensor_copy(out=zb[:, j, 0:2], in_=zb[:, j, G:G + 2]),
                     r=[t_zb[j]], w=[t_zb[j]])
                yield

        def back(g):
            qT_, t_qT_ = qTs[g % 2], t_qTs[g % 2]
            mixA_, t_mixA_ = mixAs[g % 2], t_mixAs[g % 2]
            nkb = 4 * g + 4
            LOOK = 2

            def emit_S(hd, half, j):
                qc = hd // 2
                pb = (hd % 2) * 64 + half * 32
                off = max(0, j - 4 * g) * 128
                ps, t_ps = nbank()
                k.op("pe", lambda h: h.matmul(
                    ps[:, off:G], lhsT=kT[pb:pb + 32, qc, j * 128:(j + 1) * 128],
                    rhs=qT_[pb:pb + 32, qc, off:G], start=True, stop=True, tile_position=(pb, 0)),
                     r=[t_kT[qc][j // NB], t_qT_[qc]], w=[t_ps])
                e, t_e = eT[cnt["E"] % 4], t_eT[cnt["E"] % 4]
                cnt["E"] += 1
                k.op("act", lambda h: h.activation(out=e[:, off:G], in_=ps[:, off:G], func=AF.Exp, scale=sc),
                     r=[t_ps], w=[t_e])
                if j >= 4 * g:
                    k.op("pool", lambda h: h.tensor_tensor(out=e[:, off:off + 128], in0=e[:, off:off + 128],
                                                           in1=mask[:], op=ALU.mult), r=[t_e, t_mask], w=[t_e])
                return (hd, half, j, off, e, t_e)

            def emit_PV(item):
                hd, half, j, off, e, t_e = item
                k.op("pe", lambda h: h.matmul(psO[half][0:65, off:G], lhsT=Vt[:, j, hd, :], rhs=e[:, off:G],
                                              start=(j == 0), stop=(j == nkb - 1)),
                     r=[t_e, t_V[j]], w=[t_psO[half]])
                if j != nkb - 1:
                    return
                k.op("dve", lambda h: h.tensor_copy(out=uT[0:65, half, :], in_=psO[half][0:65, :]),
                     r=[t_psO[half]], w=[t_uT[half]])
                for blk in range(NB):
                    k.op("pe", lambda h, blk=blk: h.transpose(
                        out=psTr1[:, blk, :], in_=uT[0:65, half, blk * 128:(blk + 1) * 128],
                        identity=self.ident_f[0:65, 0:65]), r=[t_uT[half], self.t_const], w=[t_psTr1])
                k.op("dve", lambda h: h.reciprocal(out=rl[:, half, :], in_=psTr1[:, :, 64]),
                     r=[t_psTr1], w=[t_rl])
                if half == 0:
                    k.op("dve", lambda h: h.tensor_tensor(out=t1[:], in0=psTr1[:, :, 0:64],
                                                          in1=bcast_mid(rl[:, 0, :], 64), op=ALU.mult),
                         r=[t_psTr1, t_rl], w=[t_t1])
                    return
                k.op("dve", lambda h: h.tensor_scalar(out=rl[:, 1, :], in0=rl[:, 1, :], scalar1=cst[:, 5:6],
                                                      scalar2=None, op0=ALU.mult), r=[t_rl, t_cst], w=[t_rl])
                k.op("dve", lambda h: h.tensor_tensor(out=oat[:, :, hd, :], in0=psTr1[:, :, 0:64],
                                                      in1=bcast_mid(rl[:, 1, :], 64), op=ALU.mult),
                     r=[t_psTr1, t_rl], w=[t_oat])
                k.op("dve", lambda h: h.tensor_tensor(out=oat[:, :, hd, :], in0=t1[:], in1=oat[:, :, hd, :],
                                                      op=ALU.subtract), r=[t_t1, t_oat], w=[t_oat])

            pend = []
            for hd in range(6):
                for half in range(2):
                    for j in range(nkb):
                        pend.append(emit_S(hd, half, j))
                        if len(pend) > LOOK:
                            emit_PV(pend.pop(0))
                        yield
            while pend:
                emit_PV(pend.pop(0))
            yield
            k.op("pool", lambda h: h.tensor_tensor(out=osq[:], in0=oat[:], in1=oat[:], op=ALU.mult),
                 r=[t_oat], w=[t_osq])
            k.op("dve", lambda h: h.tensor_reduce(out=ss[:, 0, :], in_=osq[:].rearrange("p a b c -> p (a b) c"),
                                                  axis=AX.X, op=ALU.add), r=[t_osq], w=[t_ss])
            k.op("act", lambda h: h.activation(out=ss[:, 1, :], in_=ss[:, 0, :], func=AF.Sqrt, scale=1.0 / 64.0,
                                               bias=cst[:, 6:7]), r=[t_ss, t_cst], w=[t_ss])
            k.op("dve", lambda h: h.reciprocal(out=ss[:, 2, :], in_=ss[:, 1, :]), r=[t_ss], w=[t_ss])
            k.op("dve", lambda h: h.tensor_tensor(out=osq[:].rearrange("p a b c -> p (a b) c"),
                                                  in0=oat[:].rearrange("p a b c -> p (a b) c"),
                                                  in1=bcast_mid(ss[:, 2, :], 64), op=ALU.mult),
                 r=[t_oat, t_ss], w=[t_osq])
            k.op("dve", lambda h: h.tensor_tensor(
                out=oab[:].rearrange("p a (b c) -> p (a b) c", c=64), in0=osq[:].rearrange("p a b c -> p (a b) c"),
                in1=subg[:].unsqueeze(1).to_broadcast([128, NB * 6, 64]), op=ALU.mult),
                 r=[t_osq, t_subg], w=[t_oab])
            yield
            pT, t_pT = self.psT[0], self.t_psT[0]
            for blk in range(NB):
                for c in range(3):
                    k.op("pe", lambda h, blk=blk, c=c: h.transpose(out=pT[:, c, :],
                                                                   in_=oab[:, blk, c * 128:(c + 1) * 128],
                                                                   identity=self.ident_b[:]),
                         r=[t_oab, self.t_const], w=[t_pT])
                k.op("dve", lambda h, blk=blk: h.tensor_copy(out=mixA_[:, 0:3, blk * 128:(blk + 1) * 128],
                                                             in_=pT[:, 0:3, :]),
                     r=[t_pT], w=t_mixA_[0:3])
                yield
            k.dma("sp", self.mxa[:, g * G:(g + 1) * G].rearrange("(c p) t -> p c t", p=128), mixA_[:],
                  r=t_mixA_, w=[self.t_mxa[g]])

        mixAs = [mixA, k.sb("mixA2", [128, 5, G], BF16)]
        t_mixAs = [t_mixA, [Tk() for _ in range(5)]]
        for _ in front(0):
            pass
        for g in range(NG):
            bk = back(g)
            fr = front(g + 1) if g + 1 < NG else iter(())
            n_att = 12 * (4 * g + 4)
            n_fr = 4 + 23 + 4 + 2
            ratio = max(1, n_att // n_fr)
            fr_done = False
            i = 0
            for _ in bk:
                i += 1
                if not fr_done and i % ratio == 0:
                    try:
                        next(fr)
                    except StopIteration:
                        fr_done = True
            for _ in fr:
                pass
        k.pop_scope()

    def pass_mixb(self, l, src, t_src, dst, t_dst):
        k = self.k
        k.push_scope()
        inp = self.inp
        C = 64
        NCH = G // C
        c0 = math.exp(-0.5)
        wout = k.sb("wout", [128, 8, D], BF16)
        t_wout = [Tk() for _ in range(8)]
        for kc in range(8):
            k.dma("pool", wout[:, kc, :], inp["w_out"][l, kc * 128:(kc + 1) * 128, :], w=[t_wout[kc]])
        waup = k.sb("waup", [128, 384], BF16)
        gup = k.sb("gup", [128, 384], BF16)
        t_lw = Tk()
        k.dma("pool", waup[0:64, :], inp["rwkv_w_up"][l], w=[t_lw])
        k.dma("pool", waup[64:128, :], inp["rwkv_a_up"][l], w=[t_lw])
        k.dma("pool", gup[:], inp["rwkv_g_up"][l], w=[t_lw])
        pc = k.sb("pc", [128, 8, 3], F32)
        t_pc = Tk()
        for n, nm in enumerate(["rwkv_w0", "rwkv_a0", "rwkv_k_k", "rwkv_k_a", "rwkv_r_k"]):
            srcap = inp[nm][l]
            if nm == "rwkv_r_k":
                srcap = srcap.rearrange("a b -> (a b)")
            k.dma("sp", pc[:, n, :], srcap.rearrange("(j p) -> p j", p=128), w=[t_pc],
                  allow_slow_non_contiguous=True)
        k.op("dve", lambda h: h.tensor_scalar(out=pc[:, 5, :], in0=pc[:, 3, :], scalar1=-1.0, scalar2=1.0,
                                              op0=ALU.mult, op1=ALU.add), r=[t_pc], w=[t_pc])
        k.op("pool", lambda h: h.memset(pc[:, 6, :], GN_EPS), w=[t_pc])
        mu = k.sb("mu", [128, 11], F32)
        t_mu = Tk()
        k.dma("sp", mu[:], inp["shift_mu"][l].rearrange("(j p) -> p j", p=128), w=[t_mu],
              allow_slow_non_contiguous=True)
        lng = k.sb("lng", [128, 2, 384], F32)
        t_lng = Tk()
        k.dma("sp", lng[:, 0, :], inp["lnx_g"][l:l + 1, :].partition_broadcast(128), w=[t_lng])
        k.dma("sp", lng[:, 1, :], inp["lnx_b"][l:l + 1, :].partition_broadcast(128), w=[t_lng])
        bones = k.sb("bones", [128, 128], BF16)
        t_bones = Tk()
        k.op("pool", lambda h: h.memset(bones[:], 0.0), w=[t_bones])
        k.op("pool", lambda h: h.memset(bones[0:64, 0:64], 1.0), w=[t_bones])
        k.op("pool", lambda h: h.memset(bones[64:128, 64:128], 1.0), w=[t_bones])
        msk = k.sb("msk", [128, 3, 64], F32)
        t_msk = Tk()
        k.op("pool", lambda h: h.memset(msk[:], 1.0), w=[t_msk])
        k.op("pool", lambda h: h.affine_select(out=msk[0:64, 0, :], in_=msk[0:64, 0, :], pattern=[[1, 64]],
                                               compare_op=ALU.is_ge, fill=0.0, base=-1, channel_multiplier=-1),
             r=[t_msk], w=[t_msk])
        k.op("pool", lambda h: h.affine_select(out=msk[0:64, 1, :], in_=msk[0:64, 1, :], pattern=[[1, 64]],
                                               compare_op=ALU.is_ge, fill=0.0, base=0, channel_multiplier=-1),
             r=[t_msk], w=[t_msk])
        k.op("pool", lambda h: h.affine_select(out=msk[0:64, 2, :], in_=msk[0:64, 2, :], pattern=[[-1, 64]],
                                               compare_op=ALU.is_ge, fill=0.0, base=-1, channel_multiplier=1),
             r=[t_msk], w=[t_msk])
        k.dma("sp", msk[64:128], msk[0:64], r=[t_msk], w=[t_msk])

        GB = 256
        NCH = GB // C
        NGB = S // GB
        NQ = S // C
        NR = 4
        rmask = k.sb("rmask", [128, GB], F32)
        t_rmask = Tk()
        k.op("pool", lambda h: h.memset(rmask[:], 1.0), w=[t_rmask])
        k.op("pool", lambda h: h.memset(rmask[:].rearrange("p (c t) -> p c t", t=C)[:, :, 0:1], 0.0), w=[t_rmask])
        idpl = k.sb("idpl", [128, 64], BF16); t_idpl = Tk()
        k.op("pool", lambda h: h.tensor_copy(out=idpl[0:64, :], in_=self.ident_b[0:64, 0:64]),
             r=[self.t_const], w=[t_idpl])
        k.op("pool", lambda h: h.tensor_copy(out=idpl[64:128, :], in_=self.ident_b[64:128, 64:128]),
             r=[self.t_const], w=[t_idpl])
        tk2 = lambda: [Tk(), Tk()]
        F3 = [128, 3, GB]
        pt = k.sb("pt", [128, 11, GB], F32); t_pt = Tk()
        halo = k.sb("halo", [128, 11, 1], F32); t_halo = Tk()
        k.op("pool", lambda h: h.memset(halo[:], 0.0), w=[t_halo])
        z = k.sb("z", [128, 11, GB], F32); t_z = Tk()
        sig = k.sb("sig", F3, F32); t_sig = Tk()
        aa = k.sb("aa", F3, F32); t_aa = Tk()
        kkn = k.sb("kkn", F3, F32); t_kkn = Tk()
        kmod = k.sb("kmod", F3, F32); t_kmod = Tk()
        beta = k.sb("beta", F3, F32); t_beta = Tk()
        cs = k.sb("cs", F3, F32); t_cs = Tk()
        tA = pt[:, 0:3, :]; t_tA = Tk()
        tB = pt[:, 3:6, :]; t_tB = Tk()
        t_alias = [t_tA, t_tB]
        b16 = k.sb("b16", F3, BF16); t_b16 = Tk()
        twa = k.sb("twa", [128, GB], BF16); t_twa = Tk()
        sg = k.sb("sg", [128, GB], BF16); t_sg = Tk()
        def gset(i):
            d = {}
            for nm in ["rt", "kt", "bt", "at", "k2", "b2"]:
                d[nm] = k.sb(nm + str(i), F3, BF16); d["t_" + nm] = Tk()
            d["gT"] = k.sb("gT%d" % i, F3, F32); d["t_gT"] = Tk()
            d["bonus"] = k.sb("bonus%d" % i, F3, F32); d["t_bonus"] = Tk()
            d["WC"] = k.sb("WC%d" % i, [128, 3, NCH], F32); d["t_WC"] = Tk()
            d["Vg"] = k.sb("Vg%d" % i, [128, NCH, 3, 64], BF16); d["t_Vg"] = [tk2() for _ in range(NCH)]
            d["ysb"] = k.sb("ysb%d" % i, [128, NCH // 2, 384], F32); d["t_ysb"] = [Tk() for _ in range(NCH)]
            return d
        GS = [gset(0), gset(1)]
        H6 = [128, 3, 64]
        def cset(i):
            d = {}
            for nm in ["X0", "X1", "Xt0", "Xt1", "P0", "P1", "Pt0", "Pt1", "Lak", "Arb", "Ark"]:
                d[nm] = k.sb("%s_%d" % (nm, i), H6, BF16); d["t_" + nm] = tk2()
            d["kbtok"] = k.sb("kbtok%d" % i, [128, 2, 3, 64], BF16); d["t_kbtok"] = tk2()
            return d
        CS = [cset(i) for i in range(NR)]
        r0b = k.sb("r0b", H6, BF16); t_r0b = tk2()
        Ub = k.sb("Ub", H6, BF16); t_Ub = tk2()
        Tst = k.sb("Tst", [128, 3, 64], F32); t_T = tk2()
        Tb = k.sb("Tb", [128, 3, 64], BF16); t_Tb = tk2()
        k.op("pool", lambda h: h.memset(Tst[:], 0.0), w=t_T)
        k.op("pool", lambda h: h.memset(Tb[:], 0.0), w=t_Tb)
        ysq = k.sb("ysq", [128, NCH // 2, 384], F32); t_ysq = Tk()
        gst = k.sb("gst", [128, 6, NCH * 3], F32); t_gst = Tk()
        potmp = [k.sb("potmp%d" % i, [128, GB], F32) for i in range(2)]; t_potmp = [Tk() for _ in range(2)]
        mixT = k.sb("mixT", [128, 8, GB], BF16); t_mixT = [Tk() for _ in range(8)]
        xr = [k.sb("xrb%d" % i, [128, D], F32) for i in range(2)]; t_xr = [Tk() for _ in range(2)]
        psG = [k.ps("psG%d" % i, [128, 512], F32) for i in range(6)]
        t_psG = [Tk() for _ in range(6)]
        psX = [k.ps("psX%d" % i, [128, 512], F32) for i in range(2)]
        t_psX = [Tk() for _ in range(2)]
        st = {"n": 0, "x": 0, "s": 0, "pt": 0}

        def nps():
            i = st["s"] % 2
            st["s"] += 1
            return psX[i], t_psX[i]

        def npair():
            i = st["n"] % 3
            st["n"] += 1
            return ((psG[2 * i], t_psG[2 * i]), (psG[2 * i + 1], t_psG[2 * i + 1]))

        def v3(ps, hb):
            return ps[hb:hb + 64, 0:192].rearrange("p (a b) -> p a b", a=3)

        def b3(col):
            return bcast_mid(col, GB)

        def pre(gb):
            gs = slice(gb * GB, (gb + 1) * GB)
            d = GS[gb % 2]
            tprw = self.t_prw[(gb * GB) // G]
            k.dma("sp", pt[:], self.prw[:, gs].rearrange("(c p) t -> p c t", p=128), r=[tprw],
                  w=[t_pt] + t_alias)
            yield
            k.op("pool", lambda h: h.tensor_tensor(out=z[:, :, 1:GB], in0=pt[:, :, 0:GB - 1], in1=pt[:, :, 1:GB],
                                                   op=ALU.subtract), r=[t_pt] + t_alias, w=[t_z])
            k.op("pool", lambda h: h.tensor_tensor(out=z[:, :, 0:1], in0=halo[:], in1=pt[:, :, 0:1],
                                                   op=ALU.subtract), r=[t_pt, t_halo] + t_alias, w=[t_z])
            yield
            k.op("dve", lambda h: h.tensor_tensor(out=z[:], in0=z[:], in1=bcast_mid(mu[:], GB), op=ALU.mult),
                 r=[t_z, t_mu], w=[t_z])
            yield
            k.op("pool", lambda h: h.tensor_tensor(out=z[:], in0=z[:], in1=pt[:], op=ALU.add),
                 r=[t_z, t_pt] + t_alias, w=[t_z])
            k.op("pool", lambda h: h.tensor_copy(out=halo[:], in_=pt[:, :, GB - 1:GB]), r=[t_pt] + t_alias,
                 w=[t_halo])
            yield
            zr, zk, zv = z[:, 0:3, :], z[:, 3:6, :], z[:, 6:9, :]
            k.op("act", lambda h: h.activation(out=twa[0:64, :], in_=z[0:64, 9, :], func=AF.Tanh), r=[t_z], w=[t_twa])
            k.op("act", lambda h: h.activation(out=twa[64:128, :], in_=z[64:128, 9, :], func=AF.Copy),
                 r=[t_z], w=[t_twa])
            k.op("act", lambda h: h.activation(out=sg[:], in_=z[:, 10, :], func=AF.Sigmoid), r=[t_z], w=[t_sg])
            yield
            for fc in range(3):
                ps, t_ps = nps()
                k.op("pe", lambda h, ps=ps, fc=fc: h.matmul(ps[:, 0:GB], lhsT=waup[0:64, fc * 128:(fc + 1) * 128],
                                                            rhs=twa[0:64, :], start=True, stop=True,
                                                            tile_position=(0, 0)), r=[t_lw, t_twa], w=[t_ps])
                k.op("act", lambda h, ps=ps, fc=fc: h.activation(out=sig[:, fc, :], in_=ps[:, 0:GB], func=AF.Sigmoid,
                                                                 bias=pc[:, 0, fc:fc + 1]), r=[t_ps, t_pc], w=[t_sig])
                yield
            for fc in range(3):
                ps, t_ps = nps()
                k.op("pe", lambda h, ps=ps, fc=fc: h.matmul(ps[:, 0:GB], lhsT=waup[64:128, fc * 128:(fc + 1) * 128],
                                                            rhs=twa[64:128, :], start=True, stop=True,
                                                            tile_position=(64, 0)), r=[t_lw, t_twa], w=[t_ps])
                k.op("act", lambda h, ps=ps, fc=fc: h.activation(out=aa[:, fc, :], in_=ps[:, 0:GB], func=AF.Sigmoid,
                                                                 bias=pc[:, 1, fc:fc + 1]), r=[t_ps, t_pc], w=[t_aa])
                yield
            for fc in range(3):
                ps, t_ps = nps()
                k.op("pe", lambda h, ps=ps, fc=fc: h.matmul(ps[:, 0:GB], lhsT=gup[:, fc * 128:(fc + 1) * 128],
                                                            rhs=sg[:], start=True, stop=True),
                     r=[t_lw, t_sg], w=[t_ps])
                k.op("dve", lambda h, ps=ps, fc=fc: h.tensor_copy(out=d["gT"][:, fc, :], in_=ps[:, 0:GB]),
                     r=[t_ps], w=[d["t_gT"]])
                yield
            k.op("pool", lambda h: h.tensor_tensor(out=tA, in0=zk, in1=b3(pc[:, 2, :]), op=ALU.mult),
                 r=[t_z, t_pc], w=[t_tA])
            k.op("pool", lambda h: h.tensor_tensor(out=b16[:], in0=tA, in1=tA, op=ALU.mult), r=[t_tA], w=[t_b16])
            yield
            for fc in range(3):
                ps, t_ps = nps()
                k.op("pe", lambda h, ps=ps, fc=fc: h.matmul(ps[:, 0:GB], lhsT=bones[:], rhs=b16[:, fc, :], start=True,
                                                            stop=True), r=[t_bones, t_b16], w=[t_ps])
                k.op("act", lambda h, ps=ps, fc=fc: h.activation(out=tB[:, fc, :], in_=ps[:, 0:GB], func=AF.Sqrt),
                     r=[t_ps], w=[t_tB])
                yield
            k.op("dve", lambda h: h.tensor_scalar(out=tB, in0=tB, scalar1=1e-12, scalar2=None, op0=ALU.max),
                 r=[t_tB], w=[t_tB])
            k.op("dve", lambda h: h.reciprocal(out=tB, in_=tB), r=[t_tB], w=[t_tB])
            yield
            k.op("pool", lambda h: h.tensor_tensor(out=kkn[:], in0=tA, in1=tB, op=ALU.mult),
                 r=[t_tA, t_tB], w=[t_kkn])
            k.op("pool", lambda h: h.tensor_tensor(out=tA, in0=aa[:], in1=b3(pc[:, 3, :]), op=ALU.mult),
                 r=[t_aa, t_pc, t_kkn], w=[t_tA])
            yield
            k.op("pool", lambda h: h.tensor_tensor(out=tA, in0=tA, in1=b3(pc[:, 5, :]), op=ALU.add),
                 r=[t_tA, t_pc], w=[t_tA])
            k.op("dve", lambda h: h.tensor_tensor(out=kmod[:], in0=zk, in1=tA, op=ALU.mult),
                 r=[t_z, t_tA], w=[t_kmod])
            yield
            k.op("pool", lambda h: h.tensor_tensor(out=beta[:], in0=kkn[:], in1=aa[:], op=ALU.mult),
                 r=[t_kkn, t_aa], w=[t_beta])
            k.op("dve", lambda h: h.tensor_tensor(out=tA, in0=zr, in1=kmod[:], op=ALU.mult),
                 r=[t_z, t_kmod], w=[t_tA])
            yield
            k.op("pool", lambda h: h.tensor_tensor(out=b16[:], in0=tA, in1=b3(pc[:, 4, :]), op=ALU.mult),
                 r=[t_tA, t_pc], w=[t_b16])
            yield
            for fc in range(3):
                ps, t_ps = nps()
                k.op("pe", lambda h, ps=ps, fc=fc: h.matmul(ps[:, 0:GB], lhsT=bones[:], rhs=b16[:, fc, :], start=True,
                                                            stop=True), r=[t_bones, t_b16], w=[t_ps])
                k.op("dve", lambda h, ps=ps, fc=fc: h.tensor_tensor(out=d["bonus"][:, fc, :], in0=ps[:, 0:GB],
                                                                    in1=z[:, 6 + fc, :], op=ALU.mult),
                     r=[t_ps, t_z], w=[d["t_bonus"]])
                yield
            for fc in range(3):
                k.op("dve", lambda h, fc=fc: h.tensor_tensor_scan(out=cs[:, fc, :], data0=rmask[:],
                                                                  data1=sig[:, fc, :], initial=0.0, op0=ALU.mult,
                                                                  op1=ALU.add), r=[t_rmask, t_sig], w=[t_cs])
            yield
            csC = cs[:].rearrange("p f (c t) -> p f c t", t=C)[:, :, :, C - 1]
            k.op("act", lambda h: h.activation(out=tA, in_=cs[:], func=AF.Exp, scale=-c0), r=[t_cs], w=[t_tA])
            k.op("dve", lambda h: h.tensor_tensor(out=d["rt"][:], in0=zr, in1=tA, op=ALU.mult),
                 r=[t_z, t_tA], w=[d["t_rt"]])
            yield
            k.op("act", lambda h: h.activation(out=tB, in_=cs[:], func=AF.Exp, scale=c0), r=[t_cs], w=[t_tB])
            k.op("pool", lambda h: h.tensor_tensor(out=d["kt"][:], in0=kmod[:], in1=tB, op=ALU.mult),
                 r=[t_kmod, t_tB], w=[d["t_kt"]])
            k.op("dve", lambda h: h.tensor_tensor(out=d["bt"][:], in0=beta[:], in1=tB, op=ALU.mult),
                 r=[t_beta, t_tB], w=[d["t_bt"]])
            yield
            k.op("pool", lambda h: h.tensor_tensor(out=tA, in0=cs[:], in1=sig[:], op=ALU.subtract),
                 r=[t_cs, t_sig], w=[t_tA])
            k.op("act", lambda h: h.activation(out=tA, in_=tA, func=AF.Exp, scale=-c0), r=[t_tA], w=[t_tA])
            k.op("dve", lambda h: h.scalar_tensor_tensor(out=d["at"][:], in0=kkn[:], scalar=-1.0, in1=tA,
                                                         op0=ALU.mult, op1=ALU.mult), r=[t_kkn, t_tA], w=[d["t_at"]])
            yield
            k.op("pool", lambda h: h.tensor_tensor(
                out=tB.rearrange("p f (c t) -> p f c t", t=C),
                in0=csC.unsqueeze(3).to_broadcast([128, 3, NCH, C]),
                in1=cs[:].rearrange("p f (c t) -> p f c t", t=C), op=ALU.subtract), r=[t_cs], w=[t_tB])
            k.op("act", lambda h: h.activation(out=tB, in_=tB, func=AF.Exp, scale=-c0), r=[t_tB], w=[t_tB])
            k.op("pool", lambda h: h.tensor_tensor(out=d["k2"][:], in0=kmod[:], in1=tB, op=ALU.mult),
                 r=[t_kmod, t_tB], w=[d["t_k2"]])
            k.op("dve", lambda h: h.tensor_tensor(out=d["b2"][:], in0=beta[:], in1=tB, op=ALU.mult),
                 r=[t_beta, t_tB], w=[d["t_b2"]])
            k.op("act", lambda h: h.activation(out=d["WC"][:], in_=csC, func=AF.Exp, scale=-c0),
                 r=[t_cs], w=[d["t_WC"]])
            yield
            for c in range(NCH):
                cl = slice(c * C, (c + 1) * C)
                pair = npair()
                for fc in range(3):
                    for par in range(2):
                        hb = par * 64
                        ps, t_ps = pair[par]
                        k.op("pe", lambda h, ps=ps, fc=fc, hb=hb, cl=cl: h.matmul(
                            ps[hb:hb + 64, fc * 64:(fc + 1) * 64], lhsT=z[hb:hb + 64, 6 + fc, cl],
                            rhs=self.ident_f[hb:hb + 64, hb:hb + 64], start=True, stop=True,
                            tile_position=(hb, hb)), r=[t_z, self.t_const], w=[t_ps])
                for par in range(2):
                    hb = par * 64
                    ps, t_ps = pair[par]
                    k.op("act", lambda h, ps=ps, c=c, hb=hb: h.activation(
                        out=d["Vg"][hb:hb + 64, c, :, :], in_=v3(ps, hb), func=AF.Copy),
                         r=[t_ps], w=[d["t_Vg"][c][par]])
                yield

        def mm6(pair, lf, rf, rtoks, lf2=None, rf2=None, rtoks2=None):
            for fc in range(3):
                for par in range(2):
                    hd = 2 * fc + par
                    hb = par * 64
                    ps, t_ps = pair[par]
                    k.op("pe", lambda h, ps=ps, hd=hd, hb=hb, fc=fc: h.matmul(
                        ps[hb:hb + 64, fc * 64:(fc + 1) * 64], lhsT=lf(hd), rhs=rf(hd), start=True,
                        stop=(lf2 is None), tile_position=(hb, hb)), r=rtoks(par), w=[t_ps])
                    if lf2 is not None:
                        k.op("pe", lambda h, ps=ps, hd=hd, hb=hb, fc=fc: h.matmul(
                            ps[hb:hb + 64, fc * 64:(fc + 1) * 64], lhsT=lf2(hd), rhs=rf2(hd), start=False,
                            stop=True, tile_position=(hb, hb)), r=rtoks2(par), w=[t_ps])

        def ev_mask(pair, dst, t_dst, mi):
            for par in range(2):
                hb = par * 64
                ps, t_ps = pair[par]
                k.op("dve", lambda h, ps=ps, hb=hb: h.tensor_tensor(
                    out=dst[hb:hb + 64], in0=v3(ps, hb),
                    in1=msk[hb:hb + 64, mi, :].unsqueeze(1).to_broadcast([64, 3, 64]), op=ALU.mult),
                     r=[t_ps, t_msk], w=[t_dst[par]])

        def ev_copy(pair, dst, t_dst):
            for par in range(2):
                hb = par * 64
                ps, t_ps = pair[par]
                k.op("act", lambda h, ps=ps, hb=hb: h.activation(out=dst[hb:hb + 64], in_=v3(ps, hb),
                                                                 func=AF.Copy), r=[t_ps], w=[t_dst[par]])

        def ev_add(pair, dst, t_dst, old, t_old):
            for par in range(2):
                hb = par * 64
                ps, t_ps = pair[par]
                k.op("dve", lambda h, ps=ps, hb=hb: h.tensor_tensor(
                    out=dst[hb:hb + 64], in0=v3(ps, hb), in1=old[hb:hb + 64], op=ALU.add),
                     r=[t_ps, t_old[par]], w=[t_dst[par]])

        def pl(tile, hd):
            hb = (hd % 2) * 64
            return tile[hb:hb + 64, hd // 2, :]

        def chunkA(q):
            gb, c = q // NCH, q % NCH
            d = GS[gb % 2]
            e = CS[q % NR]
            cl = slice(c * C, (c + 1) * C)

            def fm(tile, hd):
                hb = (hd % 2) * 64
                return tile[hb:hb + 64, hd // 2, cl]
            bt, at, kt, rt = d["bt"], d["at"], d["kt"], d["rt"]
            t_bt, t_at, t_kt, t_rt = d["t_bt"], d["t_at"], d["t_kt"], d["t_rt"]
            fmt = lambda t1, t2: (lambda par: [t1, t2])
            pair = npair()
            mm6(pair, lambda hd: fm(bt, hd), lambda hd: fm(at, hd), fmt(t_bt, t_at))
            ev_mask(pair, e["X0"], e["t_X0"], 0)
            yield
            pair = npair()
            mm6(pair, lambda hd: fm(at, hd), lambda hd: fm(bt, hd), fmt(t_bt, t_at))
            ev_mask(pair, e["Xt0"], e["t_Xt0"], 2)
            yield
            pair = npair()
            mm6(pair, lambda hd: fm(kt, hd), lambda hd: fm(at, hd), fmt(t_kt, t_at))
            ev_mask(pair, e["Lak"], e["t_Lak"], 0)
            yield
            pair = npair()
            mm6(pair, lambda hd: fm(bt, hd), lambda hd: fm(rt, hd), fmt(t_bt, t_rt))
            ev_mask(pair, e["Arb"], e["t_Arb"], 1)
            yield
            pair = npair()
            mm6(pair, lambda hd: fm(kt, hd), lambda hd: fm(rt, hd), fmt(t_kt, t_rt))
            ev_mask(pair, e["Ark"], e["t_Ark"], 1)
            yield
            idb = idpl[:].unsqueeze(1).to_broadcast([128, 3, 64])
            k.op("pool", lambda h: h.tensor_tensor(out=e["P0"][:], in0=e["X0"][:], in1=idb, op=ALU.add),
                 r=e["t_X0"] + [t_idpl], w=e["t_P0"])
            k.op("pool", lambda h: h.tensor_tensor(out=e["Pt0"][:], in0=e["Xt0"][:], in1=idb, op=ALU.add),
                 r=e["t_Xt0"] + [t_idpl], w=e["t_Pt0"])
            pair = npair()
            for n_, nm_ in enumerate(["k2", "b2"]):
                for fc in range(3):
                    for par in range(2):
                        hb = par * 64
                        ps, t_ps = pair[par]
                        k.op("pe", lambda h, ps=ps, n_=n_, fc=fc, hb=hb, nm_=nm_: h.matmul(
                            ps[hb:hb + 64, (n_ * 3 + fc) * 64:(n_ * 3 + fc + 1) * 64],
                            lhsT=d[nm_][hb:hb + 64, fc, cl], rhs=self.ident_b[hb:hb + 64, hb:hb + 64],
                            start=True, stop=True, tile_position=(hb, hb)),
                             r=[d["t_" + nm_], self.t_const], w=[t_ps])
            for par in range(2):
                hb = par * 64
                ps, t_ps = pair[par]
                k.op("act", lambda h, ps=ps, hb=hb: h.activation(
                    out=e["kbtok"][hb:hb + 64].rearrange("p a b c -> p (a b c)"),
                    in_=ps[hb:hb + 64, 0:384], func=AF.Copy), r=[t_ps], w=[e["t_kbtok"][par]])
            yield
            cur = 0
            nsteps = 5
            for stp in range(nsteps):
                nxt = 1 - cur
                last = (stp == nsteps - 1)
                Xc, Xn, Xtc, Xtn = e["X%d" % cur], e["X%d" % nxt], e["Xt%d" % cur], e["Xt%d" % nxt]
                t_Xc, t_Xn, t_Xtc, t_Xtn = e["t_X%d" % cur], e["t_X%d" % nxt], e["t_Xt%d" % cur], e["t_Xt%d" % nxt]
                Pc, Pn, Ptc, Ptn = e["P%d" % cur], e["P%d" % nxt], e["Pt%d" % cur], e["Pt%d" % nxt]
                t_Pc, t_Pn, t_Ptc, t_Ptn = e["t_P%d" % cur], e["t_P%d" % nxt], e["t_Pt%d" % cur], e["t_Pt%d" % nxt]
                pair = npair()
                mm6(pair, lambda hd: pl(Xtc, hd), lambda hd: pl(Xc, hd), lambda par: [t_Xtc[par], t_Xc[par]])
                ev_copy(pair, Xn, t_Xn)
                yield
                if not last:
                    pair = npair()
                    mm6(pair, lambda hd: pl(Xc, hd), lambda hd: pl(Xtc, hd), lambda par: [t_Xtc[par], t_Xc[par]])
                    ev_copy(pair, Xtn, t_Xtn)
                    yield
                pair = npair()
                mm6(pair, lambda hd: pl(Ptc, hd), lambda hd: pl(Xn, hd), lambda par: [t_Ptc[par], t_Xn[par]])
                ev_add(pair, Pn, t_Pn, Pc, t_Pc)
                yield
                if not last:
                    pair = npair()
                    mm6(pair, lambda hd: pl(Xn, hd), lambda hd: pl(Ptc, hd), lambda par: [t_Ptc[par], t_Xn[par]])
                    ev_add(pair, Ptn, t_Ptn, Ptc, t_Ptc)
                    yield
                cur = nxt
            assert cur == 1

        def chain(q):
            gb, c = q // NCH, q % NCH
            d = GS[gb % 2]
            e = CS[q % NR]
            cl = slice(c * C, (c + 1) * C)
            Pf, t_Pf = e["P1"], e["t_P1"]
            Vg, t_Vg = d["Vg"], d["t_Vg"]

            def fm(tile, hd):
                hb = (hd % 2) * 64
                return tile[hb:hb + 64, hd // 2, cl]
            tbs = lambda hd: Tb[(hd % 2) * 64:(hd % 2) * 64 + 64, hd // 2, :]
            vgs = lambda hd: Vg[(hd % 2) * 64:(hd % 2) * 64 + 64, c, hd // 2, :]
            pair = npair()
            mm6(pair, lambda hd: fm(d["at"], hd), tbs, lambda par: [d["t_at"], t_Tb[par]],
                lambda hd: pl(e["Lak"], hd), vgs, lambda par: [e["t_Lak"][par], t_Vg[c][par]])
            ev_copy(pair, r0b, t_r0b)
            yield
            pair = npair()
            mm6(pair, lambda hd: pl(Pf, hd), lambda hd: pl(r0b, hd), lambda par: [t_Pf[par], t_r0b[par]])
            ev_copy(pair, Ub, t_Ub)
            yield
            pair = npair()
            po = (c % 2) * 64
            for fc in range(3):
                for par in range(2):
                    hd = 2 * fc + par
                    hb = par * 64
                    ps, t_ps = pair[par]
                    fs = slice(fc * 64, (fc + 1) * 64)
                    k.op("pe", lambda h, ps=ps, hd=hd, hb=hb, fs=fs, fc=fc: h.matmul(
                        ps[po:po + 64, fs], lhsT=fm(d["rt"], hd), rhs=Tb[hb:hb + 64, fc, :], start=True, stop=False,
                        tile_position=(hb, po)), r=[d["t_rt"], t_Tb[par]], w=[t_ps])
                    k.op("pe", lambda h, ps=ps, hd=hd, hb=hb, fs=fs: h.matmul(
                        ps[po:po + 64, fs], lhsT=pl(e["Arb"], hd), rhs=pl(Ub, hd), start=False, stop=False,
                        tile_position=(hb, po)), r=[e["t_Arb"][par], t_Ub[par]], w=[t_ps])
                    k.op("pe", lambda h, ps=ps, hd=hd, hb=hb, fs=fs, fc=fc: h.matmul(
                        ps[po:po + 64, fs], lhsT=pl(e["Ark"], hd), rhs=Vg[hb:hb + 64, c, fc, :], start=False,
                        stop=True, tile_position=(hb, po)), r=[e["t_Ark"][par], t_Vg[c][par]], w=[t_ps])
            pair2 = npair()
            kb = e["kbtok"]
            mm6(pair2, lambda hd: kb[(hd % 2) * 64:(hd % 2) * 64 + 64, 1, hd // 2, :], lambda hd: pl(Ub, hd),
                lambda par: [e["t_kbtok"][par], t_Ub[par]],
                lambda hd: kb[(hd % 2) * 64:(hd % 2) * 64 + 64, 0, hd // 2, :], vgs,
                lambda par: [e["t_kbtok"][par], t_Vg[c][par]])
            for par in range(2):
                hb = par * 64
                ps, t_ps = pair2[par]
                k.op("dve", lambda h, hb=hb: h.tensor_tensor(
                    out=Tst[hb:hb + 64], in0=Tst[hb:hb + 64], in1=bcast_mid(d["WC"][hb:hb + 64, :, c], 64),
                    op=ALU.mult), r=[t_T[par], d["t_WC"]], w=[t_T[par]])
                k.op("dve", lambda h, ps=ps, hb=hb: h.tensor_tensor(
                    out=Tst[hb:hb + 64], in0=v3(ps, hb), in1=Tst[hb:hb + 64], op=ALU.add),
                     r=[t_ps, t_T[par]], w=[t_T[par]])
                k.op("pool", lambda h, hb=hb: h.tensor_copy(out=Tb[hb:hb + 64], in_=Tst[hb:hb + 64]),
                     r=[t_T[par]], w=[t_Tb[par]])
            for par in range(2):
                ps, t_ps = pair[par]
                k.op("act", lambda h, ps=ps, par=par: h.activation(
                    out=d["ysb"][po:po + 64, c // 2, :].rearrange("p (a b d) -> p a b d", a=3, b=2)[:, :, par, :],
                    in_=ps[po:po + 64, 0:192].rearrange("p (a d) -> p a d", a=3), func=AF.Copy),
                     r=[t_ps], w=[d["t_ysb"][c]])
            yield

        def post(gb):
            d = GS[gb % 2]
            gs = slice(gb * GB, (gb + 1) * GB)
            ysb, t_ysb = d["ysb"], d["t_ysb"]
            k.dma("sp", mixT[:, 0:5, :], self.mxa[:, gs].rearrange("(c p) t -> p c t", p=128),
                  r=[self.t_mxa[(gb * GB) // G]], w=t_mixT[0:5])
            yv = ysb[:].rearrange("p c (a b) -> p (c a) b", b=64)
            qv = ysq[:].rearrange("p c (a b) -> p (c a) b", b=64)
            k.op("dve", lambda h: h.tensor_reduce(out=gst[:, 0, :], in_=yv, axis=AX.X, op=ALU.add),
                 r=t_ysb, w=[t_gst])
            k.op("pool", lambda h: h.tensor_tensor(out=ysq[:], in0=ysb[:], in1=ysb[:], op=ALU.mult),
                 r=t_ysb, w=[t_ysq])
            yield
            k.op("dve", lambda h: h.tensor_reduce(out=gst[:, 1, :], in_=qv, axis=AX.X, op=ALU.add),
                 r=[t_ysq], w=[t_gst])
            k.op("dve", lambda h: h.tensor_scalar(out=gst[:, 0:2, :], in0=gst[:, 0:2, :], scalar1=1.0 / 64.0,
                                                  scalar2=None, op0=ALU.mult), r=[t_gst], w=[t_gst])
            k.op("dve", lambda h: h.tensor_tensor(out=gst[:, 2, :], in0=gst[:, 0, :], in1=gst[:, 0, :], op=ALU.mult),
                 r=[t_gst], w=[t_gst])
            k.op("dve", lambda h: h.tensor_tensor(out=gst[:, 3, :], in0=gst[:, 1, :], in1=gst[:, 2, :],
                                                  op=ALU.subtract), r=[t_gst], w=[t_gst])
            yield
            k.op("act", lambda h: h.activation(out=gst[:, 4, :], in_=gst[:, 3, :], func=AF.Sqrt,
                                               bias=pc[:, 6, 0:1]), r=[t_gst, t_pc], w=[t_gst])
            k.op("dve", lambda h: h.reciprocal(out=gst[:, 5, :], in_=gst[:, 4, :]), r=[t_gst], w=[t_gst])
            k.op("dve", lambda h: h.tensor_tensor(out=qv, in0=yv, in1=bcast_mid(gst[:, 0, :], 64), op=ALU.subtract),
                 r=t_ysb + [t_gst, t_ysq], w=[t_ysq])
            yield
            k.op("dve", lambda h: h.tensor_tensor(out=qv, in0=qv, in1=bcast_mid(gst[:, 5, :], 64), op=ALU.mult),
                 r=[t_gst, t_ysq], w=[t_ysq])
            k.op("pool", lambda h: h.tensor_tensor(out=ysq[:], in0=ysq[:],
                                                   in1=lng[:, 0, :].unsqueeze(1).to_broadcast([128, NCH // 2, 384]),
                                                   op=ALU.mult), r=[t_ysq, t_lng], w=[t_ysq])
            k.op("pool", lambda h: h.tensor_tensor(out=ysq[:], in0=ysq[:],
                                                   in1=lng[:, 1, :].unsqueeze(1).to_broadcast([128, NCH // 2, 384]),
                                                   op=ALU.add), r=[t_ysq, t_lng], w=[t_ysq])
            yield
            for fc in range(3):
                ps, t_ps = nps()
                for c2 in range(NCH // 2):
                    k.op("pe", lambda h, ps=ps, fc=fc, c2=c2: h.transpose(
                        out=ps[:, c2 * 128:(c2 + 1) * 128], in_=ysq[:, c2, fc * 128:(fc + 1) * 128],
                        identity=self.ident_f[:]), r=[t_ysq, self.t_const], w=[t_ps])
                i = st["pt"] % 2
                st["pt"] += 1
                k.op("dve", lambda h, ps=ps, fc=fc, i=i: h.tensor_tensor(out=potmp[i][:], in0=ps[:, 0:GB],
                                                                         in1=d["bonus"][:, fc, :], op=ALU.add),
                     r=[t_ps, d["t_bonus"]], w=[t_potmp[i]])
                k.op("pool", lambda h, fc=fc, i=i: h.tensor_tensor(out=mixT[:, 5 + fc, :], in0=potmp[i][:],
                                                                   in1=d["gT"][:, fc, :], op=ALU.mult),
                     r=[t_potmp[i], d["t_gT"]], w=[t_mixT[5 + fc]])
                yield
            if self.mxb is not None:
                k.dma("sp", self.mxb[:, gs].rearrange("(c p) t -> p c t", p=128), mixT[:, 5:8, :],
                      r=t_mixT[5:8], w=[self.t_mxb])
            for blk in range(GB // 128):
                r0 = gb * GB + blk * 128
                tg = r0 // G
                i = st["x"] % 2
                st["x"] += 1
                k.dma("sp", xr[i][:], src[r0:r0 + 128, :], r=[t_src[tg]], w=[t_xr[i]])
                pairw = npair()
                for half in range(2):
                    ps, t_ps = pairw[half]
                    for kc in range(8):
                        k.op("pe", lambda h, ps=ps, kc=kc, blk=blk, half=half: h.matmul(
                            ps[:], lhsT=mixT[:, kc, blk * 128:(blk + 1) * 128],
                            rhs=wout[:, kc, half * 512:(half + 1) * 512], start=(kc == 0), stop=(kc == 7)),
                             r=[t_mixT[kc], t_wout[kc]], w=[t_ps])
                    k.op("dve", lambda h, ps=ps, half=half, i=i: h.tensor_tensor(
                        out=xr[i][:, half * 512:(half + 1) * 512], in0=ps[:],
                        in1=xr[i][:, half * 512:(half + 1) * 512], op=ALU.add), r=[t_ps, t_xr[i]], w=[t_xr[i]])
                k.dma("sp", dst[r0:r0 + 128, :], xr[i][:], r=[t_xr[i]], w=[t_dst[tg]])
                yield

        ngb = self.dbg.get("mb_ng", NGB)
        nq = ngb * NCH
        for _ in pre(0):
            pass
        pre_done = 1
        post_done = 0
        chainq = 0
        nextA = 0
        actA = []
        doneA = set()
        g_chain = None
        g_pre = None
        g_post = None
        post_ready = []
        while chainq < nq or g_post is not None or post_ready:
            while len(actA) < 2 and nextA < nq and (nextA // NCH) < pre_done and nextA < chainq + NR:
                actA.append([nextA, chunkA(nextA)])
                nextA += 1
            if g_chain is None and chainq < nq and chainq in doneA:
                g_chain = chain(chainq)
            if g_pre is None and pre_done < ngb and pre_done - 2 < post_done:
                g_pre = pre(pre_done)
            if g_post is None and post_ready:
                g_post = post(post_ready.pop(0))
            progressed = False
            if g_chain is not None:
                progressed = True
                try:
                    next(g_chain)
                except StopIteration:
                    g_chain = None
                    if (chainq + 1) % NCH == 0:
                        post_ready.append(chainq // NCH)
                    chainq += 1
            for it in list(actA):
                progressed = True
                try:
                    next(it[1])
                except StopIteration:
                    doneA.add(it[0])
                    actA.remove(it)
            if g_pre is not None:
                progressed = True
                try:
                    next(g_pre)
                except StopIteration:
                    g_pre = None
                    pre_done += 1
            if g_post is not None:
                progressed = True
                try:
                    next(g_post)
                except StopIteration:
                    g_post = None
                    post_done += 1
            assert progressed, "scheduler stalled"
        k.pop_scope()

    def build(self):
        for ph in self.phases:
            kind = ph[0]
            srcs = {"x": (self.inp["x"], self.t_x), "xa": (self.xa, self.t_xa), "xb": (self.xb, self.t_xb)}
            if kind == "MA":
                _, l, src = ph
                self.pass_mixa(l, srcs[src][0], srcs[src][1])
            dsts = {"xa": (self.xa, self.t_xa), "xb": (self.xb, self.t_xb), "y": (self.y, self.t_y)}
            if kind == "MB":
                _, l, src, dst = ph
                self.pass_mixb(l, srcs[src][0], srcs[src][1], dsts[dst][0], dsts[dst][1])
            if kind == "F":
                _, l, src, dst, final = ph
                srcs = {"x": (self.inp["x"], self.t_x), "xa": (self.xa, self.t_xa), "xb": (self.xb, self.t_xb)}
                dsts = {"xa": (self.xa, self.t_xa), "xb": (self.xb, self.t_xb), "y": (self.y, self.t_y)}
                self.pass_ffn(l, srcs[src][0], srcs[src][1], dsts[dst][0], dsts[dst][1], final)
        self.k.finish()


FULL_PHASES = [("MA", 0, "x"), ("MB", 0, "x", "xa"), ("F", 0, "xa", "xb", False),
               ("MA", 1, "xb"), ("MB", 1, "xb", "xa"), ("F", 1, "xa", "y", True)]


def build_nc(phases=None, dbg=None):
    dbg = dbg or {}
    nc = bass.Bass("TRN2", target_bir_lowering=False)
    p = Prog(nc, phases or FULL_PHASES, dbg)
    p.build()
    return nc, p


def kernel(**inputs):
    nc, _ = build_nc()
    in_maps = []
    for b in range(8):
        m = {}
        for name in INPUT_SHAPES:
            a = np.asarray(inputs[name], dtype=np.float32)
            m[name] = np.ascontiguousarray(a[b]) if name == 'x' else np.ascontiguousarray(a)
        in_maps.append(m)
    res = run_bass_kernel_spmd(nc, in_maps, core_ids=list(range(8)))
    return np.stack([np.asarray(r["y"]) for r in res.results], axis=0).astype(np.float32)
```

```python
import math
from contextlib import ExitStack
import numpy as np
import concourse.bass as bass
import concourse.mybir as mybir
from concourse.bass_utils import run_bass_kernel_spmd

F32 = mybir.dt.float32
BF16 = mybir.dt.bfloat16
ALU = mybir.AluOpType
AF = mybir.ActivationFunctionType
AX = mybir.AxisListType

S = 4096
D = 1024
DFF = 4096
NIN = 3328
G = 512
NG = S // G
NB = G // 128
DEPTH = 2
HD = 64
NORM_EPS = 1e-6
GN_EPS = 64e-5

INPUT_SHAPES = {
    'x': [S, D], 'norm_mix_g': [2, D], 'w_in': [2, D, NIN], 'lam_q1': [2, 32], 'lam_k1': [2, 32],
    'lam_q2': [2, 32], 'lam_k2': [2, 32], 'subln_g': [2, 64], 'conv_w': [2, 3, 256],
    'shift_mu': [2, 1408], 'rwkv_w0': [2, 384], 'rwkv_w_up': [2, 64, 384], 'rwkv_a0': [2, 384],
    'rwkv_a_up': [2, 64, 384], 'rwkv_g_up': [2, 128, 384], 'rwkv_k_k': [2, 384], 'rwkv_k_a': [2, 384],
    'rwkv_r_k': [2, 6, 64], 'lnx_g': [2, 384], 'lnx_b': [2, 384], 'w_out': [2, D, D],
    'norm_mlp_g': [2, D], 'w_mlp_up': [2, D, DFF], 'w_mlp_down': [2, DFF, D], 'final_norm_g': [D],
}


class Tk:
    __slots__ = ("w", "r", "name")

    def __init__(self, name=""):
        self.w = None
        self.r = {}
        self.name = name


class KB:
    EPOCH = 6000
    NDS = 32

    def __init__(self, nc):
        self.nc = nc
        self.eng = {"pe": nc.tensor, "act": nc.scalar, "dve": nc.vector, "pool": nc.gpsimd, "sp": nc.sync}
        self.cnt = {e: 0 for e in self.eng}
        self.semh = {}
        self.seen = {e: {} for e in self.eng}
        self.maxep = {e: {} for e in self.eng}
        self.dsem = [("dma", i) for i in range(self.NDS)]
        for kx in self.dsem:
            self.semh[kx] = nc.alloc_semaphore(f"dma{kx[1]}")
        self.dval = [0] * self.NDS
        self.dnext = 0
        self.dnext_sw = 0
        self.ndma = 0
        self.nwait = 0
        self._n = 0
        self.root = ExitStack()
        self.scope = None

    def sb(self, name, shape, dt):
        self._n += 1
        cm = self.nc.sbuf_tensor("%s_%d" % (name, self._n), list(shape), dt)
        return (self.scope or self.root).enter_context(cm)

    def ps(self, name, shape, dt):
        self._n += 1
        cm = self.nc.psum_tensor("%s_%d" % (name, self._n), list(shape), dt)
        return (self.scope or self.root).enter_context(cm)

    def push_scope(self):
        self.scope = ExitStack()

    def pop_scope(self):
        self.barrier()
        self.scope.close()
        self.scope = None

    def _cursem(self, e):
        key = (e, self.cnt[e] // self.EPOCH)
        if key not in self.semh:
            self.semh[key] = self.nc.alloc_semaphore(f"s_{e}_{key[1]}")
        return key

    def _wait(self, e, tok):
        key, val = tok[0], tok[1]
        seen = self.seen[e]
        if seen.get(key, 0) >= val:
            return
        if key[0] != "dma":
            if self.maxep[e].get(key[0], -1) > key[1]:
                return
            self.maxep[e][key[0]] = max(self.maxep[e].get(key[0], -1), key[1])
        self.eng[e].wait_ge(self.semh[key], val)
        self.nwait += 1
        seen[key] = val

    def _deps(self, e, reads, writes, is_dma):
        for t in reads:
            if t.w is not None:
                tok = t.w
                if (not is_dma) and tok[2] == e and e == "pe":
                    continue
                self._wait(e, tok)
        for t in writes:
            if t.w is not None:
                tok = t.w
                if is_dma or tok[2] != e or tok[3] or e != "pe":
                    self._wait(e, tok)
            for rk, tok in t.r.items():
                if isinstance(tok, list):
                    for tk in tok:
                        self._wait(e, tk)
                elif is_dma or tok[2] != e or e != "pe":
                    self._wait(e, tok)

    def _record(self, tok, reads, writes):
        for t in reads:
            if tok[3]:
                t.r.setdefault("dma", []).append(tok)
            else:
                t.r[tok[2]] = tok
        for t in writes:
            t.w = tok
            t.r = {}

    def op(self, e, fn, r=(), w=()):
        self._deps(e, r, w, False)
        key = self._cursem(e)
        ins = fn(self.eng[e])
        self.cnt[e] += 1
        val = self.cnt[e] - key[1] * self.EPOCH
        ins.then_inc(self.semh[key], 1)
        tok = (key, val, e, False)
        self._record(tok, r, w)
        return tok

    def dma(self, q, out, in_, r=(), w=(), **kw):
        self._deps(q, r, w, True)
        if q == "pool":
            slot = self.NDS - 8 + self.dnext_sw
            self.dnext_sw = (self.dnext_sw + 1) % 8
        else:
            slot = self.dnext
            self.dnext = (self.dnext + 1) % (self.NDS - 8)
        key = self.dsem[slot]
        if self.dval[slot] > 0:
            self._wait(q, (key, self.dval[slot], "dma", True))
        ins = self.eng[q].dma_start(out=out, in_=in_, **kw)
        self.dval[slot] += 16
        ins.then_inc(self.semh[key], 16)
        tok = (key, self.dval[slot], "dma", True)
        self._record(tok, r, w)
        self.ndma += 1
        return tok

    def barrier(self):
        lasts = {}
        for e in ("pe", "act", "dve", "pool"):
            if self.cnt[e] > 0:
                key = (e, (self.cnt[e] - 1) // self.EPOCH)
                lasts[e] = (key, self.cnt[e] - key[1] * self.EPOCH, e, False)
        for e in self.eng:
            for e2, tok in lasts.items():
                if e2 != e:
                    self._wait(e, tok)
            for slot in range(self.NDS):
                if self.dval[slot] > 0:
                    self._wait(e, (self.dsem[slot], self.dval[slot], "dma", True))

    def finish(self):
        for slot in range(self.NDS):
            if self.dval[slot] > 0:
                self._wait("sp", (self.dsem[slot], self.dval[slot], "dma", True))


def bcast_mid(ap2d, n):
    p, j = ap2d.shape
    return ap2d.unsqueeze(2).to_broadcast([p, j, n])


class Prog:
    def __init__(self, nc, phases, dbg=None):
        self.nc = nc
        self.k = KB(nc)
        self.phases = phases
        self.dbg = dbg if dbg is not None else {}
        k = self.k
        self.inp = {}
        for name, shp in INPUT_SHAPES.items():
            self.inp[name] = nc.dram_tensor(name, shp, F32, kind="ExternalInput").ap()
        self.y = nc.dram_tensor("y", [S, D], F32, kind="ExternalOutput").ap()
        self.xa = nc.dram_tensor("xa", [S, D], F32).ap()
        self.xb = nc.dram_tensor("xb", [S, D], F32).ap()
        def dram(name, shape, dt):
            kind = "ExternalOutput" if name in self.dbg else ("ExternalInput" if ("in:" + name) in self.dbg else "Internal")
            return nc.dram_tensor(name, shape, dt, kind=kind).ap()
        self.prw = dram("prw", [1408, S], F32)
        self.t_prw = [Tk() for _ in range(NG)]
        self.mxa = dram("mxa", [5 * 128, S], BF16)
        self.mxb = dram("mxb", [3 * 128, S], BF16) if "mxb" in self.dbg else None
        self.t_mxb = Tk()
        self.t_mxa = [Tk() for _ in range(NG)]
        self.t_xa = [Tk("xa%d" % i) for i in range(NG)]
        self.t_xb = [Tk("xb%d" % i) for i in range(NG)]
        self.t_x = [Tk("x%d" % i) for i in range(NG)]
        self.t_y = [Tk("y%d" % i) for i in range(NG)]
        self.ident_b = k.sb("ident_b", [128, 128], BF16)
        self.ident_f = k.sb("ident_f", [128, 128], F32)
        self.t_const = Tk("const")
        self.eps_t = k.sb("eps_t", [128, 1], F32)
        k.op("pool", lambda h: h.memset(self.ident_b[:], 1.0), w=[self.t_const])
        k.op("pool", lambda h: h.affine_select(out=self.ident_b[:], in_=self.ident_b[:], pattern=[[1, 128]],
                                               compare_op=ALU.is_equal, fill=0.0, base=0, channel_multiplier=-1),
             r=[self.t_const], w=[self.t_const])
        k.op("pool", lambda h: h.memset(self.ident_f[:], 1.0), w=[self.t_const])
        k.op("pool", lambda h: h.affine_select(out=self.ident_f[:], in_=self.ident_f[:], pattern=[[1, 128]],
                                               compare_op=ALU.is_equal, fill=0.0, base=0, channel_multiplier=-1),
             r=[self.t_const], w=[self.t_const])
        k.op("pool", lambda h: h.memset(self.eps_t[:], NORM_EPS), w=[self.t_const])
        self.gcol = k.sb("gcol", [128, 4, 8], F32)
        self.t_gcol = Tk("gcol")
        for n, (nm, l) in enumerate([("norm_mix_g", 0), ("norm_mix_g", 1), ("norm_mlp_g", 0), ("norm_mlp_g", 1)]):
            k.dma("sp", self.gcol[:, n, :], self.inp[nm][l].rearrange("(j p) -> p j", p=128),
                  w=[self.t_gcol], allow_slow_non_contiguous=True)
        self.nblk = 0
        self.dumped = set()


    def alloc_norm(self):
        k = self.k
        self.xt = [k.sb("xt%d" % i, [128, D], F32) for i in range(2)]
        self.t_xt = [Tk("xt%d" % i) for i in range(2)]
        self.xn = [k.sb("xn%d" % i, [128, D], BF16) for i in range(2)]
        self.t_xn = [Tk("xn%d" % i) for i in range(2)]
        self.junk = k.sb("junk", [128, D], BF16)
        self.t_junk = Tk("junk")
        self.stat = [k.sb("stat%d" % i, [128, 4], F32) for i in range(2)]
        self.t_stat = [Tk("stat%d" % i) for i in range(2)]
        self.hT = k.sb("hT", [128, 8, G], BF16)
        self.t_hT = [Tk("hT%d" % i) for i in range(NB)]
        self.psT = [k.ps("psT%d" % i, [128, 8, 128], BF16) for i in range(1)]
        self.t_psT = [Tk("psT%d" % i) for i in range(1)]

    def dump(self, name, ap, toks):
        if "dump" not in self.dbg or name in self.dumped:
            return
        self.dumped.add(name)
        d = self.nc.dram_tensor("d_" + name, list(ap.shape), ap.dtype, kind="ExternalOutput").ap()
        self.k.dma("sp", d, ap, r=list(toks), w=[Tk()])

    def norm_a(self, src_ap, src_tk):
        k = self.k
        i = self.nblk % 2
        self.nblk += 1
        xt, t_xt, xn, t_xn, st, t_st = self.xt[i], self.t_xt[i], self.xn[i], self.t_xn[i], self.stat[i], self.t_stat[i]
        k.dma("sp", xt[:], src_ap, r=[src_tk], w=[t_xt])
        k.op("act", lambda h: h.activation(out=self.junk[:], in_=xt[:], func=AF.Square, accum_out=st[:, 0:1]),
             r=[t_xt], w=[self.t_junk, t_st])
        k.op("act", lambda h: h.activation(out=st[:, 1:2], in_=st[:, 0:1], func=AF.Sqrt, scale=1.0 / D,
                                           bias=self.eps_t[:, 0:1]),
             r=[t_st, self.t_const], w=[t_st])
        k.op("dve", lambda h: h.reciprocal(out=st[:, 2:3], in_=st[:, 1:2]), r=[t_st], w=[t_st])
        k.op("dve", lambda h: h.tensor_scalar(out=xn[:], in0=xt[:], scalar1=st[:, 2:3], scalar2=None, op0=ALU.mult),
             r=[t_xt, t_st], w=[t_xn])
        return xn, t_xn

    def norm_b(self, xn, t_xn, gidx, blk, hT, t_hT):
        k = self.k
        pT, t_pT = self.psT[0], self.t_psT[0]
        for j in range(8):
            k.op("pe", lambda h, j=j: h.transpose(out=pT[:, j, :], in_=xn[:, j * 128:(j + 1) * 128],
                                                   identity=self.ident_b[:]),
                 r=[t_xn, self.t_const], w=[t_pT])
        k.op("dve", lambda h: h.tensor_tensor(out=hT[:, :, blk * 128:(blk + 1) * 128], in0=pT[:],
                                              in1=bcast_mid(self.gcol[:, gidx, :], 128), op=ALU.mult),
             r=[t_pT, self.t_gcol], w=[t_hT[blk]])

    def norm_block(self, src_ap, src_tk, gidx, blk, hT=None, t_hT=None):
        hT = self.hT if hT is None else hT
        t_hT = self.t_hT if t_hT is None else t_hT
        xn, t_xn = self.norm_a(src_ap, src_tk)
        self.norm_b(xn, t_xn, gidx, blk, hT, t_hT)

    def pass_ffn(self, l, src, t_src, dst, t_dst, final):
        k = self.k
        k.push_scope()
        self.alloc_norm()
        if final:
            self.gfin = k.sb("gfin", [128, D], F32)
            self.t_gfin = Tk("gfin")
            k.dma("sp", self.gfin[:], self.inp["final_norm_g"].unsqueeze(0).partition_broadcast(128),
                  w=[self.t_gfin])
        wup = k.sb("wup", [128, 8, DFF], BF16)
        wdn = k.sb("wdn", [128, 32, D], BF16)
        t_wup = [Tk() for _ in range(8)]
        t_wdn = [Tk() for _ in range(8)]
        for kc in range(8):
            k.dma("pool", wup[:, kc, :], self.inp["w_mlp_up"][l, kc * 128:(kc + 1) * 128, :], w=[t_wup[kc]])
        for c4 in range(8):
            k.dma("pool", wdn[:, c4 * 4:(c4 + 1) * 4, :],
                  self.inp["w_mlp_down"][l, c4 * 512:(c4 + 1) * 512, :].rearrange("(c p) n -> p c n", p=128),
                  w=[t_wdn[c4]])
        aT = k.sb("aT", [128, 32, G], BF16)
        t_aT = [Tk() for _ in range(32)]
        rt = [k.sb("rt%d" % i, [128, G], F32) for i in range(2)]
        t_rt = [Tk() for _ in range(2)]
        psU = [k.ps("psU%d" % i, [128, G], F32) for i in range(2)]
        t_psU = [Tk() for _ in range(2)]
        psD = [k.ps("psD%d" % i, [128, 512], F32) for i in range(2)]
        t_psD = [Tk() for _ in range(2)]
        xr = [k.sb("xr%d" % i, [128, D], F32) for i in range(2)]
        t_xr = [Tk() for _ in range(2)]
        xo, t_xo = xr, t_xr
        st2 = [k.sb("st2_%d" % i, [128, 4], F32) for i in range(2)]
        t_st2 = [Tk() for _ in range(2)]
        n_o = 0
        hTs = [self.hT, k.sb("hTf2", [128, 8, G], BF16)]
        t_hTs = [self.t_hT, [Tk() for _ in range(NB)]]
        for blk in range(NB):
            self.norm_block(src[blk * 128:(blk + 1) * 128, :], t_src[0], 2 + l, blk, hTs[0], t_hTs[0])
        for g in range(NG):
            hT, t_hT = hTs[g % 2], t_hTs[g % 2]
            for c in range(32):
                ps, t_ps = psU[c % 2], t_psU[c % 2]
                for kc in range(8):
                    k.op("pe", lambda h, kc=kc, c=c, ps=ps: h.matmul(ps[:], lhsT=wup[:, kc, c * 128:(c + 1) * 128],
                                                                    rhs=hT[:, kc, :], start=(kc == 0),
                                                                    stop=(kc == 7)),
                         r=[t_wup[kc]] + t_hT, w=[t_ps])
                r_, t_r = rt[c % 2], t_rt[c % 2]
                k.op("act", lambda h, ps=ps, r_=r_: h.activation(out=r_[:], in_=ps[:], func=AF.Relu),
                     r=[t_ps], w=[t_r])
                k.op("pool", lambda h, r_=r_, c=c: h.tensor_tensor(out=aT[:, c, :], in0=r_[:], in1=r_[:],
                                                                   op=ALU.mult),
                     r=[t_r], w=[t_aT[c]])
            for blk in range(NB):
                r0 = g * G + blk * 128
                i = n_o % 2
                n_o += 1
                if g + 1 < NG:
                    r1 = (g + 1) * G + blk * 128
                    nxn = self.norm_a(src[r1:r1 + 128, :], t_src[g + 1])
                k.dma("sp", xr[i][:], src[r0:r0 + 128, :], r=[t_src[g]], w=[t_xr[i]])
                for half in range(2):
                    ps, t_ps = psD[half], t_psD[half]
                    for c in range(32):
                        k.op("pe", lambda h, c=c, ps=ps, half=half, blk=blk: h.matmul(
                            ps[:], lhsT=aT[:, c, blk * 128:(blk + 1) * 128],
                            rhs=wdn[:, c, half * 512:(half + 1) * 512], start=(c == 0), stop=(c == 31)),
                             r=[t_aT[c], t_wdn[c // 4]], w=[t_ps])
                    k.op("dve", lambda h, ps=ps, half=half, i=i: h.tensor_tensor(
                        out=xo[i][:, half * 512:(half + 1) * 512], in0=ps[:],
                        in1=xr[i][:, half * 512:(half + 1) * 512], op=ALU.add),
                         r=[t_ps, t_xr[i]], w=[t_xr[i]])
                if final:
                    st, t_st = st2[i], t_st2[i]
                    k.op("act", lambda h, i=i, st=st: h.activation(out=self.junk[:], in_=xo[i][:], func=AF.Square,
                                                                   accum_out=st[:, 0:1]),
                         r=[t_xo[i]], w=[self.t_junk, t_st])
                    k.op("act", lambda h, st=st: h.activation(out=st[:, 1:2], in_=st[:, 0:1], func=AF.Sqrt,
                                                              scale=1.0 / D, bias=self.eps_t[:, 0:1]),
                         r=[t_st, self.t_const], w=[t_st])
                    k.op("dve", lambda h, st=st: h.reciprocal(out=st[:, 2:3], in_=st[:, 1:2]), r=[t_st], w=[t_st])
                    k.op("dve", lambda h, i=i, st=st: h.scalar_tensor_tensor(
                        out=xo[i][:], in0=xo[i][:], scalar=st[:, 2:3], in1=self.gfin[:], op0=ALU.mult, op1=ALU.mult),
                         r=[t_xo[i], t_st, self.t_gfin], w=[t_xo[i]])
                k.dma("sp", dst[r0:r0 + 128, :], xo[i][:], r=[t_xo[i]], w=[t_dst[g]])
                if g + 1 < NG:
                    self.norm_b(nxn[0], nxn[1], 2 + l, blk, hTs[(g + 1) % 2], t_hTs[(g + 1) % 2])
        k.pop_scope()


    def pass_mixa(self, l, src, t_src):
        k = self.k
        k.push_scope()
        self.alloc_norm()
        inp = self.inp
        lambda_init = 0.8 - 0.6 * math.exp(-0.3 * l)
        win = k.sb("win", [128, 8, NIN], BF16)
        t_win = [Tk() for _ in range(8)]
        for kc in range(8):
            k.dma("pool", win[:, kc, :], inp["w_in"][l, kc * 128:(kc + 1) * 128, :], w=[t_win[kc]])
        cst = k.sb("cst", [128, 8], F32)
        t_cst = Tk()
        lq = k.sb("lq", [128, 4, 32], F32)
        t_lq = Tk()
        for i, nm in enumerate(["lam_q1", "lam_k1", "lam_q2", "lam_k2"]):
            k.dma("sp", lq[:, i, :], inp[nm][l:l + 1, :].partition_broadcast(128), w=[t_lq])
        k.op("dve", lambda h: h.tensor_tensor(out=lq[:, 0, :], in0=lq[:, 0, :], in1=lq[:, 1, :], op=ALU.mult),
             r=[t_lq], w=[t_lq])
        k.op("dve", lambda h: h.tensor_tensor(out=lq[:, 2, :], in0=lq[:, 2, :], in1=lq[:, 3, :], op=ALU.mult),
             r=[t_lq], w=[t_lq])
        k.op("dve", lambda h: h.tensor_reduce(out=cst[:, 0:1], in_=lq[:, 0, :], axis=AX.X, op=ALU.add),
             r=[t_lq], w=[t_cst])
        k.op("dve", lambda h: h.tensor_reduce(out=cst[:, 1:2], in_=lq[:, 2, :], axis=AX.X, op=ALU.add),
             r=[t_lq], w=[t_cst])
        k.op("act", lambda h: h.activation(out=cst[:, 2:4], in_=cst[:, 0:2], func=AF.Exp), r=[t_cst], w=[t_cst])
        k.op("dve", lambda h: h.tensor_tensor(out=cst[:, 4:5], in0=cst[:, 2:3], in1=cst[:, 3:4], op=ALU.subtract),
             r=[t_cst], w=[t_cst])
        k.op("dve", lambda h: h.tensor_scalar(out=cst[:, 5:6], in0=cst[:, 4:5], scalar1=float(lambda_init),
                                              scalar2=None, op0=ALU.add), r=[t_cst], w=[t_cst])
        k.op("pool", lambda h: h.memset(cst[:, 6:7], NORM_EPS), w=[t_cst])
        subg = k.sb("subg", [128, 64], F32)
        t_subg = Tk()
        k.dma("sp", subg[:], inp["subln_g"][l:l + 1, :].partition_broadcast(128), w=[t_subg])
        k.op("dve", lambda h: h.tensor_scalar(out=subg[:], in0=subg[:], scalar1=float(1.0 - lambda_init),
                                              scalar2=None, op0=ALU.mult), r=[t_subg], w=[t_subg])
        cw = k.sb("cw", [128, 2, 3], F32)
        t_cw = Tk()
        for j in range(2):
            k.dma("sp", cw[:, j, :], inp["conv_w"][l, :, j * 128:(j + 1) * 128].rearrange("t p -> p t"),
                  w=[t_cw], allow_slow_non_contiguous=True)
        mask = k.sb("mask", [128, 128], BF16)
        t_mask = Tk()
        k.op("pool", lambda h: h.memset(mask[:], 1.0), w=[t_mask])
        k.op("pool", lambda h: h.affine_select(out=mask[:], in_=mask[:], pattern=[[1, 128]], compare_op=ALU.is_ge,
                                               fill=0.0, base=0, channel_multiplier=-1), r=[t_mask], w=[t_mask])
        kT = k.sb("kT", [128, 3, S], BF16)
        t_kT = [[Tk() for _ in range(NG)] for _ in range(3)]
        Vt = k.sb("Vt", [128, S // 128, 6, 65], BF16)
        t_V = [Tk() for _ in range(S // 128)]
        for b8 in range(0, S // 128, 8):
            k.op("pool", lambda h, b8=b8: h.memset(Vt[:, b8:b8 + 8], 1.0), w=t_V[b8:b8 + 8])
        qT = k.sb("qT", [128, 3, G], BF16)
        t_qT = [Tk() for _ in range(3)]
        cv = k.sb("cv", [128, 6, G], F32)
        t_cv = [Tk() for _ in range(6)]
        zb = k.sb("zb", [128, 2, G + 2], F32)
        t_zb = [Tk() for _ in range(2)]
        k.op("pool", lambda h: h.memset(zb[:], 0.0), w=t_zb)
        ycv = k.sb("ycv", [128, G], F32)
        t_ycv = Tk()
        stg = [k.sb("stg%d" % i, [128, G], F32) for i in range(2)]
        t_stg = [Tk() for _ in range(2)]
        eT = [k.sb("eT%d" % i, [128, G], BF16) for i in range(4)]
        t_eT = [Tk() for _ in range(4)]
        uT = k.sb("uT", [128, 2, G], F32)
        t_uT = [Tk() for _ in range(2)]
        oat = k.sb("oat", [128, NB, 6, 64], F32)
        t_oat = Tk()
        osq = k.sb("osq", [128, NB, 6, 64], F32)
        t_osq = Tk()
        t1 = k.sb("t1", [128, NB, 64], F32)
        t_t1 = Tk()
        rl = k.sb("rl", [128, 2, NB], F32)
        t_rl = Tk()
        ss = k.sb("ss", [128, 3, NB * 6], F32)
        t_ss = Tk()
        oab = k.sb("oab", [128, NB, 384], BF16)
        t_oab = Tk()
        mixA = k.sb("mixA", [128, 5, G], BF16)
        t_mixA = [Tk() for _ in range(5)]
        NPA = 4
        psA = [k.ps("psA%d" % i, [128, G], F32) for i in range(NPA)]
        t_psA = [Tk() for _ in range(NPA)]
        psO = [k.ps("psO%d" % i, [128, G], F32) for i in range(2)]
        t_psO = [Tk() for _ in range(2)]
        psTr1 = k.ps("psTr", [128, NB, 65], F32)
        t_psTr1 = Tk()
        hTs = [self.hT, k.sb("hT2", [128, 8, G], BF16)]
        t_hTs = [self.t_hT, [Tk() for _ in range(NB)]]
        qTs = [qT, k.sb("qT2", [128, 3, G], BF16)]
        t_qTs = [t_qT, [Tk() for _ in range(3)]]
        cvs = [cv, k.sb("cv2", [128, 6, G], F32)]
        t_cvs = [t_cv, [Tk() for _ in range(6)]]
        cnt = {"A": 0, "S": 0, "E": 0}
        sc = 1.0 / math.sqrt(32.0)

        def nbank():
            i = cnt["A"] % NPA
            cnt["A"] += 1
            return psA[i], t_psA[i]

        def front(g):
            hT, t_hT = hTs[g % 2], t_hTs[g % 2]
            qT_, t_qT_ = qTs[g % 2], t_qTs[g % 2]
            cv_, t_cv_ = cvs[g % 2], t_cvs[g % 2]
            for blk in range(NB):
                r0 = g * G + blk * 128
                self.norm_block(src[r0:r0 + 128, :], t_src[g], l, blk, hT, t_hT)
                yield
            for c in list(range(0, 6)) + list(range(9, 26)):
                ps, t_ps = nbank()
                for kc in range(8):
                    k.op("pe", lambda h, kc=kc, c=c, ps=ps: h.matmul(ps[:], lhsT=win[:, kc, c * 128:(c + 1) * 128],
                                                                    rhs=hT[:, kc, :], start=(kc == 0),
                                                                    stop=(kc == 7)),
                         r=[t_win[kc]] + t_hT, w=[t_ps])
                if c < 3:
                    k.op("dve", lambda h, ps=ps, c=c: h.tensor_copy(out=qT_[:, c, :], in_=ps[:]),
                         r=[t_ps], w=[t_qT_[c]])
                elif c < 6:
                    k.op("dve", lambda h, ps=ps, c=c: h.tensor_copy(out=kT[:, c - 3, g * G:(g + 1) * G], in_=ps[:]),
                         r=[t_ps], w=[t_kT[c - 3][g]])
                elif c < 15:
                    k.op("dve", lambda h, ps=ps, c=c: h.tensor_copy(out=cv_[:, c - 9, :], in_=ps[:]),
                         r=[t_ps], w=[t_cv_[c - 9]])
                else:
                    i = cnt["S"] % 2
                    cnt["S"] += 1
                    k.op("dve", lambda h, ps=ps, i=i: h.tensor_copy(out=stg[i][:], in_=ps[:]), r=[t_ps], w=[t_stg[i]])
                    k.dma("sp", self.prw[(c - 15) * 128:(c - 14) * 128, g * G:(g + 1) * G], stg[i][:],
                          r=[t_stg[i]], w=[self.t_prw[g]])
                yield
            for blk in range(NB):
                ps, t_ps = nbank()
                for kc in range(8):
                    k.op("pe", lambda h, kc=kc, ps=ps, blk=blk: h.matmul(
                        ps[:, 0:384], lhsT=hT[:, kc, blk * 128:(blk + 1) * 128], rhs=win[:, kc, 768:1152],
                        start=(kc == 0), stop=(kc == 7)), r=[t_win[kc], t_hT[blk]], w=[t_ps])
                k.op("dve", lambda h, ps=ps, blk=blk: h.tensor_copy(
                    out=Vt[:, g * NB + blk, :, 0:64], in_=ps[:, 0:384].rearrange("p (a b) -> p a b", a=6)),
                     r=[t_ps], w=[t_V[g * NB + blk]])
                yield
            for j in range(2):
                k.op("pool", lambda h, j=j: h.tensor_tensor(out=zb[:, j, 2:G + 2], in0=cv_[:, 2 + j, :],
                                                            in1=cv_[:, 4 + j, :], op=ALU.mult),
                     r=[t_cv_[2 + j], t_cv_[4 + j]], w=[t_zb[j]])
                k.op("pool", lambda h, j=j: h.tensor_scalar(out=ycv[:], in0=zb[:, j, 0:G], scalar1=cw[:, j, 0:1],
                                                            scalar2=None, op0=ALU.mult),
                     r=[t_zb[j], t_cw], w=[t_ycv])
                k.op("dve", lambda h, j=j: h.scalar_tensor_tensor(out=ycv[:], in0=zb[:, j, 1:G + 1],
                                                                  scalar=cw[:, j, 1:2], in1=ycv[:],
                                                                  op0=ALU.mult, op1=ALU.add),
                     r=[t_zb[j], t_cw, t_ycv], w=[t_ycv])
                k.op("dve", lambda h, j=j: h.scalar_tensor_tensor(out=ycv[:], in0=zb[:, j, 2:G + 2],
                                                                  scalar=cw[:, j, 2:3], in1=ycv[:],
                                                                  op0=ALU.mult, op1=ALU.add),
                     r=[t_zb[j], t_cw, t_ycv], w=[t_ycv])
                k.op("pool", lambda h, j=j: h.tensor_tensor(out=mixAs[g % 2][:, 3 + j, :], in0=ycv[:],
                                                            in1=cv_[:, j, :], op=ALU.mult),
                     r=[t_ycv, t_cv_[j]], w=[t_mixAs[g % 2][3 + j]])
                k.op("pool", lambda h, j=j: h.tensor_copy(out=zb[:, j, 0:2], in_=zb[:, j, G:G + 2]),
                     r=[t_zb[j]], w=[t_zb[j]])
                yield

        def back(g):
            qT_, t_qT_ = qTs[g % 2], t_qTs[g % 2]
            mixA_, t_mixA_ = mixAs[g % 2], t_mixAs[g % 2]
            nkb = 4 * g + 4
            LOOK = 1

            def emit_S(hd, half, j):
                qc = hd // 2
                pb = (hd % 2) * 64 + half * 32
                off = max(0, j - 4 * g) * 128
                ps, t_ps = nbank()
                k.op("pe", lambda h: h.matmul(
                    ps[:, off:G], lhsT=kT[pb:pb + 32, qc, j * 128:(j + 1) * 128],
                    rhs=qT_[pb:pb + 32, qc, off:G], start=True, stop=True, tile_position=(pb, 0)),
                     r=[t_kT[qc][j // NB], t_qT_[qc]], w=[t_ps])
                e, t_e = eT[cnt["E"] % 4], t_eT[cnt["E"] % 4]
                cnt["E"] += 1
                k.op("act", lambda h: h.activation(out=e[:, off:G], in_=ps[:, off:G], func=AF.Exp, scale=sc),
                     r=[t_ps], w=[t_e])
                if j >= 4 * g:
                    k.op("pool", lambda h: h.tensor_tensor(out=e[:, off:off + 128], in0=e[:, off:off + 128],
                                                           in1=mask[:], op=ALU.mult), r=[t_e, t_mask], w=[t_e])
                return (hd, half, j, off, e, t_e)

            def emit_PV(item):
                hd, half, j, off, e, t_e = item
                k.op("pe", lambda h: h.matmul(psO[half][0:65, off:G], lhsT=Vt[:, j, hd, :], rhs=e[:, off:G],
                                              start=(j == 0), stop=(j == nkb - 1)),
                     r=[t_e, t_V[j]], w=[t_psO[half]])
                if j != nkb - 1:
                    return
                k.op("dve", lambda h: h.tensor_copy(out=uT[0:65, half, :], in_=psO[half][0:65, :]),
                     r=[t_psO[half]], w=[t_uT[half]])
                for blk in range(NB):
                    k.op("pe", lambda h, blk=blk: h.transpose(
                        out=psTr1[:, blk, :], in_=uT[0:65, half, blk * 128:(blk + 1) * 128],
                        identity=self.ident_f[0:65, 0:65]), r=[t_uT[half], self.t_const], w=[t_psTr1])
                k.op("dve", lambda h: h.reciprocal(out=rl[:, half, :], in_=psTr1[:, :, 64]),
                     r=[t_psTr1], w=[t_rl])
                if half == 0:
                    k.op("dve", lambda h: h.tensor_tensor(out=t1[:], in0=psTr1[:, :, 0:64],
                                                          in1=bcast_mid(rl[:, 0, :], 64), op=ALU.mult),
                         r=[t_psTr1, t_rl], w=[t_t1])
                    return
                k.op("dve", lambda h: h.tensor_scalar(out=rl[:, 1, :], in0=rl[:, 1, :], scalar1=cst[:, 5:6],
                                                      scalar2=None, op0=ALU.mult), r=[t_rl, t_cst], w=[t_rl])
                k.op("dve", lambda h: h.tensor_tensor(out=oat[:, :, hd, :], in0=psTr1[:, :, 0:64],
                                                      in1=bcast_mid(rl[:, 1, :], 64), op=ALU.mult),
                     r=[t_psTr1, t_rl], w=[t_oat])
                k.op("dve", lambda h: h.tensor_tensor(out=oat[:, :, hd, :], in0=t1[:], in1=oat[:, :, hd, :],
                                                      op=ALU.subtract), r=[t_t1, t_oat], w=[t_oat])

            pend = []
            for hd in range(6):
                for j in range(nkb):
                    for half in range(2):
                        pend.append(emit_S(hd, half, j))
                    while len(pend) > 2 * LOOK:
                        emit_PV(pend.pop(0))
                    yield
            while pend:
                emit_PV(pend.pop(0))
            yield
            k.op("pool", lambda h: h.tensor_tensor(out=osq[:], in0=oat[:], in1=oat[:], op=ALU.mult),
                 r=[t_oat], w=[t_osq])
            k.op("dve", lambda h: h.tensor_reduce(out=ss[:, 0, :], in_=osq[:].rearrange("p a b c -> p (a b) c"),
                                                  axis=AX.X, op=ALU.add), r=[t_osq], w=[t_ss])
            k.op("act", lambda h: h.activation(out=ss[:, 1, :], in_=ss[:, 0, :], func=AF.Sqrt, scale=1.0 / 64.0,
                                               bias=cst[:, 6:7]), r=[t_ss, t_cst], w=[t_ss])
            k.op("dve", lambda h: h.reciprocal(out=ss[:, 2, :], in_=ss[:, 1, :]), r=[t_ss], w=[t_ss])
            k.op("dve", lambda h: h.tensor_tensor(out=osq[:].rearrange("p a b c -> p (a b) c"),
                                                  in0=oat[:].rearrange("p a b c -> p (a b) c"),
                                                  in1=bcast_mid(ss[:, 2, :], 64), op=ALU.mult),
                 r=[t_oat, t_ss], w=[t_osq])
            k.op("dve", lambda h: h.tensor_tensor(
                out=oab[:].rearrange("p a (b c) -> p (a b) c", c=64), in0=osq[:].rearrange("p a b c -> p (a b) c"),
                in1=subg[:].unsqueeze(1).to_broadcast([128, NB * 6, 64]), op=ALU.mult),
                 r=[t_osq, t_subg], w=[t_oab])
            yield
            pT, t_pT = self.psT[0], self.t_psT[0]
            for blk in range(NB):
                for c in range(3):
                    k.op("pe", lambda h, blk=blk, c=c: h.transpose(out=pT[:, c, :],
                                                                   in_=oab[:, blk, c * 128:(c + 1) * 128],
                                                                   identity=self.ident_b[:]),
                         r=[t_oab, self.t_const], w=[t_pT])
                k.op("dve", lambda h, blk=blk: h.tensor_copy(out=mixA_[:, 0:3, blk * 128:(blk + 1) * 128],
                                                             in_=pT[:, 0:3, :]),
                     r=[t_pT], w=t_mixA_[0:3])
                yield
            k.dma("sp", self.mxa[:, g * G:(g + 1) * G].rearrange("(c p) t -> p c t", p=128), mixA_[:],
                  r=t_mixA_, w=[self.t_mxa[g]])

        mixAs = [mixA, k.sb("mixA2", [128, 5, G], BF16)]
        t_mixAs = [t_mixA, [Tk() for _ in range(5)]]
        for _ in front(0):
            pass
        for g in range(NG):
            bk = back(g)
            fr = front(g + 1) if g + 1 < NG else iter(())
            n_att = 6 * (4 * g + 4)
            n_fr = 4 + 23 + 4 + 2
            ratio = max(1, n_att // n_fr)
            fr_done = False
            i = 0
            for _ in bk:
                i += 1
                if not fr_done and i % ratio == 0:
                    try:
                        next(fr)
                    except StopIteration:
                        fr_done = True
            for _ in fr:
                pass
        k.pop_scope()

    def pass_mixb(self, l, src, t_src, dst, t_dst):
        k = self.k
        k.push_scope()
        inp = self.inp
        C = 128
        c0 = math.exp(-0.5)
        wout = k.sb("wout", [128, 8, D], BF16)
        t_wout = [Tk() for _ in range(8)]
        for kc in range(8):
            k.dma("pool", wout[:, kc, :], inp["w_out"][l, kc * 128:(kc + 1) * 128, :], w=[t_wout[kc]])
        waup = k.sb("waup", [128, 384], BF16)
        gup = k.sb("gup", [128, 384], BF16)
        t_lw = Tk()
        k.dma("pool", waup[0:64, :], inp["rwkv_w_up"][l], w=[t_lw])
        k.dma("pool", waup[64:128, :], inp["rwkv_a_up"][l], w=[t_lw])
        k.dma("pool", gup[:], inp["rwkv_g_up"][l], w=[t_lw])
        pc = k.sb("pc", [128, 8, 3], F32)
        t_pc = Tk()
        for n, nm in enumerate(["rwkv_w0", "rwkv_a0", "rwkv_k_k", "rwkv_k_a", "rwkv_r_k"]):
            srcap = inp[nm][l]
            if nm == "rwkv_r_k":
                srcap = srcap.rearrange("a b -> (a b)")
            k.dma("sp", pc[:, n, :], srcap.rearrange("(j p) -> p j", p=128), w=[t_pc],
                  allow_slow_non_contiguous=True)
        k.op("dve", lambda h: h.tensor_scalar(out=pc[:, 5, :], in0=pc[:, 3, :], scalar1=-1.0, scalar2=1.0,
                                              op0=ALU.mult, op1=ALU.add), r=[t_pc], w=[t_pc])
        k.op("pool", lambda h: h.memset(pc[:, 6, :], GN_EPS), w=[t_pc])
        mu = k.sb("mu", [128, 11], F32)
        t_mu = Tk()
        k.dma("sp", mu[:], inp["shift_mu"][l].rearrange("(j p) -> p j", p=128), w=[t_mu],
              allow_slow_non_contiguous=True)
        lng = k.sb("lng", [128, 2, 384], F32)
        t_lng = Tk()
        k.dma("sp", lng[:, 0, :], inp["lnx_g"][l:l + 1, :].partition_broadcast(128), w=[t_lng])
        k.dma("sp", lng[:, 1, :], inp["lnx_b"][l:l + 1, :].partition_broadcast(128), w=[t_lng])
        bones = k.sb("bones", [128, 128], BF16)
        t_bones = Tk()
        k.op("pool", lambda h: h.memset(bones[:], 0.0), w=[t_bones])
        k.op("pool", lambda h: h.memset(bones[0:64, 0:64], 1.0), w=[t_bones])
        k.op("pool", lambda h: h.memset(bones[64:128, 64:128], 1.0), w=[t_bones])
        msk = k.sb("msk", [128, 6, 128], F32)
        t_msk = Tk()
        k.op("pool", lambda h: h.memset(msk[:], 1.0), w=[t_msk])
        k.op("pool", lambda h: h.affine_select(out=msk[:, 0, :], in_=msk[:, 0, :], pattern=[[1, 128]],
                                               compare_op=ALU.is_ge, fill=0.0, base=-1, channel_multiplier=-1),
             r=[t_msk], w=[t_msk])
        k.op("pool", lambda h: h.affine_select(out=msk[:, 1, :], in_=msk[:, 1, :], pattern=[[1, 128]],
                                               compare_op=ALU.is_ge, fill=0.0, base=0, channel_multiplier=-1),
             r=[t_msk], w=[t_msk])
        k.op("pool", lambda h: h.affine_select(out=msk[:, 2, :], in_=msk[:, 2, :], pattern=[[-1, 128]],
                                               compare_op=ALU.is_ge, fill=0.0, base=-1, channel_multiplier=1),
             r=[t_msk], w=[t_msk])
        k.op("pool", lambda h: h.memset(msk[:, 3:6, :], 0.0), w=[t_msk])
        k.op("pool", lambda h: h.tensor_copy(out=msk[0:64, 3:5, 0:64], in_=msk[0:64, 0:3:2, 0:64]),
             r=[t_msk], w=[t_msk])
        k.op("pool", lambda h: h.tensor_copy(out=msk[64:128, 3:5, 64:128], in_=msk[64:128, 0:3:2, 64:128]),
             r=[t_msk], w=[t_msk])
        k.op("pool", lambda h: h.memset(msk[0:64, 5, 64:128], 1.0), r=[t_msk], w=[t_msk])

        GB = 256
        NCH = GB // C
        NGB = S // GB
        NQ = S // C
        NR = 3
        rmask = k.sb("rmask", [128, GB], F32)
        t_rmask = Tk()
        k.op("pool", lambda h: h.memset(rmask[:], 1.0), w=[t_rmask])
        k.op("pool", lambda h: h.memset(rmask[:].rearrange("p (c t) -> p c t", t=C)[:, :, 0:1], 0.0), w=[t_rmask])
        F3 = [128, 3, GB]
        pt = k.sb("pt", [128, 11, GB], F32); t_pt = Tk()
        halo = k.sb("halo", [128, 11, 1], F32); t_halo = Tk()
        k.op("pool", lambda h: h.memset(halo[:], 0.0), w=[t_halo])
        z = k.sb("z", [128, 11, GB], F32); t_z = [Tk()]
        sig = k.sb("sig", F3, F32); t_sig = [Tk() for _ in range(3)]
        aa = k.sb("aa", F3, F32); t_aa = [Tk() for _ in range(3)]
        kkn = k.sb("kkn", F3, F32); t_kkn = [Tk() for _ in range(3)]
        kmod = k.sb("kmod", F3, F32); t_kmod = [Tk() for _ in range(3)]
        beta = k.sb("beta", F3, F32); t_beta = [Tk() for _ in range(3)]
        cs = k.sb("cs", F3, F32); t_cs = [Tk() for _ in range(3)]
        tA = pt[:, 0:3, :]; t_tA = [Tk() for _ in range(3)]
        tB = pt[:, 3:6, :]; t_tB = [Tk() for _ in range(3)]
        tC = pt[:, 6:9, :]; t_tC = [Tk() for _ in range(3)]
        tD = k.sb("tD", F3, F32); t_tD = [Tk() for _ in range(3)]
        tE = k.sb("tE", F3, F32); t_tE = [Tk() for _ in range(3)]
        t_alias = t_tA + t_tB + t_tC
        b16 = k.sb("b16", F3, BF16); t_b16 = [Tk() for _ in range(3)]
        twa = k.sb("twa", [128, GB], BF16); t_twa = Tk()
        sg = k.sb("sg", [128, GB], BF16); t_sg = Tk()
        def gset(i):
            d = {}
            for nm in ["rt", "kt", "bt", "at", "k2", "b2"]:
                d[nm] = k.sb(nm + str(i), F3, BF16); d["t_" + nm] = [Tk() for _ in range(3)]
            for nm in ["atp", "rtp", "btp"]:
                d[nm] = k.sb(nm + str(i), [128, 3, NCH, 2, C], BF16); d["t_" + nm] = [Tk() for _ in range(3)]
                k.op("pool", lambda h, t_=d[nm]: h.memset(t_[:], 0.0), w=d["t_" + nm])
            d["WC"] = k.sb("WC%d" % i, [128, 3, NCH], F32); d["t_WC"] = [Tk() for _ in range(3)]
            d["Vg"] = k.sb("Vg%d" % i, [128, NCH, 384], BF16); d["t_Vg"] = [Tk() for _ in range(NCH)]
            return d

        def pset(i):
            d = {}
            d["gT"] = k.sb("gT%d" % i, F3, BF16); d["t_gT"] = [Tk() for _ in range(3)]
            d["bonus"] = k.sb("bonus%d" % i, F3, BF16); d["t_bonus"] = [Tk() for _ in range(3)]
            d["ysb"] = k.sb("ysb%d" % i, [128, NCH, 384], F32); d["t_ysb"] = [Tk() for _ in range(NCH)]
            return d
        NGS = 2
        GS = [gset(i) for i in range(NGS)]
        NGP = 3
        GP = [pset(i) for i in range(NGP)]
        H6 = [128, 6, 128]
        def cset(i):
            d = {}
            for nm in ["X0", "X1", "Xt0", "Xt1", "P0", "P1", "Lak", "Arb", "Ark", "X12"]:
                d[nm] = k.sb("%s_%d" % (nm, i), H6, BF16); d["t_" + nm] = Tk()
            d["kbtok"] = k.sb("kbtok%d" % i, [128, 2, 384], BF16); d["t_kbtok"] = Tk()
            return d
        CS = [cset(i) for i in range(NR)]
        r0b = k.sb("r0b", [128, 384], BF16); t_r0b = Tk()
        Ub = k.sb("Ub", [128, 384], BF16); t_Ub = Tk()
        u1b = k.sb("u1b", [128, 384], BF16); t_u1b = Tk()
        u2b = k.sb("u2b", [128, 384], BF16); t_u2b = Tk()
        Tst = k.sb("Tst", [128, 3, 128], F32); t_T = Tk()
        Tbd = k.sb("Tbd", [128, 3, 128], BF16); t_Tb = Tk()
        k.op("pool", lambda h: h.memset(Tst[:], 0.0), w=[t_T])
        k.op("pool", lambda h: h.memset(Tbd[:], 0.0), w=[t_Tb])
        ysq = k.sb("ysq", [128, NCH, 384], F32); t_ysq = Tk()
        gst = k.sb("gst", [128, 6, NCH * 6], F32); t_gst = Tk()
        potmp = [k.sb("potmp%d" % i, [128, GB], F32) for i in range(2)]; t_potmp = [Tk() for _ in range(2)]
        mixT = k.sb("mixT", [128, 8, GB], BF16); t_mixT = [Tk() for _ in range(8)]
        xr = [k.sb("xrb0", [128, D], F32)] * 2; t_xr = [Tk()] * 2
        NPD = 3
        psD = [k.ps("psD%d" % i, [128, 1024], F32) for i in range(NPD)]
        t_psD = [Tk() for _ in range(NPD)]
        psX = k.ps("psX", [128, 512], F32)
        t_psX = [Tk() for _ in range(2)]
        psTb = k.ps("psTb", [128, 2, 384], BF16)
        t_psTb = Tk()
        st = {"n": 0, "x": 0, "s": 0, "pt": 0}

        def nps():
            return psX[:, 0:256], t_psX[0]

        def npd():
            i = st["n"] % NPD
            st["n"] += 1
            return psD[i], t_psD[i]

        def b3(col):
            return bcast_mid(col, GB)

        def h6(ps):
            return ps[:, 0:768].rearrange("p (a b) -> p a b", a=6)

        def pre_head(gb):
            gs = slice(gb * GB, (gb + 1) * GB)
            tprw = self.t_prw[(gb * GB) // G]
            k.dma("sp", pt[:], self.prw[:, gs].rearrange("(c p) t -> p c t", p=128), r=[tprw],
                  w=[t_pt] + t_alias)
            yield
            k.op("pool", lambda h: h.tensor_tensor(out=z[:, :, 1:GB], in0=pt[:, :, 0:GB - 1], in1=pt[:, :, 1:GB],
                                                   op=ALU.subtract), r=[t_pt] + t_alias, w=t_z)
            k.op("pool", lambda h: h.tensor_tensor(out=z[:, :, 0:1], in0=halo[:], in1=pt[:, :, 0:1],
                                                   op=ALU.subtract), r=[t_pt, t_halo] + t_alias, w=t_z)
            yield
            k.op("dve", lambda h: h.tensor_tensor(out=z[:], in0=z[:], in1=bcast_mid(mu[:], GB), op=ALU.mult),
                 r=t_z + [t_mu], w=t_z)
            yield
            k.op("dve", lambda h: h.tensor_tensor(out=z[:], in0=z[:], in1=pt[:], op=ALU.add),
                 r=t_z + [t_pt] + t_alias, w=t_z)
            k.op("pool", lambda h: h.tensor_copy(out=halo[:], in_=pt[:, :, GB - 1:GB]), r=[t_pt] + t_alias,
                 w=[t_halo])
            yield
            k.op("act", lambda h: h.activation(out=twa[0:64, :], in_=z[0:64, 9, :], func=AF.Tanh), r=t_z, w=[t_twa])
            k.op("act", lambda h: h.activation(out=twa[64:128, :], in_=z[64:128, 9, :], func=AF.Copy),
                 r=t_z, w=[t_twa])
            k.op("act", lambda h: h.activation(out=sg[:], in_=z[:, 10, :], func=AF.Sigmoid), r=t_z, w=[t_sg])
            yield

        def pre_fc(gb, fc):
            d = GS[gb % NGS]
            dp = GP[gb % NGP]
            zr, zk, zv = z[:, fc, :], z[:, 3 + fc, :], z[:, 6 + fc, :]
            tz = t_z
            A_, B_, C_, D_, E_ = tA[:, fc, :], tB[:, fc, :], tC[:, fc, :], tD[:, fc, :], tE[:, fc, :]
            tA_, tB_, tC_, tD_, tE_ = t_tA[fc], t_tB[fc], t_tC[fc], t_tD[fc], t_tE[fc]
            col = lambda n: pc[:, n, fc:fc + 1]
            fs = slice(fc * 128, (fc + 1) * 128)
            csf = cs[:, fc, :]
            csC = csf.rearrange("p (c t) -> p c t", t=C)[:, :, C - 1]
            k.op("act", lambda h: h.activation(out=A_, in_=zk, func=AF.Copy, scale=col(2)), r=tz + [t_pc], w=[tA_])
            ps, t_ps = nps()
            k.op("pe", lambda h: h.matmul(ps, lhsT=waup[0:64, fs], rhs=twa[0:64, :], start=True, stop=True,
                                          tile_position=(0, 0)), r=[t_lw, t_twa], w=[t_ps])
            k.op("act", lambda h: h.activation(out=sig[:, fc, :], in_=ps, func=AF.Sigmoid, bias=col(0)),
                 r=[t_ps, t_pc], w=[t_sig[fc]])
            ps, t_ps = nps()
            k.op("pe", lambda h: h.matmul(ps, lhsT=waup[64:128, fs], rhs=twa[64:128, :], start=True, stop=True,
                                          tile_position=(64, 0)), r=[t_lw, t_twa], w=[t_ps])
            k.op("act", lambda h: h.activation(out=aa[:, fc, :], in_=ps, func=AF.Sigmoid, bias=col(1)),
                 r=[t_ps, t_pc], w=[t_aa[fc]])
            ps, t_ps = nps()
            k.op("pe", lambda h: h.matmul(ps, lhsT=gup[:, fs], rhs=sg[:], start=True, stop=True),
                 r=[t_lw, t_sg], w=[t_ps])
            k.op("dve", lambda h: h.tensor_copy(out=dp["gT"][:, fc, :], in_=ps), r=[t_ps], w=[dp["t_gT"][fc]])
            yield
            k.op("pool", lambda h: h.tensor_tensor(out=b16[:, fc, :], in0=A_, in1=A_, op=ALU.mult),
                 r=[tA_], w=[t_b16[fc]])
            k.op("dve", lambda h: h.tensor_tensor_scan(out=csf, data0=rmask[:], data1=sig[:, fc, :], initial=0.0,
                                                       op0=ALU.mult, op1=ALU.add),
                 r=[t_rmask, t_sig[fc]], w=[t_cs[fc]])
            k.op("act", lambda h: h.activation(out=kmod[:, fc, :], in_=aa[:, fc, :], func=AF.Identity,
                                               scale=col(3), bias=col(5)), r=[t_aa[fc], t_pc], w=[t_kmod[fc]])
            yield
            ps, t_ps = nps()
            k.op("pe", lambda h: h.matmul(ps, lhsT=bones[:], rhs=b16[:, fc, :], start=True, stop=True),
                 r=[t_bones, t_b16[fc]], w=[t_ps])
            k.op("act", lambda h: h.activation(out=B_, in_=ps, func=AF.Sqrt), r=[t_ps], w=[tB_])
            k.op("pool", lambda h: h.tensor_tensor(out=kmod[:, fc, :], in0=kmod[:, fc, :], in1=zk, op=ALU.mult),
                 r=[t_kmod[fc]] + tz, w=[t_kmod[fc]])
            k.op("act", lambda h: h.activation(out=C_, in_=csf, func=AF.Exp, scale=c0), r=[t_cs[fc]], w=[tC_])
            k.op("act", lambda h: h.activation(out=D_, in_=csf, func=AF.Exp, scale=-c0), r=[t_cs[fc]], w=[tD_])
            k.op("pool", lambda h: h.tensor_tensor(
                out=E_.rearrange("p (c t) -> p c t", t=C), in0=csC.unsqueeze(2).to_broadcast([128, NCH, C]),
                in1=csf.rearrange("p (c t) -> p c t", t=C), op=ALU.subtract), r=[t_cs[fc]], w=[tE_])
            k.op("act", lambda h: h.activation(out=d["WC"][:, fc, :], in_=csC, func=AF.Exp, scale=-c0),
                 r=[t_cs[fc]], w=[d["t_WC"][fc]])
            yield
            k.op("dve", lambda h: h.tensor_scalar(out=B_, in0=B_, scalar1=1e-12, scalar2=None, op0=ALU.max),
                 r=[tB_], w=[tB_])
            k.op("dve", lambda h: h.reciprocal(out=B_, in_=B_), r=[tB_], w=[tB_])
            k.op("dve", lambda h: h.tensor_tensor(out=d["kt"][:, fc, :], in0=kmod[:, fc, :], in1=C_, op=ALU.mult),
                 r=[t_kmod[fc], tC_], w=[d["t_kt"][fc]])
            k.op("dve", lambda h: h.tensor_tensor(out=d["rt"][:, fc, :], in0=zr, in1=D_, op=ALU.mult),
                 r=tz + [tD_], w=[d["t_rt"][fc]])
            k.op("act", lambda h: h.activation(out=E_, in_=E_, func=AF.Exp, scale=-c0), r=[tE_], w=[tE_])
            yield
            k.op("pool", lambda h: h.tensor_tensor(out=kkn[:, fc, :], in0=A_, in1=B_, op=ALU.mult),
                 r=[tA_, tB_], w=[t_kkn[fc]])
            k.op("dve", lambda h: h.tensor_tensor(out=d["k2"][:, fc, :], in0=kmod[:, fc, :], in1=E_, op=ALU.mult),
                 r=[t_kmod[fc], tE_], w=[d["t_k2"][fc]])
            k.op("pool", lambda h: h.tensor_tensor(out=D_, in0=csf, in1=sig[:, fc, :], op=ALU.subtract),
                 r=[t_cs[fc], t_sig[fc], d["t_rt"][fc]], w=[tD_])
            yield
            k.op("pool", lambda h: h.tensor_tensor(out=A_, in0=zr, in1=kmod[:, fc, :], op=ALU.mult),
                 r=tz + [t_kmod[fc], t_kkn[fc]], w=[tA_])
            k.op("dve", lambda h: h.tensor_tensor(out=beta[:, fc, :], in0=kkn[:, fc, :], in1=aa[:, fc, :],
                                                  op=ALU.mult), r=[t_kkn[fc], t_aa[fc]], w=[t_beta[fc]])
            k.op("act", lambda h: h.activation(out=D_, in_=D_, func=AF.Exp, scale=-c0), r=[tD_], w=[tD_])
            yield
            k.op("act", lambda h: h.activation(out=b16[:, fc, :], in_=A_, func=AF.Copy, scale=col(4)),
                 r=[tA_, t_pc], w=[t_b16[fc]])
            k.op("dve", lambda h: h.tensor_tensor(out=d["bt"][:, fc, :], in0=beta[:, fc, :], in1=C_, op=ALU.mult),
                 r=[t_beta[fc], tC_], w=[d["t_bt"][fc]])
            k.op("dve", lambda h: h.tensor_tensor(out=d["b2"][:, fc, :], in0=beta[:, fc, :], in1=E_, op=ALU.mult),
                 r=[t_beta[fc], tE_], w=[d["t_b2"][fc]])
            k.op("dve", lambda h: h.scalar_tensor_tensor(out=d["at"][:, fc, :], in0=kkn[:, fc, :], scalar=-1.0,
                                                         in1=D_, op0=ALU.mult, op1=ALU.mult),
                 r=[t_kkn[fc], tD_], w=[d["t_at"][fc]])
            yield
            ps, t_ps = nps()
            k.op("pe", lambda h: h.matmul(ps, lhsT=bones[:], rhs=b16[:, fc, :], start=True, stop=True),
                 r=[t_bones, t_b16[fc]], w=[t_ps])
            k.op("dve", lambda h: h.tensor_tensor(out=dp["bonus"][:, fc, :], in0=ps, in1=zv, op=ALU.mult),
                 r=[t_ps] + tz, w=[dp["t_bonus"][fc]])
            for nm, pn, eng in [("at", "atp", "pool"), ("rt", "rtp", "act"), ("bt", "btp", "pool")]:
                for par in range(2):
                    hb = par * 64
                    src_ = d[nm][hb:hb + 64, fc, :].rearrange("p (c t) -> p c t", t=C)
                    dst_ = d[pn][hb:hb + 64, fc, :, par, :]
                    if eng == "pool":
                        k.op("pool", lambda h, src_=src_, dst_=dst_: h.tensor_copy(out=dst_, in_=src_),
                             r=[d["t_" + nm][fc]], w=[d["t_" + pn][fc]])
                    else:
                        k.op("act", lambda h, src_=src_, dst_=dst_: h.activation(out=dst_, in_=src_, func=AF.Copy),
                             r=[d["t_" + nm][fc]], w=[d["t_" + pn][fc]])
            yield

        def pre_tail(gb):
            d = GS[gb % NGS]
            for c in range(NCH):
                cl = slice(c * C, (c + 1) * C)
                ps, t_ps = npd()
                for fc in range(3):
                    k.op("pe", lambda h, ps=ps, fc=fc, cl=cl: h.transpose(
                        out=ps[:, fc * 128:(fc + 1) * 128], in_=z[:, 6 + fc, cl], identity=self.ident_f[:]),
                         r=t_z + [self.t_const], w=[t_ps])
                k.op("act", lambda h, ps=ps, c=c: h.activation(out=d["Vg"][:, c, :], in_=ps[:, 0:384], func=AF.Copy),
                     r=[t_ps], w=[d["t_Vg"][c]])
                yield

        def pre(gb):
            yield from pre_head(gb)
            gens = [pre_fc(gb, fc) for fc in range(3)]
            gens.append(pre_tail(gb))
            while gens:
                for g_ in list(gens):
                    try:
                        next(g_)
                    except StopIteration:
                        gens.remove(g_)
                yield

        def chunkA(q):
            gb, c = q // NCH, q % NCH
            d = GS[gb % NGS]
            e = CS[q % NR]
            cl = slice(c * C, (c + 1) * C)

            def amat(lname, pname, *dsts):
                ps, t_ps = npd()
                for fc in range(3):
                    k.op("pe", lambda h, ps=ps, fc=fc: h.matmul(
                        ps[:, fc * 256:(fc + 1) * 256], lhsT=d[lname][:, fc, cl],
                        rhs=d[pname][:, fc, c, :, :].rearrange("p a b -> p (a b)"), start=True, stop=True),
                         r=[d["t_" + lname][fc], d["t_" + pname][fc]], w=[t_ps])
                for dst, mi in dsts:
                    k.op("dve", lambda h, ps=ps, dst=dst, mi=mi: h.tensor_tensor(
                        out=e[dst][:], in0=h6(ps), in1=msk[:, mi, :].unsqueeze(1).to_broadcast([128, 6, 128]),
                        op=ALU.mult), r=[t_ps, t_msk], w=[e["t_" + dst]])
            amat("bt", "atp", ("X0", 3), ("X12", 5))
            amat("at", "btp", ("Xt0", 4))
            for n_, nm_ in enumerate(["k2", "b2"]):
                for fc in range(3):
                    k.op("pe", lambda h, n_=n_, fc=fc, nm_=nm_: h.transpose(
                        out=psTb[:, n_, fc * 128:(fc + 1) * 128], in_=d[nm_][:, fc, cl], identity=self.ident_b[:]),
                         r=[d["t_" + nm_][fc], self.t_const], w=[t_psTb])
            k.op("act", lambda h: h.activation(out=e["kbtok"][:], in_=psTb[:], func=AF.Copy),
                 r=[t_psTb], w=[e["t_kbtok"]])
            yield
            amat("kt", "atp", ("Lak", 0))
            amat("bt", "rtp", ("Arb", 1))
            amat("kt", "rtp", ("Ark", 1))
            idb = self.ident_b[:].unsqueeze(1).to_broadcast([128, 6, 128])
            k.op("pool", lambda h: h.tensor_tensor(out=e["P0"][:], in0=e["X0"][:], in1=idb, op=ALU.add),
                 r=[e["t_X0"], self.t_const], w=[e["t_P0"]])
            yield
            nsteps = 5
            for lv in range(1, nsteps + 2):
                Xc, Xtc = e["X%d" % ((lv - 1) % 2)], e["Xt%d" % ((lv - 1) % 2)]
                t_Xc, t_Xtc = e["t_X%d" % ((lv - 1) % 2)], e["t_Xt%d" % ((lv - 1) % 2)]
                Xn, Xtn = e["X%d" % (lv % 2)], e["Xt%d" % (lv % 2)]
                t_Xn, t_Xtn = e["t_X%d" % (lv % 2)], e["t_Xt%d" % (lv % 2)]
                if lv >= 2:
                    Pold, Pnew = e["P%d" % (lv % 2)], e["P%d" % ((lv - 1) % 2)]
                    t_Pold, t_Pnew = e["t_P%d" % (lv % 2)], e["t_P%d" % ((lv - 1) % 2)]
                    ps, t_ps = npd()
                    for hd in range(6):
                        k.op("pe", lambda h, ps=ps, hd=hd: h.matmul(ps[:, hd * 128:(hd + 1) * 128],
                                                                    lhsT=Xtc[:, hd, :], rhs=Pold[:, hd, :],
                                                                    start=True, stop=True),
                             r=[t_Xtc, t_Pold], w=[t_ps])
                    k.op("dve", lambda h, ps=ps: h.tensor_tensor(out=Pnew[:], in0=h6(ps), in1=Pold[:], op=ALU.add),
                         r=[t_ps, t_Pold], w=[t_Pnew])
                if lv <= nsteps:
                    ps, t_ps = npd()
                    for hd in range(6):
                        k.op("pe", lambda h, ps=ps, hd=hd: h.matmul(ps[:, hd * 128:(hd + 1) * 128],
                                                                    lhsT=Xc[:, hd, :], rhs=Xtc[:, hd, :],
                                                                    start=True, stop=True),
                             r=[t_Xc, t_Xtc], w=[t_ps])
                    k.op("act", lambda h, ps=ps: h.activation(out=Xtn[:], in_=h6(ps), func=AF.Copy),
                         r=[t_ps], w=[t_Xtn])
                    if lv < nsteps:
                        ps, t_ps = npd()
                        for hd in range(6):
                            k.op("pe", lambda h, ps=ps, hd=hd: h.matmul(ps[:, hd * 128:(hd + 1) * 128],
                                                                        lhsT=Xtc[:, hd, :], rhs=Xc[:, hd, :],
                                                                        start=True, stop=True),
                                 r=[t_Xc, t_Xtc], w=[t_ps])
                        k.op("act", lambda h, ps=ps: h.activation(out=Xn[:], in_=h6(ps), func=AF.Copy),
                             r=[t_ps], w=[t_Xn])
                yield

        def chain(q):
            gb, c = q // NCH, q % NCH
            d = GS[gb % NGS]
            e = CS[q % NR]
            cl = slice(c * C, (c + 1) * C)
            Pf, t_Pf = e["P1"], e["t_P1"]
            Vg, t_Vg = d["Vg"], d["t_Vg"]
            ps, t_ps = npd()
            for fc in range(3):
                k.op("pe", lambda h, ps=ps, fc=fc: h.matmul(ps[:, fc * 128:(fc + 1) * 128], lhsT=d["at"][:, fc, cl],
                                                            rhs=Tbd[:, fc, :], start=True, stop=False),
                     r=[d["t_at"][fc], t_Tb], w=[t_ps])
                for hd in (2 * fc, 2 * fc + 1):
                    k.op("pe", lambda h, ps=ps, hd=hd: h.matmul(ps[:, hd * 64:(hd + 1) * 64], lhsT=e["Lak"][:, hd, :],
                                                                rhs=Vg[:, c, hd * 64:(hd + 1) * 64], start=False,
                                                                stop=(hd % 2 == 1)),
                         r=[e["t_Lak"], t_Vg[c]], w=[t_ps])
            k.op("act", lambda h, ps=ps: h.activation(out=r0b[:], in_=ps[:, 0:384], func=AF.Copy), r=[t_ps], w=[t_r0b])
            yield
            ps, t_ps = npd()
            for hd in range(6):
                k.op("pe", lambda h, ps=ps, hd=hd: h.matmul(ps[:, hd * 64:(hd + 1) * 64], lhsT=Pf[:, hd, :],
                                                            rhs=r0b[:, hd * 64:(hd + 1) * 64], start=True, stop=True),
                     r=[t_Pf, t_r0b], w=[t_ps])
            k.op("act", lambda h, ps=ps: h.activation(out=u1b[:], in_=ps[:, 0:384], func=AF.Copy), r=[t_ps], w=[t_u1b])
            yield
            ps, t_ps = npd()
            for hd in range(6):
                k.op("pe", lambda h, ps=ps, hd=hd: h.matmul(ps[:, hd * 64:(hd + 1) * 64], lhsT=e["X12"][:, hd, :],
                                                            rhs=u1b[:, hd * 64:(hd + 1) * 64], start=True, stop=True),
                     r=[e["t_X12"], t_u1b], w=[t_ps])
            k.op("act", lambda h, ps=ps: h.activation(out=u2b[:], in_=ps[:, 0:384], func=AF.Copy), r=[t_ps], w=[t_u2b])
            yield
            ps, t_ps = npd()
            for hd in range(6):
                k.op("pe", lambda h, ps=ps, hd=hd: h.matmul(ps[:, hd * 64:(hd + 1) * 64], lhsT=Pf[:, hd, :],
                                                            rhs=u2b[:, hd * 64:(hd + 1) * 64], start=True, stop=True),
                     r=[t_Pf, t_u2b], w=[t_ps])
            k.op("dve", lambda h, ps=ps: h.tensor_tensor(out=Ub[:], in0=ps[:, 0:384], in1=u1b[:], op=ALU.add),
                 r=[t_ps, t_u1b], w=[t_Ub])
            yield
            ps2, t_ps2 = npd()
            kb = e["kbtok"]
            for fc in range(3):
                fs = slice(fc * 128, (fc + 1) * 128)
                k.op("pe", lambda h, fs=fs: h.matmul(ps2[:, fs], lhsT=kb[:, 1, fs], rhs=Ub[:, fs], start=True,
                                                     stop=False), r=[e["t_kbtok"], t_Ub], w=[t_ps2])
                k.op("pe", lambda h, fs=fs: h.matmul(ps2[:, fs], lhsT=kb[:, 0, fs], rhs=Vg[:, c, fs], start=False,
                                                     stop=True), r=[e["t_kbtok"], t_Vg[c]], w=[t_ps2])
            ps, t_ps = npd()
            for fc in range(3):
                k.op("pe", lambda h, ps=ps, fc=fc: h.matmul(ps[:, fc * 128:(fc + 1) * 128], lhsT=d["rt"][:, fc, cl],
                                                            rhs=Tbd[:, fc, :], start=True, stop=False),
                     r=[d["t_rt"][fc], t_Tb], w=[t_ps])
                for hd in (2 * fc, 2 * fc + 1):
                    hs = slice(hd * 64, (hd + 1) * 64)
                    k.op("pe", lambda h, ps=ps, hd=hd, hs=hs: h.matmul(ps[:, hs], lhsT=e["Arb"][:, hd, :],
                                                                       rhs=Ub[:, hs], start=False, stop=False),
                         r=[e["t_Arb"], t_Ub], w=[t_ps])
                    k.op("pe", lambda h, ps=ps, hd=hd, hs=hs: h.matmul(ps[:, hs], lhsT=e["Ark"][:, hd, :],
                                                                       rhs=Vg[:, c, hs], start=False,
                                                                       stop=(hd % 2 == 1)),
                         r=[e["t_Ark"], t_Vg[c]], w=[t_ps])
            for par in range(2):
                hb = par * 64
                tv = Tst[hb:hb + 64, :, hb:hb + 64]
                k.op("dve", lambda h, hb=hb, tv=tv: h.tensor_tensor(
                    out=tv, in0=tv, in1=bcast_mid(d["WC"][hb:hb + 64, :, c], 64), op=ALU.mult),
                     r=[t_T] + d["t_WC"], w=[t_T])
                k.op("dve", lambda h, hb=hb, tv=tv: h.tensor_tensor(
                    out=tv, in0=ps2[hb:hb + 64, 0:384].rearrange("p (f x) -> p f x", f=3)[:, :, hb:hb + 64], in1=tv,
                    op=ALU.add), r=[t_ps2, t_T], w=[t_T])
                k.op("pool", lambda h, hb=hb, tv=tv: h.tensor_copy(out=Tbd[hb:hb + 64, :, hb:hb + 64], in_=tv),
                     r=[t_T], w=[t_Tb])
            k.op("act", lambda h, ps=ps: h.activation(out=GP[gb % NGP]["ysb"][:, c, :], in_=ps[:, 0:384],
                                                      func=AF.Copy), r=[t_ps], w=[GP[gb % NGP]["t_ysb"][c]])
            yield

        def post(gb):
            d = GP[gb % NGP]
            gs = slice(gb * GB, (gb + 1) * GB)
            ysb, t_ysb = d["ysb"], d["t_ysb"]
            k.dma("sp", mixT[:, 0:5, :], self.mxa[:, gs].rearrange("(c p) t -> p c t", p=128),
                  r=[self.t_mxa[(gb * GB) // G]], w=t_mixT[0:5])
            yv = ysb[:].rearrange("p c (a b) -> p (c a) b", b=64)
            qv = ysq[:].rearrange("p c (a b) -> p (c a) b", b=64)
            k.op("dve", lambda h: h.tensor_reduce(out=gst[:, 0, :], in_=yv, axis=AX.X, op=ALU.add),
                 r=t_ysb, w=[t_gst])
            k.op("pool", lambda h: h.tensor_tensor(out=ysq[:], in0=ysb[:], in1=ysb[:], op=ALU.mult),
                 r=t_ysb, w=[t_ysq])
            yield
            k.op("dve", lambda h: h.tensor_reduce(out=gst[:, 1, :], in_=qv, axis=AX.X, op=ALU.add),
                 r=[t_ysq], w=[t_gst])
            k.op("dve", lambda h: h.tensor_scalar(out=gst[:, 0:2, :], in0=gst[:, 0:2, :], scalar1=1.0 / 64.0,
                                                  scalar2=None, op0=ALU.mult), r=[t_gst], w=[t_gst])
            k.op("dve", lambda h: h.tensor_tensor(out=gst[:, 2, :], in0=gst[:, 0, :], in1=gst[:, 0, :], op=ALU.mult),
                 r=[t_gst], w=[t_gst])
            k.op("dve", lambda h: h.tensor_tensor(out=gst[:, 3, :], in0=gst[:, 1, :], in1=gst[:, 2, :],
                                                  op=ALU.subtract), r=[t_gst], w=[t_gst])
            yield
            k.op("act", lambda h: h.activation(out=gst[:, 4, :], in_=gst[:, 3, :], func=AF.Sqrt,
                                               bias=pc[:, 6, 0:1]), r=[t_gst, t_pc], w=[t_gst])
            k.op("dve", lambda h: h.reciprocal(out=gst[:, 5, :], in_=gst[:, 4, :]), r=[t_gst], w=[t_gst])
            k.op("dve", lambda h: h.tensor_tensor(out=qv, in0=yv, in1=bcast_mid(gst[:, 0, :], 64), op=ALU.subtract),
                 r=t_ysb + [t_gst, t_ysq], w=[t_ysq])
            yield
            k.op("dve", lambda h: h.tensor_tensor(out=qv, in0=qv, in1=bcast_mid(gst[:, 5, :], 64), op=ALU.mult),
                 r=[t_gst, t_ysq], w=[t_ysq])
            k.op("pool", lambda h: h.tensor_tensor(out=ysq[:], in0=ysq[:],
                                                   in1=lng[:, 0, :].unsqueeze(1).to_broadcast([128, NCH, 384]),
                                                   op=ALU.mult), r=[t_ysq, t_lng], w=[t_ysq])
            k.op("pool", lambda h: h.tensor_tensor(out=ysq[:], in0=ysq[:],
                                                   in1=lng[:, 1, :].unsqueeze(1).to_broadcast([128, NCH, 384]),
                                                   op=ALU.add), r=[t_ysq, t_lng], w=[t_ysq])
            yield
            for fc in range(3):
                ps, t_ps = nps()
                for c2 in range(NCH):
                    k.op("pe", lambda h, ps=ps, fc=fc, c2=c2: h.transpose(
                        out=ps[:, c2 * 128:(c2 + 1) * 128], in_=ysq[:, c2, fc * 128:(fc + 1) * 128],
                        identity=self.ident_f[:]), r=[t_ysq, self.t_const], w=[t_ps])
                i = st["pt"] % 2
                st["pt"] += 1
                k.op("dve", lambda h, ps=ps, fc=fc, i=i: h.tensor_tensor(out=potmp[i][:], in0=ps,
                                                                         in1=d["bonus"][:, fc, :], op=ALU.add),
                     r=[t_ps, d["t_bonus"][fc]], w=[t_potmp[i]])
                k.op("pool", lambda h, fc=fc, i=i: h.tensor_tensor(out=mixT[:, 5 + fc, :], in0=potmp[i][:],
                                                                   in1=d["gT"][:, fc, :], op=ALU.mult),
                     r=[t_potmp[i], d["t_gT"][fc]], w=[t_mixT[5 + fc]])
                yield
            if self.mxb is not None:
                k.dma("sp", self.mxb[:, gs].rearrange("(c p) t -> p c t", p=128), mixT[:, 5:8, :],
                      r=t_mixT[5:8], w=[self.t_mxb])
            for blk in range(GB // 128):
                r0 = gb * GB + blk * 128
                tg = r0 // G
                i = st["x"] % 2
                st["x"] += 1
                k.dma("sp", xr[i][:], src[r0:r0 + 128, :], r=[t_src[tg]], w=[t_xr[i]])
                psw, t_psw = npd()
                for half in range(2):
                    ps, t_ps = psw[:, half * 512:(half + 1) * 512], t_psw
                    for kc in range(8):
                        k.op("pe", lambda h, ps=ps, kc=kc, blk=blk, half=half: h.matmul(
                            ps, lhsT=mixT[:, kc, blk * 128:(blk + 1) * 128],
                            rhs=wout[:, kc, half * 512:(half + 1) * 512], start=(kc == 0), stop=(kc == 7)),
                             r=[t_mixT[kc], t_wout[kc]], w=[t_ps])
                    k.op("dve", lambda h, ps=ps, half=half, i=i: h.tensor_tensor(
                        out=xr[i][:, half * 512:(half + 1) * 512], in0=ps,
                        in1=xr[i][:, half * 512:(half + 1) * 512], op=ALU.add), r=[t_ps, t_xr[i]], w=[t_xr[i]])
                k.dma("sp", dst[r0:r0 + 128, :], xr[i][:], r=[t_xr[i]], w=[t_dst[tg]])
                yield

        ngb = self.dbg.get("mb_ng", NGB)
        nq = ngb * NCH
        for _ in pre(0):
            pass
        pre_done = 1
        post_done = 0
        chainq = 0
        nextA = 0
        actA = []
        doneA = set()
        g_chain = None
        g_pre = None
        g_post = None
        post_ready = []
        while chainq < nq or g_post is not None or post_ready:
            while len(actA) < 2 and nextA < nq and (nextA // NCH) < pre_done and nextA < chainq + NR:
                actA.append([nextA, chunkA(nextA)])
                nextA += 1
            if g_chain is None and chainq < nq and chainq in doneA:
                g_chain = chain(chainq)
            if (g_pre is None and pre_done < ngb and pre_done - NGP < post_done
                    and chainq >= (pre_done - 1) * NCH):
                g_pre = pre(pre_done)
            if g_post is None and post_ready:
                g_post = post(post_ready.pop(0))
            progressed = False
            if g_chain is not None:
                progressed = True
                try:
                    next(g_chain)
                except StopIteration:
                    g_chain = None
                    if (chainq + 1) % NCH == 0:
                        post_ready.append(chainq // NCH)
                    chainq += 1
            for it in list(actA):
                progressed = True
                try:
                    next(it[1])
                except StopIteration:
                    doneA.add(it[0])
                    actA.remove(it)
            if g_pre is not None:
                progressed = True
                try:
                    next(g_pre)
                except StopIteration:
                    g_pre = None
                    pre_done += 1
            if g_post is not None:
                progressed = True
                try:
                    next(g_post)
                except StopIteration:
                    g_post = None
                    post_done += 1
            assert progressed, "scheduler stalled"
        k.pop_scope()

    def build(self):
        for ph in self.phases:
            kind = ph[0]
            srcs = {"x": (self.inp["x"], self.t_x), "xa": (self.xa, self.t_xa), "xb": (self.xb, self.t_xb)}
            if kind == "MA":
                _, l, src = ph
                self.pass_mixa(l, srcs[src][0], srcs[src][1])
            dsts = {"xa": (self.xa, self.t_xa), "xb": (self.xb, self.t_xb), "y": (self.y, self.t_y)}
            if kind == "MB":
                _, l, src, dst = ph
                self.pass_mixb(l, srcs[src][0], srcs[src][1], dsts[dst][0], dsts[dst][1])
            if kind == "F":
                _, l, src, dst, final = ph
                srcs = {"x": (self.inp["x"], self.t_x), "xa": (self.xa, self.t_xa), "xb": (self.xb, self.t_xb)}
                dsts = {"xa": (self.xa, self.t_xa), "xb": (self.xb, self.t_xb), "y": (self.y, self.t_y)}
                self.pass_ffn(l, srcs[src][0], srcs[src][1], dsts[dst][0], dsts[dst][1], final)
        self.k.finish()


FULL_PHASES = [("MA", 0, "x"), ("MB", 0, "x", "xa"), ("F", 0, "xa", "xb", False),
               ("MA", 1, "xb"), ("MB", 1, "xb", "xa"), ("F", 1, "xa", "y", True)]


def build_nc(phases=None, dbg=None):
    dbg = dbg or {}
    nc = bass.Bass("TRN2", target_bir_lowering=False)
    p = Prog(nc, phases or FULL_PHASES, dbg)
    p.build()
    return nc, p


def kernel(**inputs):
    nc, _ = build_nc()
    in_maps = []
    for b in range(8):
        m = {}
        for name in INPUT_SHAPES:
            a = np.asarray(inputs[name], dtype=np.float32)
            m[name] = np.ascontiguousarray(a[b]) if name == 'x' else np.ascontiguousarray(a)
        in_maps.append(m)
    res = run_bass_kernel_spmd(nc, in_maps, core_ids=list(range(8)))
    return np.stack([np.asarray(r["y"]) for r in res.results], axis=0).astype(np.float32)
```

```python
import math
from contextlib import ExitStack
import numpy as np
import concourse.bass as bass
import concourse.mybir as mybir
from concourse.bass_utils import run_bass_kernel_spmd

F32 = mybir.dt.float32
BF16 = mybir.dt.bfloat16
ALU = mybir.AluOpType
AF = mybir.ActivationFunctionType
AX = mybir.AxisListType

S = 4096
D = 1024
DFF = 4096
NIN = 3328
G = 512
NG = S // G
NB = G // 128
DEPTH = 2
HD = 64
NORM_EPS = 1e-6
GN_EPS = 64e-5

INPUT_SHAPES = {
    'x': [S, D], 'norm_mix_g': [2, D], 'w_in': [2, D, NIN], 'lam_q1': [2, 32], 'lam_k1': [2, 32],
    'lam_q2': [2, 32], 'lam_k2': [2, 32], 'subln_g': [2, 64], 'conv_w': [2, 3, 256],
    'shift_mu': [2, 1408], 'rwkv_w0': [2, 384], 'rwkv_w_up': [2, 64, 384], 'rwkv_a0': [2, 384],
    'rwkv_a_up': [2, 64, 384], 'rwkv_g_up': [2, 128, 384], 'rwkv_k_k': [2, 384], 'rwkv_k_a': [2, 384],
    'rwkv_r_k': [2, 6, 64], 'lnx_g': [2, 384], 'lnx_b': [2, 384], 'w_out': [2, D, D],
    'norm_mlp_g': [2, D], 'w_mlp_up': [2, D, DFF], 'w_mlp_down': [2, DFF, D], 'final_norm_g': [D],
}


class Tk:
    __slots__ = ("w", "r", "name")

    def __init__(self, name=""):
        self.w = None
        self.r = {}
        self.name = name


class KB:
    EPOCH = 6000
    NDS = 32

    def __init__(self, nc):
        self.nc = nc
        self.eng = {"pe": nc.tensor, "act": nc.scalar, "dve": nc.vector, "pool": nc.gpsimd, "sp": nc.sync}
        self.cnt = {e: 0 for e in self.eng}
        self.semh = {}
        self.seen = {e: {} for e in self.eng}
        self.maxep = {e: {} for e in self.eng}
        self.dsem = [("dma", i) for i in range(self.NDS)]
        for kx in self.dsem:
            self.semh[kx] = nc.alloc_semaphore(f"dma{kx[1]}")
        self.dval = [0] * self.NDS
        self.dnext = 0
        self.dnext_sw = 0
        self.ndma = 0
        self.nwait = 0
        self._n = 0
        self.root = ExitStack()
        self.scope = None

    def sb(self, name, shape, dt):
        self._n += 1
        cm = self.nc.sbuf_tensor("%s_%d" % (name, self._n), list(shape), dt)
        return (self.scope or self.root).enter_context(cm)

    def ps(self, name, shape, dt):
        self._n += 1
        cm = self.nc.psum_tensor("%s_%d" % (name, self._n), list(shape), dt)
        return (self.scope or self.root).enter_context(cm)

    def push_scope(self):
        self.scope = ExitStack()

    def pop_scope(self):
        self.barrier()
        self.scope.close()
        self.scope = None

    def _cursem(self, e):
        key = (e, self.cnt[e] // self.EPOCH)
        if key not in self.semh:
            self.semh[key] = self.nc.alloc_semaphore(f"s_{e}_{key[1]}")
        return key

    def _wait(self, e, tok):
        key, val = tok[0], tok[1]
        seen = self.seen[e]
        if seen.get(key, 0) >= val:
            return
        if key[0] != "dma":
            if self.maxep[e].get(key[0], -1) > key[1]:
                return
            self.maxep[e][key[0]] = max(self.maxep[e].get(key[0], -1), key[1])
        self.eng[e].wait_ge(self.semh[key], val)
        self.nwait += 1
        seen[key] = val

    def _deps(self, e, reads, writes, is_dma):
        for t in reads:
            if t.w is not None:
                tok = t.w
                if (not is_dma) and tok[2] == e and e == "pe":
                    continue
                self._wait(e, tok)
        for t in writes:
            if t.w is not None:
                tok = t.w
                if is_dma or tok[2] != e or tok[3] or e != "pe":
                    self._wait(e, tok)
            for rk, tok in t.r.items():
                if isinstance(tok, list):
                    for tk in tok:
                        self._wait(e, tk)
                elif is_dma or tok[2] != e or e != "pe":
                    self._wait(e, tok)

    def _record(self, tok, reads, writes):
        for t in reads:
            if tok[3]:
                t.r.setdefault("dma", []).append(tok)
            else:
                t.r[tok[2]] = tok
        for t in writes:
            t.w = tok
            t.r = {}

    def op(self, e, fn, r=(), w=()):
        self._deps(e, r, w, False)
        key = self._cursem(e)
        ins = fn(self.eng[e])
        self.cnt[e] += 1
        val = self.cnt[e] - key[1] * self.EPOCH
        ins.then_inc(self.semh[key], 1)
        tok = (key, val, e, False)
        self._record(tok, r, w)
        return tok

    def dma(self, q, out, in_, r=(), w=(), **kw):
        self._deps(q, r, w, True)
        if q == "pool":
            slot = self.NDS - 8 + self.dnext_sw
            self.dnext_sw = (self.dnext_sw + 1) % 8
        else:
            slot = self.dnext
            self.dnext = (self.dnext + 1) % (self.NDS - 8)
        key = self.dsem[slot]
        if self.dval[slot] > 0:
            self._wait(q, (key, self.dval[slot], "dma", True))
        ins = self.eng[q].dma_start(out=out, in_=in_, **kw)
        self.dval[slot] += 16
        ins.then_inc(self.semh[key], 16)
        tok = (key, self.dval[slot], "dma", True)
        self._record(tok, r, w)
        self.ndma += 1
        return tok

    def barrier(self):
        lasts = {}
        for e in ("pe", "act", "dve", "pool"):
            if self.cnt[e] > 0:
                key = (e, (self.cnt[e] - 1) // self.EPOCH)
                lasts[e] = (key, self.cnt[e] - key[1] * self.EPOCH, e, False)
        for e in self.eng:
            for e2, tok in lasts.items():
                if e2 != e:
                    self._wait(e, tok)
            for slot in range(self.NDS):
                if self.dval[slot] > 0:
                    self._wait(e, (self.dsem[slot], self.dval[slot], "dma", True))

    def finish(self):
        for slot in range(self.NDS):
            if self.dval[slot] > 0:
                self._wait("sp", (self.dsem[slot], self.dval[slot], "dma", True))


def bcast_mid(ap2d, n):
    p, j = ap2d.shape
    return ap2d.unsqueeze(2).to_broadcast([p, j, n])


class Prog:
    def __init__(self, nc, phases, dbg=None):
        self.nc = nc
        self.k = KB(nc)
        self.phases = phases
        self.dbg = dbg if dbg is not None else {}
        k = self.k
        self.inp = {}
        for name, shp in INPUT_SHAPES.items():
            self.inp[name] = nc.dram_tensor(name, shp, F32, kind="ExternalInput").ap()
        self.y = nc.dram_tensor("y", [S, D], F32, kind="ExternalOutput").ap()
        self.xa = nc.dram_tensor("xa", [S, D], F32).ap()
        self.xb = nc.dram_tensor("xb", [S, D], F32).ap()
        def dram(name, shape, dt):
            kind = "ExternalOutput" if name in self.dbg else ("ExternalInput" if ("in:" + name) in self.dbg else "Internal")
            return nc.dram_tensor(name, shape, dt, kind=kind).ap()
        self.prw = dram("prw", [1408, S], F32)
        self.t_prw = [Tk() for _ in range(NG)]
        self.mxa = dram("mxa", [5 * 128, S], BF16)
        self.mxb = dram("mxb", [3 * 128, S], BF16) if "mxb" in self.dbg else None
        self.t_mxb = Tk()
        self.t_mxa = [Tk() for _ in range(NG)]
        self.t_xa = [Tk("xa%d" % i) for i in range(NG)]
        self.t_xb = [Tk("xb%d" % i) for i in range(NG)]
        self.t_x = [Tk("x%d" % i) for i in range(NG)]
        self.t_y = [Tk("y%d" % i) for i in range(NG)]
        self.ident_b = k.sb("ident_b", [128, 128], BF16)
        self.ident_f = k.sb("ident_f", [128, 128], F32)
        self.t_const = Tk("const")
        self.eps_t = k.sb("eps_t", [128, 1], F32)
        k.op("pool", lambda h: h.memset(self.ident_b[:], 1.0), w=[self.t_const])
        k.op("pool", lambda h: h.affine_select(out=self.ident_b[:], in_=self.ident_b[:], pattern=[[1, 128]],
                                               compare_op=ALU.is_equal, fill=0.0, base=0, channel_multiplier=-1),
             r=[self.t_const], w=[self.t_const])
        k.op("pool", lambda h: h.memset(self.ident_f[:], 1.0), w=[self.t_const])
        k.op("pool", lambda h: h.affine_select(out=self.ident_f[:], in_=self.ident_f[:], pattern=[[1, 128]],
                                               compare_op=ALU.is_equal, fill=0.0, base=0, channel_multiplier=-1),
             r=[self.t_const], w=[self.t_const])
        k.op("pool", lambda h: h.memset(self.eps_t[:], NORM_EPS), w=[self.t_const])
        self.gcol = k.sb("gcol", [128, 4, 8], F32)
        self.t_gcol = Tk("gcol")
        for n, (nm, l) in enumerate([("norm_mix_g", 0), ("norm_mix_g", 1), ("norm_mlp_g", 0), ("norm_mlp_g", 1)]):
            k.dma("sp", self.gcol[:, n, :], self.inp[nm][l].rearrange("(j p) -> p j", p=128),
                  w=[self.t_gcol], allow_slow_non_contiguous=True)
        self.nblk = 0
        self.dumped = set()


    def alloc_norm(self):
        k = self.k
        self.xt = [k.sb("xt%d" % i, [128, D], F32) for i in range(2)]
        self.t_xt = [Tk("xt%d" % i) for i in range(2)]
        self.xn = [k.sb("xn%d" % i, [128, D], BF16) for i in range(2)]
        self.t_xn = [Tk("xn%d" % i) for i in range(2)]
        self.junk = k.sb("junk", [128, D], BF16)
        self.t_junk = Tk("junk")
        self.stat = [k.sb("stat%d" % i, [128, 4], F32) for i in range(2)]
        self.t_stat = [Tk("stat%d" % i) for i in range(2)]
        self.hT = k.sb("hT", [128, 8, G], BF16)
        self.t_hT = [Tk("hT%d" % i) for i in range(NB)]
        self.psT = [k.ps("psT%d" % i, [128, 8, 128], BF16) for i in range(1)]
        self.t_psT = [Tk("psT%d" % i) for i in range(1)]

    def dump(self, name, ap, toks):
        if "dump" not in self.dbg or name in self.dumped:
            return
        self.dumped.add(name)
        d = self.nc.dram_tensor("d_" + name, list(ap.shape), ap.dtype, kind="ExternalOutput").ap()
        self.k.dma("sp", d, ap, r=list(toks), w=[Tk()])

    def norm_a(self, src_ap, src_tk):
        k = self.k
        i = self.nblk % 2
        self.nblk += 1
        xt, t_xt, xn, t_xn, st, t_st = self.xt[i], self.t_xt[i], self.xn[i], self.t_xn[i], self.stat[i], self.t_stat[i]
        k.dma("sp", xt[:], src_ap, r=[src_tk], w=[t_xt])
        k.op("act", lambda h: h.activation(out=self.junk[:], in_=xt[:], func=AF.Square, accum_out=st[:, 0:1]),
             r=[t_xt], w=[self.t_junk, t_st])
        k.op("act", lambda h: h.activation(out=st[:, 1:2], in_=st[:, 0:1], func=AF.Sqrt, scale=1.0 / D,
                                           bias=self.eps_t[:, 0:1]),
             r=[t_st, self.t_const], w=[t_st])
        k.op("dve", lambda h: h.reciprocal(out=st[:, 2:3], in_=st[:, 1:2]), r=[t_st], w=[t_st])
        k.op("dve", lambda h: h.tensor_scalar(out=xn[:], in0=xt[:], scalar1=st[:, 2:3], scalar2=None, op0=ALU.mult),
             r=[t_xt, t_st], w=[t_xn])
        return xn, t_xn

    def norm_b(self, xn, t_xn, gidx, blk, hT, t_hT):
        k = self.k
        pT, t_pT = self.psT[0], self.t_psT[0]
        for j in range(8):
            k.op("pe", lambda h, j=j: h.transpose(out=pT[:, j, :], in_=xn[:, j * 128:(j + 1) * 128],
                                                   identity=self.ident_b[:]),
                 r=[t_xn, self.t_const], w=[t_pT])
        k.op("dve", lambda h: h.tensor_tensor(out=hT[:, :, blk * 128:(blk + 1) * 128], in0=pT[:],
                                              in1=bcast_mid(self.gcol[:, gidx, :], 128), op=ALU.mult),
             r=[t_pT, self.t_gcol], w=[t_hT[blk]])

    def norm_block(self, src_ap, src_tk, gidx, blk, hT=None, t_hT=None):
        hT = self.hT if hT is None else hT
        t_hT = self.t_hT if t_hT is None else t_hT
        xn, t_xn = self.norm_a(src_ap, src_tk)
        self.norm_b(xn, t_xn, gidx, blk, hT, t_hT)

    def pass_ffn(self, l, src, t_src, dst, t_dst, final):
        k = self.k
        k.push_scope()
        self.alloc_norm()
        if final:
            self.gfin = k.sb("gfin", [128, D], F32)
            self.t_gfin = Tk("gfin")
            k.dma("sp", self.gfin[:], self.inp["final_norm_g"].unsqueeze(0).partition_broadcast(128),
                  w=[self.t_gfin])
        wup = k.sb("wup", [128, 8, DFF], BF16)
        wdn = k.sb("wdn", [128, 32, D], BF16)
        t_wup = [[Tk() for _ in range(4)] for _ in range(8)]
        t_wdn = [Tk() for _ in range(8)]
        for cb in range(4):
            for kc in range(8):
                k.dma("pool", wup[:, kc, cb * 1024:(cb + 1) * 1024],
                      self.inp["w_mlp_up"][l, kc * 128:(kc + 1) * 128, cb * 1024:(cb + 1) * 1024],
                      w=[t_wup[kc][cb]])
        for c4 in range(8):
            k.dma("pool", wdn[:, c4 * 4:(c4 + 1) * 4, :],
                  self.inp["w_mlp_down"][l, c4 * 512:(c4 + 1) * 512, :].rearrange("(c p) n -> p c n", p=128),
                  w=[t_wdn[c4]])
        aT = k.sb("aT", [128, 32, G], BF16)
        t_aT = [Tk() for _ in range(32)]
        rt = [k.sb("rt%d" % i, [128, G], F32) for i in range(2)]
        t_rt = [Tk() for _ in range(2)]
        psU = [k.ps("psU%d" % i, [128, G], F32) for i in range(2)]
        t_psU = [Tk() for _ in range(2)]
        psD = [k.ps("psD%d" % i, [128, 512], F32) for i in range(2)]
        t_psD = [Tk() for _ in range(2)]
        xr = [k.sb("xr%d" % i, [128, D], F32) for i in range(2)]
        t_xr = [Tk() for _ in range(2)]
        xo, t_xo = xr, t_xr
        st2 = [k.sb("st2_%d" % i, [128, 4], F32) for i in range(2)]
        t_st2 = [Tk() for _ in range(2)]
        n_o = 0
        hTs = [self.hT, k.sb("hTf2", [128, 8, G], BF16)]
        t_hTs = [self.t_hT, [Tk() for _ in range(NB)]]
        for blk in range(NB):
            self.norm_block(src[blk * 128:(blk + 1) * 128, :], t_src[0], 2 + l, blk, hTs[0], t_hTs[0])
        for g in range(NG):
            hT, t_hT = hTs[g % 2], t_hTs[g % 2]
            for c in range(32):
                ps, t_ps = psU[c % 2], t_psU[c % 2]
                for kc in range(8):
                    k.op("pe", lambda h, kc=kc, c=c, ps=ps: h.matmul(ps[:], lhsT=wup[:, kc, c * 128:(c + 1) * 128],
                                                                    rhs=hT[:, kc, :], start=(kc == 0),
                                                                    stop=(kc == 7)),
                         r=[t_wup[kc][c // 8]] + t_hT, w=[t_ps])
                r_, t_r = rt[c % 2], t_rt[c % 2]
                k.op("act", lambda h, ps=ps, r_=r_: h.activation(out=r_[:], in_=ps[:], func=AF.Relu),
                     r=[t_ps], w=[t_r])
                k.op("pool", lambda h, r_=r_, c=c: h.tensor_tensor(out=aT[:, c, :], in0=r_[:], in1=r_[:],
                                                                   op=ALU.mult),
                     r=[t_r], w=[t_aT[c]])
            for blk in range(NB):
                r0 = g * G + blk * 128
                i = n_o % 2
                n_o += 1
                if g + 1 < NG:
                    r1 = (g + 1) * G + blk * 128
                    nxn = self.norm_a(src[r1:r1 + 128, :], t_src[g + 1])
                k.dma("sp", xr[i][:], src[r0:r0 + 128, :], r=[t_src[g]], w=[t_xr[i]])
                for half in range(2):
                    ps, t_ps = psD[half], t_psD[half]
                    for c in range(32):
                        k.op("pe", lambda h, c=c, ps=ps, half=half, blk=blk: h.matmul(
                            ps[:], lhsT=aT[:, c, blk * 128:(blk + 1) * 128],
                            rhs=wdn[:, c, half * 512:(half + 1) * 512], start=(c == 0), stop=(c == 31)),
                             r=[t_aT[c], t_wdn[c // 4]], w=[t_ps])
                    k.op("dve", lambda h, ps=ps, half=half, i=i: h.tensor_tensor(
                        out=xo[i][:, half * 512:(half + 1) * 512], in0=ps[:],
                        in1=xr[i][:, half * 512:(half + 1) * 512], op=ALU.add),
                         r=[t_ps, t_xr[i]], w=[t_xr[i]])
                if final:
                    st, t_st = st2[i], t_st2[i]
                    k.op("act", lambda h, i=i, st=st: h.activation(out=self.junk[:], in_=xo[i][:], func=AF.Square,
                                                                   accum_out=st[:, 0:1]),
                         r=[t_xo[i]], w=[self.t_junk, t_st])
                    k.op("act", lambda h, st=st: h.activation(out=st[:, 1:2], in_=st[:, 0:1], func=AF.Sqrt,
                                                              scale=1.0 / D, bias=self.eps_t[:, 0:1]),
                         r=[t_st, self.t_const], w=[t_st])
                    k.op("dve", lambda h, st=st: h.reciprocal(out=st[:, 2:3], in_=st[:, 1:2]), r=[t_st], w=[t_st])
                    k.op("dve", lambda h, i=i, st=st: h.scalar_tensor_tensor(
                        out=xo[i][:], in0=xo[i][:], scalar=st[:, 2:3], in1=self.gfin[:], op0=ALU.mult, op1=ALU.mult),
                         r=[t_xo[i], t_st, self.t_gfin], w=[t_xo[i]])
                k.dma("sp", dst[r0:r0 + 128, :], xo[i][:], r=[t_xo[i]], w=[t_dst[g]])
                if g + 1 < NG:
                    self.norm_b(nxn[0], nxn[1], 2 + l, blk, hTs[(g + 1) % 2], t_hTs[(g + 1) % 2])
        k.pop_scope()


    def pass_mixa(self, l, src, t_src):
        k = self.k
        k.push_scope()
        self.alloc_norm()
        inp = self.inp
        lambda_init = 0.8 - 0.6 * math.exp(-0.3 * l)
        win = k.sb("win", [128, 8, NIN], BF16)
        wb_edges = [0, 768, 1152, 2304, NIN]
        t_win = [[Tk() for _ in range(4)] for _ in range(8)]
        for bi in (0, 2, 3, 1):
            c0_, c1_ = wb_edges[bi], wb_edges[bi + 1]
            for kc in range(8):
                k.dma("pool", win[:, kc, c0_:c1_], inp["w_in"][l, kc * 128:(kc + 1) * 128, c0_:c1_],
                      w=[t_win[kc][bi]])

        def wblk(c):
            col = c * 128
            return 0 if col < 768 else (1 if col < 1152 else (2 if col < 2304 else 3))
        cst = k.sb("cst", [128, 8], F32)
        t_cst = Tk()
        lq = k.sb("lq", [128, 4, 32], F32)
        t_lq = Tk()
        for i, nm in enumerate(["lam_q1", "lam_k1", "lam_q2", "lam_k2"]):
            k.dma("sp", lq[:, i, :], inp[nm][l:l + 1, :].partition_broadcast(128), w=[t_lq])
        k.op("dve", lambda h: h.tensor_tensor(out=lq[:, 0, :], in0=lq[:, 0, :], in1=lq[:, 1, :], op=ALU.mult),
             r=[t_lq], w=[t_lq])
        k.op("dve", lambda h: h.tensor_tensor(out=lq[:, 2, :], in0=lq[:, 2, :], in1=lq[:, 3, :], op=ALU.mult),
             r=[t_lq], w=[t_lq])
        k.op("dve", lambda h: h.tensor_reduce(out=cst[:, 0:1], in_=lq[:, 0, :], axis=AX.X, op=ALU.add),
             r=[t_lq], w=[t_cst])
        k.op("dve", lambda h: h.tensor_reduce(out=cst[:, 1:2], in_=lq[:, 2, :], axis=AX.X, op=ALU.add),
             r=[t_lq], w=[t_cst])
        k.op("act", lambda h: h.activation(out=cst[:, 2:4], in_=cst[:, 0:2], func=AF.Exp), r=[t_cst], w=[t_cst])
        k.op("dve", lambda h: h.tensor_tensor(out=cst[:, 4:5], in0=cst[:, 2:3], in1=cst[:, 3:4], op=ALU.subtract),
             r=[t_cst], w=[t_cst])
        k.op("dve", lambda h: h.tensor_scalar(out=cst[:, 5:6], in0=cst[:, 4:5], scalar1=float(lambda_init),
                                              scalar2=None, op0=ALU.add), r=[t_cst], w=[t_cst])
        k.op("pool", lambda h: h.memset(cst[:, 6:7], NORM_EPS), w=[t_cst])
        subg = k.sb("subg", [128, 64], F32)
        t_subg = Tk()
        k.dma("sp", subg[:], inp["subln_g"][l:l + 1, :].partition_broadcast(128), w=[t_subg])
        k.op("dve", lambda h: h.tensor_scalar(out=subg[:], in0=subg[:], scalar1=float(1.0 - lambda_init),
                                              scalar2=None, op0=ALU.mult), r=[t_subg], w=[t_subg])
        cw = k.sb("cw", [128, 2, 3], F32)
        t_cw = Tk()
        for j in range(2):
            k.dma("sp", cw[:, j, :], inp["conv_w"][l, :, j * 128:(j + 1) * 128].rearrange("t p -> p t"),
                  w=[t_cw], allow_slow_non_contiguous=True)
        mask = k.sb("mask", [128, 128], BF16)
        t_mask = Tk()
        k.op("pool", lambda h: h.memset(mask[:], 1.0), w=[t_mask])
        k.op("pool", lambda h: h.affine_select(out=mask[:], in_=mask[:], pattern=[[1, 128]], compare_op=ALU.is_ge,
                                               fill=0.0, base=0, channel_multiplier=-1), r=[t_mask], w=[t_mask])
        kT = k.sb("kT", [128, 3, S], BF16)
        t_kT = [[Tk() for _ in range(NG)] for _ in range(3)]
        Vt = k.sb("Vt", [128, S // 128, 6, 65], BF16)
        t_V = [Tk() for _ in range(S // 128)]
        for b8 in range(0, S // 128, 8):
            k.op("pool", lambda h, b8=b8: h.memset(Vt[:, b8:b8 + 8], 1.0), w=t_V[b8:b8 + 8])
        qT = k.sb("qT", [128, 3, G], BF16)
        t_qT = [Tk() for _ in range(3)]
        cv = k.sb("cv", [128, 6, G], F32)
        t_cv = [Tk() for _ in range(6)]
        zb = k.sb("zb", [128, 2, G + 2], F32)
        t_zb = [Tk() for _ in range(2)]
        k.op("pool", lambda h: h.memset(zb[:], 0.0), w=t_zb)
        ycv = k.sb("ycv", [128, G], F32)
        t_ycv = Tk()
        stg = [k.sb("stg%d" % i, [128, G], F32) for i in range(2)]
        t_stg = [Tk() for _ in range(2)]
        eT = [k.sb("eT%d" % i, [128, G], BF16) for i in range(4)]
        t_eT = [Tk() for _ in range(4)]
        uT = k.sb("uT", [128, 2, G], F32)
        t_uT = [Tk() for _ in range(2)]
        oat = k.sb("oat", [128, NB, 6, 64], F32)
        t_oat = Tk()
        osq = k.sb("osq", [128, NB, 6, 64], F32)
        t_osq = Tk()
        t1 = k.sb("t1", [128, NB, 64], F32)
        t_t1 = Tk()
        rl = k.sb("rl", [128, 2, NB], F32)
        t_rl = Tk()
        ss = k.sb("ss", [128, 3, NB * 6], F32)
        t_ss = Tk()
        oab = k.sb("oab", [128, NB, 384], BF16)
        t_oab = Tk()
        mixA = k.sb("mixA", [128, 5, G], BF16)
        t_mixA = [Tk() for _ in range(5)]
        NPA = 4
        psA = [k.ps("psA%d" % i, [128, G], F32) for i in range(NPA)]
        t_psA = [Tk() for _ in range(NPA)]
        psO = [k.ps("psO%d" % i, [128, G], F32) for i in range(2)]
        t_psO = [Tk() for _ in range(2)]
        psTr1 = k.ps("psTr", [128, NB, 65], F32)
        t_psTr1 = Tk()
        hTs = [self.hT, k.sb("hT2", [128, 8, G], BF16)]
        t_hTs = [self.t_hT, [Tk() for _ in range(NB)]]
        qTs = [qT, k.sb("qT2", [128, 3, G], BF16)]
        t_qTs = [t_qT, [Tk() for _ in range(3)]]
        cvs = [cv, k.sb("cv2", [128, 6, G], F32)]
        t_cvs = [t_cv, [Tk() for _ in range(6)]]
        cnt = {"A": 0, "S": 0, "E": 0}
        sc = 1.0 / math.sqrt(32.0)

        def nbank():
            i = cnt["A"] % NPA
            cnt["A"] += 1
            return psA[i], t_psA[i]

        def front(g):
            hT, t_hT = hTs[g % 2], t_hTs[g % 2]
            qT_, t_qT_ = qTs[g % 2], t_qTs[g % 2]
            cv_, t_cv_ = cvs[g % 2], t_cvs[g % 2]
            for blk in range(NB):
                r0 = g * G + blk * 128
                self.norm_block(src[r0:r0 + 128, :], t_src[g], l, blk, hT, t_hT)
                yield
            for c in list(range(0, 6)) + list(range(9, 26)):
                ps, t_ps = nbank()
                for kc in range(8):
                    k.op("pe", lambda h, kc=kc, c=c, ps=ps: h.matmul(ps[:], lhsT=win[:, kc, c * 128:(c + 1) * 128],
                                                                    rhs=hT[:, kc, :], start=(kc == 0),
                                                                    stop=(kc == 7)),
                         r=[t_win[kc][wblk(c)]] + t_hT, w=[t_ps])
                if c < 3:
                    k.op("dve", lambda h, ps=ps, c=c: h.tensor_copy(out=qT_[:, c, :], in_=ps[:]),
                         r=[t_ps], w=[t_qT_[c]])
                elif c < 6:
                    k.op("dve", lambda h, ps=ps, c=c: h.tensor_copy(out=kT[:, c - 3, g * G:(g + 1) * G], in_=ps[:]),
                         r=[t_ps], w=[t_kT[c - 3][g]])
                elif c < 15:
                    k.op("dve", lambda h, ps=ps, c=c: h.tensor_copy(out=cv_[:, c - 9, :], in_=ps[:]),
                         r=[t_ps], w=[t_cv_[c - 9]])
                else:
                    i = cnt["S"] % 2
                    cnt["S"] += 1
                    k.op("dve", lambda h, ps=ps, i=i: h.tensor_copy(out=stg[i][:], in_=ps[:]), r=[t_ps], w=[t_stg[i]])
                    k.dma("sp", self.prw[(c - 15) * 128:(c - 14) * 128, g * G:(g + 1) * G], stg[i][:],
                          r=[t_stg[i]], w=[self.t_prw[g]])
                yield
            for blk in range(NB):
                ps, t_ps = nbank()
                for kc in range(8):
                    k.op("pe", lambda h, kc=kc, ps=ps, blk=blk: h.matmul(
                        ps[:, 0:384], lhsT=hT[:, kc, blk * 128:(blk + 1) * 128], rhs=win[:, kc, 768:1152],
                        start=(kc == 0), stop=(kc == 7)), r=[t_win[kc][1], t_hT[blk]], w=[t_ps])
                k.op("dve", lambda h, ps=ps, blk=blk: h.tensor_copy(
                    out=Vt[:, g * NB + blk, :, 0:64], in_=ps[:, 0:384].rearrange("p (a b) -> p a b", a=6)),
                     r=[t_ps], w=[t_V[g * NB + blk]])
                yield
            for j in range(2):
                k.op("pool", lambda h, j=j: h.tensor_tensor(out=zb[:, j, 2:G + 2], in0=cv_[:, 2 + j, :],
                                                            in1=cv_[:, 4 + j, :], op=ALU.mult),
                     r=[t_cv_[2 + j], t_cv_[4 + j]], w=[t_zb[j]])
                k.op("pool", lambda h, j=j: h.tensor_scalar(out=ycv[:], in0=zb[:, j, 0:G], scalar1=cw[:, j, 0:1],
                                                            scalar2=None, op0=ALU.mult),
                     r=[t_zb[j], t_cw], w=[t_ycv])
                k.op("dve", lambda h, j=j: h.scalar_tensor_tensor(out=ycv[:], in0=zb[:, j, 1:G + 1],
                                                                  scalar=cw[:, j, 1:2], in1=ycv[:],
                                                                  op0=ALU.mult, op1=ALU.add),
                     r=[t_zb[j], t_cw, t_ycv], w=[t_ycv])
                k.op("dve", lambda h, j=j: h.scalar_tensor_tensor(out=ycv[:], in0=zb[:, j, 2:G + 2],
                                                                  scalar=cw[:, j, 2:3], in1=ycv[:],
                                                                  op0=ALU.mult, op1=ALU.add),
                     r=[t_zb[j], t_cw, t_ycv], w=[t_ycv])
                k.op("pool", lambda h, j=j: h.tensor_tensor(out=mixAs[g % 2][:, 3 + j, :], in0=ycv[:],
                                                            in1=cv_[:, j, :], op=ALU.mult),
                     r=[t_ycv, t_cv_[j]], w=[t_mixAs[g % 2][3 + j]])
                k.op("pool", lambda h, j=j: h.tensor_copy(out=zb[:, j, 0:2], in_=zb[:, j, G:G + 2]),
                     r=[t_zb[j]], w=[t_zb[j]])
                yield

        def back(g):
            qT_, t_qT_ = qTs[g % 2], t_qTs[g % 2]
            mixA_, t_mixA_ = mixAs[g % 2], t_mixAs[g % 2]
            nkb = 4 * g + 4
            LOOK = 1

            def emit_S(hd, half, j):
                qc = hd // 2
                pb = (hd % 2) * 64 + half * 32
                off = max(0, j - 4 * g) * 128
                ps, t_ps = nbank()
                k.op("pe", lambda h: h.matmul(
                    ps[:, off:G], lhsT=kT[pb:pb + 32, qc, j * 128:(j + 1) * 128],
                    rhs=qT_[pb:pb + 32, qc, off:G], start=True, stop=True, tile_position=(pb, 0)),
                     r=[t_kT[qc][j // NB], t_qT_[qc]], w=[t_ps])
                e, t_e = eT[cnt["E"] % 4], t_eT[cnt["E"] % 4]
                cnt["E"] += 1
                k.op("act", lambda h: h.activation(out=e[:, off:G], in_=ps[:, off:G], func=AF.Exp, scale=sc),
                     r=[t_ps], w=[t_e])
                if j >= 4 * g:
                    k.op("pool", lambda h: h.tensor_tensor(out=e[:, off:off + 128], in0=e[:, off:off + 128],
                                                           in1=mask[:], op=ALU.mult), r=[t_e, t_mask], w=[t_e])
                return (hd, half, j, off, e, t_e)

            def emit_PV(item):
                hd, half, j, off, e, t_e = item
                k.op("pe", lambda h: h.matmul(psO[half][0:65, off:G], lhsT=Vt[:, j, hd, :], rhs=e[:, off:G],
                                              start=(j == 0), stop=(j == nkb - 1)),
                     r=[t_e, t_V[j]], w=[t_psO[half]])
                if j != nkb - 1:
                    return
                k.op("dve", lambda h: h.tensor_copy(out=uT[0:65, half, :], in_=psO[half][0:65, :]),
                     r=[t_psO[half]], w=[t_uT[half]])
                for blk in range(NB):
                    k.op("pe", lambda h, blk=blk: h.transpose(
                        out=psTr1[:, blk, :], in_=uT[0:65, half, blk * 128:(blk + 1) * 128],
                        identity=self.ident_f[0:65, 0:65]), r=[t_uT[half], self.t_const], w=[t_psTr1])
                k.op("dve", lambda h: h.reciprocal(out=rl[:, half, :], in_=psTr1[:, :, 64]),
                     r=[t_psTr1], w=[t_rl])
                if half == 0:
                    k.op("dve", lambda h: h.tensor_tensor(out=t1[:], in0=psTr1[:, :, 0:64],
                                                          in1=bcast_mid(rl[:, 0, :], 64), op=ALU.mult),
                         r=[t_psTr1, t_rl], w=[t_t1])
                    return
                k.op("dve", lambda h: h.tensor_scalar(out=rl[:, 1, :], in0=rl[:, 1, :], scalar1=cst[:, 5:6],
                                                      scalar2=None, op0=ALU.mult), r=[t_rl, t_cst], w=[t_rl])
                k.op("dve", lambda h: h.tensor_tensor(out=oat[:, :, hd, :], in0=psTr1[:, :, 0:64],
                                                      in1=bcast_mid(rl[:, 1, :], 64), op=ALU.mult),
                     r=[t_psTr1, t_rl], w=[t_oat])
                k.op("dve", lambda h: h.tensor_tensor(out=oat[:, :, hd, :], in0=t1[:], in1=oat[:, :, hd, :],
                                                      op=ALU.subtract), r=[t_t1, t_oat], w=[t_oat])

            pend = []
            for hd in range(6):
                for j in range(nkb):
                    for half in range(2):
                        pend.append(emit_S(hd, half, j))
                    while len(pend) > 2 * LOOK:
                        emit_PV(pend.pop(0))
                    yield
            while pend:
                emit_PV(pend.pop(0))
            yield
            k.op("pool", lambda h: h.tensor_tensor(out=osq[:], in0=oat[:], in1=oat[:], op=ALU.mult),
                 r=[t_oat], w=[t_osq])
            k.op("dve", lambda h: h.tensor_reduce(out=ss[:, 0, :], in_=osq[:].rearrange("p a b c -> p (a b) c"),
                                                  axis=AX.X, op=ALU.add), r=[t_osq], w=[t_ss])
            k.op("act", lambda h: h.activation(out=ss[:, 1, :], in_=ss[:, 0, :], func=AF.Sqrt, scale=1.0 / 64.0,
                                               bias=cst[:, 6:7]), r=[t_ss, t_cst], w=[t_ss])
            k.op("dve", lambda h: h.reciprocal(out=ss[:, 2, :], in_=ss[:, 1, :]), r=[t_ss], w=[t_ss])
            k.op("dve", lambda h: h.tensor_tensor(out=osq[:].rearrange("p a b c -> p (a b) c"),
                                                  in0=oat[:].rearrange("p a b c -> p (a b) c"),
                                                  in1=bcast_mid(ss[:, 2, :], 64), op=ALU.mult),
                 r=[t_oat, t_ss], w=[t_osq])
            k.op("dve", lambda h: h.tensor_tensor(
                out=oab[:].rearrange("p a (b c) -> p (a b) c", c=64), in0=osq[:].rearrange("p a b c -> p (a b) c"),
                in1=subg[:].unsqueeze(1).to_broadcast([128, NB * 6, 64]), op=ALU.mult),
                 r=[t_osq, t_subg], w=[t_oab])
            yield
            pT, t_pT = self.psT[0], self.t_psT[0]
            for blk in range(NB):
                for c in range(3):
                    k.op("pe", lambda h, blk=blk, c=c: h.transpose(out=pT[:, c, :],
                                                                   in_=oab[:, blk, c * 128:(c + 1) * 128],
                                                                   identity=self.ident_b[:]),
                         r=[t_oab, self.t_const], w=[t_pT])
                k.op("dve", lambda h, blk=blk: h.tensor_copy(out=mixA_[:, 0:3, blk * 128:(blk + 1) * 128],
                                                             in_=pT[:, 0:3, :]),
                     r=[t_pT], w=t_mixA_[0:3])
                yield
            k.dma("sp", self.mxa[:, g * G:(g + 1) * G].rearrange("(c p) t -> p c t", p=128), mixA_[:],
                  r=t_mixA_, w=[self.t_mxa[g]])

        mixAs = [mixA, k.sb("mixA2", [128, 5, G], BF16)]
        t_mixAs = [t_mixA, [Tk() for _ in range(5)]]
        for _ in front(0):
            pass
        for g in range(NG):
            bk = back(g)
            fr = front(g + 1) if g + 1 < NG else iter(())
            n_att = 6 * (4 * g + 4)
            n_fr = 4 + 23 + 4 + 2
            ratio = max(1, n_att // n_fr)
            fr_done = False
            i = 0
            for _ in bk:
                i += 1
                if not fr_done and i % ratio == 0:
                    try:
                        next(fr)
                    except StopIteration:
                        fr_done = True
            for _ in fr:
                pass
        k.pop_scope()

    def pass_mixb(self, l, src, t_src, dst, t_dst):
        k = self.k
        k.push_scope()
        inp = self.inp
        C = 128
        c0 = math.exp(-0.5)
        wout = k.sb("wout", [128, 8, D], BF16)
        t_wout = [Tk() for _ in range(8)]
        for kc in range(8):
            k.dma("pool", wout[:, kc, :], inp["w_out"][l, kc * 128:(kc + 1) * 128, :], w=[t_wout[kc]])
        waup = k.sb("waup", [128, 384], BF16)
        gup = k.sb("gup", [128, 384], BF16)
        t_lw = Tk()
        k.dma("pool", waup[0:64, :], inp["rwkv_w_up"][l], w=[t_lw])
        k.dma("pool", waup[64:128, :], inp["rwkv_a_up"][l], w=[t_lw])
        k.dma("pool", gup[:], inp["rwkv_g_up"][l], w=[t_lw])
        pc = k.sb("pc", [128, 8, 3], F32)
        t_pc = Tk()
        for n, nm in enumerate(["rwkv_w0", "rwkv_a0", "rwkv_k_k", "rwkv_k_a", "rwkv_r_k"]):
            srcap = inp[nm][l]
            if nm == "rwkv_r_k":
                srcap = srcap.rearrange("a b -> (a b)")
            k.dma("sp", pc[:, n, :], srcap.rearrange("(j p) -> p j", p=128), w=[t_pc],
                  allow_slow_non_contiguous=True)
        k.op("dve", lambda h: h.tensor_scalar(out=pc[:, 5, :], in0=pc[:, 3, :], scalar1=-1.0, scalar2=1.0,
                                              op0=ALU.mult, op1=ALU.add), r=[t_pc], w=[t_pc])
        k.op("pool", lambda h: h.memset(pc[:, 6, :], GN_EPS), w=[t_pc])
        mu = k.sb("mu", [128, 11], F32)
        t_mu = Tk()
        k.dma("sp", mu[:], inp["shift_mu"][l].rearrange("(j p) -> p j", p=128), w=[t_mu],
              allow_slow_non_contiguous=True)
        lng = k.sb("lng", [128, 2, 384], F32)
        t_lng = Tk()
        k.dma("sp", lng[:, 0, :], inp["lnx_g"][l:l + 1, :].partition_broadcast(128), w=[t_lng])
        k.dma("sp", lng[:, 1, :], inp["lnx_b"][l:l + 1, :].partition_broadcast(128), w=[t_lng])
        bones = k.sb("bones", [128, 128], BF16)
        t_bones = Tk()
        k.op("pool", lambda h: h.memset(bones[:], 0.0), w=[t_bones])
        k.op("pool", lambda h: h.memset(bones[0:64, 0:64], 1.0), w=[t_bones])
        k.op("pool", lambda h: h.memset(bones[64:128, 64:128], 1.0), w=[t_bones])
        msk = k.sb("msk", [128, 6, 128], F32)
        t_msk = Tk()
        k.op("pool", lambda h: h.memset(msk[:], 1.0), w=[t_msk])
        k.op("pool", lambda h: h.affine_select(out=msk[:, 0, :], in_=msk[:, 0, :], pattern=[[1, 128]],
                                               compare_op=ALU.is_ge, fill=0.0, base=-1, channel_multiplier=-1),
             r=[t_msk], w=[t_msk])
        k.op("pool", lambda h: h.affine_select(out=msk[:, 1, :], in_=msk[:, 1, :], pattern=[[1, 128]],
                                               compare_op=ALU.is_ge, fill=0.0, base=0, channel_multiplier=-1),
             r=[t_msk], w=[t_msk])
        k.op("pool", lambda h: h.affine_select(out=msk[:, 2, :], in_=msk[:, 2, :], pattern=[[-1, 128]],
                                               compare_op=ALU.is_ge, fill=0.0, base=-1, channel_multiplier=1),
             r=[t_msk], w=[t_msk])
        k.op("pool", lambda h: h.memset(msk[:, 3:6, :], 0.0), w=[t_msk])
        k.op("pool", lambda h: h.tensor_copy(out=msk[0:64, 3:5, 0:64], in_=msk[0:64, 0:3:2, 0:64]),
             r=[t_msk], w=[t_msk])
        k.op("pool", lambda h: h.tensor_copy(out=msk[64:128, 3:5, 64:128], in_=msk[64:128, 0:3:2, 64:128]),
             r=[t_msk], w=[t_msk])
        k.op("pool", lambda h: h.memset(msk[0:64, 5, 64:128], 1.0), r=[t_msk], w=[t_msk])

        GB = 256
        NCH = GB // C
        NGB = S // GB
        NQ = S // C
        NR = 3
        rmask = k.sb("rmask", [128, GB], F32)
        t_rmask = Tk()
        k.op("pool", lambda h: h.memset(rmask[:], 1.0), w=[t_rmask])
        k.op("pool", lambda h: h.memset(rmask[:].rearrange("p (c t) -> p c t", t=C)[:, :, 0:1], 0.0), w=[t_rmask])
        F3 = [128, 3, GB]
        pt = k.sb("pt", [128, 11, GB], F32); t_pt = Tk()
        halo = k.sb("halo", [128, 11, 1], F32); t_halo = Tk()
        k.op("pool", lambda h: h.memset(halo[:], 0.0), w=[t_halo])
        z = k.sb("z", [128, 11, GB], F32); t_z = [Tk()]
        sig = k.sb("sig", F3, F32); t_sig = [Tk() for _ in range(3)]
        aa = k.sb("aa", F3, F32); t_aa = [Tk() for _ in range(3)]
        kkn = k.sb("kkn", F3, F32); t_kkn = [Tk() for _ in range(3)]
        kmod = k.sb("kmod", F3, F32); t_kmod = [Tk() for _ in range(3)]
        beta = k.sb("beta", F3, F32); t_beta = [Tk() for _ in range(3)]
        cs = k.sb("cs", F3, F32); t_cs = [Tk() for _ in range(3)]
        tA = pt[:, 0:3, :]; t_tA = [Tk() for _ in range(3)]
        tB = pt[:, 3:6, :]; t_tB = [Tk() for _ in range(3)]
        tC = pt[:, 6:9, :]; t_tC = [Tk() for _ in range(3)]
        tD = k.sb("tD", F3, F32); t_tD = [Tk() for _ in range(3)]
        tE = k.sb("tE", F3, F32); t_tE = [Tk() for _ in range(3)]
        t_alias = t_tA + t_tB + t_tC
        b16 = k.sb("b16", F3, BF16); t_b16 = [Tk() for _ in range(3)]
        twa = k.sb("twa", [128, GB], BF16); t_twa = Tk()
        sg = k.sb("sg", [128, GB], BF16); t_sg = Tk()
        def gset(i):
            d = {}
            for nm in ["rt", "kt", "bt", "at", "k2", "b2"]:
                d[nm] = k.sb(nm + str(i), F3, BF16); d["t_" + nm] = [Tk() for _ in range(3)]
            for nm in ["atp", "rtp", "btp"]:
                d[nm] = k.sb(nm + str(i), [128, 3, NCH, 2, C], BF16); d["t_" + nm] = [Tk() for _ in range(3)]
                k.op("pool", lambda h, t_=d[nm]: h.memset(t_[:], 0.0), w=d["t_" + nm])
            d["WC"] = k.sb("WC%d" % i, [128, 3, NCH], F32); d["t_WC"] = [Tk() for _ in range(3)]
            d["Vg"] = k.sb("Vg%d" % i, [128, NCH, 384], BF16); d["t_Vg"] = [Tk() for _ in range(NCH)]
            return d

        def pset(i):
            d = {}
            d["gT"] = k.sb("gT%d" % i, F3, BF16); d["t_gT"] = [Tk() for _ in range(3)]
            d["bonus"] = k.sb("bonus%d" % i, F3, BF16); d["t_bonus"] = [Tk() for _ in range(3)]
            d["ysb"] = k.sb("ysb%d" % i, [128, NCH, 384], F32); d["t_ysb"] = [Tk() for _ in range(NCH)]
            return d
        NGS = 2
        GS = [gset(i) for i in range(NGS)]
        NGP = 3
        GP = [pset(i) for i in range(NGP)]
        H6 = [128, 6, 128]
        def cset(i):
            d = {}
            for nm in ["X0", "X1", "Xt0", "Xt1", "P0", "P1", "Lak", "Arb", "Ark", "X12"]:
                d[nm] = k.sb("%s_%d" % (nm, i), H6, BF16); d["t_" + nm] = Tk()
            d["kbtok"] = k.sb("kbtok%d" % i, [128, 2, 384], BF16); d["t_kbtok"] = Tk()
            return d
        CS = [cset(i) for i in range(NR)]
        r0b = k.sb("r0b", [128, 384], BF16); t_r0b = Tk()
        Ub = k.sb("Ub", [128, 384], BF16); t_Ub = Tk()
        u1b = k.sb("u1b", [128, 384], BF16); t_u1b = Tk()
        u2b = k.sb("u2b", [128, 384], BF16); t_u2b = Tk()
        Tst = k.sb("Tst", [128, 3, 128], F32); t_T = Tk()
        Tbd = k.sb("Tbd", [128, 3, 128], BF16); t_Tb = Tk()
        k.op("pool", lambda h: h.memset(Tst[:], 0.0), w=[t_T])
        k.op("pool", lambda h: h.memset(Tbd[:], 0.0), w=[t_Tb])
        ysq = k.sb("ysq", [128, NCH, 384], F32); t_ysq = Tk()
        gst = k.sb("gst", [128, 6, NCH * 6], F32); t_gst = Tk()
        potmp = [k.sb("potmp%d" % i, [128, GB], F32) for i in range(2)]; t_potmp = [Tk() for _ in range(2)]
        mixT = k.sb("mixT", [128, 8, GB], BF16); t_mixT = [Tk() for _ in range(8)]
        xr = [k.sb("xrb0", [128, D], F32)] * 2; t_xr = [Tk()] * 2
        NPD = 3
        psD = [k.ps("psD%d" % i, [128, 1024], F32) for i in range(NPD)]
        t_psD = [Tk() for _ in range(NPD)]
        psX = k.ps("psX", [128, 512], F32)
        t_psX = [Tk() for _ in range(2)]
        psTb = k.ps("psTb", [128, 2, 384], BF16)
        t_psTb = Tk()
        st = {"n": 0, "x": 0, "s": 0, "pt": 0}

        def nps():
            return psX[:, 0:256], t_psX[0]

        def npd():
            i = st["n"] % NPD
            st["n"] += 1
            return psD[i], t_psD[i]

        def b3(col):
            return bcast_mid(col, GB)

        def h6(ps):
            return ps[:, 0:768].rearrange("p (a b) -> p a b", a=6)

        def pre_head(gb):
            gs = slice(gb * GB, (gb + 1) * GB)
            tprw = self.t_prw[(gb * GB) // G]
            k.dma("sp", pt[:], self.prw[:, gs].rearrange("(c p) t -> p c t", p=128), r=[tprw],
                  w=[t_pt] + t_alias)
            yield
            k.op("pool", lambda h: h.tensor_tensor(out=z[:, :, 1:GB], in0=pt[:, :, 0:GB - 1], in1=pt[:, :, 1:GB],
                                                   op=ALU.subtract), r=[t_pt] + t_alias, w=t_z)
            k.op("pool", lambda h: h.tensor_tensor(out=z[:, :, 0:1], in0=halo[:], in1=pt[:, :, 0:1],
                                                   op=ALU.subtract), r=[t_pt, t_halo] + t_alias, w=t_z)
            yield
            k.op("dve", lambda h: h.tensor_tensor(out=z[:], in0=z[:], in1=bcast_mid(mu[:], GB), op=ALU.mult),
                 r=t_z + [t_mu], w=t_z)
            yield
            k.op("dve", lambda h: h.tensor_tensor(out=z[:], in0=z[:], in1=pt[:], op=ALU.add),
                 r=t_z + [t_pt] + t_alias, w=t_z)
            k.op("pool", lambda h: h.tensor_copy(out=halo[:], in_=pt[:, :, GB - 1:GB]), r=[t_pt] + t_alias,
                 w=[t_halo])
            yield
            k.op("act", lambda h: h.activation(out=twa[0:64, :], in_=z[0:64, 9, :], func=AF.Tanh), r=t_z, w=[t_twa])
            k.op("act", lambda h: h.activation(out=twa[64:128, :], in_=z[64:128, 9, :], func=AF.Copy),
                 r=t_z, w=[t_twa])
            k.op("act", lambda h: h.activation(out=sg[:], in_=z[:, 10, :], func=AF.Sigmoid), r=t_z, w=[t_sg])
            yield

        def pre_fc(gb, fc):
            d = GS[gb % NGS]
            dp = GP[gb % NGP]
            zr, zk, zv = z[:, fc, :], z[:, 3 + fc, :], z[:, 6 + fc, :]
            tz = t_z
            A_, B_, C_, D_, E_ = tA[:, fc, :], tB[:, fc, :], tC[:, fc, :], tD[:, fc, :], tE[:, fc, :]
            tA_, tB_, tC_, tD_, tE_ = t_tA[fc], t_tB[fc], t_tC[fc], t_tD[fc], t_tE[fc]
            col = lambda n: pc[:, n, fc:fc + 1]
            fs = slice(fc * 128, (fc + 1) * 128)
            csf = cs[:, fc, :]
            csC = csf.rearrange("p (c t) -> p c t", t=C)[:, :, C - 1]
            k.op("act", lambda h: h.activation(out=A_, in_=zk, func=AF.Copy, scale=col(2)), r=tz + [t_pc], w=[tA_])
            ps, t_ps = nps()
            k.op("pe", lambda h: h.matmul(ps, lhsT=waup[0:64, fs], rhs=twa[0:64, :], start=True, stop=True,
                                          tile_position=(0, 0)), r=[t_lw, t_twa], w=[t_ps])
            k.op("act", lambda h: h.activation(out=sig[:, fc, :], in_=ps, func=AF.Sigmoid, bias=col(0)),
                 r=[t_ps, t_pc], w=[t_sig[fc]])
            ps, t_ps = nps()
            k.op("pe", lambda h: h.matmul(ps, lhsT=waup[64:128, fs], rhs=twa[64:128, :], start=True, stop=True,
                                          tile_position=(64, 0)), r=[t_lw, t_twa], w=[t_ps])
            k.op("act", lambda h: h.activation(out=aa[:, fc, :], in_=ps, func=AF.Sigmoid, bias=col(1)),
                 r=[t_ps, t_pc], w=[t_aa[fc]])
            ps, t_ps = nps()
            k.op("pe", lambda h: h.matmul(ps, lhsT=gup[:, fs], rhs=sg[:], start=True, stop=True),
                 r=[t_lw, t_sg], w=[t_ps])
            k.op("dve", lambda h: h.tensor_copy(out=dp["gT"][:, fc, :], in_=ps), r=[t_ps], w=[dp["t_gT"][fc]])
            yield
            k.op("pool", lambda h: h.tensor_tensor(out=b16[:, fc, :], in0=A_, in1=A_, op=ALU.mult),
                 r=[tA_], w=[t_b16[fc]])
            k.op("dve", lambda h: h.tensor_tensor_scan(out=csf, data0=rmask[:], data1=sig[:, fc, :], initial=0.0,
                                                       op0=ALU.mult, op1=ALU.add),
                 r=[t_rmask, t_sig[fc]], w=[t_cs[fc]])
            k.op("act", lambda h: h.activation(out=kmod[:, fc, :], in_=aa[:, fc, :], func=AF.Identity,
                                               scale=col(3), bias=col(5)), r=[t_aa[fc], t_pc], w=[t_kmod[fc]])
            yield
            ps, t_ps = nps()
            k.op("pe", lambda h: h.matmul(ps, lhsT=bones[:], rhs=b16[:, fc, :], start=True, stop=True),
                 r=[t_bones, t_b16[fc]], w=[t_ps])
            k.op("act", lambda h: h.activation(out=B_, in_=ps, func=AF.Sqrt), r=[t_ps], w=[tB_])
            k.op("pool", lambda h: h.tensor_tensor(out=kmod[:, fc, :], in0=kmod[:, fc, :], in1=zk, op=ALU.mult),
                 r=[t_kmod[fc]] + tz, w=[t_kmod[fc]])
            k.op("act", lambda h: h.activation(out=C_, in_=csf, func=AF.Exp, scale=c0), r=[t_cs[fc]], w=[tC_])
            k.op("act", lambda h: h.activation(out=D_, in_=csf, func=AF.Exp, scale=-c0), r=[t_cs[fc]], w=[tD_])
            k.op("pool", lambda h: h.tensor_tensor(
                out=E_.rearrange("p (c t) -> p c t", t=C), in0=csC.unsqueeze(2).to_broadcast([128, NCH, C]),
                in1=csf.rearrange("p (c t) -> p c t", t=C), op=ALU.subtract), r=[t_cs[fc]], w=[tE_])
            k.op("act", lambda h: h.activation(out=d["WC"][:, fc, :], in_=csC, func=AF.Exp, scale=-c0),
                 r=[t_cs[fc]], w=[d["t_WC"][fc]])
            yield
            k.op("dve", lambda h: h.tensor_scalar(out=B_, in0=B_, scalar1=1e-12, scalar2=None, op0=ALU.max),
                 r=[tB_], w=[tB_])
            k.op("dve", lambda h: h.reciprocal(out=B_, in_=B_), r=[tB_], w=[tB_])
            k.op("dve", lambda h: h.tensor_tensor(out=d["kt"][:, fc, :], in0=kmod[:, fc, :], in1=C_, op=ALU.mult),
                 r=[t_kmod[fc], tC_], w=[d["t_kt"][fc]])
            k.op("dve", lambda h: h.tensor_tensor(out=d["rt"][:, fc, :], in0=zr, in1=D_, op=ALU.mult),
                 r=tz + [tD_], w=[d["t_rt"][fc]])
            k.op("act", lambda h: h.activation(out=E_, in_=E_, func=AF.Exp, scale=-c0), r=[tE_], w=[tE_])
            yield
            k.op("pool", lambda h: h.tensor_tensor(out=kkn[:, fc, :], in0=A_, in1=B_, op=ALU.mult),
                 r=[tA_, tB_], w=[t_kkn[fc]])
            k.op("dve", lambda h: h.tensor_tensor(out=d["k2"][:, fc, :], in0=kmod[:, fc, :], in1=E_, op=ALU.mult),
                 r=[t_kmod[fc], tE_], w=[d["t_k2"][fc]])
            k.op("pool", lambda h: h.tensor_tensor(out=D_, in0=csf, in1=sig[:, fc, :], op=ALU.subtract),
                 r=[t_cs[fc], t_sig[fc], d["t_rt"][fc]], w=[tD_])
            yield
            k.op("pool", lambda h: h.tensor_tensor(out=A_, in0=zr, in1=kmod[:, fc, :], op=ALU.mult),
                 r=tz + [t_kmod[fc], t_kkn[fc]], w=[tA_])
            k.op("dve", lambda h: h.tensor_tensor(out=beta[:, fc, :], in0=kkn[:, fc, :], in1=aa[:, fc, :],
                                                  op=ALU.mult), r=[t_kkn[fc], t_aa[fc]], w=[t_beta[fc]])
            k.op("act", lambda h: h.activation(out=D_, in_=D_, func=AF.Exp, scale=-c0), r=[tD_], w=[tD_])
            yield
            k.op("act", lambda h: h.activation(out=b16[:, fc, :], in_=A_, func=AF.Copy, scale=col(4)),
                 r=[tA_, t_pc], w=[t_b16[fc]])
            k.op("dve", lambda h: h.tensor_tensor(out=d["bt"][:, fc, :], in0=beta[:, fc, :], in1=C_, op=ALU.mult),
                 r=[t_beta[fc], tC_], w=[d["t_bt"][fc]])
            k.op("dve", lambda h: h.tensor_tensor(out=d["b2"][:, fc, :], in0=beta[:, fc, :], in1=E_, op=ALU.mult),
                 r=[t_beta[fc], tE_], w=[d["t_b2"][fc]])
            k.op("dve", lambda h: h.scalar_tensor_tensor(out=d["at"][:, fc, :], in0=kkn[:, fc, :], scalar=-1.0,
                                                         in1=D_, op0=ALU.mult, op1=ALU.mult),
                 r=[t_kkn[fc], tD_], w=[d["t_at"][fc]])
            yield
            ps, t_ps = nps()
            k.op("pe", lambda h: h.matmul(ps, lhsT=bones[:], rhs=b16[:, fc, :], start=True, stop=True),
                 r=[t_bones, t_b16[fc]], w=[t_ps])
            k.op("dve", lambda h: h.tensor_tensor(out=dp["bonus"][:, fc, :], in0=ps, in1=zv, op=ALU.mult),
                 r=[t_ps] + tz, w=[dp["t_bonus"][fc]])
            for nm, pn, eng in [("at", "atp", "pool"), ("rt", "rtp", "act"), ("bt", "btp", "pool")]:
                for par in range(2):
                    hb = par * 64
                    src_ = d[nm][hb:hb + 64, fc, :].rearrange("p (c t) -> p c t", t=C)
                    dst_ = d[pn][hb:hb + 64, fc, :, par, :]
                    if eng == "pool":
                        k.op("pool", lambda h, src_=src_, dst_=dst_: h.tensor_copy(out=dst_, in_=src_),
                             r=[d["t_" + nm][fc]], w=[d["t_" + pn][fc]])
                    else:
                        k.op("act", lambda h, src_=src_, dst_=dst_: h.activation(out=dst_, in_=src_, func=AF.Copy),
                             r=[d["t_" + nm][fc]], w=[d["t_" + pn][fc]])
            yield

        def pre_tail(gb):
            d = GS[gb % NGS]
            for c in range(NCH):
                cl = slice(c * C, (c + 1) * C)
                ps, t_ps = npd()
                for fc in range(3):
                    k.op("pe", lambda h, ps=ps, fc=fc, cl=cl: h.transpose(
                        out=ps[:, fc * 128:(fc + 1) * 128], in_=z[:, 6 + fc, cl], identity=self.ident_f[:]),
                         r=t_z + [self.t_const], w=[t_ps])
                k.op("act", lambda h, ps=ps, c=c: h.activation(out=d["Vg"][:, c, :], in_=ps[:, 0:384], func=AF.Copy),
                     r=[t_ps], w=[d["t_Vg"][c]])
                yield

        def pre(gb):
            yield from pre_head(gb)
            gens = [pre_fc(gb, fc) for fc in range(3)]
            gens.append(pre_tail(gb))
            while gens:
                for g_ in list(gens):
                    try:
                        next(g_)
                    except StopIteration:
                        gens.remove(g_)
                yield

        def chunkA(q):
            gb, c = q // NCH, q % NCH
            d = GS[gb % NGS]
            e = CS[q % NR]
            cl = slice(c * C, (c + 1) * C)

            def amat(lname, pname, *dsts):
                ps, t_ps = npd()
                for fc in range(3):
                    k.op("pe", lambda h, ps=ps, fc=fc: h.matmul(
                        ps[:, fc * 256:(fc + 1) * 256], lhsT=d[lname][:, fc, cl],
                        rhs=d[pname][:, fc, c, :, :].rearrange("p a b -> p (a b)"), start=True, stop=True),
                         r=[d["t_" + lname][fc], d["t_" + pname][fc]], w=[t_ps])
                for dst, mi in dsts:
                    k.op("dve", lambda h, ps=ps, dst=dst, mi=mi: h.tensor_tensor(
                        out=e[dst][:], in0=h6(ps), in1=msk[:, mi, :].unsqueeze(1).to_broadcast([128, 6, 128]),
                        op=ALU.mult), r=[t_ps, t_msk], w=[e["t_" + dst]])
            amat("bt", "atp", ("X0", 3), ("X12", 5))
            amat("at", "btp", ("Xt0", 4))
            for n_, nm_ in enumerate(["k2", "b2"]):
                for fc in range(3):
                    k.op("pe", lambda h, n_=n_, fc=fc, nm_=nm_: h.transpose(
                        out=psTb[:, n_, fc * 128:(fc + 1) * 128], in_=d[nm_][:, fc, cl], identity=self.ident_b[:]),
                         r=[d["t_" + nm_][fc], self.t_const], w=[t_psTb])
            k.op("act", lambda h: h.activation(out=e["kbtok"][:], in_=psTb[:], func=AF.Copy),
                 r=[t_psTb], w=[e["t_kbtok"]])
            yield
            amat("kt", "atp", ("Lak", 0))
            amat("bt", "rtp", ("Arb", 1))
            amat("kt", "rtp", ("Ark", 1))
            idb = self.ident_b[:].unsqueeze(1).to_broadcast([128, 6, 128])
            k.op("pool", lambda h: h.tensor_tensor(out=e["P0"][:], in0=e["X0"][:], in1=idb, op=ALU.add),
                 r=[e["t_X0"], self.t_const], w=[e["t_P0"]])
            yield
            nsteps = 5
            for lv in range(1, nsteps + 2):
                Xc, Xtc = e["X%d" % ((lv - 1) % 2)], e["Xt%d" % ((lv - 1) % 2)]
                t_Xc, t_Xtc = e["t_X%d" % ((lv - 1) % 2)], e["t_Xt%d" % ((lv - 1) % 2)]
                Xn, Xtn = e["X%d" % (lv % 2)], e["Xt%d" % (lv % 2)]
                t_Xn, t_Xtn = e["t_X%d" % (lv % 2)], e["t_Xt%d" % (lv % 2)]
                if lv >= 2:
                    Pold, Pnew = e["P%d" % (lv % 2)], e["P%d" % ((lv - 1) % 2)]
                    t_Pold, t_Pnew = e["t_P%d" % (lv % 2)], e["t_P%d" % ((lv - 1) % 2)]
                    ps, t_ps = npd()
                    for hd in range(6):
                        k.op("pe", lambda h, ps=ps, hd=hd: h.matmul(ps[:, hd * 128:(hd + 1) * 128],
                                                                    lhsT=Xtc[:, hd, :], rhs=Pold[:, hd, :],
                                                                    start=True, stop=True),
                             r=[t_Xtc, t_Pold], w=[t_ps])
                    k.op("dve", lambda h, ps=ps: h.tensor_tensor(out=Pnew[:], in0=h6(ps), in1=Pold[:], op=ALU.add),
                         r=[t_ps, t_Pold], w=[t_Pnew])
                if lv <= nsteps:
                    ps, t_ps = npd()
                    for hd in range(6):
                        k.op("pe", lambda h, ps=ps, hd=hd: h.matmul(ps[:, hd * 128:(hd + 1) * 128],
                                                                    lhsT=Xc[:, hd, :], rhs=Xtc[:, hd, :],
                                                                    start=True, stop=True),
                             r=[t_Xc, t_Xtc], w=[t_ps])
                    k.op("act", lambda h, ps=ps: h.activation(out=Xtn[:], in_=h6(ps), func=AF.Copy),
                         r=[t_ps], w=[t_Xtn])
                    if lv < nsteps:
                        ps, t_ps = npd()
                        for hd in range(6):
                            k.op("pe", lambda h, ps=ps, hd=hd: h.matmul(ps[:, hd * 128:(hd + 1) * 128],
                                                                        lhsT=Xtc[:, hd, :], rhs=Xc[:, hd, :],
                                                                        start=True, stop=True),
                                 r=[t_Xc, t_Xtc], w=[t_ps])
                        k.op("act", lambda h, ps=ps: h.activation(out=Xn[:], in_=h6(ps), func=AF.Copy),
                             r=[t_ps], w=[t_Xn])
                yield

        def chain(q):
            gb, c = q // NCH, q % NCH
            d = GS[gb % NGS]
            e = CS[q % NR]
            cl = slice(c * C, (c + 1) * C)
            Pf, t_Pf = e["P1"], e["t_P1"]
            Vg, t_Vg = d["Vg"], d["t_Vg"]
            ps, t_ps = npd()
            for fc in range(3):
                k.op("pe", lambda h, ps=ps, fc=fc: h.matmul(ps[:, fc * 128:(fc + 1) * 128], lhsT=d["at"][:, fc, cl],
                                                            rhs=Tbd[:, fc, :], start=True, stop=False),
                     r=[d["t_at"][fc], t_Tb], w=[t_ps])
                for hd in (2 * fc, 2 * fc + 1):
                    k.op("pe", lambda h, ps=ps, hd=hd: h.matmul(ps[:, hd * 64:(hd + 1) * 64], lhsT=e["Lak"][:, hd, :],
                                                                rhs=Vg[:, c, hd * 64:(hd + 1) * 64], start=False,
                                                                stop=(hd % 2 == 1)),
                         r=[e["t_Lak"], t_Vg[c]], w=[t_ps])
            k.op("act", lambda h, ps=ps: h.activation(out=r0b[:], in_=ps[:, 0:384], func=AF.Copy), r=[t_ps], w=[t_r0b])
            yield
            ps, t_ps = npd()
            for hd in range(6):
                k.op("pe", lambda h, ps=ps, hd=hd: h.matmul(ps[:, hd * 64:(hd + 1) * 64], lhsT=Pf[:, hd, :],
                                                            rhs=r0b[:, hd * 64:(hd + 1) * 64], start=True, stop=True),
                     r=[t_Pf, t_r0b], w=[t_ps])
            k.op("act", lambda h, ps=ps: h.activation(out=u1b[:], in_=ps[:, 0:384], func=AF.Copy), r=[t_ps], w=[t_u1b])
            yield
            ps, t_ps = npd()
            for hd in range(6):
                k.op("pe", lambda h, ps=ps, hd=hd: h.matmul(ps[:, hd * 64:(hd + 1) * 64], lhsT=e["X12"][:, hd, :],
                                                            rhs=u1b[:, hd * 64:(hd + 1) * 64], start=True, stop=True),
                     r=[e["t_X12"], t_u1b], w=[t_ps])
            k.op("act", lambda h, ps=ps: h.activation(out=u2b[:], in_=ps[:, 0:384], func=AF.Copy), r=[t_ps], w=[t_u2b])
            yield
            ps, t_ps = npd()
            for hd in range(6):
                k.op("pe", lambda h, ps=ps, hd=hd: h.matmul(ps[:, hd * 64:(hd + 1) * 64], lhsT=Pf[:, hd, :],
                                                            rhs=u2b[:, hd * 64:(hd + 1) * 64], start=True, stop=True),
                     r=[t_Pf, t_u2b], w=[t_ps])
            k.op("dve", lambda h, ps=ps: h.tensor_tensor(out=Ub[:], in0=ps[:, 0:384], in1=u1b[:], op=ALU.add),
                 r=[t_ps, t_u1b], w=[t_Ub])
            yield
            ps2, t_ps2 = npd()
            kb = e["kbtok"]
            for fc in range(3):
                fs = slice(fc * 128, (fc + 1) * 128)
                k.op("pe", lambda h, fs=fs: h.matmul(ps2[:, fs], lhsT=kb[:, 1, fs], rhs=Ub[:, fs], start=True,
                                                     stop=False), r=[e["t_kbtok"], t_Ub], w=[t_ps2])
                k.op("pe", lambda h, fs=fs: h.matmul(ps2[:, fs], lhsT=kb[:, 0, fs], rhs=Vg[:, c, fs], start=False,
                                                     stop=True), r=[e["t_kbtok"], t_Vg[c]], w=[t_ps2])
            ps, t_ps = npd()
            for fc in range(3):
                k.op("pe", lambda h, ps=ps, fc=fc: h.matmul(ps[:, fc * 128:(fc + 1) * 128], lhsT=d["rt"][:, fc, cl],
                                                            rhs=Tbd[:, fc, :], start=True, stop=False),
                     r=[d["t_rt"][fc], t_Tb], w=[t_ps])
                for hd in (2 * fc, 2 * fc + 1):
                    hs = slice(hd * 64, (hd + 1) * 64)
                    k.op("pe", lambda h, ps=ps, hd=hd, hs=hs: h.matmul(ps[:, hs], lhsT=e["Arb"][:, hd, :],
                                                                       rhs=Ub[:, hs], start=False, stop=False),
                         r=[e["t_Arb"], t_Ub], w=[t_ps])
                    k.op("pe", lambda h, ps=ps, hd=hd, hs=hs: h.matmul(ps[:, hs], lhsT=e["Ark"][:, hd, :],
                                                                       rhs=Vg[:, c, hs], start=False,
                                                                       stop=(hd % 2 == 1)),
                         r=[e["t_Ark"], t_Vg[c]], w=[t_ps])
            for par in range(2):
                hb = par * 64
                tv = Tst[hb:hb + 64, :, hb:hb + 64]
                k.op("dve", lambda h, hb=hb, tv=tv: h.tensor_tensor(
                    out=tv, in0=tv, in1=bcast_mid(d["WC"][hb:hb + 64, :, c], 64), op=ALU.mult),
                     r=[t_T] + d["t_WC"], w=[t_T])
                k.op("dve", lambda h, hb=hb, tv=tv: h.tensor_tensor(
                    out=tv, in0=ps2[hb:hb + 64, 0:384].rearrange("p (f x) -> p f x", f=3)[:, :, hb:hb + 64], in1=tv,
                    op=ALU.add), r=[t_ps2, t_T], w=[t_T])
                k.op("pool", lambda h, hb=hb, tv=tv: h.tensor_copy(out=Tbd[hb:hb + 64, :, hb:hb + 64], in_=tv),
                     r=[t_T], w=[t_Tb])
            k.op("act", lambda h, ps=ps: h.activation(out=GP[gb % NGP]["ysb"][:, c, :], in_=ps[:, 0:384],
                                                      func=AF.Copy), r=[t_ps], w=[GP[gb % NGP]["t_ysb"][c]])
            yield

        def post(gb):
            d = GP[gb % NGP]
            gs = slice(gb * GB, (gb + 1) * GB)
            ysb, t_ysb = d["ysb"], d["t_ysb"]
            k.dma("sp", mixT[:, 0:5, :], self.mxa[:, gs].rearrange("(c p) t -> p c t", p=128),
                  r=[self.t_mxa[(gb * GB) // G]], w=t_mixT[0:5])
            yv = ysb[:].rearrange("p c (a b) -> p (c a) b", b=64)
            qv = ysq[:].rearrange("p c (a b) -> p (c a) b", b=64)
            k.op("dve", lambda h: h.tensor_reduce(out=gst[:, 0, :], in_=yv, axis=AX.X, op=ALU.add),
                 r=t_ysb, w=[t_gst])
            k.op("pool", lambda h: h.tensor_tensor(out=ysq[:], in0=ysb[:], in1=ysb[:], op=ALU.mult),
                 r=t_ysb, w=[t_ysq])
            yield
            k.op("dve", lambda h: h.tensor_reduce(out=gst[:, 1, :], in_=qv, axis=AX.X, op=ALU.add),
                 r=[t_ysq], w=[t_gst])
            k.op("dve", lambda h: h.tensor_scalar(out=gst[:, 0:2, :], in0=gst[:, 0:2, :], scalar1=1.0 / 64.0,
                                                  scalar2=None, op0=ALU.mult), r=[t_gst], w=[t_gst])
            k.op("dve", lambda h: h.tensor_tensor(out=gst[:, 2, :], in0=gst[:, 0, :], in1=gst[:, 0, :], op=ALU.mult),
                 r=[t_gst], w=[t_gst])
            k.op("dve", lambda h: h.tensor_tensor(out=gst[:, 3, :], in0=gst[:, 1, :], in1=gst[:, 2, :],
                                                  op=ALU.subtract), r=[t_gst], w=[t_gst])
            yield
            k.op("act", lambda h: h.activation(out=gst[:, 4, :], in_=gst[:, 3, :], func=AF.Sqrt,
                                               bias=pc[:, 6, 0:1]), r=[t_gst, t_pc], w=[t_gst])
            k.op("dve", lambda h: h.reciprocal(out=gst[:, 5, :], in_=gst[:, 4, :]), r=[t_gst], w=[t_gst])
            k.op("dve", lambda h: h.tensor_tensor(out=qv, in0=yv, in1=bcast_mid(gst[:, 0, :], 64), op=ALU.subtract),
                 r=t_ysb + [t_gst, t_ysq], w=[t_ysq])
            yield
            k.op("dve", lambda h: h.tensor_tensor(out=qv, in0=qv, in1=bcast_mid(gst[:, 5, :], 64), op=ALU.mult),
                 r=[t_gst, t_ysq], w=[t_ysq])
            k.op("pool", lambda h: h.tensor_tensor(out=ysq[:], in0=ysq[:],
                                                   in1=lng[:, 0, :].unsqueeze(1).to_broadcast([128, NCH, 384]),
                                                   op=ALU.mult), r=[t_ysq, t_lng], w=[t_ysq])
            k.op("pool", lambda h: h.tensor_tensor(out=ysq[:], in0=ysq[:],
                                                   in1=lng[:, 1, :].unsqueeze(1).to_broadcast([128, NCH, 384]),
                                                   op=ALU.add), r=[t_ysq, t_lng], w=[t_ysq])
            yield
            for fc in range(3):
                ps, t_ps = nps()
                for c2 in range(NCH):
                    k.op("pe", lambda h, ps=ps, fc=fc, c2=c2: h.transpose(
                        out=ps[:, c2 * 128:(c2 + 1) * 128], in_=ysq[:, c2, fc * 128:(fc + 1) * 128],
                        identity=self.ident_f[:]), r=[t_ysq, self.t_const], w=[t_ps])
                i = st["pt"] % 2
                st["pt"] += 1
                k.op("dve", lambda h, ps=ps, fc=fc, i=i: h.tensor_tensor(out=potmp[i][:], in0=ps,
                                                                         in1=d["bonus"][:, fc, :], op=ALU.add),
                     r=[t_ps, d["t_bonus"][fc]], w=[t_potmp[i]])
                k.op("pool", lambda h, fc=fc, i=i: h.tensor_tensor(out=mixT[:, 5 + fc, :], in0=potmp[i][:],
                                                                   in1=d["gT"][:, fc, :], op=ALU.mult),
                     r=[t_potmp[i], d["t_gT"][fc]], w=[t_mixT[5 + fc]])
                yield
            if self.mxb is not None:
                k.dma("sp", self.mxb[:, gs].rearrange("(c p) t -> p c t", p=128), mixT[:, 5:8, :],
                      r=t_mixT[5:8], w=[self.t_mxb])
            for blk in range(GB // 128):
                r0 = gb * GB + blk * 128
                tg = r0 // G
                i = st["x"] % 2
                st["x"] += 1
                k.dma("sp", xr[i][:], src[r0:r0 + 128, :], r=[t_src[tg]], w=[t_xr[i]])
                psw, t_psw = npd()
                for half in range(2):
                    ps, t_ps = psw[:, half * 512:(half + 1) * 512], t_psw
                    for kc in range(8):
                        k.op("pe", lambda h, ps=ps, kc=kc, blk=blk, half=half: h.matmul(
                            ps, lhsT=mixT[:, kc, blk * 128:(blk + 1) * 128],
                            rhs=wout[:, kc, half * 512:(half + 1) * 512], start=(kc == 0), stop=(kc == 7)),
                             r=[t_mixT[kc], t_wout[kc]], w=[t_ps])
                    k.op("dve", lambda h, ps=ps, half=half, i=i: h.tensor_tensor(
                        out=xr[i][:, half * 512:(half + 1) * 512], in0=ps,
                        in1=xr[i][:, half * 512:(half + 1) * 512], op=ALU.add), r=[t_ps, t_xr[i]], w=[t_xr[i]])
                k.dma("sp", dst[r0:r0 + 128, :], xr[i][:], r=[t_xr[i]], w=[t_dst[tg]])
                yield

        ngb = self.dbg.get("mb_ng", NGB)
        nq = ngb * NCH
        for _ in pre(0):
            pass
        pre_done = 1
        post_done = 0
        chainq = 0
        nextA = 0
        actA = []
        doneA = set()
        g_chain = None
        g_pre = None
        g_post = None
        post_ready = []
        while chainq < nq or g_post is not None or post_ready:
            while len(actA) < 3 and nextA < nq and (nextA // NCH) < pre_done and nextA < chainq + NR:
                actA.append([nextA, chunkA(nextA)])
                nextA += 1
            if g_chain is None and chainq < nq and chainq in doneA:
                g_chain = chain(chainq)
            if (g_pre is None and pre_done < ngb and pre_done - NGP < post_done
                    and chainq >= (pre_done - 1) * NCH):
                g_pre = pre(pre_done)
            if g_post is None and post_ready:
                g_post = post(post_ready.pop(0))
            progressed = False
            for which in ("chain", "A", "pre", "post"):
                if which == "chain" and g_chain is not None:
                    progressed = True
                    try:
                        next(g_chain)
                    except StopIteration:
                        g_chain = None
                        if (chainq + 1) % NCH == 0:
                            post_ready.append(chainq // NCH)
                        chainq += 1
                        if chainq < nq and chainq in doneA:
                            g_chain = chain(chainq)
                elif which == "A":
                    for it in list(actA):
                        progressed = True
                        try:
                            next(it[1])
                        except StopIteration:
                            doneA.add(it[0])
                            actA.remove(it)
                elif which == "pre" and g_pre is not None:
                    progressed = True
                    try:
                        next(g_pre)
                    except StopIteration:
                        g_pre = None
                        pre_done += 1
                elif which == "post" and g_post is not None:
                    progressed = True
                    try:
                        next(g_post)
                    except StopIteration:
                        g_post = None
                        post_done += 1
            assert progressed, "scheduler stalled"
        k.pop_scope()

    def build(self):
        for ph in self.phases:
            kind = ph[0]
            srcs = {"x": (self.inp["x"], self.t_x), "xa": (self.xa, self.t_xa), "xb": (self.xb, self.t_xb)}
            if kind == "MA":
                _, l, src = ph
                self.pass_mixa(l, srcs[src][0], srcs[src][1])
            dsts = {"xa": (self.xa, self.t_xa), "xb": (self.xb, self.t_xb), "y": (self.y, self.t_y)}
            if kind == "MB":
                _, l, src, dst = ph
                self.pass_mixb(l, srcs[src][0], srcs[src][1], dsts[dst][0], dsts[dst][1])
            if kind == "F":
                _, l, src, dst, final = ph
                srcs = {"x": (self.inp["x"], self.t_x), "xa": (self.xa, self.t_xa), "xb": (self.xb, self.t_xb)}
                dsts = {"xa": (self.xa, self.t_xa), "xb": (self.xb, self.t_xb), "y": (self.y, self.t_y)}
                self.pass_ffn(l, srcs[src][0], srcs[src][1], dsts[dst][0], dsts[dst][1], final)
        self.k.finish()


FULL_PHASES = [("MA", 0, "x"), ("MB", 0, "x", "xa"), ("F", 0, "xa", "xb", False),
               ("MA", 1, "xb"), ("MB", 1, "xb", "xa"), ("F", 1, "xa", "y", True)]


def build_nc(phases=None, dbg=None):
    dbg = dbg or {}
    nc = bass.Bass("TRN2", target_bir_lowering=False)
    p = Prog(nc, phases or FULL_PHASES, dbg)
    p.build()
    return nc, p


def kernel(**inputs):
    nc, _ = build_nc()
    in_maps = []
    for b in range(8):
        m = {}
        for name in INPUT_SHAPES:
            a = np.asarray(inputs[name], dtype=np.float32)
            m[name] = np.ascontiguousarray(a[b]) if name == 'x' else np.ascontiguousarray(a)
        in_maps.append(m)
    res = run_bass_kernel_spmd(nc, in_maps, core_ids=list(range(8)))
    return np.stack([np.asarray(r["y"]) for r in res.results], axis=0).astype(np.float32)
```

```python
import math
from contextlib import ExitStack
import numpy as np
import concourse.bass as bass
import concourse.mybir as mybir
from concourse.bass_utils import run_bass_kernel_spmd

F32 = mybir.dt.float32
BF16 = mybir.dt.bfloat16
ALU = mybir.AluOpType
AF = mybir.ActivationFunctionType
AX = mybir.AxisListType

S = 4096
D = 1024
DFF = 4096
NIN = 3328
G = 512
NG = S // G
NB = G // 128
DEPTH = 2
HD = 64
NORM_EPS = 1e-6
GN_EPS = 64e-5

INPUT_SHAPES = {
    'x': [S, D], 'norm_mix_g': [2, D], 'w_in': [2, D, NIN], 'lam_q1': [2, 32], 'lam_k1': [2, 32],
    'lam_q2': [2, 32], 'lam_k2': [2, 32], 'subln_g': [2, 64], 'conv_w': [2, 3, 256],
    'shift_mu': [2, 1408], 'rwkv_w0': [2, 384], 'rwkv_w_up': [2, 64, 384], 'rwkv_a0': [2, 384],
    'rwkv_a_up': [2, 64, 384], 'rwkv_g_up': [2, 128, 384], 'rwkv_k_k': [2, 384], 'rwkv_k_a': [2, 384],
    'rwkv_r_k': [2, 6, 64], 'lnx_g': [2, 384], 'lnx_b': [2, 384], 'w_out': [2, D, D],
    'norm_mlp_g': [2, D], 'w_mlp_up': [2, D, DFF], 'w_mlp_down': [2, DFF, D], 'final_norm_g': [D],
}


class Tk:
    __slots__ = ("w", "r", "name")

    def __init__(self, name=""):
        self.w = None
        self.r = {}
        self.name = name


class KB:
    EPOCH = 6000
    NDS = 32

    def __init__(self, nc):
        self.nc = nc
        self.eng = {"pe": nc.tensor, "act": nc.scalar, "dve": nc.vector, "pool": nc.gpsimd, "sp": nc.sync}
        self.cnt = {e: 0 for e in self.eng}
        self.semh = {}
        self.seen = {e: {} for e in self.eng}
        self.maxep = {e: {} for e in self.eng}
        self.dsem = [("dma", i) for i in range(self.NDS)]
        for kx in self.dsem:
            self.semh[kx] = nc.alloc_semaphore(f"dma{kx[1]}")
        self.dval = [0] * self.NDS
        self.dnext = 0
        self.dnext_sw = 0
        self.ndma = 0
        self.nwait = 0
        self._n = 0
        self.root = ExitStack()
        self.scope = None

    def sb(self, name, shape, dt):
        self._n += 1
        cm = self.nc.sbuf_tensor("%s_%d" % (name, self._n), list(shape), dt)
        return (self.scope or self.root).enter_context(cm)

    def ps(self, name, shape, dt):
        self._n += 1
        cm = self.nc.psum_tensor("%s_%d" % (name, self._n), list(shape), dt)
        return (self.scope or self.root).enter_context(cm)

    def push_scope(self):
        self.scope = ExitStack()

    def pop_scope(self):
        self.barrier()
        self.scope.close()
        self.scope = None

    def _cursem(self, e):
        key = (e, self.cnt[e] // self.EPOCH)
        if key not in self.semh:
            self.semh[key] = self.nc.alloc_semaphore(f"s_{e}_{key[1]}")
        return key

    def _wait(self, e, tok):
        key, val = tok[0], tok[1]
        seen = self.seen[e]
        if seen.get(key, 0) >= val:
            return
        if key[0] != "dma":
            if self.maxep[e].get(key[0], -1) > key[1]:
                return
            self.maxep[e][key[0]] = max(self.maxep[e].get(key[0], -1), key[1])
        self.eng[e].wait_ge(self.semh[key], val)
        self.nwait += 1
        seen[key] = val

    def _deps(self, e, reads, writes, is_dma):
        for t in reads:
            if t.w is not None:
                tok = t.w
                if (not is_dma) and tok[2] == e and e == "pe":
                    continue
                self._wait(e, tok)
        for t in writes:
            if t.w is not None:
                tok = t.w
                if is_dma or tok[2] != e or tok[3] or e != "pe":
                    self._wait(e, tok)
            for rk, tok in t.r.items():
                if isinstance(tok, list):
                    for tk in tok:
                        self._wait(e, tk)
                elif is_dma or tok[2] != e or e != "pe":
                    self._wait(e, tok)

    def _record(self, tok, reads, writes):
        for t in reads:
            if tok[3]:
                t.r.setdefault("dma", []).append(tok)
            else:
                t.r[tok[2]] = tok
        for t in writes:
            t.w = tok
            t.r = {}

    def op(self, e, fn, r=(), w=()):
        self._deps(e, r, w, False)
        key = self._cursem(e)
        ins = fn(self.eng[e])
        self.cnt[e] += 1
        val = self.cnt[e] - key[1] * self.EPOCH
        ins.then_inc(self.semh[key], 1)
        tok = (key, val, e, False)
        self._record(tok, r, w)
        return tok

    def dma(self, q, out, in_, r=(), w=(), **kw):
        self._deps(q, r, w, True)
        if q == "pool":
            slot = self.NDS - 8 + self.dnext_sw
            self.dnext_sw = (self.dnext_sw + 1) % 8
        else:
            slot = self.dnext
            self.dnext = (self.dnext + 1) % (self.NDS - 8)
        key = self.dsem[slot]
        if self.dval[slot] > 0:
            self._wait(q, (key, self.dval[slot], "dma", True))
        ins = self.eng[q].dma_start(out=out, in_=in_, **kw)
        self.dval[slot] += 16
        ins.then_inc(self.semh[key], 16)
        tok = (key, self.dval[slot], "dma", True)
        self._record(tok, r, w)
        self.ndma += 1
        return tok

    def barrier(self):
        lasts = {}
        for e in ("pe", "act", "dve", "pool"):
            if self.cnt[e] > 0:
                key = (e, (self.cnt[e] - 1) // self.EPOCH)
                lasts[e] = (key, self.cnt[e] - key[1] * self.EPOCH, e, False)
        for e in self.eng:
            for e2, tok in lasts.items():
                if e2 != e:
                    self._wait(e, tok)
            for slot in range(self.NDS):
                if self.dval[slot] > 0:
                    self._wait(e, (self.dsem[slot], self.dval[slot], "dma", True))

    def finish(self):
        for slot in range(self.NDS):
            if self.dval[slot] > 0:
                self._wait("sp", (self.dsem[slot], self.dval[slot], "dma", True))


def bcast_mid(ap2d, n):
    p, j = ap2d.shape
    return ap2d.unsqueeze(2).to_broadcast([p, j, n])


class Prog:
    def __init__(self, nc, phases, dbg=None):
        self.nc = nc
        self.k = KB(nc)
        self.phases = phases
        self.dbg = dbg if dbg is not None else {}
        k = self.k
        self.inp = {}
        for name, shp in INPUT_SHAPES.items():
            self.inp[name] = nc.dram_tensor(name, shp, F32, kind="ExternalInput").ap()
        self.y = nc.dram_tensor("y", [S, D], F32, kind="ExternalOutput").ap()
        self.xa = nc.dram_tensor("xa", [S, D], F32).ap()
        self.xb = nc.dram_tensor("xb", [S, D], F32).ap()
        def dram(name, shape, dt):
            kind = "ExternalOutput" if name in self.dbg else ("ExternalInput" if ("in:" + name) in self.dbg else "Internal")
            return nc.dram_tensor(name, shape, dt, kind=kind).ap()
        self.prw = dram("prw", [1408, S], F32)
        self.t_prw = [Tk() for _ in range(NG)]
        self.mxa = dram("mxa", [5 * 128, S], BF16)
        self.mxb = dram("mxb", [3 * 128, S], BF16) if "mxb" in self.dbg else None
        self.t_mxb = Tk()
        self.t_mxa = [Tk() for _ in range(NG)]
        self.t_xa = [Tk("xa%d" % i) for i in range(NG)]
        self.t_xb = [Tk("xb%d" % i) for i in range(NG)]
        self.t_x = [Tk("x%d" % i) for i in range(NG)]
        self.t_y = [Tk("y%d" % i) for i in range(NG)]
        self.ident_b = k.sb("ident_b", [128, 128], BF16)
        self.ident_f = k.sb("ident_f", [128, 128], F32)
        self.t_const = Tk("const")
        self.eps_t = k.sb("eps_t", [128, 1], F32)
        k.op("pool", lambda h: h.memset(self.ident_b[:], 1.0), w=[self.t_const])
        k.op("pool", lambda h: h.affine_select(out=self.ident_b[:], in_=self.ident_b[:], pattern=[[1, 128]],
                                               compare_op=ALU.is_equal, fill=0.0, base=0, channel_multiplier=-1),
             r=[self.t_const], w=[self.t_const])
        k.op("pool", lambda h: h.memset(self.ident_f[:], 1.0), w=[self.t_const])
        k.op("pool", lambda h: h.affine_select(out=self.ident_f[:], in_=self.ident_f[:], pattern=[[1, 128]],
                                               compare_op=ALU.is_equal, fill=0.0, base=0, channel_multiplier=-1),
             r=[self.t_const], w=[self.t_const])
        k.op("pool", lambda h: h.memset(self.eps_t[:], NORM_EPS), w=[self.t_const])
        self.gcol = k.sb("gcol", [128, 4, 8], F32)
        self.t_gcol = Tk("gcol")
        for n, (nm, l) in enumerate([("norm_mix_g", 0), ("norm_mix_g", 1), ("norm_mlp_g", 0), ("norm_mlp_g", 1)]):
            k.dma("sp", self.gcol[:, n, :], self.inp[nm][l].rearrange("(j p) -> p j", p=128),
                  w=[self.t_gcol], allow_slow_non_contiguous=True)
        self.nblk = 0
        self.dumped = set()


    def alloc_norm(self):
        k = self.k
        self.xt = [k.sb("xt%d" % i, [128, D], F32) for i in range(2)]
        self.t_xt = [Tk("xt%d" % i) for i in range(2)]
        self.xn = [k.sb("xn%d" % i, [128, D], BF16) for i in range(2)]
        self.t_xn = [Tk("xn%d" % i) for i in range(2)]
        self.junk = k.sb("junk", [128, D], BF16)
        self.t_junk = Tk("junk")
        self.stat = [k.sb("stat%d" % i, [128, 4], F32) for i in range(2)]
        self.t_stat = [Tk("stat%d" % i) for i in range(2)]
        self.hT = k.sb("hT", [128, 8, G], BF16)
        self.t_hT = [Tk("hT%d" % i) for i in range(NB)]
        self.psT = [k.ps("psT%d" % i, [128, 8, 128], BF16) for i in range(1)]
        self.t_psT = [Tk("psT%d" % i) for i in range(1)]

    def dump(self, name, ap, toks):
        if "dump" not in self.dbg or name in self.dumped:
            return
        self.dumped.add(name)
        d = self.nc.dram_tensor("d_" + name, list(ap.shape), ap.dtype, kind="ExternalOutput").ap()
        self.k.dma("sp", d, ap, r=list(toks), w=[Tk()])

    def norm_a(self, src_ap, src_tk):
        k = self.k
        i = self.nblk % 2
        self.nblk += 1
        xt, t_xt, xn, t_xn, st, t_st = self.xt[i], self.t_xt[i], self.xn[i], self.t_xn[i], self.stat[i], self.t_stat[i]
        k.dma("sp", xt[:], src_ap, r=[src_tk], w=[t_xt])
        k.op("act", lambda h: h.activation(out=self.junk[:], in_=xt[:], func=AF.Square, accum_out=st[:, 0:1]),
             r=[t_xt], w=[self.t_junk, t_st])
        k.op("act", lambda h: h.activation(out=st[:, 1:2], in_=st[:, 0:1], func=AF.Sqrt, scale=1.0 / D,
                                           bias=self.eps_t[:, 0:1]),
             r=[t_st, self.t_const], w=[t_st])
        k.op("dve", lambda h: h.reciprocal(out=st[:, 2:3], in_=st[:, 1:2]), r=[t_st], w=[t_st])
        k.op("dve", lambda h: h.tensor_scalar(out=xn[:], in0=xt[:], scalar1=st[:, 2:3], scalar2=None, op0=ALU.mult),
             r=[t_xt, t_st], w=[t_xn])
        return xn, t_xn

    def norm_b(self, xn, t_xn, gidx, blk, hT, t_hT):
        k = self.k
        pT, t_pT = self.psT[0], self.t_psT[0]
        for j in range(8):
            k.op("pe", lambda h, j=j: h.transpose(out=pT[:, j, :], in_=xn[:, j * 128:(j + 1) * 128],
                                                   identity=self.ident_b[:]),
                 r=[t_xn, self.t_const], w=[t_pT])
        k.op("dve", lambda h: h.tensor_tensor(out=hT[:, :, blk * 128:(blk + 1) * 128], in0=pT[:],
                                              in1=bcast_mid(self.gcol[:, gidx, :], 128), op=ALU.mult),
             r=[t_pT, self.t_gcol], w=[t_hT[blk]])

    def norm_block(self, src_ap, src_tk, gidx, blk, hT=None, t_hT=None):
        hT = self.hT if hT is None else hT
        t_hT = self.t_hT if t_hT is None else t_hT
        xn, t_xn = self.norm_a(src_ap, src_tk)
        self.norm_b(xn, t_xn, gidx, blk, hT, t_hT)

    def pass_ffn(self, l, src, t_src, dst, t_dst, final):
        k = self.k
        k.push_scope()
        self.alloc_norm()
        if final:
            self.gfin = k.sb("gfin", [128, D], F32)
            self.t_gfin = Tk("gfin")
            k.dma("sp", self.gfin[:], self.inp["final_norm_g"].unsqueeze(0).partition_broadcast(128),
                  w=[self.t_gfin])
        wup = k.sb("wup", [128, 8, DFF], BF16)
        wdn = k.sb("wdn", [128, 32, D], BF16)
        t_wup = [[Tk() for _ in range(4)] for _ in range(8)]
        t_wdn = [Tk() for _ in range(8)]
        for cb in range(4):
            for kc in range(8):
                k.dma("pool", wup[:, kc, cb * 1024:(cb + 1) * 1024],
                      self.inp["w_mlp_up"][l, kc * 128:(kc + 1) * 128, cb * 1024:(cb + 1) * 1024],
                      w=[t_wup[kc][cb]])
        for c4 in range(8):
            k.dma("pool", wdn[:, c4 * 4:(c4 + 1) * 4, :],
                  self.inp["w_mlp_down"][l, c4 * 512:(c4 + 1) * 512, :].rearrange("(c p) n -> p c n", p=128),
                  w=[t_wdn[c4]])
        aT = k.sb("aT", [128, 32, G], BF16)
        t_aT = [Tk() for _ in range(32)]
        rt = [k.sb("rt%d" % i, [128, G], F32) for i in range(2)]
        t_rt = [Tk() for _ in range(2)]
        psU = [k.ps("psU%d" % i, [128, G], F32) for i in range(2)]
        t_psU = [Tk() for _ in range(2)]
        psD = [k.ps("psD%d" % i, [128, 512], F32) for i in range(2)]
        t_psD = [Tk() for _ in range(2)]
        xr = [k.sb("xr%d" % i, [128, D], F32) for i in range(2)]
        t_xr = [Tk() for _ in range(2)]
        xo, t_xo = xr, t_xr
        st2 = [k.sb("st2_%d" % i, [128, 4], F32) for i in range(2)]
        t_st2 = [Tk() for _ in range(2)]
        n_o = 0
        hTs = [self.hT, k.sb("hTf2", [128, 8, G], BF16)]
        t_hTs = [self.t_hT, [Tk() for _ in range(NB)]]
        for blk in range(NB):
            self.norm_block(src[blk * 128:(blk + 1) * 128, :], t_src[0], 2 + l, blk, hTs[0], t_hTs[0])
        for g in range(NG):
            hT, t_hT = hTs[g % 2], t_hTs[g % 2]
            for c in range(32):
                ps, t_ps = psU[c % 2], t_psU[c % 2]
                for kc in range(8):
                    k.op("pe", lambda h, kc=kc, c=c, ps=ps: h.matmul(ps[:], lhsT=wup[:, kc, c * 128:(c + 1) * 128],
                                                                    rhs=hT[:, kc, :], start=(kc == 0),
                                                                    stop=(kc == 7)),
                         r=[t_wup[kc][c // 8]] + t_hT, w=[t_ps])
                r_, t_r = rt[c % 2], t_rt[c % 2]
                k.op("act", lambda h, ps=ps, r_=r_: h.activation(out=r_[:], in_=ps[:], func=AF.Relu),
                     r=[t_ps], w=[t_r])
                k.op("pool", lambda h, r_=r_, c=c: h.tensor_tensor(out=aT[:, c, :], in0=r_[:], in1=r_[:],
                                                                   op=ALU.mult),
                     r=[t_r], w=[t_aT[c]])
            for blk in range(NB):
                r0 = g * G + blk * 128
                i = n_o % 2
                n_o += 1
                if g + 1 < NG:
                    r1 = (g + 1) * G + blk * 128
                    nxn = self.norm_a(src[r1:r1 + 128, :], t_src[g + 1])
                k.dma("sp", xr[i][:], src[r0:r0 + 128, :], r=[t_src[g]], w=[t_xr[i]])
                for half in range(2):
                    ps, t_ps = psD[half], t_psD[half]
                    for c in range(32):
                        k.op("pe", lambda h, c=c, ps=ps, half=half, blk=blk: h.matmul(
                            ps[:], lhsT=aT[:, c, blk * 128:(blk + 1) * 128],
                            rhs=wdn[:, c, half * 512:(half + 1) * 512], start=(c == 0), stop=(c == 31)),
                             r=[t_aT[c], t_wdn[c // 4]], w=[t_ps])
                    k.op("dve", lambda h, ps=ps, half=half, i=i: h.tensor_tensor(
                        out=xo[i][:, half * 512:(half + 1) * 512], in0=ps[:],
                        in1=xr[i][:, half * 512:(half + 1) * 512], op=ALU.add),
                         r=[t_ps, t_xr[i]], w=[t_xr[i]])
                if final:
                    st, t_st = st2[i], t_st2[i]
                    k.op("act", lambda h, i=i, st=st: h.activation(out=self.junk[:], in_=xo[i][:], func=AF.Square,
                                                                   accum_out=st[:, 0:1]),
                         r=[t_xo[i]], w=[self.t_junk, t_st])
                    k.op("act", lambda h, st=st: h.activation(out=st[:, 1:2], in_=st[:, 0:1], func=AF.Sqrt,
                                                              scale=1.0 / D, bias=self.eps_t[:, 0:1]),
                         r=[t_st, self.t_const], w=[t_st])
                    k.op("dve", lambda h, st=st: h.reciprocal(out=st[:, 2:3], in_=st[:, 1:2]), r=[t_st], w=[t_st])
                    k.op("dve", lambda h, i=i, st=st: h.scalar_tensor_tensor(
                        out=xo[i][:], in0=xo[i][:], scalar=st[:, 2:3], in1=self.gfin[:], op0=ALU.mult, op1=ALU.mult),
                         r=[t_xo[i], t_st, self.t_gfin], w=[t_xo[i]])
                k.dma("sp", dst[r0:r0 + 128, :], xo[i][:], r=[t_xo[i]], w=[t_dst[g]])
                if g + 1 < NG:
                    self.norm_b(nxn[0], nxn[1], 2 + l, blk, hTs[(g + 1) % 2], t_hTs[(g + 1) % 2])
        k.pop_scope()


    def pass_mixa(self, l, src, t_src):
        k = self.k
        k.push_scope()
        self.alloc_norm()
        inp = self.inp
        lambda_init = 0.8 - 0.6 * math.exp(-0.3 * l)
        win = k.sb("win", [128, 8, NIN], BF16)
        wb_edges = [0, 768, 1152, 2304, NIN]
        t_win = [[Tk() for _ in range(4)] for _ in range(8)]
        for bi in (0, 2, 3, 1):
            c0_, c1_ = wb_edges[bi], wb_edges[bi + 1]
            for kc in range(8):
                k.dma("pool", win[:, kc, c0_:c1_], inp["w_in"][l, kc * 128:(kc + 1) * 128, c0_:c1_],
                      w=[t_win[kc][bi]])

        def wblk(c):
            col = c * 128
            return 0 if col < 768 else (1 if col < 1152 else (2 if col < 2304 else 3))
        cst = k.sb("cst", [128, 8], F32)
        t_cst = Tk()
        lq = k.sb("lq", [128, 4, 32], F32)
        t_lq = Tk()
        for i, nm in enumerate(["lam_q1", "lam_k1", "lam_q2", "lam_k2"]):
            k.dma("sp", lq[:, i, :], inp[nm][l:l + 1, :].partition_broadcast(128), w=[t_lq])
        k.op("dve", lambda h: h.tensor_tensor(out=lq[:, 0, :], in0=lq[:, 0, :], in1=lq[:, 1, :], op=ALU.mult),
             r=[t_lq], w=[t_lq])
        k.op("dve", lambda h: h.tensor_tensor(out=lq[:, 2, :], in0=lq[:, 2, :], in1=lq[:, 3, :], op=ALU.mult),
             r=[t_lq], w=[t_lq])
        k.op("dve", lambda h: h.tensor_reduce(out=cst[:, 0:1], in_=lq[:, 0, :], axis=AX.X, op=ALU.add),
             r=[t_lq], w=[t_cst])
        k.op("dve", lambda h: h.tensor_reduce(out=cst[:, 1:2], in_=lq[:, 2, :], axis=AX.X, op=ALU.add),
             r=[t_lq], w=[t_cst])
        k.op("act", lambda h: h.activation(out=cst[:, 2:4], in_=cst[:, 0:2], func=AF.Exp), r=[t_cst], w=[t_cst])
        k.op("dve", lambda h: h.tensor_tensor(out=cst[:, 4:5], in0=cst[:, 2:3], in1=cst[:, 3:4], op=ALU.subtract),
             r=[t_cst], w=[t_cst])
        k.op("dve", lambda h: h.tensor_scalar(out=cst[:, 5:6], in0=cst[:, 4:5], scalar1=float(lambda_init),
                                              scalar2=None, op0=ALU.add), r=[t_cst], w=[t_cst])
        k.op("pool", lambda h: h.memset(cst[:, 6:7], NORM_EPS), w=[t_cst])
        subg = k.sb("subg", [128, 64], F32)
        t_subg = Tk()
        k.dma("sp", subg[:], inp["subln_g"][l:l + 1, :].partition_broadcast(128), w=[t_subg])
        k.op("dve", lambda h: h.tensor_scalar(out=subg[:], in0=subg[:], scalar1=float(1.0 - lambda_init),
                                              scalar2=None, op0=ALU.mult), r=[t_subg], w=[t_subg])
        cw = k.sb("cw", [128, 2, 3], F32)
        t_cw = Tk()
        for j in range(2):
            k.dma("sp", cw[:, j, :], inp["conv_w"][l, :, j * 128:(j + 1) * 128].rearrange("t p -> p t"),
                  w=[t_cw], allow_slow_non_contiguous=True)
        mask = k.sb("mask", [128, 128], BF16)
        t_mask = Tk()
        k.op("pool", lambda h: h.memset(mask[:], 1.0), w=[t_mask])
        k.op("pool", lambda h: h.affine_select(out=mask[:], in_=mask[:], pattern=[[1, 128]], compare_op=ALU.is_ge,
                                               fill=0.0, base=0, channel_multiplier=-1), r=[t_mask], w=[t_mask])
        kT = k.sb("kT", [128, 3, S], BF16)
        t_kT = [[Tk() for _ in range(NG)] for _ in range(3)]
        Vt = k.sb("Vt", [128, S // 128, 6, 65], BF16)
        t_V = [Tk() for _ in range(S // 128)]
        for b8 in range(0, S // 128, 8):
            k.op("pool", lambda h, b8=b8: h.memset(Vt[:, b8:b8 + 8], 1.0), w=t_V[b8:b8 + 8])
        qT = k.sb("qT", [128, 3, G], BF16)
        t_qT = [Tk() for _ in range(3)]
        cv = k.sb("cv", [128, 6, G], F32)
        t_cv = [Tk() for _ in range(6)]
        zb = k.sb("zb", [128, 2, G + 2], F32)
        t_zb = [Tk() for _ in range(2)]
        k.op("pool", lambda h: h.memset(zb[:], 0.0), w=t_zb)
        ycv = k.sb("ycv", [128, G], F32)
        t_ycv = Tk()
        stg = [k.sb("stg%d" % i, [128, G], F32) for i in range(2)]
        t_stg = [Tk() for _ in range(2)]
        eT = [k.sb("eT%d" % i, [128, G], BF16) for i in range(4)]
        t_eT = [Tk() for _ in range(4)]
        uT = k.sb("uT", [128, 2, G], F32)
        t_uT = [Tk() for _ in range(2)]
        oat = k.sb("oat", [128, NB, 6, 64], F32)
        t_oat = Tk()
        osq = k.sb("osq", [128, NB, 6, 64], F32)
        t_osq = Tk()
        t1 = k.sb("t1", [128, NB, 64], F32)
        t_t1 = Tk()
        rl = k.sb("rl", [128, 2, NB], F32)
        t_rl = Tk()
        ss = k.sb("ss", [128, 3, NB * 6], F32)
        t_ss = Tk()
        oab = k.sb("oab", [128, NB, 384], BF16)
        t_oab = Tk()
        mixA = k.sb("mixA", [128, 5, G], BF16)
        t_mixA = [Tk() for _ in range(5)]
        NPA = 4
        psA = [k.ps("psA%d" % i, [128, G], F32) for i in range(NPA)]
        t_psA = [Tk() for _ in range(NPA)]
        psO = [k.ps("psO%d" % i, [128, G], F32) for i in range(2)]
        t_psO = [Tk() for _ in range(2)]
        psTr1 = k.ps("psTr", [128, NB, 65], F32)
        t_psTr1 = Tk()
        hTs = [self.hT, k.sb("hT2", [128, 8, G], BF16)]
        t_hTs = [self.t_hT, [Tk() for _ in range(NB)]]
        qTs = [qT, k.sb("qT2", [128, 3, G], BF16)]
        t_qTs = [t_qT, [Tk() for _ in range(3)]]
        cvs = [cv, k.sb("cv2", [128, 6, G], F32)]
        t_cvs = [t_cv, [Tk() for _ in range(6)]]
        cnt = {"A": 0, "S": 0, "E": 0}
        sc = 1.0 / math.sqrt(32.0)

        def nbank():
            i = cnt["A"] % NPA
            cnt["A"] += 1
            return psA[i], t_psA[i]

        def front(g):
            hT, t_hT = hTs[g % 2], t_hTs[g % 2]
            qT_, t_qT_ = qTs[g % 2], t_qTs[g % 2]
            cv_, t_cv_ = cvs[g % 2], t_cvs[g % 2]
            for blk in range(NB):
                r0 = g * G + blk * 128
                self.norm_block(src[r0:r0 + 128, :], t_src[g], l, blk, hT, t_hT)
                yield
            for c in list(range(0, 6)) + list(range(9, 26)):
                ps, t_ps = nbank()
                for kc in range(8):
                    k.op("pe", lambda h, kc=kc, c=c, ps=ps: h.matmul(ps[:], lhsT=win[:, kc, c * 128:(c + 1) * 128],
                                                                    rhs=hT[:, kc, :], start=(kc == 0),
                                                                    stop=(kc == 7)),
                         r=[t_win[kc][wblk(c)]] + t_hT, w=[t_ps])
                if c < 3:
                    k.op("dve", lambda h, ps=ps, c=c: h.tensor_copy(out=qT_[:, c, :], in_=ps[:]),
                         r=[t_ps], w=[t_qT_[c]])
                elif c < 6:
                    k.op("dve", lambda h, ps=ps, c=c: h.tensor_copy(out=kT[:, c - 3, g * G:(g + 1) * G], in_=ps[:]),
                         r=[t_ps], w=[t_kT[c - 3][g]])
                elif c < 15:
                    k.op("dve", lambda h, ps=ps, c=c: h.tensor_copy(out=cv_[:, c - 9, :], in_=ps[:]),
                         r=[t_ps], w=[t_cv_[c - 9]])
                else:
                    i = cnt["S"] % 2
                    cnt["S"] += 1
                    k.op("dve", lambda h, ps=ps, i=i: h.tensor_copy(out=stg[i][:], in_=ps[:]), r=[t_ps], w=[t_stg[i]])
                    k.dma("sp", self.prw[(c - 15) * 128:(c - 14) * 128, g * G:(g + 1) * G], stg[i][:],
                          r=[t_stg[i]], w=[self.t_prw[g]])
                yield
            for blk in range(NB):
                ps, t_ps = nbank()
                for kc in range(8):
                    k.op("pe", lambda h, kc=kc, ps=ps, blk=blk: h.matmul(
                        ps[:, 0:384], lhsT=hT[:, kc, blk * 128:(blk + 1) * 128], rhs=win[:, kc, 768:1152],
                        start=(kc == 0), stop=(kc == 7)), r=[t_win[kc][1], t_hT[blk]], w=[t_ps])
                k.op("dve", lambda h, ps=ps, blk=blk: h.tensor_copy(
                    out=Vt[:, g * NB + blk, :, 0:64], in_=ps[:, 0:384].rearrange("p (a b) -> p a b", a=6)),
                     r=[t_ps], w=[t_V[g * NB + blk]])
                yield
            for j in range(2):
                k.op("pool", lambda h, j=j: h.tensor_tensor(out=zb[:, j, 2:G + 2], in0=cv_[:, 2 + j, :],
                                                            in1=cv_[:, 4 + j, :], op=ALU.mult),
                     r=[t_cv_[2 + j], t_cv_[4 + j]], w=[t_zb[j]])
                k.op("dve", lambda h, j=j: h.tensor_scalar(out=ycv[:], in0=zb[:, j, 0:G], scalar1=cw[:, j, 0:1],
                                                           scalar2=None, op0=ALU.mult),
                     r=[t_zb[j], t_cw], w=[t_ycv])
                k.op("dve", lambda h, j=j: h.scalar_tensor_tensor(out=ycv[:], in0=zb[:, j, 1:G + 1],
                                                                  scalar=cw[:, j, 1:2], in1=ycv[:],
                                                                  op0=ALU.mult, op1=ALU.add),
                     r=[t_zb[j], t_cw, t_ycv], w=[t_ycv])
                k.op("dve", lambda h, j=j: h.scalar_tensor_tensor(out=ycv[:], in0=zb[:, j, 2:G + 2],
                                                                  scalar=cw[:, j, 2:3], in1=ycv[:],
                                                                  op0=ALU.mult, op1=ALU.add),
                     r=[t_zb[j], t_cw, t_ycv], w=[t_ycv])
                k.op("pool", lambda h, j=j: h.tensor_tensor(out=mixAs[g % 2][:, 3 + j, :], in0=ycv[:],
                                                            in1=cv_[:, j, :], op=ALU.mult),
                     r=[t_ycv, t_cv_[j]], w=[t_mixAs[g % 2][3 + j]])
                k.op("pool", lambda h, j=j: h.tensor_copy(out=zb[:, j, 0:2], in_=zb[:, j, G:G + 2]),
                     r=[t_zb[j]], w=[t_zb[j]])
                yield

        def back(g):
            qT_, t_qT_ = qTs[g % 2], t_qTs[g % 2]
            mixA_, t_mixA_ = mixAs[g % 2], t_mixAs[g % 2]
            nkb = 4 * g + 4
            LOOK = 1

            def emit_S(hd, half, j):
                qc = hd // 2
                pb = (hd % 2) * 64 + half * 32
                off = max(0, j - 4 * g) * 128
                ps, t_ps = nbank()
                k.op("pe", lambda h: h.matmul(
                    ps[:, off:G], lhsT=kT[pb:pb + 32, qc, j * 128:(j + 1) * 128],
                    rhs=qT_[pb:pb + 32, qc, off:G], start=True, stop=True, tile_position=(pb, 0)),
                     r=[t_kT[qc][j // NB], t_qT_[qc]], w=[t_ps])
                e, t_e = eT[cnt["E"] % 4], t_eT[cnt["E"] % 4]
                cnt["E"] += 1
                k.op("act", lambda h: h.activation(out=e[:, off:G], in_=ps[:, off:G], func=AF.Exp, scale=sc),
                     r=[t_ps], w=[t_e])
                if j >= 4 * g:
                    k.op("dve", lambda h: h.tensor_tensor(out=e[:, off:off + 128], in0=e[:, off:off + 128],
                                                          in1=mask[:], op=ALU.mult), r=[t_e, t_mask], w=[t_e])
                return (hd, half, j, off, e, t_e)

            def emit_PV(item):
                hd, half, j, off, e, t_e = item
                k.op("pe", lambda h: h.matmul(psO[half][0:65, off:G], lhsT=Vt[:, j, hd, :], rhs=e[:, off:G],
                                              start=(j == 0), stop=(j == nkb - 1)),
                     r=[t_e, t_V[j]], w=[t_psO[half]])
                if j != nkb - 1:
                    return
                k.op("dve", lambda h: h.tensor_copy(out=uT[0:65, half, :], in_=psO[half][0:65, :]),
                     r=[t_psO[half]], w=[t_uT[half]])
                for blk in range(NB):
                    k.op("pe", lambda h, blk=blk: h.transpose(
                        out=psTr1[:, blk, :], in_=uT[0:65, half, blk * 128:(blk + 1) * 128],
                        identity=self.ident_f[0:65, 0:65]), r=[t_uT[half], self.t_const], w=[t_psTr1])
                k.op("dve", lambda h: h.reciprocal(out=rl[:, half, :], in_=psTr1[:, :, 64]),
                     r=[t_psTr1], w=[t_rl])
                if half == 0:
                    k.op("dve", lambda h: h.tensor_tensor(out=t1[:], in0=psTr1[:, :, 0:64],
                                                          in1=bcast_mid(rl[:, 0, :], 64), op=ALU.mult),
                         r=[t_psTr1, t_rl], w=[t_t1])
                    return
                k.op("dve", lambda h: h.tensor_scalar(out=rl[:, 1, :], in0=rl[:, 1, :], scalar1=cst[:, 5:6],
                                                      scalar2=None, op0=ALU.mult), r=[t_rl, t_cst], w=[t_rl])
                k.op("dve", lambda h: h.tensor_tensor(out=oat[:, :, hd, :], in0=psTr1[:, :, 0:64],
                                                      in1=bcast_mid(rl[:, 1, :], 64), op=ALU.mult),
                     r=[t_psTr1, t_rl], w=[t_oat])
                k.op("dve", lambda h: h.tensor_tensor(out=oat[:, :, hd, :], in0=t1[:], in1=oat[:, :, hd, :],
                                                      op=ALU.subtract), r=[t_t1, t_oat], w=[t_oat])

            pend = []
            for hd in range(6):
                for j in range(nkb):
                    for half in range(2):
                        pend.append(emit_S(hd, half, j))
                    while len(pend) > 2 * LOOK:
                        emit_PV(pend.pop(0))
                    yield
            while pend:
                emit_PV(pend.pop(0))
            yield
            k.op("pool", lambda h: h.tensor_tensor(out=osq[:], in0=oat[:], in1=oat[:], op=ALU.mult),
                 r=[t_oat], w=[t_osq])
            k.op("dve", lambda h: h.tensor_reduce(out=ss[:, 0, :], in_=osq[:].rearrange("p a b c -> p (a b) c"),
                                                  axis=AX.X, op=ALU.add), r=[t_osq], w=[t_ss])
            k.op("act", lambda h: h.activation(out=ss[:, 1, :], in_=ss[:, 0, :], func=AF.Sqrt, scale=1.0 / 64.0,
                                               bias=cst[:, 6:7]), r=[t_ss, t_cst], w=[t_ss])
            k.op("dve", lambda h: h.reciprocal(out=ss[:, 2, :], in_=ss[:, 1, :]), r=[t_ss], w=[t_ss])
            k.op("dve", lambda h: h.tensor_tensor(out=osq[:].rearrange("p a b c -> p (a b) c"),
                                                  in0=oat[:].rearrange("p a b c -> p (a b) c"),
                                                  in1=bcast_mid(ss[:, 2, :], 64), op=ALU.mult),
                 r=[t_oat, t_ss], w=[t_osq])
            k.op("dve", lambda h: h.tensor_tensor(
                out=oab[:].rearrange("p a (b c) -> p (a b) c", c=64), in0=osq[:].rearrange("p a b c -> p (a b) c"),
                in1=subg[:].unsqueeze(1).to_broadcast([128, NB * 6, 64]), op=ALU.mult),
                 r=[t_osq, t_subg], w=[t_oab])
            yield
            pT, t_pT = self.psT[0], self.t_psT[0]
            for blk in range(NB):
                for c in range(3):
                    k.op("pe", lambda h, blk=blk, c=c: h.transpose(out=pT[:, c, :],
                                                                   in_=oab[:, blk, c * 128:(c + 1) * 128],
                                                                   identity=self.ident_b[:]),
                         r=[t_oab, self.t_const], w=[t_pT])
                k.op("dve", lambda h, blk=blk: h.tensor_copy(out=mixA_[:, 0:3, blk * 128:(blk + 1) * 128],
                                                             in_=pT[:, 0:3, :]),
                     r=[t_pT], w=t_mixA_[0:3])
                yield
            k.dma("sp", self.mxa[:, g * G:(g + 1) * G].rearrange("(c p) t -> p c t", p=128), mixA_[:],
                  r=t_mixA_, w=[self.t_mxa[g]])

        mixAs = [mixA, k.sb("mixA2", [128, 5, G], BF16)]
        t_mixAs = [t_mixA, [Tk() for _ in range(5)]]
        for _ in front(0):
            pass
        for g in range(NG):
            bk = back(g)
            fr = front(g + 1) if g + 1 < NG else iter(())
            n_att = 6 * (4 * g + 4)
            n_fr = 4 + 23 + 4 + 2
            ratio = max(1, n_att // n_fr)
            fr_done = False
            i = 0
            for _ in bk:
                i += 1
                if not fr_done and i % ratio == 0:
                    try:
                        next(fr)
                    except StopIteration:
                        fr_done = True
            for _ in fr:
                pass
        k.pop_scope()

    def pass_mixb(self, l, src, t_src, dst, t_dst):
        k = self.k
        k.push_scope()
        inp = self.inp
        C = 128
        c0 = math.exp(-0.5)
        wout = k.sb("wout", [128, 8, D], BF16)
        t_wout = [Tk() for _ in range(8)]
        for kc in range(8):
            k.dma("pool", wout[:, kc, :], inp["w_out"][l, kc * 128:(kc + 1) * 128, :], w=[t_wout[kc]])
        waup = k.sb("waup", [128, 384], BF16)
        gup = k.sb("gup", [128, 384], BF16)
        t_lw = Tk()
        k.dma("pool", waup[0:64, :], inp["rwkv_w_up"][l], w=[t_lw])
        k.dma("pool", waup[64:128, :], inp["rwkv_a_up"][l], w=[t_lw])
        k.dma("pool", gup[:], inp["rwkv_g_up"][l], w=[t_lw])
        pc = k.sb("pc", [128, 8, 3], F32)
        t_pc = Tk()
        for n, nm in enumerate(["rwkv_w0", "rwkv_a0", "rwkv_k_k", "rwkv_k_a", "rwkv_r_k"]):
            srcap = inp[nm][l]
            if nm == "rwkv_r_k":
                srcap = srcap.rearrange("a b -> (a b)")
            k.dma("sp", pc[:, n, :], srcap.rearrange("(j p) -> p j", p=128), w=[t_pc],
                  allow_slow_non_contiguous=True)
        k.op("dve", lambda h: h.tensor_scalar(out=pc[:, 5, :], in0=pc[:, 3, :], scalar1=-1.0, scalar2=1.0,
                                              op0=ALU.mult, op1=ALU.add), r=[t_pc], w=[t_pc])
        k.op("pool", lambda h: h.memset(pc[:, 6, :], GN_EPS), w=[t_pc])
        mu = k.sb("mu", [128, 11], F32)
        t_mu = Tk()
        k.dma("sp", mu[:], inp["shift_mu"][l].rearrange("(j p) -> p j", p=128), w=[t_mu],
              allow_slow_non_contiguous=True)
        lng = k.sb("lng", [128, 2, 384], F32)
        t_lng = Tk()
        k.dma("sp", lng[:, 0, :], inp["lnx_g"][l:l + 1, :].partition_broadcast(128), w=[t_lng])
        k.dma("sp", lng[:, 1, :], inp["lnx_b"][l:l + 1, :].partition_broadcast(128), w=[t_lng])
        bones = k.sb("bones", [128, 128], BF16)
        t_bones = Tk()
        k.op("pool", lambda h: h.memset(bones[:], 0.0), w=[t_bones])
        k.op("pool", lambda h: h.memset(bones[0:64, 0:64], 1.0), w=[t_bones])
        k.op("pool", lambda h: h.memset(bones[64:128, 64:128], 1.0), w=[t_bones])
        msk = k.sb("msk", [128, 6, 128], F32)
        t_msk = Tk()
        k.op("pool", lambda h: h.memset(msk[:], 1.0), w=[t_msk])
        k.op("pool", lambda h: h.affine_select(out=msk[:, 0, :], in_=msk[:, 0, :], pattern=[[1, 128]],
                                               compare_op=ALU.is_ge, fill=0.0, base=-1, channel_multiplier=-1),
             r=[t_msk], w=[t_msk])
        k.op("pool", lambda h: h.affine_select(out=msk[:, 1, :], in_=msk[:, 1, :], pattern=[[1, 128]],
                                               compare_op=ALU.is_ge, fill=0.0, base=0, channel_multiplier=-1),
             r=[t_msk], w=[t_msk])
        k.op("pool", lambda h: h.affine_select(out=msk[:, 2, :], in_=msk[:, 2, :], pattern=[[-1, 128]],
                                               compare_op=ALU.is_ge, fill=0.0, base=-1, channel_multiplier=1),
             r=[t_msk], w=[t_msk])
        k.op("pool", lambda h: h.memset(msk[:, 3:6, :], 0.0), w=[t_msk])
        k.op("pool", lambda h: h.tensor_copy(out=msk[0:64, 3:5, 0:64], in_=msk[0:64, 0:3:2, 0:64]),
             r=[t_msk], w=[t_msk])
        k.op("pool", lambda h: h.tensor_copy(out=msk[64:128, 3:5, 64:128], in_=msk[64:128, 0:3:2, 64:128]),
             r=[t_msk], w=[t_msk])
        k.op("pool", lambda h: h.memset(msk[0:64, 5, 64:128], 1.0), r=[t_msk], w=[t_msk])

        GB = 256
        NCH = GB // C
        NGB = S // GB
        NQ = S // C
        NR = 3
        rmask = k.sb("rmask", [128, GB], F32)
        t_rmask = Tk()
        k.op("pool", lambda h: h.memset(rmask[:], 1.0), w=[t_rmask])
        k.op("pool", lambda h: h.memset(rmask[:].rearrange("p (c t) -> p c t", t=C)[:, :, 0:1], 0.0), w=[t_rmask])
        F3 = [128, 3, GB]
        pt = k.sb("pt", [128, 11, GB], F32); t_pt = Tk()
        halo = k.sb("halo", [128, 11, 1], F32); t_halo = Tk()
        k.op("pool", lambda h: h.memset(halo[:], 0.0), w=[t_halo])
        z = k.sb("z", [128, 11, GB], F32); t_z = [Tk()]
        sig = k.sb("sig", F3, F32); t_sig = [Tk() for _ in range(3)]
        aa = k.sb("aa", F3, F32); t_aa = [Tk() for _ in range(3)]
        kkn = k.sb("kkn", F3, F32); t_kkn = [Tk() for _ in range(3)]
        kmod = k.sb("kmod", F3, F32); t_kmod = [Tk() for _ in range(3)]
        beta = k.sb("beta", F3, F32); t_beta = [Tk() for _ in range(3)]
        cs = k.sb("cs", F3, F32); t_cs = [Tk() for _ in range(3)]
        tA = pt[:, 0:3, :]; t_tA = [Tk() for _ in range(3)]
        tB = pt[:, 3:6, :]; t_tB = [Tk() for _ in range(3)]
        tC = pt[:, 6:9, :]; t_tC = [Tk() for _ in range(3)]
        tD = k.sb("tD", F3, F32); t_tD = [Tk() for _ in range(3)]
        tE = k.sb("tE", F3, F32); t_tE = [Tk() for _ in range(3)]
        t_alias = t_tA + t_tB + t_tC
        b16 = k.sb("b16", F3, BF16); t_b16 = [Tk() for _ in range(3)]
        twa = k.sb("twa", [128, GB], BF16); t_twa = Tk()
        sg = k.sb("sg", [128, GB], BF16); t_sg = Tk()
        def gset(i):
            d = {}
            for nm in ["rt", "kt", "bt", "at", "k2", "b2"]:
                d[nm] = k.sb(nm + str(i), F3, BF16); d["t_" + nm] = [Tk() for _ in range(3)]
            for nm in ["atp", "rtp", "btp"]:
                d[nm] = k.sb(nm + str(i), [128, 3, NCH, 2, C], BF16); d["t_" + nm] = [Tk() for _ in range(3)]
                k.op("pool", lambda h, t_=d[nm]: h.memset(t_[:], 0.0), w=d["t_" + nm])
            d["WC"] = k.sb("WC%d" % i, [128, 3, NCH], F32); d["t_WC"] = [Tk() for _ in range(3)]
            d["Vg"] = k.sb("Vg%d" % i, [128, NCH, 384], BF16); d["t_Vg"] = [Tk() for _ in range(NCH)]
            return d

        def pset(i):
            d = {}
            d["gT"] = k.sb("gT%d" % i, F3, BF16); d["t_gT"] = [Tk() for _ in range(3)]
            d["bonus"] = k.sb("bonus%d" % i, F3, BF16); d["t_bonus"] = [Tk() for _ in range(3)]
            d["ysb"] = k.sb("ysb%d" % i, [128, NCH, 384], F32); d["t_ysb"] = [Tk() for _ in range(NCH)]
            return d
        NGS = 2
        GS = [gset(i) for i in range(NGS)]
        NGP = 3
        GP = [pset(i) for i in range(NGP)]
        H6 = [128, 6, 128]
        def cset(i):
            d = {}
            for nm in ["X0", "X1", "Xt0", "Xt1", "P0", "P1", "Lak", "Arb", "Ark", "X12"]:
                d[nm] = k.sb("%s_%d" % (nm, i), H6, BF16); d["t_" + nm] = Tk()
            d["kbtok"] = k.sb("kbtok%d" % i, [128, 2, 384], BF16); d["t_kbtok"] = Tk()
            return d
        CS = [cset(i) for i in range(NR)]
        r0b = k.sb("r0b", [128, 384], BF16); t_r0b = Tk()
        Ub = k.sb("Ub", [128, 384], BF16); t_Ub = Tk()
        u1b = k.sb("u1b", [128, 384], BF16); t_u1b = Tk()
        u2b = k.sb("u2b", [128, 384], BF16); t_u2b = Tk()
        Tst = k.sb("Tst", [128, 3, 128], F32); t_T = Tk()
        Tbd = k.sb("Tbd", [128, 3, 128], BF16); t_Tb = Tk()
        k.op("pool", lambda h: h.memset(Tst[:], 0.0), w=[t_T])
        k.op("pool", lambda h: h.memset(Tbd[:], 0.0), w=[t_Tb])
        ysq = k.sb("ysq", [128, NCH, 384], F32); t_ysq = Tk()
        gst = k.sb("gst", [128, 6, NCH * 6], F32); t_gst = Tk()
        potmp = [k.sb("potmp%d" % i, [128, GB], F32) for i in range(2)]; t_potmp = [Tk() for _ in range(2)]
        mixT = k.sb("mixT", [128, 8, GB], BF16); t_mixT = [Tk() for _ in range(8)]
        xr = [k.sb("xrb0", [128, D], F32)] * 2; t_xr = [Tk()] * 2
        NPD = 3
        psD = [k.ps("psD%d" % i, [128, 1024], F32) for i in range(NPD)]
        t_psD = [Tk() for _ in range(NPD)]
        psX = k.ps("psX", [128, 512], F32)
        t_psX = [Tk() for _ in range(2)]
        psTb = k.ps("psTb", [128, 2, 384], BF16)
        t_psTb = Tk()
        st = {"n": 0, "x": 0, "s": 0, "pt": 0}

        def nps():
            return psX[:, 0:256], t_psX[0]

        def npd():
            i = st["n"] % NPD
            st["n"] += 1
            return psD[i], t_psD[i]

        def b3(col):
            return bcast_mid(col, GB)

        def h6(ps):
            return ps[:, 0:768].rearrange("p (a b) -> p a b", a=6)

        def pre_head(gb):
            gs = slice(gb * GB, (gb + 1) * GB)
            tprw = self.t_prw[(gb * GB) // G]
            k.dma("sp", pt[:], self.prw[:, gs].rearrange("(c p) t -> p c t", p=128), r=[tprw],
                  w=[t_pt] + t_alias)
            yield
            k.op("pool", lambda h: h.tensor_tensor(out=z[:, :, 1:GB], in0=pt[:, :, 0:GB - 1], in1=pt[:, :, 1:GB],
                                                   op=ALU.subtract), r=[t_pt] + t_alias, w=t_z)
            k.op("pool", lambda h: h.tensor_tensor(out=z[:, :, 0:1], in0=halo[:], in1=pt[:, :, 0:1],
                                                   op=ALU.subtract), r=[t_pt, t_halo] + t_alias, w=t_z)
            yield
            k.op("dve", lambda h: h.tensor_tensor(out=z[:], in0=z[:], in1=bcast_mid(mu[:], GB), op=ALU.mult),
                 r=t_z + [t_mu], w=t_z)
            yield
            k.op("dve", lambda h: h.tensor_tensor(out=z[:], in0=z[:], in1=pt[:], op=ALU.add),
                 r=t_z + [t_pt] + t_alias, w=t_z)
            k.op("pool", lambda h: h.tensor_copy(out=halo[:], in_=pt[:, :, GB - 1:GB]), r=[t_pt] + t_alias,
                 w=[t_halo])
            yield
            k.op("act", lambda h: h.activation(out=twa[0:64, :], in_=z[0:64, 9, :], func=AF.Tanh), r=t_z, w=[t_twa])
            k.op("act", lambda h: h.activation(out=twa[64:128, :], in_=z[64:128, 9, :], func=AF.Copy),
                 r=t_z, w=[t_twa])
            k.op("act", lambda h: h.activation(out=sg[:], in_=z[:, 10, :], func=AF.Sigmoid), r=t_z, w=[t_sg])
            yield

        def pre_fc(gb, fc):
            d = GS[gb % NGS]
            dp = GP[gb % NGP]
            zr, zk, zv = z[:, fc, :], z[:, 3 + fc, :], z[:, 6 + fc, :]
            tz = t_z
            A_, B_, C_, D_, E_ = tA[:, fc, :], tB[:, fc, :], tC[:, fc, :], tD[:, fc, :], tE[:, fc, :]
            tA_, tB_, tC_, tD_, tE_ = t_tA[fc], t_tB[fc], t_tC[fc], t_tD[fc], t_tE[fc]
            col = lambda n: pc[:, n, fc:fc + 1]
            fs = slice(fc * 128, (fc + 1) * 128)
            csf = cs[:, fc, :]
            csC = csf.rearrange("p (c t) -> p c t", t=C)[:, :, C - 1]
            k.op("act", lambda h: h.activation(out=A_, in_=zk, func=AF.Copy, scale=col(2)), r=tz + [t_pc], w=[tA_])
            ps, t_ps = nps()
            k.op("pe", lambda h: h.matmul(ps, lhsT=waup[0:64, fs], rhs=twa[0:64, :], start=True, stop=True,
                                          tile_position=(0, 0)), r=[t_lw, t_twa], w=[t_ps])
            k.op("act", lambda h: h.activation(out=sig[:, fc, :], in_=ps, func=AF.Sigmoid, bias=col(0)),
                 r=[t_ps, t_pc], w=[t_sig[fc]])
            ps, t_ps = nps()
            k.op("pe", lambda h: h.matmul(ps, lhsT=waup[64:128, fs], rhs=twa[64:128, :], start=True, stop=True,
                                          tile_position=(64, 0)), r=[t_lw, t_twa], w=[t_ps])
            k.op("act", lambda h: h.activation(out=aa[:, fc, :], in_=ps, func=AF.Sigmoid, bias=col(1)),
                 r=[t_ps, t_pc], w=[t_aa[fc]])
            ps, t_ps = nps()
            k.op("pe", lambda h: h.matmul(ps, lhsT=gup[:, fs], rhs=sg[:], start=True, stop=True),
                 r=[t_lw, t_sg], w=[t_ps])
            k.op("dve", lambda h: h.tensor_copy(out=dp["gT"][:, fc, :], in_=ps), r=[t_ps], w=[dp["t_gT"][fc]])
            yield
            k.op("pool", lambda h: h.tensor_tensor(out=b16[:, fc, :], in0=A_, in1=A_, op=ALU.mult),
                 r=[tA_], w=[t_b16[fc]])
            k.op("dve", lambda h: h.tensor_tensor_scan(out=csf, data0=rmask[:], data1=sig[:, fc, :], initial=0.0,
                                                       op0=ALU.mult, op1=ALU.add),
                 r=[t_rmask, t_sig[fc]], w=[t_cs[fc]])
            k.op("act", lambda h: h.activation(out=kmod[:, fc, :], in_=aa[:, fc, :], func=AF.Identity,
                                               scale=col(3), bias=col(5)), r=[t_aa[fc], t_pc], w=[t_kmod[fc]])
            yield
            ps, t_ps = nps()
            k.op("pe", lambda h: h.matmul(ps, lhsT=bones[:], rhs=b16[:, fc, :], start=True, stop=True),
                 r=[t_bones, t_b16[fc]], w=[t_ps])
            k.op("act", lambda h: h.activation(out=B_, in_=ps, func=AF.Sqrt), r=[t_ps], w=[tB_])
            k.op("pool", lambda h: h.tensor_tensor(out=kmod[:, fc, :], in0=kmod[:, fc, :], in1=zk, op=ALU.mult),
                 r=[t_kmod[fc]] + tz, w=[t_kmod[fc]])
            k.op("act", lambda h: h.activation(out=C_, in_=csf, func=AF.Exp, scale=c0), r=[t_cs[fc]], w=[tC_])
            k.op("act", lambda h: h.activation(out=D_, in_=csf, func=AF.Exp, scale=-c0), r=[t_cs[fc]], w=[tD_])
            k.op("pool", lambda h: h.tensor_tensor(
                out=E_.rearrange("p (c t) -> p c t", t=C), in0=csC.unsqueeze(2).to_broadcast([128, NCH, C]),
                in1=csf.rearrange("p (c t) -> p c t", t=C), op=ALU.subtract), r=[t_cs[fc]], w=[tE_])
            k.op("act", lambda h: h.activation(out=d["WC"][:, fc, :], in_=csC, func=AF.Exp, scale=-c0),
                 r=[t_cs[fc]], w=[d["t_WC"][fc]])
            yield
            k.op("dve", lambda h: h.tensor_scalar(out=B_, in0=B_, scalar1=1e-12, scalar2=None, op0=ALU.max),
                 r=[tB_], w=[tB_])
            k.op("dve", lambda h: h.reciprocal(out=B_, in_=B_), r=[tB_], w=[tB_])
            k.op("dve", lambda h: h.tensor_tensor(out=d["kt"][:, fc, :], in0=kmod[:, fc, :], in1=C_, op=ALU.mult),
                 r=[t_kmod[fc], tC_], w=[d["t_kt"][fc]])
            k.op("dve", lambda h: h.tensor_tensor(out=d["rt"][:, fc, :], in0=zr, in1=D_, op=ALU.mult),
                 r=tz + [tD_], w=[d["t_rt"][fc]])
            k.op("act", lambda h: h.activation(out=E_, in_=E_, func=AF.Exp, scale=-c0), r=[tE_], w=[tE_])
            yield
            k.op("pool", lambda h: h.tensor_tensor(out=kkn[:, fc, :], in0=A_, in1=B_, op=ALU.mult),
                 r=[tA_, tB_], w=[t_kkn[fc]])
            k.op("dve", lambda h: h.tensor_tensor(out=d["k2"][:, fc, :], in0=kmod[:, fc, :], in1=E_, op=ALU.mult),
                 r=[t_kmod[fc], tE_], w=[d["t_k2"][fc]])
            k.op("pool", lambda h: h.tensor_tensor(out=D_, in0=csf, in1=sig[:, fc, :], op=ALU.subtract),
                 r=[t_cs[fc], t_sig[fc], d["t_rt"][fc]], w=[tD_])
            yield
            k.op("pool", lambda h: h.tensor_tensor(out=A_, in0=zr, in1=kmod[:, fc, :], op=ALU.mult),
                 r=tz + [t_kmod[fc], t_kkn[fc]], w=[tA_])
            k.op("dve", lambda h: h.tensor_tensor(out=beta[:, fc, :], in0=kkn[:, fc, :], in1=aa[:, fc, :],
                                                  op=ALU.mult), r=[t_kkn[fc], t_aa[fc]], w=[t_beta[fc]])
            k.op("act", lambda h: h.activation(out=D_, in_=D_, func=AF.Exp, scale=-c0), r=[tD_], w=[tD_])
            yield
            k.op("act", lambda h: h.activation(out=b16[:, fc, :], in_=A_, func=AF.Copy, scale=col(4)),
                 r=[tA_, t_pc], w=[t_b16[fc]])
            k.op("dve", lambda h: h.tensor_tensor(out=d["bt"][:, fc, :], in0=beta[:, fc, :], in1=C_, op=ALU.mult),
                 r=[t_beta[fc], tC_], w=[d["t_bt"][fc]])
            k.op("dve", lambda h: h.tensor_tensor(out=d["b2"][:, fc, :], in0=beta[:, fc, :], in1=E_, op=ALU.mult),
                 r=[t_beta[fc], tE_], w=[d["t_b2"][fc]])
            k.op("dve", lambda h: h.scalar_tensor_tensor(out=d["at"][:, fc, :], in0=kkn[:, fc, :], scalar=-1.0,
                                                         in1=D_, op0=ALU.mult, op1=ALU.mult),
                 r=[t_kkn[fc], tD_], w=[d["t_at"][fc]])
            yield
            ps, t_ps = nps()
            k.op("pe", lambda h: h.matmul(ps, lhsT=bones[:], rhs=b16[:, fc, :], start=True, stop=True),
                 r=[t_bones, t_b16[fc]], w=[t_ps])
            k.op("dve", lambda h: h.tensor_tensor(out=dp["bonus"][:, fc, :], in0=ps, in1=zv, op=ALU.mult),
                 r=[t_ps] + tz, w=[dp["t_bonus"][fc]])
            for nm, pn, eng in [("at", "atp", "pool"), ("rt", "rtp", "act"), ("bt", "btp", "pool")]:
                for par in range(2):
                    hb = par * 64
                    src_ = d[nm][hb:hb + 64, fc, :].rearrange("p (c t) -> p c t", t=C)
                    dst_ = d[pn][hb:hb + 64, fc, :, par, :]
                    if eng == "pool":
                        k.op("pool", lambda h, src_=src_, dst_=dst_: h.tensor_copy(out=dst_, in_=src_),
                             r=[d["t_" + nm][fc]], w=[d["t_" + pn][fc]])
                    else:
                        k.op("act", lambda h, src_=src_, dst_=dst_: h.activation(out=dst_, in_=src_, func=AF.Copy),
                             r=[d["t_" + nm][fc]], w=[d["t_" + pn][fc]])
            yield

        def pre_tail(gb):
            d = GS[gb % NGS]
            for c in range(NCH):
                cl = slice(c * C, (c + 1) * C)
                ps, t_ps = npd()
                for fc in range(3):
                    k.op("pe", lambda h, ps=ps, fc=fc, cl=cl: h.transpose(
                        out=ps[:, fc * 128:(fc + 1) * 128], in_=z[:, 6 + fc, cl], identity=self.ident_f[:]),
                         r=t_z + [self.t_const], w=[t_ps])
                k.op("act", lambda h, ps=ps, c=c: h.activation(out=d["Vg"][:, c, :], in_=ps[:, 0:384], func=AF.Copy),
                     r=[t_ps], w=[d["t_Vg"][c]])
                yield

        def pre(gb):
            yield from pre_head(gb)
            gens = [pre_fc(gb, fc) for fc in range(3)]
            gens.append(pre_tail(gb))
            while gens:
                for g_ in list(gens):
                    try:
                        next(g_)
                    except StopIteration:
                        gens.remove(g_)
                yield

        def chunkA(q):
            gb, c = q // NCH, q % NCH
            d = GS[gb % NGS]
            e = CS[q % NR]
            cl = slice(c * C, (c + 1) * C)

            def amat(lname, pname, *dsts):
                ps, t_ps = npd()
                for fc in range(3):
                    k.op("pe", lambda h, ps=ps, fc=fc: h.matmul(
                        ps[:, fc * 256:(fc + 1) * 256], lhsT=d[lname][:, fc, cl],
                        rhs=d[pname][:, fc, c, :, :].rearrange("p a b -> p (a b)"), start=True, stop=True),
                         r=[d["t_" + lname][fc], d["t_" + pname][fc]], w=[t_ps])
                for dst, mi in dsts:
                    k.op("dve", lambda h, ps=ps, dst=dst, mi=mi: h.tensor_tensor(
                        out=e[dst][:], in0=h6(ps), in1=msk[:, mi, :].unsqueeze(1).to_broadcast([128, 6, 128]),
                        op=ALU.mult), r=[t_ps, t_msk], w=[e["t_" + dst]])
            amat("bt", "atp", ("X0", 3), ("X12", 5))
            amat("at", "btp", ("Xt0", 4))
            for n_, nm_ in enumerate(["k2", "b2"]):
                for fc in range(3):
                    k.op("pe", lambda h, n_=n_, fc=fc, nm_=nm_: h.transpose(
                        out=psTb[:, n_, fc * 128:(fc + 1) * 128], in_=d[nm_][:, fc, cl], identity=self.ident_b[:]),
                         r=[d["t_" + nm_][fc], self.t_const], w=[t_psTb])
            k.op("act", lambda h: h.activation(out=e["kbtok"][:], in_=psTb[:], func=AF.Copy),
                 r=[t_psTb], w=[e["t_kbtok"]])
            yield
            amat("kt", "atp", ("Lak", 0))
            amat("bt", "rtp", ("Arb", 1))
            amat("kt", "rtp", ("Ark", 1))
            idb = self.ident_b[:].unsqueeze(1).to_broadcast([128, 6, 128])
            k.op("pool", lambda h: h.tensor_tensor(out=e["P0"][:], in0=e["X0"][:], in1=idb, op=ALU.add),
                 r=[e["t_X0"], self.t_const], w=[e["t_P0"]])
            yield
            nsteps = 5
            for lv in range(1, nsteps + 2):
                Xc, Xtc = e["X%d" % ((lv - 1) % 2)], e["Xt%d" % ((lv - 1) % 2)]
                t_Xc, t_Xtc = e["t_X%d" % ((lv - 1) % 2)], e["t_Xt%d" % ((lv - 1) % 2)]
                Xn, Xtn = e["X%d" % (lv % 2)], e["Xt%d" % (lv % 2)]
                t_Xn, t_Xtn = e["t_X%d" % (lv % 2)], e["t_Xt%d" % (lv % 2)]
                if lv >= 2:
                    Pold, Pnew = e["P%d" % (lv % 2)], e["P%d" % ((lv - 1) % 2)]
                    t_Pold, t_Pnew = e["t_P%d" % (lv % 2)], e["t_P%d" % ((lv - 1) % 2)]
                    ps, t_ps = npd()
                    for hd in range(6):
                        k.op("pe", lambda h, ps=ps, hd=hd: h.matmul(ps[:, hd * 128:(hd + 1) * 128],
                                                                    lhsT=Xtc[:, hd, :], rhs=Pold[:, hd, :],
                                                                    start=True, stop=True),
                             r=[t_Xtc, t_Pold], w=[t_ps])
                    k.op("dve", lambda h, ps=ps: h.tensor_tensor(out=Pnew[:], in0=h6(ps), in1=Pold[:], op=ALU.add),
                         r=[t_ps, t_Pold], w=[t_Pnew])
                if lv <= nsteps:
                    ps, t_ps = npd()
                    for hd in range(6):
                        k.op("pe", lambda h, ps=ps, hd=hd: h.matmul(ps[:, hd * 128:(hd + 1) * 128],
                                                                    lhsT=Xc[:, hd, :], rhs=Xtc[:, hd, :],
                                                                    start=True, stop=True),
                             r=[t_Xc, t_Xtc], w=[t_ps])
                    k.op("act", lambda h, ps=ps: h.activation(out=Xtn[:], in_=h6(ps), func=AF.Copy),
                         r=[t_ps], w=[t_Xtn])
                    if lv < nsteps:
                        ps, t_ps = npd()
                        for hd in range(6):
                            k.op("pe", lambda h, ps=ps, hd=hd: h.matmul(ps[:, hd * 128:(hd + 1) * 128],
                                                                        lhsT=Xtc[:, hd, :], rhs=Xc[:, hd, :],
                                                                        start=True, stop=True),
                                 r=[t_Xc, t_Xtc], w=[t_ps])
                        k.op("act", lambda h, ps=ps: h.activation(out=Xn[:], in_=h6(ps), func=AF.Copy),
                             r=[t_ps], w=[t_Xn])
                yield

        def chain(q):
            gb, c = q // NCH, q % NCH
            d = GS[gb % NGS]
            e = CS[q % NR]
            cl = slice(c * C, (c + 1) * C)
            Pf, t_Pf = e["P1"], e["t_P1"]
            Vg, t_Vg = d["Vg"], d["t_Vg"]
            ps, t_ps = npd()
            for fc in range(3):
                k.op("pe", lambda h, ps=ps, fc=fc: h.matmul(ps[:, fc * 128:(fc + 1) * 128], lhsT=d["at"][:, fc, cl],
                                                            rhs=Tbd[:, fc, :], start=True, stop=False),
                     r=[d["t_at"][fc], t_Tb], w=[t_ps])
                for hd in (2 * fc, 2 * fc + 1):
                    k.op("pe", lambda h, ps=ps, hd=hd: h.matmul(ps[:, hd * 64:(hd + 1) * 64], lhsT=e["Lak"][:, hd, :],
                                                                rhs=Vg[:, c, hd * 64:(hd + 1) * 64], start=False,
                                                                stop=(hd % 2 == 1)),
                         r=[e["t_Lak"], t_Vg[c]], w=[t_ps])
            k.op("act", lambda h, ps=ps: h.activation(out=r0b[:], in_=ps[:, 0:384], func=AF.Copy), r=[t_ps], w=[t_r0b])
            yield
            ps, t_ps = npd()
            for hd in range(6):
                k.op("pe", lambda h, ps=ps, hd=hd: h.matmul(ps[:, hd * 64:(hd + 1) * 64], lhsT=Pf[:, hd, :],
                                                            rhs=r0b[:, hd * 64:(hd + 1) * 64], start=True, stop=True),
                     r=[t_Pf, t_r0b], w=[t_ps])
            k.op("act", lambda h, ps=ps: h.activation(out=u1b[:], in_=ps[:, 0:384], func=AF.Copy), r=[t_ps], w=[t_u1b])
            yield
            ps, t_ps = npd()
            for hd in range(6):
                k.op("pe", lambda h, ps=ps, hd=hd: h.matmul(ps[:, hd * 64:(hd + 1) * 64], lhsT=e["X12"][:, hd, :],
                                                            rhs=u1b[:, hd * 64:(hd + 1) * 64], start=True, stop=True),
                     r=[e["t_X12"], t_u1b], w=[t_ps])
            k.op("act", lambda h, ps=ps: h.activation(out=u2b[:], in_=ps[:, 0:384], func=AF.Copy), r=[t_ps], w=[t_u2b])
            yield
            ps, t_ps = npd()
            for hd in range(6):
                k.op("pe", lambda h, ps=ps, hd=hd: h.matmul(ps[:, hd * 64:(hd + 1) * 64], lhsT=Pf[:, hd, :],
                                                            rhs=u2b[:, hd * 64:(hd + 1) * 64], start=True, stop=True),
                     r=[t_Pf, t_u2b], w=[t_ps])
            k.op("dve", lambda h, ps=ps: h.tensor_tensor(out=Ub[:], in0=ps[:, 0:384], in1=u1b[:], op=ALU.add),
                 r=[t_ps, t_u1b], w=[t_Ub])
            yield
            ps2, t_ps2 = npd()
            kb = e["kbtok"]
            for fc in range(3):
                fs = slice(fc * 128, (fc + 1) * 128)
                k.op("pe", lambda h, fs=fs: h.matmul(ps2[:, fs], lhsT=kb[:, 1, fs], rhs=Ub[:, fs], start=True,
                                                     stop=False), r=[e["t_kbtok"], t_Ub], w=[t_ps2])
                k.op("pe", lambda h, fs=fs: h.matmul(ps2[:, fs], lhsT=kb[:, 0, fs], rhs=Vg[:, c, fs], start=False,
                                                     stop=True), r=[e["t_kbtok"], t_Vg[c]], w=[t_ps2])
            ps, t_ps = npd()
            for fc in range(3):
                k.op("pe", lambda h, ps=ps, fc=fc: h.matmul(ps[:, fc * 128:(fc + 1) * 128], lhsT=d["rt"][:, fc, cl],
                                                            rhs=Tbd[:, fc, :], start=True, stop=False),
                     r=[d["t_rt"][fc], t_Tb], w=[t_ps])
                for hd in (2 * fc, 2 * fc + 1):
                    hs = slice(hd * 64, (hd + 1) * 64)
                    k.op("pe", lambda h, ps=ps, hd=hd, hs=hs: h.matmul(ps[:, hs], lhsT=e["Arb"][:, hd, :],
                                                                       rhs=Ub[:, hs], start=False, stop=False),
                         r=[e["t_Arb"], t_Ub], w=[t_ps])
                    k.op("pe", lambda h, ps=ps, hd=hd, hs=hs: h.matmul(ps[:, hs], lhsT=e["Ark"][:, hd, :],
                                                                       rhs=Vg[:, c, hs], start=False,
                                                                       stop=(hd % 2 == 1)),
                         r=[e["t_Ark"], t_Vg[c]], w=[t_ps])
            for par in range(2):
                hb = par * 64
                tv = Tst[hb:hb + 64, :, hb:hb + 64]
                k.op("dve", lambda h, hb=hb, tv=tv: h.tensor_tensor(
                    out=tv, in0=tv, in1=bcast_mid(d["WC"][hb:hb + 64, :, c], 64), op=ALU.mult),
                     r=[t_T] + d["t_WC"], w=[t_T])
                k.op("dve", lambda h, hb=hb, tv=tv: h.tensor_tensor(
                    out=tv, in0=ps2[hb:hb + 64, 0:384].rearrange("p (f x) -> p f x", f=3)[:, :, hb:hb + 64], in1=tv,
                    op=ALU.add), r=[t_ps2, t_T], w=[t_T])
                k.op("pool", lambda h, hb=hb, tv=tv: h.tensor_copy(out=Tbd[hb:hb + 64, :, hb:hb + 64], in_=tv),
                     r=[t_T], w=[t_Tb])
            k.op("act", lambda h, ps=ps: h.activation(out=GP[gb % NGP]["ysb"][:, c, :], in_=ps[:, 0:384],
                                                      func=AF.Copy), r=[t_ps], w=[GP[gb % NGP]["t_ysb"][c]])
            yield

        def post(gb):
            d = GP[gb % NGP]
            gs = slice(gb * GB, (gb + 1) * GB)
            ysb, t_ysb = d["ysb"], d["t_ysb"]
            k.dma("sp", mixT[:, 0:5, :], self.mxa[:, gs].rearrange("(c p) t -> p c t", p=128),
                  r=[self.t_mxa[(gb * GB) // G]], w=t_mixT[0:5])
            yv = ysb[:].rearrange("p c (a b) -> p (c a) b", b=64)
            qv = ysq[:].rearrange("p c (a b) -> p (c a) b", b=64)
            k.op("dve", lambda h: h.tensor_reduce(out=gst[:, 0, :], in_=yv, axis=AX.X, op=ALU.add),
                 r=t_ysb, w=[t_gst])
            k.op("pool", lambda h: h.tensor_tensor(out=ysq[:], in0=ysb[:], in1=ysb[:], op=ALU.mult),
                 r=t_ysb, w=[t_ysq])
            yield
            k.op("dve", lambda h: h.tensor_reduce(out=gst[:, 1, :], in_=qv, axis=AX.X, op=ALU.add),
                 r=[t_ysq], w=[t_gst])
            k.op("dve", lambda h: h.tensor_scalar(out=gst[:, 0:2, :], in0=gst[:, 0:2, :], scalar1=1.0 / 64.0,
                                                  scalar2=None, op0=ALU.mult), r=[t_gst], w=[t_gst])
            k.op("dve", lambda h: h.tensor_tensor(out=gst[:, 2, :], in0=gst[:, 0, :], in1=gst[:, 0, :], op=ALU.mult),
                 r=[t_gst], w=[t_gst])
            k.op("dve", lambda h: h.tensor_tensor(out=gst[:, 3, :], in0=gst[:, 1, :], in1=gst[:, 2, :],
                                                  op=ALU.subtract), r=[t_gst], w=[t_gst])
            yield
            k.op("act", lambda h: h.activation(out=gst[:, 4, :], in_=gst[:, 3, :], func=AF.Sqrt,
                                               bias=pc[:, 6, 0:1]), r=[t_gst, t_pc], w=[t_gst])
            k.op("dve", lambda h: h.reciprocal(out=gst[:, 5, :], in_=gst[:, 4, :]), r=[t_gst], w=[t_gst])
            k.op("dve", lambda h: h.tensor_tensor(out=qv, in0=yv, in1=bcast_mid(gst[:, 0, :], 64), op=ALU.subtract),
                 r=t_ysb + [t_gst, t_ysq], w=[t_ysq])
            yield
            k.op("dve", lambda h: h.tensor_tensor(out=qv, in0=qv, in1=bcast_mid(gst[:, 5, :], 64), op=ALU.mult),
                 r=[t_gst, t_ysq], w=[t_ysq])
            k.op("pool", lambda h: h.tensor_tensor(out=ysq[:], in0=ysq[:],
                                                   in1=lng[:, 0, :].unsqueeze(1).to_broadcast([128, NCH, 384]),
                                                   op=ALU.mult), r=[t_ysq, t_lng], w=[t_ysq])
            k.op("pool", lambda h: h.tensor_tensor(out=ysq[:], in0=ysq[:],
                                                   in1=lng[:, 1, :].unsqueeze(1).to_broadcast([128, NCH, 384]),
                                                   op=ALU.add), r=[t_ysq, t_lng], w=[t_ysq])
            yield
            for fc in range(3):
                ps, t_ps = nps()
                for c2 in range(NCH):
                    k.op("pe", lambda h, ps=ps, fc=fc, c2=c2: h.transpose(
                        out=ps[:, c2 * 128:(c2 + 1) * 128], in_=ysq[:, c2, fc * 128:(fc + 1) * 128],
                        identity=self.ident_f[:]), r=[t_ysq, self.t_const], w=[t_ps])
                i = st["pt"] % 2
                st["pt"] += 1
                k.op("dve", lambda h, ps=ps, fc=fc, i=i: h.tensor_tensor(out=potmp[i][:], in0=ps,
                                                                         in1=d["bonus"][:, fc, :], op=ALU.add),
                     r=[t_ps, d["t_bonus"][fc]], w=[t_potmp[i]])
                k.op("pool", lambda h, fc=fc, i=i: h.tensor_tensor(out=mixT[:, 5 + fc, :], in0=potmp[i][:],
                                                                   in1=d["gT"][:, fc, :], op=ALU.mult),
                     r=[t_potmp[i], d["t_gT"][fc]], w=[t_mixT[5 + fc]])
                yield
            if self.mxb is not None:
                k.dma("sp", self.mxb[:, gs].rearrange("(c p) t -> p c t", p=128), mixT[:, 5:8, :],
                      r=t_mixT[5:8], w=[self.t_mxb])
            for blk in range(GB // 128):
                r0 = gb * GB + blk * 128
                tg = r0 // G
                i = st["x"] % 2
                st["x"] += 1
                k.dma("sp", xr[i][:], src[r0:r0 + 128, :], r=[t_src[tg]], w=[t_xr[i]])
                psw, t_psw = npd()
                for half in range(2):
                    ps, t_ps = psw[:, half * 512:(half + 1) * 512], t_psw
                    for kc in range(8):
                        k.op("pe", lambda h, ps=ps, kc=kc, blk=blk, half=half: h.matmul(
                            ps, lhsT=mixT[:, kc, blk * 128:(blk + 1) * 128],
                            rhs=wout[:, kc, half * 512:(half + 1) * 512], start=(kc == 0), stop=(kc == 7)),
                             r=[t_mixT[kc], t_wout[kc]], w=[t_ps])
                    k.op("dve", lambda h, ps=ps, half=half, i=i: h.tensor_tensor(
                        out=xr[i][:, half * 512:(half + 1) * 512], in0=ps,
                        in1=xr[i][:, half * 512:(half + 1) * 512], op=ALU.add), r=[t_ps, t_xr[i]], w=[t_xr[i]])
                k.dma("sp", dst[r0:r0 + 128, :], xr[i][:], r=[t_xr[i]], w=[t_dst[tg]])
                yield

        ngb = self.dbg.get("mb_ng", NGB)
        nq = ngb * NCH
        for _ in pre(0):
            pass
        pre_done = 1
        post_done = 0
        chainq = 0
        nextA = 0
        actA = []
        doneA = set()
        g_chain = None
        g_pre = None
        g_post = None
        post_ready = []
        while chainq < nq or g_post is not None or post_ready:
            while len(actA) < 3 and nextA < nq and (nextA // NCH) < pre_done and nextA < chainq + NR:
                actA.append([nextA, chunkA(nextA)])
                nextA += 1
            if g_chain is None and chainq < nq and chainq in doneA:
                g_chain = chain(chainq)
            if (g_pre is None and pre_done < ngb and pre_done - NGP < post_done
                    and chainq >= (pre_done - 1) * NCH):
                g_pre = pre(pre_done)
            if g_post is None and post_ready:
                g_post = post(post_ready.pop(0))
            progressed = False
            for which in ("chain", "A", "pre", "post"):
                if which == "chain" and g_chain is not None:
                    progressed = True
                    try:
                        next(g_chain)
                    except StopIteration:
                        g_chain = None
                        if (chainq + 1) % NCH == 0:
                            post_ready.append(chainq // NCH)
                        chainq += 1
                        if chainq < nq and chainq in doneA:
                            g_chain = chain(chainq)
                elif which == "A":
                    for it in list(actA):
                        progressed = True
                        try:
                            next(it[1])
                        except StopIteration:
                            doneA.add(it[0])
                            actA.remove(it)
                elif which == "pre" and g_pre is not None:
                    progressed = True
                    try:
                        next(g_pre)
                    except StopIteration:
                        g_pre = None
                        pre_done += 1
                elif which == "post" and g_post is not None:
                    progressed = True
                    try:
                        next(g_post)
                    except StopIteration:
                        g_post = None
                        post_done += 1
            assert progressed, "scheduler stalled"
        k.pop_scope()

    def build(self):
        for ph in self.phases:
            kind = ph[0]
            srcs = {"x": (self.inp["x"], self.t_x), "xa": (self.xa, self.t_xa), "xb": (self.xb, self.t_xb)}
            if kind == "MA":
                _, l, src = ph
                self.pass_mixa(l, srcs[src][0], srcs[src][1])
            dsts = {"xa": (self.xa, self.t_xa), "xb": (self.xb, self.t_xb), "y": (self.y, self.t_y)}
            if kind == "MB":
                _, l, src, dst = ph
                self.pass_mixb(l, srcs[src][0], srcs[src][1], dsts[dst][0], dsts[dst][1])
            if kind == "F":
                _, l, src, dst, final = ph
                srcs = {"x": (self.inp["x"], self.t_x), "xa": (self.xa, self.t_xa), "xb": (self.xb, self.t_xb)}
                dsts = {"xa": (self.xa, self.t_xa), "xb": (self.xb, self.t_xb), "y": (self.y, self.t_y)}
                self.pass_ffn(l, srcs[src][0], srcs[src][1], dsts[dst][0], dsts[dst][1], final)
        self.k.finish()


FULL_PHASES = [("MA", 0, "x"), ("MB", 0, "x", "xa"), ("F", 0, "xa", "xb", False),
               ("MA", 1, "xb"), ("MB", 1, "xb", "xa"), ("F", 1, "xa", "y", True)]


def build_nc(phases=None, dbg=None):
    dbg = dbg or {}
    nc = bass.Bass("TRN2", target_bir_lowering=False)
    p = Prog(nc, phases or FULL_PHASES, dbg)
    p.build()
    return nc, p


def kernel(**inputs):
    nc, _ = build_nc()
    in_maps = []
    for b in range(8):
        m = {}
        for name in INPUT_SHAPES:
            a = np.asarray(inputs[name], dtype=np.float32)
            m[name] = np.ascontiguousarray(a[b]) if name == 'x' else np.ascontiguousarray(a)
        in_maps.append(m)
    res = run_bass_kernel_spmd(nc, in_maps, core_ids=list(range(8)))
    return np.stack([np.asarray(r["y"]) for r in res.results], axis=0).astype(np.float32)
```
